# Optimizing a Trainium2 kernel written in Bass

```python
import jax, jax.numpy as jnp
from jax import lax
import numpy as np

D_MODEL = 1024
BATCH = 8
SEQ = 4096
DEPTH = 2

GRID_W = 64
CTX_LEN = 256
HEAD_DIM = 64
CONV_W = 512
CONV_K = 31
NA_HEADS = 8
NA_W = NA_HEADS * HEAD_DIM
NA_KH_MAX = 8
NA_KW = 16
NA_QCW = 16
NA_KCW = 32
NA_NCB = GRID_W // NA_QCW
GQA_Q_HEADS = 8
GQA_KV_HEADS = 2
GQA_Q_W = GQA_Q_HEADS * HEAD_DIM
GQA_KV_W = GQA_KV_HEADS * HEAD_DIM
Q_BLOCK = 128
N_BRANCH = 3
FFN_HIDDEN = ((8 * D_MODEL + 767) // 768) * 256
ROPE_THETA = 10000.0
EPS = 1e-6
NEG = -1e30
IN_SPLITS = (2 * CONV_W, NA_W, NA_W, NA_W, GQA_Q_W, GQA_KV_W, GQA_KV_W, N_BRANCH * D_MODEL)
IN_WIDTH = sum(IN_SPLITS)

kernel_name = "hybrid_conv_natten_gqa_diffusion_trunk"


def rmsnorm(x, g):
    xf = x.astype(jnp.float32)
    y = xf * lax.rsqrt(jnp.mean(xf * xf, axis=-1, keepdims=True) + EPS)
    return (y * g.astype(jnp.float32)).astype(x.dtype)


def layernorm(x, g, b):
    xf = x.astype(jnp.float32)
    mu = jnp.mean(xf, axis=-1, keepdims=True)
    var = jnp.mean(jnp.square(xf - mu), axis=-1, keepdims=True)
    y = (xf - mu) * lax.rsqrt(var + EPS)
    return (y * g.astype(jnp.float32) + b.astype(jnp.float32)).astype(x.dtype)


def modulate(h, shift, scale):
    return h * (1 + scale) + shift


def split_in(p):
    return jnp.split(p, np.cumsum(IN_SPLITS)[:-1].tolist(), axis=-1)


def heads(t, n):
    return t.reshape(t.shape[:-1] + (n, HEAD_DIM))


def axial_rope(x, pos_row, pos_col):
    half = x.shape[-1] // 2
    freqs = jnp.power(ROPE_THETA, -jnp.arange(0, half, 2, dtype=jnp.float32) / half)

    def rot(xa, pos):
        ang = pos[:, None] * freqs
        cos = jnp.cos(ang)[None, :, None, :].astype(x.dtype)
        sin = jnp.sin(ang)[None, :, None, :].astype(x.dtype)
        x1, x2 = jnp.split(xa, 2, axis=-1)
        return jnp.concatenate([x1 * cos - x2 * sin, x1 * sin + x2 * cos], axis=-1)

    return jnp.concatenate([rot(x[..., :half], pos_row), rot(x[..., half:], pos_col)], axis=-1)


def conv_branch(a, conv_w, conv_b, ln_g, ln_b, w_out):
    u = a[..., :CONV_W] * jax.nn.sigmoid(a[..., CONV_W:])
    y = lax.conv_general_dilated(
        u, conv_w[:, None, :], window_strides=(1,), padding=[(CONV_K // 2, CONV_K // 2)],
        dimension_numbers=("NWC", "WIO", "NWC"), feature_group_count=CONV_W) + conv_b
    y = jax.nn.silu(layernorm(y, ln_g, ln_b))
    return y @ w_out


def dense_attention(q, k, v):
    B, T, Hq, dh = q.shape
    Hkv = k.shape[2]
    qg = q.reshape(B, T, Hkv, Hq // Hkv, dh) * (dh ** -0.5)
    s = jnp.einsum("bqhgd,bkhd->bhgqk", qg, k).astype(jnp.float32)
    p = jax.nn.softmax(s, axis=-1).astype(v.dtype)
    return jnp.einsum("bhgqk,bkhd->bqhgd", p, v).reshape(B, T, Hq * dh)


def gqa_latent(q, k, v, kc, vc):
    B, S, Hq, dh = q.shape
    Hkv = k.shape[2]
    G = Hq // Hkv
    k_all = jnp.concatenate([k, kc], axis=1)
    v_all = jnp.concatenate([v, vc], axis=1)
    nblk = S // Q_BLOCK
    qb = jnp.moveaxis((q * (dh ** -0.5)).reshape(B, nblk, Q_BLOCK, Hkv, G, dh), 1, 0)

    def one_block(q_blk):
        s = jnp.einsum("bqhgd,bkhd->bhgqk", q_blk, k_all).astype(jnp.float32)
        p = jax.nn.softmax(s, axis=-1).astype(v.dtype)
        return jnp.einsum("bhgqk,bkhd->bqhgd", p, v_all)

    o = lax.map(one_block, qb)
    return jnp.moveaxis(o, 0, 1).reshape(B, S, Hq * dh)


def na_latent(q, k, v, kc, vc, rpb):
    B, S, H, dh = q.shape
    rows = S // GRID_W
    kh = min(NA_KH_MAX, rows)
    q_cols = np.arange(GRID_W).reshape(NA_NCB, NA_QCW)
    key_cols = (np.clip(np.arange(NA_NCB) * NA_QCW - NA_KW // 2, 0, GRID_W - NA_KCW)[:, None]
                + np.arange(NA_KCW))
    win0 = np.clip(q_cols - NA_KW // 2, 0, GRID_W - NA_KW)[..., None]
    kcols = key_cols[:, None, :]
    col_mask = jnp.asarray((kcols >= win0) & (kcols < win0 + NA_KW))
    col_bias_idx = np.clip(kcols - q_cols[..., None] + NA_KW - 1, 0, 2 * NA_KW - 2)
    qg = (q * (dh ** -0.5)).reshape(B, rows, NA_NCB, NA_QCW, H, dh)
    kg = k.reshape(B, rows, GRID_W, H, dh)
    vg = v.reshape(B, rows, GRID_W, H, dh)
    n_loc = kh * NA_KCW

    def one_row(r):
        r0 = jnp.clip(r - kh // 2, 0, rows - kh)
        kb = lax.dynamic_slice_in_dim(kg, r0, kh, axis=1)[:, :, key_cols]
        vb = lax.dynamic_slice_in_dim(vg, r0, kh, axis=1)[:, :, key_cols]
        qr = lax.dynamic_index_in_dim(qg, r, axis=1, keepdims=False)
        drow = r0 + jnp.arange(kh) - r + NA_KH_MAX - 1
        bias = rpb[:, drow][:, :, col_bias_idx].transpose(0, 2, 3, 1, 4)
        s_loc = jnp.einsum("bnqhd,bknchd->bhnqkc", qr, kb).astype(jnp.float32) + bias.astype(jnp.float32)
        s_loc = jnp.where(col_mask[:, :, None, :], s_loc, NEG).reshape(B, H, NA_NCB, NA_QCW, n_loc)
        s_ctx = jnp.einsum("bnqhd,bthd->bhnqt", qr, kc).astype(jnp.float32)
        p = jax.nn.softmax(jnp.concatenate([s_loc, s_ctx], axis=-1), axis=-1).astype(v.dtype)
        p_loc = p[..., :n_loc].reshape(B, H, NA_NCB, NA_QCW, kh, NA_KCW)
        p_ctx = p[..., n_loc:]
        return (jnp.einsum("bhnqkc,bknchd->bnqhd", p_loc, vb)
                + jnp.einsum("bhnqt,bthd->bnqhd", p_ctx, vc))

    o = lax.map(one_row, jnp.arange(rows))
    return jnp.moveaxis(o, 0, 1).reshape(B, S, H * dh)


def merge(br_a, br_b, br_c, gates, w_out):
    ga, gb, gc = jnp.split(gates, N_BRANCH, axis=-1)
    y = jax.nn.sigmoid(ga) * br_a + jax.nn.sigmoid(gb) * br_b + jax.nn.sigmoid(gc) * br_c
    return y @ w_out


def swiglu(h, w_in, w_out):
    gate, up = jnp.split(h @ w_in, 2, axis=-1)
    return (jax.nn.silu(gate) * up) @ w_out


def setup_inputs(seed: int = 0) -> dict:
    key = jax.random.key(seed)
    ks = jax.random.split(key, 23)
    L, D = DEPTH, D_MODEL

    def nrm(k, shape, s):
        return jax.random.normal(k, shape, jnp.float32) * s

    return {
        "x": nrm(ks[0], (BATCH, SEQ, D), 1.0),
        "c": nrm(ks[1], (BATCH, D), 1.0),
        "ctx": nrm(ks[2], (BATCH, CTX_LEN, D), 1.0),
        "c_ctx": nrm(ks[3], (D,), 1.0),
        "w_mod": nrm(ks[4], (L, D, 6 * D), 0.5 * D ** -0.5),
        "b_mod": nrm(ks[5], (L, 6 * D), 0.01),
        "norm1_g": 1.0 + nrm(ks[6], (L, D), 0.05),
        "norm2_g": 1.0 + nrm(ks[7], (L, D), 0.05),
        "w_in": nrm(ks[8], (L, D, IN_WIDTH), D ** -0.5),
        "conv_w": nrm(ks[9], (L, CONV_K, CONV_W), CONV_K ** -0.5),
        "conv_b": nrm(ks[10], (L, CONV_W), 0.01),
        "conv_ln_g": 1.0 + nrm(ks[11], (L, CONV_W), 0.05),
        "conv_ln_b": nrm(ks[12], (L, CONV_W), 0.01),
        "w_conv_out": nrm(ks[13], (L, CONV_W, D), CONV_W ** -0.5),
        "na_rpb": nrm(ks[14], (L, NA_HEADS, 2 * NA_KH_MAX - 1, 2 * NA_KW - 1), 0.1),
        "w_na_out": nrm(ks[15], (L, NA_W, D), NA_W ** -0.5),
        "q_norm_g": 1.0 + nrm(ks[16], (L, HEAD_DIM), 0.05),
        "k_norm_g": 1.0 + nrm(ks[17], (L, HEAD_DIM), 0.05),
        "w_gqa_out": nrm(ks[18], (L, GQA_Q_W, D), GQA_Q_W ** -0.5),
        "w_out": nrm(ks[19], (L, D, D), D ** -0.5),
        "w_ffn_in": nrm(ks[20], (L, D, 2 * FFN_HIDDEN), D ** -0.5),
        "w_ffn_out": nrm(ks[21], (L, FFN_HIDDEN, D), FFN_HIDDEN ** -0.5),
        "final_g": 1.0 + nrm(ks[22], (D,), 0.05),
    }


def reference(x, c, ctx, c_ctx, w_mod, b_mod, norm1_g, norm2_g, w_in, conv_w, conv_b, conv_ln_g,
              conv_ln_b, w_conv_out, na_rpb, w_na_out, q_norm_g, k_norm_g, w_gqa_out, w_out,
              w_ffn_in, w_ffn_out, final_g):
    S = x.shape[1]
    t = jnp.arange(S)
    pos_row = (t // GRID_W).astype(jnp.float32)
    pos_col = (t % GRID_W).astype(jnp.float32)
    silu_c = jax.nn.silu(c)
    silu_cc = jax.nn.silu(c_ctx)
    xc = ctx
    for l in range(DEPTH):
        last = l == DEPTH - 1
        sh1, sc1, g1, sh2, sc2, g2 = jnp.split((silu_c @ w_mod[l] + b_mod[l])[:, None, :], 6, axis=-1)
        csh1, csc1, cg1, csh2, csc2, cg2 = jnp.split(silu_cc @ w_mod[l] + b_mod[l], 6, axis=-1)

        h = modulate(rmsnorm(x, norm1_g[l]), sh1, sc1)
        hc = modulate(rmsnorm(xc, norm1_g[l]), csh1, csc1)
        a, naq, nak, nav, gq, gk, gv, gates = split_in(h @ w_in[l])
        ac, naqc, nakc, navc, gqc, gkc, gvc, gatesc = split_in(hc @ w_in[l])

        na_kc, na_vc = heads(nakc, NA_HEADS), heads(navc, NA_HEADS)
        g_kc = rmsnorm(heads(gkc, GQA_KV_HEADS), k_norm_g[l])
        g_vc = heads(gvc, GQA_KV_HEADS)

        br_a = conv_branch(a, conv_w[l], conv_b[l], conv_ln_g[l], conv_ln_b[l], w_conv_out[l])
        br_b = na_latent(heads(naq, NA_HEADS), heads(nak, NA_HEADS), heads(nav, NA_HEADS),
                         na_kc, na_vc, na_rpb[l]) @ w_na_out[l]
        q = axial_rope(rmsnorm(heads(gq, GQA_Q_HEADS), q_norm_g[l]), pos_row, pos_col)
        k = axial_rope(rmsnorm(heads(gk, GQA_KV_HEADS), k_norm_g[l]), pos_row, pos_col)
        br_c = gqa_latent(q, k, heads(gv, GQA_KV_HEADS), g_kc, g_vc) @ w_gqa_out[l]
        x = x + g1 * merge(br_a, br_b, br_c, gates, w_out[l])
        x = x + g2 * swiglu(modulate(rmsnorm(x, norm2_g[l]), sh2, sc2), w_ffn_in[l], w_ffn_out[l])

        if not last:
            cbr_a = conv_branch(ac, conv_w[l], conv_b[l], conv_ln_g[l], conv_ln_b[l], w_conv_out[l])
            cbr_b = dense_attention(heads(naqc, NA_HEADS), na_kc, na_vc) @ w_na_out[l]
            cq = rmsnorm(heads(gqc, GQA_Q_HEADS), q_norm_g[l])
            cbr_c = dense_attention(cq, g_kc, g_vc) @ w_gqa_out[l]
            xc = xc + cg1 * merge(cbr_a, cbr_b, cbr_c, gatesc, w_out[l])
            xc = xc + cg2 * swiglu(modulate(rmsnorm(xc, norm2_g[l]), csh2, csc2), w_ffn_in[l], w_ffn_out[l])
    return rmsnorm(x, final_g)
```

```python
from contextlib import ExitStack
import os
import numpy as np
import concourse.bass as bass
import concourse.mybir as mybir
from concourse.bass_utils import run_bass_kernel_spmd

F32 = mybir.dt.float32
BF16 = mybir.dt.bfloat16
AF = mybir.ActivationFunctionType
ALU = mybir.AluOpType

D = 1024
S = 4096
T = 256
ST = S + T
L = 2
GW = 64
CONV_K = 31
HALO = 15
FFN = 2816
INW = 6400
EPS = 1e-6
NEG = -1e30
NBLK = 9
ENGS = ("pe", "act", "dve", "pool", "sp")
NDMA = 8


def blk_info(b):
    if b < 8:
        return b * 512, 512
    return S, T


class _Rec:
    def __init__(self):
        self.call = None

    def __getattr__(self, name):
        def f(*a, **k):
            self.call = (name, a, k)
            return self
        return f


def _capture(fn):
    r = _Rec()
    fn(r)
    assert r.call is not None
    return r.call


class Prog:
    def __init__(self, nc):
        self.nc = nc
        self.streams = {e: [] for e in ENGS}
        self.count = {e: 0 for e in ENGS}
        self.known = {e: {} for e in ENGS}
        self.res_w = {}
        self.res_r = {}
        self.dma_use = {}
        self.dma_rr = {e: 0 for e in ENGS}
        self.sems = {}

    def _deps(self, eng, reads, writes):
        need = {}

        def add(tok):
            if tok is None:
                return
            k, v = tok
            if k == "pe" and eng == "pe":
                return
            if need.get(k, 0) < v:
                need[k] = v

        for r in reads:
            add(self.res_w.get(r))
        for w in writes:
            add(self.res_w.get(w))
            for k, v in self.res_r.get(w, {}).items():
                add((k, v))
        return need

    def _commit(self, tok, reads, writes):
        k, v = tok
        for r in reads:
            d = self.res_r.setdefault(r, {})
            if d.get(k, 0) < v:
                d[k] = v
        for w in writes:
            self.res_w[w] = tok
            self.res_r[w] = {}

    def _emit_waits(self, eng, need):
        kn = self.known[eng]
        waits = []
        for k, v in need.items():
            if kn.get(k, 0) < v:
                kn[k] = v
                waits.append((k, v))
        return waits

    def op(self, eng, fn, reads=(), writes=()):
        need = self._deps(eng, reads, writes)
        waits = self._emit_waits(eng, need)
        self.count[eng] += 1
        tok = (eng, self.count[eng])
        self.streams[eng].append((_capture(fn), waits, (eng, 1)))
        self._commit(tok, reads, writes)
        return tok

    def dma(self, q, fn, reads=(), writes=()):
        need = self._deps(q, reads, writes)
        i = self.dma_rr[q]
        self.dma_rr[q] = (i + 1) % NDMA
        key = ("dma", q, i)
        u = self.dma_use.get(key, 0)
        if u > 0 and need.get(key, 0) < 16 * u:
            need[key] = 16 * u
        waits = self._emit_waits(q, need)
        self.dma_use[key] = u + 1
        tok = (key, 16 * (u + 1))
        self.streams[q].append((_capture(fn), waits, (key, 16)))
        self._commit(tok, reads, writes)
        return tok

    def wait_all(self, eng, toks):
        need = {}
        for k, v in toks:
            if need.get(k, 0) < v:
                need[k] = v
        waits = self._emit_waits(eng, need)
        self.streams[eng].append((None, waits, None))

    def barrier(self):
        toks = self.final_tokens()
        for e in ENGS:
            self.wait_all(e, toks)

    def final_tokens(self):
        toks = [(e, self.count[e]) for e in ENGS if self.count[e]]
        toks += [(key, 16 * u) for key, u in self.dma_use.items()]
        return toks

    def emit(self, stack):
        nc = self.nc
        targets = {e: set() for e in ENGS}
        keys = set()
        for e in ENGS:
            for fn, waits, inc in self.streams[e]:
                for k, v in waits:
                    keys.add(k)
                    if isinstance(k, str):
                        targets[k].add(v)
                if inc is not None and not isinstance(inc[0], str):
                    keys.add(inc[0])
        rank = {}
        for e in ENGS:
            srt = sorted(targets[e])
            rank[e] = {v: i + 1 for i, v in enumerate(srt)}
            if srt:
                keys.add(e)
        for k in sorted(keys, key=str):
            name = k if isinstance(k, str) else "d_%s_%d" % (k[1], k[2])
            self.sems[k] = stack.enter_context(nc.semaphore("s_" + name))
        block = stack.enter_context(nc.Block())

        def runner(ename):
            def run(engine):
                idx = 0
                for fn, waits, inc in self.streams[ename]:
                    for k, v in waits:
                        engine.wait_ge(self.sems[k], rank[k][v] if isinstance(k, str) else v)
                    if fn is not None:
                        name, a, kw = fn
                        ins = getattr(engine, name)(*a, **kw)
                        if isinstance(inc[0], str):
                            idx += 1
                            if idx in rank[ename]:
                                ins.then_inc(self.sems[ename], 1)
                        else:
                            ins.then_inc(self.sems[inc[0]], inc[1])
            return run

        block.tensor(runner("pe"))
        block.scalar(runner("act"))
        block.vector(runner("dve"))
        block.gpsimd(runner("pool"))
        block.sync(runner("sp"))


def build(n_layers=L, dbg=()):
    nc = bass.Bass("TRN2", target_bir_lowering=False)
    P = Prog(nc)

    def din(name, shape, dt=F32):
        return nc.dram_tensor(name, shape, dt, kind="ExternalInput").ap()

    def dscr(name, shape, dt=BF16):
        kind = "ExternalOutput" if name in dbg else "Internal"
        return nc.dram_tensor(name, shape, dt, kind=kind).ap()

    x_d = din("x", [S, D])
    ctx_d = din("ctx", [T, D])
    cT_d = din("cT", [128, 8, 2])
    wmod_d = din("w_mod", [L, D, 6 * D])
    bmod_d = din("bmodT", [L, 128, 48])
    gn_d = din("gnT", [128, 2 * L + 1, 8])
    win_d = din("w_in", [L, D, INW])
    convw_d = din("convwT", [L, 128, 4, CONV_K])
    convv_d = din("convvT", [L, 128, 3, 4])
    wco_d = din("w_conv_out", [L, 512, D])
    tint_d = din("tint", [L, 128, 8, 22 * 64])
    tfull_d = din("tfull", [L, 128, 8, 14 * 64])
    wno_d = din("w_na_out", [L, 512, D])
    qkg_d = din("qkgT", [L, 128, 2])
    wgo_d = din("w_gqa_out", [L, 512, D])
    wout_d = din("w_out", [L, D, D])
    wfi_d = din("w_ffn_in", [L, D, 2 * FFN])
    wfo_d = din("w_ffn_out", [L, FFN, D])
    rope_d = din("rope", [2, 128, S])
    cst_d = din("consts", [4, 128, 128])
    out_d = nc.dram_tensor("out", [S, D], F32, kind="ExternalOutput").ap()

    xT_d = dscr("xT", [D, ST], F32)
    UW = S + 2 * HALO
    UCW = T + 2 * HALO
    uT_d = dscr("uT", [512, UW + UCW])
    naq_d = dscr("naqT", [512, ST])
    nak_d = dscr("nakT", [512, ST])
    gqr_d = dscr("gqrT", [512, ST])
    gkr_d = dscr("gkrT", [128, ST])
    gat_d = dscr("gatesT", [3072, ST])
    vaug_d = dscr("vaug", [ST, 10, 128])
    ca_d = dscr("caT", [512, ST])
    nao_d = dscr("naoT", [512, ST])
    gqo_d = dscr("gqoT", [512, ST])

    with ExitStack() as top:
        uid = [0]

        def sbuf(st, name, shape, dt=F32):
            uid[0] += 1
            return st.enter_context(nc.sbuf_tensor("sb%d_%s" % (uid[0], name), shape, dt))

        psum_all = top.enter_context(nc.psum_tensor("pbanks", [128, 8, 512], F32))
        psum = [psum_all[:, i, :] for i in range(8)]
        prr = [0]

        def pbank():
            i = prr[0]
            prr[0] = (i + 1) % 8
            return i

        ident_f = sbuf(top, "ident_f", [128, 128])
        cst_b = sbuf(top, "cst_b", [128, 4, 128], BF16)
        modv = sbuf(top, "modv", [128, L, 48, 2])
        bmod = sbuf(top, "bmod", [128, L, 48])
        gn = sbuf(top, "gn", [128, 2 * L + 1, 8])
        Amod = sbuf(top, "Amod", [128, L, 2, 8, 2])
        rstd = sbuf(top, "rstd", [128, ST])
        stats = rstd
        sil = sbuf(top, "sil", [128, 8, 2])
        cT = sbuf(top, "cT_s", [128, 8, 2])
        zpad = sbuf(top, "zpad", [128, 4, HALO], BF16)
        convw = sbuf(top, "convw", [128, L, 4, CONV_K])
        convv = sbuf(top, "convv", [128, L, 3, 4])
        qkg = sbuf(top, "qkg", [128, L, 2])

        IDB = cst_b[:, 0, :]
        ONESB = cst_b[:, 1, :]
        BD64 = cst_b[:, 2, :]
        PERM = cst_b[:, 3, :]

        def ld(q, out, in_, writes, reads=()):
            P.dma(q, lambda e, o=out, i=in_: e.dma_start(out=o, in_=i), reads=reads, writes=writes)

        ld("sp", ident_f[:], cst_d[0], ["ident_f"])
        ld("pool", cst_b[:], cst_d.rearrange("c p n -> p c n"), ["cst_b"])
        ld("sp", bmod[:], bmod_d.rearrange("l p c -> p l c"), ["bmod"])
        ld("sp", gn[:], gn_d, ["gn"])
        ld("sp", cT[:], cT_d, ["cT"])
        ld("sp", convw[:], convw_d.rearrange("l p c k -> p l c k"), ["convw"])
        ld("sp", convv[:], convv_d.rearrange("l p a c -> p l a c"), ["convv"])
        ld("sp", qkg[:], qkg_d.rearrange("l p a -> p l a"), ["qkg"])
        P.op("pool", lambda e: e.memset(zpad[:], 0.0), writes=["zpad"])
        uT_v = uT_d.rearrange("(c p) t -> p c t", p=128)
        for off in (0, HALO + S, UW, UW + HALO + T):
            ld("sp", uT_v[:, :, off:off + HALO], zpad[:], [("uTpad", off)], reads=["zpad"])

        P.op("act", lambda e: e.activation(out=sil[:], in_=cT[:], func=AF.Exp, scale=-1.0), reads=["cT"], writes=["sil"])
        P.op("dve", lambda e: e.tensor_scalar_add(out=sil[:], in0=sil[:], scalar1=1.0), reads=["sil"], writes=["sil"])
        P.op("dve", lambda e: e.reciprocal(out=sil[:], in_=sil[:]), reads=["sil"], writes=["sil"])
        P.op("dve", lambda e: e.tensor_mul(out=sil[:], in0=sil[:], in1=cT[:]), reads=["sil", "cT"], writes=["sil"])
        with ExitStack() as ph:
            P.barrier()
            wm = sbuf(ph, "wm", [128, 2, 8, 1024])
            it = 0
            for l in range(n_layers):
                for cg in range(6):
                    sl = (it % 2) if not os.environ.get('WM1') else 0
                    it += 1
                    for k in range(8):
                        ld("sp", wm[:, sl, k, :], wmod_d[l, k * 128:(k + 1) * 128, cg * 1024:(cg + 1) * 1024],
                           [("wm", sl, k)])
                    for jc in range(8):
                        oc = cg * 8 + jc
                        pb = pbank()
                        for k in range(8):
                            P.op("pe", lambda e, pb=pb, sl=sl, k=k, jc=jc: e.matmul(
                                psum[pb][:, 0:2], wm[:, sl, k, jc * 128:(jc + 1) * 128], sil[:, k, :],
                                start=(k == 0), stop=(k == 7)),
                                reads=[("wm", sl, k), "sil"], writes=[("ps", pb)])
                        P.op("dve", lambda e, pb=pb, l=l, oc=oc: e.tensor_scalar(
                            out=modv[:, l, oc, :], in0=psum[pb][:, 0:2], scalar1=bmod[:, l, oc:oc + 1], scalar2=None,
                            op0=ALU.add), reads=[("ps", pb), "bmod"], writes=["modv"])
            for l in range(n_layers):
                for n in range(2):
                    for j in range(2):
                        P.op("dve", lambda e, l=l, n=n, j=j: e.scalar_tensor_tensor(
                            out=Amod[:, l, n, :, j], in0=modv[:, l, (8 + 24 * n):(16 + 24 * n), j], scalar=1.0,
                            in1=gn[:, 2 * l + n, :], op0=ALU.add, op1=ALU.mult),
                            reads=["modv", "gn"], writes=["Amod"])

        def shift_ap(l, n, k, j):
            return modv[:, l, 24 * n + k, j:j + 1]

        def gate_ap(l, n, k, j):
            return modv[:, l, 16 + 24 * n + k, j:j + 1]

        def stats_to_rstd(lo, hi, tag):
            P.op("act", lambda e: e.activation(out=rstd[:, lo:hi], in_=stats[:, lo:hi], func=AF.Ln, scale=1.0 / D, bias=EPS),
                 reads=["rstd"], writes=["rstd"])
            P.op("act", lambda e: e.activation(out=rstd[:, lo:hi], in_=rstd[:, lo:hi], func=AF.Exp, scale=-0.5),
                 reads=["rstd"], writes=["rstd"])

        def sumsq_block(src_tile_fn, src_res, t0, N, sqt, sqres):
            pb = pbank()
            for k in range(8):
                P.op("act", lambda e, k=k: e.activation(out=sqt[:, k, 0:N], in_=src_tile_fn(k), func=AF.Square),
                     reads=[src_res(k)], writes=[(sqres, k)])
                P.op("pe", lambda e, k=k, pb=pb: e.matmul(psum[pb][:, 0:N], ONESB, sqt[:, k, 0:N], start=(k == 0), stop=(k == 7)),
                     reads=[(sqres, k), "cst_b"], writes=[("ps", pb)])
            P.op("dve", lambda e, pb=pb: e.tensor_copy(out=stats[:, t0:t0 + N], in_=psum[pb][:, 0:N]),
                 reads=[("ps", pb)], writes=["rstd"])

        xT_v = xT_d.rearrange("(k p) t -> p k t", p=128)
        with ExitStack() as ph:
            P.barrier()
            xin = sbuf(ph, "xin", [128, 2, D])
            xst = sbuf(ph, "xst", [128, 2, 8, 512])
            sqt = sbuf(ph, "sqt0", [128, 8, 512], BF16)
            ti = 0
            for b in range(NBLK):
                t0, N = blk_info(b)
                sl = b % 2
                for tt in range(N // 128):
                    s2 = ti % 2
                    ti += 1
                    src = x_d[t0 + tt * 128:t0 + (tt + 1) * 128, :] if b < 8 else ctx_d[tt * 128:(tt + 1) * 128, :]
                    ld("sp", xin[:, s2, :], src, [("xin", s2)])
                    for kk in range(2):
                        pb = pbank()
                        for k4 in range(4):
                            k = kk * 4 + k4
                            P.op("pe", lambda e, pb=pb, k4=k4, k=k, s2=s2: e.transpose(
                                psum[pb][:, k4 * 128:(k4 + 1) * 128], xin[:, s2, k * 128:(k + 1) * 128], ident_f[:]),
                                reads=[("xin", s2), "ident_f"], writes=[("ps", pb)])
                        P.op("dve", lambda e, pb=pb, kk=kk, sl=sl, tt=tt: e.tensor_copy(
                            out=xst[:, sl, kk * 4:(kk + 1) * 4, tt * 128:(tt + 1) * 128],
                            in_=psum[pb][:, :].rearrange("p (k t) -> p k t", k=4)),
                            reads=[("ps", pb)], writes=[("xst", sl, kk)])
                ld("sp", xT_v[:, :, t0:t0 + N], xst[:, sl, :, 0:N], [("xT", b)], reads=[("xst", sl, 0), ("xst", sl, 1)])
                sumsq_block(lambda k, sl=sl, N=N: xst[:, sl, k, 0:N], lambda k, sl=sl: ("xst", sl, k // 4), t0, N, sqt, "sqt0")
        stats_to_rstd(0, ST, "n1l0")

        if "dump" in dbg:
            d1 = nc.dram_tensor("d_rstd", [128, ST], F32, kind="ExternalOutput").ap()
            d2 = nc.dram_tensor("d_modv", [128, L, 48, 2], F32, kind="ExternalOutput").ap()
            d3 = nc.dram_tensor("d_amod", [128, L, 2, 8, 2], F32, kind="ExternalOutput").ap()
            d4 = nc.dram_tensor("d_sil", [128, 8, 2], F32, kind="ExternalOutput").ap()
            ld("sp", d1, rstd[:], ["d1"], reads=["rstd"])
            ld("sp", d2, modv[:], ["d2"], reads=["modv"])
            ld("sp", d3, Amod[:], ["d3"], reads=["Amod"])
            ld("sp", d4, sil[:], ["d4"], reads=["sil"])
            P.wait_all("sp", P.final_tokens())
            P.emit(top)
            return nc
        def load_w_bf(dst_fn, src_fn, nk, res):
            for k in range(nk):
                ld("pool", dst_fn(k), src_fn(k), [(res, k)])

        def modulate_block(l, n, b, xt, xres, ht, hres, tmp, tmpres):
            t0, N = blk_info(b)
            j = 0 if b < 8 else 1
            for k in range(8):
                ts_ = k % 2
                P.op("pool", lambda e, k=k, ts_=ts_: e.tensor_tensor(out=tmp[:, ts_, 0:N], in0=xt(k), in1=rstd[:, t0:t0 + N], op=ALU.mult),
                     reads=[xres(k), "rstd"], writes=[(tmpres, ts_)])
                P.op("dve", lambda e, k=k, ts_=ts_: e.tensor_scalar(
                    out=ht[:, k, 0:N], in0=tmp[:, ts_, 0:N], scalar1=Amod[:, l, n, k, j:j + 1], scalar2=shift_ap(l, n, k, j),
                    op0=ALU.mult, op1=ALU.add), reads=[(tmpres, ts_), "Amod", "modv"], writes=[(hres, k)])

        for l in range(n_layers):
            last = (l == n_layers - 1) and (n_layers == L)
            nblk_q = 8 if last else NBLK
            with ExitStack() as ph:
                P.barrier()
                win = sbuf(ph, "win", [128, 8, INW], BF16)
                xt = sbuf(ph, "xtA", [128, 2, 8, 512])
                ht = sbuf(ph, "htA", [128, 8, 512], BF16)
                tmp = sbuf(ph, "tmpA", [128, 2, 512])
                sg = sbuf(ph, "sgA", [128, 2, 512])
                stg = sbuf(ph, "stgA", [128, 2, 8, 512], BF16)
                vst = sbuf(ph, "vstA", [128, 2, 10, 128], BF16)
                WIN_GROUPS = [(512, 1024), (0, 512), (1024, 1536), (2560, 3072), (3328, 4352), (4352, 5376), (5376, 6400),
                              (1536, 2048), (3072, 3328), (2048, 2560)]

                def wgrp(col):
                    for gi, (c0, c1) in enumerate(WIN_GROUPS):
                        if c0 <= col < c1:
                            return gi
                    raise AssertionError(col)

                for gi, (c0, c1) in enumerate(WIN_GROUPS):
                    for k in range(8):
                        ld("pool", win[:, k, c0:c1], win_d[l, k * 128:(k + 1) * 128, c0:c1], [("win", k, gi)])
                for s2 in range(2):
                    P.op("pool", lambda e, s2=s2: e.memset(vst[:, s2, :, 64:128], 1.0), writes=[("vst", s2)])

                def loadA(b):
                    t0, N = blk_info(b)
                    ld("sp", xt[:, b % 2, :, 0:N], xT_v[:, :, t0:t0 + N], [("xtA", b % 2, k) for k in range(8)], reads=[("xT", b)])

                uT_lat = uT_v[:, :, HALO:HALO + S]
                uT_ctx = uT_v[:, :, UW + HALO:UW + HALO + T]
                naq_v = naq_d.rearrange("(c p) t -> p c t", p=128)
                nak_v = nak_d.rearrange("(c p) t -> p c t", p=128)
                gqr_v = gqr_d.rearrange("(c p) t -> p c t", p=128)
                gkr_v = gkr_d.rearrange("(c p) t -> p c t", p=128)
                gat_v = gat_d.rearrange("(c p) t -> p c t", p=128)
                vaug_v = vaug_d.rearrange("(n p) h f -> p n h f", p=128)
                stgi = [0]
                vsi = [0]
                loadA(0)
                for b in range(NBLK):
                    t0, N = blk_info(b)
                    if b + 1 < NBLK:
                        loadA(b + 1)
                    sl = b % 2
                    modulate_block(l, 0, b, lambda k, sl=sl, N=N: xt[:, sl, k, 0:N], lambda k, sl=sl: ("xtA", sl, k),
                                   ht, "htA", tmp, "tmpA")

                    def proj(col, pb, N=N):
                        for k in range(8):
                            P.op("pe", lambda e, k=k: e.matmul(psum[pb][:, 0:N], win[:, k, col:col + 128], ht[:, k, 0:N],
                                                               start=(k == 0), stop=(k == 7)),
                                 reads=[("win", k, wgrp(col)), ("htA", k)], writes=[("ps", pb)])

                    def group(col0, nch, dst, kind, dres):
                        ss = stgi[0] % 2
                        stgi[0] += 1
                        for c in range(nch):
                            pb = pbank()
                            proj(col0 + c * 128, pb)
                            if kind == "sig":
                                P.op("act", lambda e, pb=pb, c=c: e.activation(out=stg[:, ss, c, 0:N], in_=psum[pb][:, 0:N], func=AF.Sigmoid),
                                     reads=[("ps", pb)], writes=[("stgA", ss, c)])
                            elif kind == "copy":
                                P.op("act", lambda e, pb=pb, c=c: e.activation(out=stg[:, ss, c, 0:N], in_=psum[pb][:, 0:N], func=AF.Identity),
                                     reads=[("ps", pb)], writes=[("stgA", ss, c)])
                            elif kind == "scale":
                                P.op("act", lambda e, pb=pb, c=c: e.activation(out=stg[:, ss, c, 0:N], in_=psum[pb][:, 0:N], func=AF.Identity, scale=0.125),
                                     reads=[("ps", pb)], writes=[("stgA", ss, c)])
                        ld("sp", dst, stg[:, ss, 0:nch, 0:N], [dres], reads=[("stgA", ss, c) for c in range(nch)])

                    ss = stgi[0] % 2
                    stgi[0] += 1
                    for c in range(4):
                        pg = pbank()
                        proj(512 + c * 128, pg)
                        s2 = c % 2
                        P.op("act", lambda e, pg=pg, s2=s2: e.activation(out=sg[:, s2, 0:N], in_=psum[pg][:, 0:N], func=AF.Sigmoid),
                             reads=[("ps", pg)], writes=[("sgA", s2)])
                        pv = pbank()
                        proj(c * 128, pv)
                        P.op("dve", lambda e, pv=pv, s2=s2, c=c, ss=ss: e.tensor_tensor(out=stg[:, ss, c, 0:N], in0=psum[pv][:, 0:N], in1=sg[:, s2, 0:N], op=ALU.mult),
                             reads=[("ps", pv), ("sgA", s2)], writes=[("stgA", ss, c)])
                    udst = uT_lat[:, :, t0:t0 + N] if b < 8 else uT_ctx
                    ld("sp", udst, stg[:, ss, 0:4, 0:N], [("uT", b)], reads=[("stgA", ss, c) for c in range(4)])
                    if b < nblk_q:
                        group(1024, 4, naq_v[:, :, t0:t0 + N], "scale", ("naq", b))
                        group(2560, 4, gqr_v[:, :, t0:t0 + N], "copy", ("gqr", b))
                        for g3 in range(3):
                            group(3328 + g3 * 1024, 8, gat_v[:, g3 * 8:(g3 + 1) * 8, t0:t0 + N], "sig", ("gat", b, g3))
                    group(1536, 4, nak_v[:, :, t0:t0 + N], "copy", ("nak", b))
                    group(3072, 1, gkr_v[:, :, t0:t0 + N], "copy", ("gkr", b))
                    for tt in range(N // 128):
                        s2 = vsi[0] % 2
                        vsi[0] += 1
                        p1 = pbank()
                        p2 = pbank()
                        for k in range(8):
                            P.op("pe", lambda e, k=k, tt=tt, p1=p1: e.matmul(psum[p1][:, 0:512], ht[:, k, tt * 128:(tt + 1) * 128], win[:, k, 2048:2560],
                                                                        start=(k == 0), stop=(k == 7)),
                                 reads=[("win", k, wgrp(2048)), ("htA", k)], writes=[("ps", p1)])
                        for k in range(8):
                            P.op("pe", lambda e, k=k, tt=tt, p2=p2: e.matmul(psum[p2][:, 0:128], ht[:, k, tt * 128:(tt + 1) * 128], win[:, k, 3200:3328],
                                                                        start=(k == 0), stop=(k == 7)),
                                 reads=[("win", k, wgrp(3200)), ("htA", k)], writes=[("ps", p2)])
                        P.op("dve", lambda e, p1=p1, s2=s2: e.tensor_copy(out=vst[:, s2, 0:8, 0:64], in_=psum[p1][:, 0:512].rearrange("p (h d) -> p h d", h=8)),
                             reads=[("ps", p1)], writes=[("vst", s2)])
                        P.op("dve", lambda e, p2=p2, s2=s2: e.tensor_copy(out=vst[:, s2, 8:10, 0:64], in_=psum[p2][:, 0:128].rearrange("p (h d) -> p h d", h=2)),
                             reads=[("ps", p2)], writes=[("vst", s2)])
                        ti = t0 // 128 + tt
                        ld("sp", vaug_v[:, ti, :, :], vst[:, s2, :, :], [("vaug", ti)], reads=[("vst", s2)])

            if "A" in dbg and l == 0:
                break
            ca_v = ca_d.rearrange("(c p) t -> p c t", p=128)
            with ExitStack() as ph:
                P.barrier()
                dg = sbuf(ph, "dg", [128, 4, CONV_K, 128], BF16)
                ut = sbuf(ph, "utB", [128, 2, 4, 512 + 2 * HALO], BF16)
                yb = sbuf(ph, "ybB", [128, 4, 512])
                ybf = sbuf(ph, "ybfB", [128, 4, 512], BF16)
                ysq = sbuf(ph, "ysqB", [128, 4, 512], BF16)
                mu = sbuf(ph, "muB", [128, 512])
                var = sbuf(ph, "varB", [128, 512])
                t1 = sbuf(ph, "t1B", [128, 2, 512])
                t2 = sbuf(ph, "t2B", [128, 2, 512])
                cast = sbuf(ph, "castB", [128, 2, 4, 512], BF16)
                for c in range(4):
                    for kk in range(CONV_K):
                        P.op("pool" if kk % 2 else "dve", lambda e, c=c, kk=kk: e.tensor_scalar(out=dg[:, c, kk, :], in0=IDB, scalar1=convw[:, l, c, kk:kk + 1], scalar2=None, op0=ALU.mult),
                             reads=["cst_b", "convw"], writes=[("dg", c, kk)])

                def loadB(b):
                    t0, N = blk_info(b)
                    if b < 8:
                        src = uT_v[:, :, t0:t0 + N + 2 * HALO]
                        rd = [("uT", bb) for bb in (b - 1, b, b + 1) if 0 <= bb < 8] + [("uTpad", 0), ("uTpad", HALO + S)]
                    else:
                        src = uT_v[:, :, UW:UW + UCW]
                        rd = [("uT", 8), ("uTpad", UW), ("uTpad", UW + HALO + T)]
                    ld("sp", ut[:, b % 2, :, 0:N + 2 * HALO], src, [("utB", b % 2)], reads=rd)

                loadB(0)
                for b in range(nblk_q):
                    t0, N = blk_info(b)
                    if b + 1 < nblk_q:
                        loadB(b + 1)
                    sl = b % 2
                    ps1 = pbank()
                    ps2 = pbank()
                    for c in range(4):
                        pb = pbank()
                        for kk in range(CONV_K):
                            P.op("pe", lambda e, c=c, kk=kk, pb=pb: e.matmul(psum[pb][:, 0:N], dg[:, c, kk, :], ut[:, sl, c, kk:kk + N],
                                                                        start=(kk == 0), stop=(kk == CONV_K - 1)),
                                 reads=[("dg", c, kk), ("utB", sl)], writes=[("ps", pb)])
                        P.op("dve", lambda e, c=c, pb=pb: e.tensor_scalar(out=yb[:, c, 0:N], in0=psum[pb][:, 0:N], scalar1=convv[:, l, 0, c:c + 1], scalar2=None, op0=ALU.add),
                             reads=[("ps", pb), "convv"], writes=[("ybB", c)])
                        P.op("pool", lambda e, c=c: e.tensor_copy(out=ybf[:, c, 0:N], in_=yb[:, c, 0:N]), reads=[("ybB", c)], writes=[("ybfB", c)])
                        P.op("act", lambda e, c=c: e.activation(out=ysq[:, c, 0:N], in_=yb[:, c, 0:N], func=AF.Square), reads=[("ybB", c)], writes=[("ysqB", c)])
                    for c in range(4):
                        P.op("pe", lambda e, c=c: e.matmul(psum[ps1][:, 0:N], ONESB, ybf[:, c, 0:N], start=(c == 0), stop=(c == 3)),
                             reads=[("ybfB", c), "cst_b"], writes=[("ps", ps1)])
                    for c in range(4):
                        P.op("pe", lambda e, c=c: e.matmul(psum[ps2][:, 0:N], ONESB, ysq[:, c, 0:N], start=(c == 0), stop=(c == 3)),
                             reads=[("ysqB", c), "cst_b"], writes=[("ps", ps2)])
                    P.op("dve", lambda e: e.tensor_scalar(out=mu[:, 0:N], in0=psum[ps1][:, 0:N], scalar1=1.0 / 512, scalar2=None, op0=ALU.mult),
                         reads=[("ps", ps1)], writes=["muB"])
                    P.op("dve", lambda e: e.tensor_tensor(out=var[:, 0:N], in0=mu[:, 0:N], in1=mu[:, 0:N], op=ALU.mult), reads=["muB"], writes=["varB"])
                    P.op("dve", lambda e: e.scalar_tensor_tensor(out=var[:, 0:N], in0=psum[ps2][:, 0:N], scalar=1.0 / 512, in1=var[:, 0:N], op0=ALU.mult, op1=ALU.subtract),
                         reads=[("ps", ps2), "varB"], writes=["varB"])
                    P.op("act", lambda e: e.activation(out=var[:, 0:N], in_=var[:, 0:N], func=AF.Ln, bias=EPS), reads=["varB"], writes=["varB"])
                    P.op("act", lambda e: e.activation(out=var[:, 0:N], in_=var[:, 0:N], func=AF.Exp, scale=-0.5), reads=["varB"], writes=["varB"])
                    for c in range(4):
                        s2 = c % 2
                        P.op("pool", lambda e, c=c, s2=s2: e.tensor_tensor(out=t1[:, s2, 0:N], in0=yb[:, c, 0:N], in1=mu[:, 0:N], op=ALU.subtract),
                             reads=[("ybB", c), "muB"], writes=[("t1B", s2)])
                        P.op("pool", lambda e, s2=s2: e.tensor_tensor(out=t1[:, s2, 0:N], in0=t1[:, s2, 0:N], in1=var[:, 0:N], op=ALU.mult),
                             reads=[("t1B", s2), "varB"], writes=[("t1B", s2)])
                        P.op("dve", lambda e, c=c, s2=s2: e.tensor_scalar(out=t1[:, s2, 0:N], in0=t1[:, s2, 0:N], scalar1=convv[:, l, 1, c:c + 1], scalar2=convv[:, l, 2, c:c + 1],
                                                                      op0=ALU.mult, op1=ALU.add), reads=[("t1B", s2), "convv"], writes=[("t1B", s2)])
                        P.op("act", lambda e, s2=s2, c=c: e.activation(out=cast[:, sl, c, 0:N], in_=t1[:, s2, 0:N], func=AF.Silu), reads=[("t1B", s2)], writes=[("castB", sl, c)])
                    ld("sp", ca_v[:, :, t0:t0 + N], cast[:, sl, :, 0:N], [("ca", b)], reads=[("castB", sl, c) for c in range(4)])

            if "B" in dbg and l == 0:
                break
            ATTN(nc, P, l, nblk_q, psum_all, psum, pbank, sbuf, ld, cst_b, qkg, naq_d, nak_d, vaug_d, gqr_d, gkr_d, nao_d, gqo_d,
                 tint_d, tfull_d, rope_d, dbg)
            if "D" in dbg and l == 0:
                break
            nao_v = nao_d.rearrange("(c p) t -> p c t", p=128)
            gqo_v = gqo_d.rearrange("(c p) t -> p c t", p=128)
            with ExitStack() as ph:
                P.barrier()
                wco = sbuf(ph, "wco", [128, 4, D], BF16)
                wno = sbuf(ph, "wno", [128, 4, D], BF16)
                wgo = sbuf(ph, "wgo", [128, 4, D], BF16)
                wo = sbuf(ph, "wo", [128, 8, D], BF16)
                xt = sbuf(ph, "xtE", [128, 2, 8, 512])
                cat = sbuf(ph, "catE", [128, 2, 4, 512], BF16)
                nat = sbuf(ph, "natE", [128, 2, 4, 512], BF16)
                gqt = sbuf(ph, "gqtE", [128, 2, 4, 512], BF16)
                gt = sbuf(ph, "gtE", [128, 2, 24, 512], BF16)
                yt = sbuf(ph, "ytE", [128, 8, 512], BF16)
                m1 = sbuf(ph, "m1E", [128, 2, 512])
                m2 = sbuf(ph, "m2E", [128, 2, 512])
                sqt = sbuf(ph, "sqtE", [128, 8, 512], BF16)
                load_w_bf(lambda k: wco[:, k, :], lambda k: wco_d[l, k * 128:(k + 1) * 128, :], 4, "wco")
                load_w_bf(lambda k: wno[:, k, :], lambda k: wno_d[l, k * 128:(k + 1) * 128, :], 4, "wno")
                load_w_bf(lambda k: wgo[:, k, :], lambda k: wgo_d[l, k * 128:(k + 1) * 128, :], 4, "wgo")
                load_w_bf(lambda k: wo[:, k, :], lambda k: wout_d[l, k * 128:(k + 1) * 128, :], 8, "wo")

                def loadE(b):
                    t0, N = blk_info(b)
                    s = b % 2
                    ld("sp", xt[:, s, :, 0:N], xT_v[:, :, t0:t0 + N], [("xtE", s, k) for k in range(8)], reads=[("xT", b)])
                    ld("sp", cat[:, s, :, 0:N], ca_v[:, :, t0:t0 + N], [("catE", s)], reads=[("ca", b)])
                    ld("sp", nat[:, s, :, 0:N], nao_v[:, :, t0:t0 + N], [("natE", s)], reads=[("nao", b)])
                    ld("sp", gqt[:, s, :, 0:N], gqo_v[:, :, t0:t0 + N], [("gqtE", s)], reads=[("gqo", b, h) for h in range(8)])
                    for g3 in range(3):
                        ld("sp", gt[:, s, g3 * 8:(g3 + 1) * 8, 0:N], gat_v[:, g3 * 8:(g3 + 1) * 8, t0:t0 + N], [("gtE", s, g3)], reads=[("gat", b, g3)])

                loadE(0)
                for b in range(nblk_q):
                    t0, N = blk_info(b)
                    j = 0 if b < 8 else 1
                    if b + 1 < nblk_q:
                        loadE(b + 1)
                    s = b % 2
                    for fc in range(8):
                        fs = slice(fc * 128, (fc + 1) * 128)
                        s2 = fc % 2
                        pa = pbank()
                        for k in range(4):
                            P.op("pe", lambda e, k=k, pa=pa, fs=fs: e.matmul(psum[pa][:, 0:N], wco[:, k, fs], cat[:, s, k, 0:N], start=(k == 0), stop=(k == 3)),
                                 reads=[("wco", k), ("catE", s)], writes=[("ps", pa)])
                        P.op("dve", lambda e, pa=pa, fc=fc, s2=s2: e.tensor_tensor(out=m1[:, s2, 0:N], in0=psum[pa][:, 0:N], in1=gt[:, s, fc, 0:N], op=ALU.mult),
                             reads=[("ps", pa), ("gtE", s, 0)], writes=[("m1E", s2)])
                        pn = pbank()
                        for h in range(4):
                            P.op("pe", lambda e, h=h, pn=pn, fs=fs: e.matmul(psum[pn][:, 0:N], wno[:, h, fs], nat[:, s, h, 0:N], start=(h == 0), stop=(h == 3)),
                                 reads=[("wno", h), ("natE", s)], writes=[("ps", pn)])
                        P.op("dve", lambda e, pn=pn, fc=fc, s2=s2: e.tensor_tensor(out=m2[:, s2, 0:N], in0=psum[pn][:, 0:N], in1=gt[:, s, 8 + fc, 0:N], op=ALU.mult),
                             reads=[("ps", pn), ("gtE", s, 1)], writes=[("m2E", s2)])
                        P.op("pool", lambda e, s2=s2: e.tensor_tensor(out=m1[:, s2, 0:N], in0=m1[:, s2, 0:N], in1=m2[:, s2, 0:N], op=ALU.add),
                             reads=[("m1E", s2), ("m2E", s2)], writes=[("m1E", s2)])
                        pg = pbank()
                        for h in range(4):
                            P.op("pe", lambda e, h=h, pg=pg, fs=fs: e.matmul(psum[pg][:, 0:N], wgo[:, h, fs], gqt[:, s, h, 0:N], start=(h == 0), stop=(h == 3)),
                                 reads=[("wgo", h), ("gqtE", s)], writes=[("ps", pg)])
                        P.op("dve", lambda e, pg=pg, fc=fc, s2=s2: e.tensor_tensor(out=m2[:, s2, 0:N], in0=psum[pg][:, 0:N], in1=gt[:, s, 16 + fc, 0:N], op=ALU.mult),
                             reads=[("ps", pg), ("gtE", s, 2)], writes=[("m2E", s2)])
                        P.op("pool", lambda e, s2=s2, fc=fc: e.tensor_tensor(out=yt[:, fc, 0:N], in0=m1[:, s2, 0:N], in1=m2[:, s2, 0:N], op=ALU.add),
                             reads=[("m1E", s2), ("m2E", s2)], writes=[("ytE", fc)])
                    for fc in range(8):
                        fs = slice(fc * 128, (fc + 1) * 128)
                        po = pbank()
                        for k in range(8):
                            P.op("pe", lambda e, k=k, po=po, fs=fs: e.matmul(psum[po][:, 0:N], wo[:, k, fs], yt[:, k, 0:N], start=(k == 0), stop=(k == 7)),
                                 reads=[("wo", k), ("ytE", k)], writes=[("ps", po)])
                        P.op("dve", lambda e, po=po, fc=fc: e.scalar_tensor_tensor(out=xt[:, s, fc, 0:N], in0=psum[po][:, 0:N], scalar=gate_ap(l, 0, fc, j), in1=xt[:, s, fc, 0:N],
                                                                              op0=ALU.mult, op1=ALU.add),
                             reads=[("ps", po), "modv", ("xtE", s, fc)], writes=[("xtE", s, fc)])
                    ld("sp", xT_v[:, :, t0:t0 + N], xt[:, s, :, 0:N], [("xT", b)], reads=[("xtE", s, k) for k in range(8)])
                    sumsq_block(lambda k, s=s, N=N: xt[:, s, k, 0:N], lambda k, s=s: ("xtE", s, k), t0, N, sqt, "sqtE")
            stats_to_rstd(0, ST if nblk_q == NBLK else S, "n2")
            if "E" in dbg and l == 0:
                break
            with ExitStack() as ph:
                P.barrier()
                wfi = sbuf(ph, "wfi", [128, 8, 2 * FFN], BF16)
                wfo = sbuf(ph, "wfo", [128, 22, D], BF16)
                xt = sbuf(ph, "xtF", [128, 1, 8, 512])
                ht = sbuf(ph, "htF", [128, 8, 512], BF16)
                tmp = sbuf(ph, "tmpF", [128, 2, 512])
                hid = sbuf(ph, "hidF", [128, 22, 512], BF16)
                sgt = sbuf(ph, "sgF", [128, 2, 512], BF16)
                sqt = ht
                HG = [(0, 4), (4, 8), (8, 12), (12, 16), (16, 20), (20, 22)]

                def hgrp(hc):
                    return hc // 4

                for (h0, h1) in HG:
                    for base in (0, FFN):
                        for k in range(8):
                            ld("pool", wfi[:, k, base + h0 * 128:base + h1 * 128], wfi_d[l, k * 128:(k + 1) * 128, base + h0 * 128:base + h1 * 128],
                               [("wfi", k, base, h0 // 4)])
                load_w_bf(lambda k: wfo[:, k, :], lambda k: wfo_d[l, k * 128:(k + 1) * 128, :], 22, "wfo")

                def loadF(b):
                    t0, N = blk_info(b)
                    for k in range(8):
                        ld("sp", xt[:, 0, k, 0:N], xT_v[:, k, t0:t0 + N], [("xtF", 0, k)], reads=[("xT", b)])

                loadF(0)
                for b in range(nblk_q):
                    t0, N = blk_info(b)
                    j = 0 if b < 8 else 1
                    s = 0
                    modulate_block(l, 1, b, lambda k, s=s, N=N: xt[:, s, k, 0:N], lambda k, s=s: ("xtF", s, k), ht, "htF", tmp, "tmpF")
                    for hc in range(22):
                        s2 = hc % 2
                        pg = pbank()
                        for k in range(8):
                            P.op("pe", lambda e, k=k, pg=pg, hc=hc: e.matmul(psum[pg][:, 0:N], wfi[:, k, hc * 128:(hc + 1) * 128], ht[:, k, 0:N], start=(k == 0), stop=(k == 7)),
                                 reads=[("wfi", k, 0, hgrp(hc)), ("htF", k)], writes=[("ps", pg)])
                        P.op("act", lambda e, pg=pg, s2=s2: e.activation(out=sgt[:, s2, 0:N], in_=psum[pg][:, 0:N], func=AF.Silu),
                             reads=[("ps", pg)], writes=[("sgF", s2)])
                        pu = pbank()
                        for k in range(8):
                            P.op("pe", lambda e, k=k, pu=pu, hc=hc: e.matmul(psum[pu][:, 0:N], wfi[:, k, FFN + hc * 128:FFN + (hc + 1) * 128], ht[:, k, 0:N], start=(k == 0), stop=(k == 7)),
                                 reads=[("wfi", k, FFN, hgrp(hc)), ("htF", k)], writes=[("ps", pu)])
                        P.op("dve", lambda e, pu=pu, s2=s2, hc=hc: e.tensor_tensor(out=hid[:, hc, 0:N], in0=psum[pu][:, 0:N], in1=sgt[:, s2, 0:N], op=ALU.mult),
                             reads=[("ps", pu), ("sgF", s2)], writes=[("hidF", hc)])
                    for fc in range(8):
                        po = pbank()
                        for hc in range(22):
                            P.op("pe", lambda e, hc=hc, po=po, fc=fc: e.matmul(psum[po][:, 0:N], wfo[:, hc, fc * 128:(fc + 1) * 128], hid[:, hc, 0:N], start=(hc == 0), stop=(hc == 21)),
                                 reads=[("wfo", hc), ("hidF", hc)], writes=[("ps", po)])
                        P.op("dve", lambda e, po=po, fc=fc: e.scalar_tensor_tensor(out=xt[:, s, fc, 0:N], in0=psum[po][:, 0:N], scalar=gate_ap(l, 1, fc, j), in1=xt[:, s, fc, 0:N],
                                                                              op0=ALU.mult, op1=ALU.add),
                             reads=[("ps", po), "modv", ("xtF", s, fc)], writes=[("xtF", s, fc)])
                    ld("sp", xT_v[:, :, t0:t0 + N], xt[:, s, :, 0:N], [("xT", b)], reads=[("xtF", s, k) for k in range(8)])
                    sumsq_block(lambda k, s=s, N=N: xt[:, s, k, 0:N], lambda k, s=s: ("xtF", s, k), t0, N, sqt, "htF")
                    if b + 1 < nblk_q:
                        loadF(b + 1)
            stats_to_rstd(0, ST if nblk_q == NBLK else S, "n1next")

        if not dbg:
            with ExitStack() as ph:
                P.barrier()
                xt = sbuf(ph, "xtZ", [128, 2, 8, 512])
                tmp = sbuf(ph, "tmpZ", [128, 2, 512])
                ot = sbuf(ph, "otZ", [128, 4, D])
                oi = 0
                ld("sp", xt[:, 0, :, :], xT_v[:, :, 0:512], [("xtZ", 0, k) for k in range(8)], reads=[("xT", 0)])
                for b in range(8):
                    t0 = b * 512
                    if b + 1 < 8:
                        ld("sp", xt[:, (b + 1) % 2, :, :], xT_v[:, :, t0 + 512:t0 + 1024], [("xtZ", (b + 1) % 2, k) for k in range(8)], reads=[("xT", b + 1)])
                    s = b % 2
                    for k in range(8):
                        s2 = k % 2
                        P.op("pool", lambda e, k=k, s2=s2: e.tensor_tensor(out=tmp[:, s2, :], in0=xt[:, s, k, :], in1=rstd[:, t0:t0 + 512], op=ALU.mult),
                             reads=[("xtZ", s, k), "rstd"], writes=[("tmpZ", s2)])
                        P.op("dve", lambda e, k=k, s2=s2: e.tensor_scalar(out=xt[:, s, k, :], in0=tmp[:, s2, :], scalar1=gn[:, 2 * L, k:k + 1], scalar2=None, op0=ALU.mult),
                             reads=[("tmpZ", s2), "gn"], writes=[("xtZ", s, k)])
                    for tt in range(4):
                        so = oi % 4
                        oi += 1
                        for kk in range(2):
                            pb = pbank()
                            for k4 in range(4):
                                k = kk * 4 + k4
                                P.op("pe", lambda e, pb=pb, k4=k4, k=k, tt=tt: e.transpose(psum[pb][:, k4 * 128:(k4 + 1) * 128], xt[:, s, k, tt * 128:(tt + 1) * 128], ident_f[:]),
                                     reads=[("xtZ", s, k), "ident_f"], writes=[("ps", pb)])
                            P.op("dve", lambda e, pb=pb, kk=kk, so=so: e.tensor_copy(out=ot[:, so, kk * 512:(kk + 1) * 512], in_=psum[pb][:, :]),
                                 reads=[("ps", pb)], writes=[("otZ", so)])
                        ld("sp", out_d[t0 + tt * 128:t0 + (tt + 1) * 128, :], ot[:, so, :], [("out", b, tt)], reads=[("otZ", so)])
        P.wait_all("sp", P.final_tokens())
        P.emit(top)
    return nc


def ATTN(nc, P, l, nblk_q, psum_all, psum, pbank, sbuf, ld, cst_b, qkg, naq_d, nak_d, vaug_d, gqr_d, gkr_d, nao_d, gqo_d,
         tint_d, tfull_d, rope_d, dbg):
    IDB = cst_b[:, 0, :]
    BD64 = cst_b[:, 2, :]
    PERM = cst_b[:, 3, :]
    naq_v = naq_d.rearrange("(c p) t -> p c t", p=128)
    nak_v = nak_d.rearrange("(c p) t -> p c t", p=128)
    gqr_v = gqr_d.rearrange("(c p) t -> p c t", p=128)
    gkr_v = gkr_d.rearrange("(c p) t -> p c t", p=128)
    vaug_v = vaug_d.rearrange("(n p) h f -> p n h f", p=128)
    nao_v = nao_d.rearrange("(h d) t -> d h t", d=64)
    gqo_v = gqo_d.rearrange("(h d) t -> d h t", d=64)
    ACC = [0, 1]
    SPAIRS = [2, 4, 6]
    LA = 1
    cnt = {"acc": 0, "s": 0, "pt": 0, "rc": 0}

    def nxt(key, n):
        v = cnt[key] % n
        cnt[key] += 1
        return v

    def attend2(N, qk_A, qk_B, nchunks, pt, ptres, rc, rcres, outs, scale):
        assert nchunks % 2 == 0
        pos = [ACC[0], ACC[1]]
        pend = []

        def emit_pv(info):
            j, items = info
            for hh in range(2):
                pi, vls = items[hh]
                for t in range(2):
                    i = 2 * j + t
                    vl, vres, _ = vls[t]
                    P.op("pe", lambda e, vl=vl, pi=pi, i=i, t=t, hh=hh: e.matmul(psum[pos[hh]][:, 0:N], vl, pt[:, pi, t, 0:N], start=(i == 0), stop=(i == nchunks - 1)),
                         reads=list(vres) + [(ptres, pi)], writes=[("ps", pos[hh])])

        for j in range(nchunks // 2):
            sbs = [SPAIRS[nxt("s", 3)], SPAIRS[nxt("s", 3)]]
            vls = [[None, None], [None, None]]
            for t in range(2):
                order = (0, 1) if t == 0 else (1, 0)
                for hh in order:
                    vls[hh][t] = (qk_A if hh == 0 else qk_B)(2 * j + t, sbs[hh] + t)
                for hh in order:
                    if vls[hh][t][2] is not None:
                        vls[hh][t][2]()
            items = []
            for hh in range(2):
                sb = sbs[hh]
                pi = nxt("pt", 4)
                P.op("act", lambda e, sb=sb, pi=pi: e.activation(out=pt[:, pi, :, 0:N], in_=psum_all[:, sb:sb + 2, 0:N], func=AF.Exp, scale=scale),
                     reads=[("ps", sb), ("ps", sb + 1)], writes=[(ptres, pi)])
                items.append((pi, vls[hh]))
            pend.append((j, items))
            if len(pend) > LA:
                emit_pv(pend.pop(0))
        while pend:
            emit_pv(pend.pop(0))
        for hh in range(2):
            po = pos[hh]
            out_ap, out_res = outs[hh]
            r2 = nxt("rc", 2)
            P.op("dve", lambda e, po=po, r2=r2: e.tensor_copy(out=rc[:, r2, 0, 0:N], in_=psum[po][:, 0:N]), reads=[("ps", po)], writes=[(rcres, r2, 0)])
            P.op("dve", lambda e, r2=r2: e.reciprocal(out=rc[0:64, r2, 1, 0:N], in_=rc[64:128, r2, 0, 0:N]), reads=[(rcres, r2, 0)], writes=[(rcres, r2, 1)])
            P.op("pool", lambda e, r2=r2, out_ap=out_ap: e.tensor_tensor(out=out_ap, in0=rc[0:64, r2, 0, 0:N], in1=rc[0:64, r2, 1, 0:N], op=ALU.mult),
                 reads=[(rcres, r2, 0), (rcres, r2, 1)], writes=[out_res])

    with ExitStack() as ph:
        P.barrier()
        tint = sbuf(ph, "tint", [128, 8, 22 * 64], BF16)
        tfull = sbuf(ph, "tfull", [128, 8, 14 * 64], BF16)
        negt = sbuf(ph, "negt", [128, 256], BF16)
        kc = sbuf(ph, "kcC", [128, 4, T], BF16)
        vc = sbuf(ph, "vcC", [128, 2, 8, 128], BF16)
        qt = sbuf(ph, "qtC", [128, 2, 4, 512], BF16)
        kt = sbuf(ph, "ktC", [128, 2, 4, 1024], BF16)
        vt = sbuf(ph, "vtC", [128, 2, 8, 8, 128], BF16)
        pt = sbuf(ph, "ptC", [128, 4, 2, 512], BF16)
        rc = sbuf(ph, "rcC", [128, 2, 2, 512])
        ost = sbuf(ph, "ostC", [64, 2, 8, 512], BF16)
        for h in range(8):
            ld("pool", tint[:, h, :], tint_d[l, :, h, :], [("tint", h)])
            ld("pool", tfull[:, h, :], tfull_d[l, :, h, :], [("tfull", h)])
        P.op("pool", lambda e: e.memset(negt[:], NEG), writes=["negt"])
        ld("sp", kc[:], nak_v[:, :, S:ST], ["kcC"], reads=[("nak", 8)])
        ld("sp", vc[:], vaug_v[:, 32:34, 0:8, :], ["vcC"], reads=[("vaug", 32), ("vaug", 33)])

        def win(qb):
            a = 8 * qb
            lo = min(max(a - 4, 0), 56)
            hi = min(max(a + 3, 0), 56) + 7
            return a, lo // 2, hi // 2

        def loadC(qb):
            s = qb % 2
            t0, N = blk_info(qb)
            ld("sp", qt[:, s, :, 0:N], naq_v[:, :, t0:t0 + N], [("qtC", s)], reads=[("naq", qb)])
            if qb < 8:
                a, j0, j1 = win(qb)
                nj = j1 - j0 + 1
                blks = sorted(set([(j * 128) // 512 for j in range(j0, j1 + 1)]))
                ld("sp", kt[:, s, :, 0:nj * 128], nak_v[:, :, j0 * 128:(j1 + 1) * 128], [("ktC", s)], reads=[("nak", bb) for bb in blks])
                ld("sp", vt[:, s, 0:nj, :, :], vaug_v[:, j0:j1 + 1, 0:8, :], [("vtC", s)], reads=[("vaug", j) for j in range(j0, j1 + 1)])

        loadC(0)
        for qb in range(nblk_q):
            t0, N = blk_info(qb)
            if qb + 1 < nblk_q:
                loadC(qb + 1)
            s = qb % 2
            if qb < 8:
                a, j0, j1 = win(qb)
                nj = j1 - j0 + 1
            else:
                nj = 0
            for ci in range(4):
                def mk(h, ci=ci, nj=nj):
                    p0 = (h % 2) * 64

                    def qk_ops(i, ps):
                        if i < nj:
                            j = j0 + i
                            P.op("pe", lambda e: e.matmul(psum[ps][:, 0:N], kt[p0:p0 + 64, s, ci, i * 128:(i + 1) * 128], qt[p0:p0 + 64, s, ci, 0:N],
                                                          start=True, stop=False), reads=[("ktC", s), ("qtC", s)], writes=[("ps", ps)])
                            E0 = 2 * j - a + 7
                            segs = []
                            if a == 0:
                                if j <= 3:
                                    mf = 13 - E0
                                    segs.append((0, 256, tfull[:, h, mf * 64:(mf + 4) * 64], ("tfull", h)))
                                else:
                                    segs.append((0, 256, negt[:, 0:256], "negt"))
                                m0 = 17 - E0 + 4
                                segs.append((256, 512, tint[:, h, m0 * 64:(m0 + 4) * 64], ("tint", h)))
                            elif a == 56:
                                m0 = 17 - E0
                                segs.append((0, 320, tint[:, h, m0 * 64:(m0 + 5) * 64], ("tint", h)))
                                if j >= 28:
                                    mf = 13 - E0 + 5
                                    segs.append((320, 512, tfull[:, h, mf * 64:(mf + 3) * 64], ("tfull", h)))
                                else:
                                    segs.append((320, 512, negt[:, 0:192], "negt"))
                            else:
                                m0 = 17 - E0
                                segs.append((0, 512, tint[:, h, m0 * 64:(m0 + 8) * 64], ("tint", h)))

                            def post():
                                for (c0, c1, tab, tres) in segs:
                                    P.op("pe", lambda e, c0=c0, c1=c1, tab=tab: e.matmul(psum[ps][:, c0:c1], IDB, tab, start=False, stop=True),
                                         reads=["cst_b", tres], writes=[("ps", ps)])
                            return vt[:, s, i, h, :], [("vtC", s)], post
                        cj = i - nj
                        P.op("pe", lambda e: e.matmul(psum[ps][:, 0:N], kc[p0:p0 + 64, ci, cj * 128:(cj + 1) * 128], qt[p0:p0 + 64, s, ci, 0:N],
                                                      start=True, stop=True), reads=["kcC", ("qtC", s)], writes=[("ps", ps)])
                        return vc[:, cj, h, :], ["vcC"], None
                    return qk_ops

                hA, hB = 2 * ci, 2 * ci + 1
                attend2(N, mk(hA), mk(hB), nj + 2, pt, "ptC", rc, "rcC",
                        [(ost[:, s, hA, 0:N], ("ostC", s, hA)), (ost[:, s, hB, 0:N], ("ostC", s, hB))], 1.0)
            ld("sp", nao_v[:, :, t0:t0 + N], ost[:, s, :, 0:N], [("nao", qb)], reads=[("ostC", s, h) for h in range(8)])

    if "C" in dbg and l == 0:
        return
    with ExitStack() as ph:
        P.barrier()
        cs = sbuf(ph, "csD", [128, 2, S])
        gq = sbuf(ph, "gqD", [128, 4, ST], BF16)
        gk = sbuf(ph, "gkD", [128, ST], BF16)
        va = sbuf(ph, "vaD", [128, 34, 2, 128], BF16)
        raw = sbuf(ph, "rawD", [128, 2, 5, 512], BF16)
        sq = sbuf(ph, "sqD", [128, 4, 512], BF16)
        rs = sbuf(ph, "rsD", [128, 4, 512])
        qn = sbuf(ph, "qnD", [128, 4, 512], BF16)
        r1 = sbuf(ph, "r1D", [128, 4, 512])
        r2t = sbuf(ph, "r2D", [128, 4, 512])
        pt = sbuf(ph, "ptD", [128, 4, 2, 512], BF16)
        rc = sbuf(ph, "rcD", [128, 2, 2, 512])
        ost = sbuf(ph, "ostD", [64, 4, 512], BF16)
        ld("sp", cs[:], rope_d.rearrange("a p t -> p a t"), ["csD"])
        for half in range(2):
            ld("sp", va[:, half * 17:(half + 1) * 17, :, :], vaug_v[:, half * 17:(half + 1) * 17, 8:10, :], [("vaD", half)],
               reads=[("vaug", j) for j in range(half * 17, (half + 1) * 17)])

        def loadD(b):
            t0, N = blk_info(b)
            s = b % 2
            if b < nblk_q:
                ld("sp", raw[:, s, 0:4, 0:N], gqr_v[:, :, t0:t0 + N], [("rawD", s, c) for c in range(4)], reads=[("gqr", b)])
            ld("sp", raw[:, s, 4:5, 0:N], gkr_v[:, :, t0:t0 + N], [("rawD", s, 4)], reads=[("gkr", b)])

        loadD(0)
        ci = 0
        for b in range(NBLK):
            t0, N = blk_info(b)
            if b + 1 < NBLK:
                loadD(b + 1)
            s = b % 2
            for c in ([0, 1, 2, 3, 4] if b < nblk_q else [4]):
                s2 = ci % 4
                ci += 1
                P.op("act", lambda e, c=c, s2=s2: e.activation(out=sq[:, s2, 0:N], in_=raw[:, s, c, 0:N], func=AF.Square),
                     reads=[("rawD", s, c)], writes=[("sqD", s2)])
                p1 = pbank()
                P.op("pe", lambda e, p1=p1, s2=s2: e.matmul(psum[p1][:, 0:N], BD64, sq[:, s2, 0:N], start=True, stop=True),
                     reads=["cst_b", ("sqD", s2)], writes=[("ps", p1)])
                P.op("act", lambda e, p1=p1, s2=s2: e.activation(out=rs[:, s2, 0:N], in_=psum[p1][:, 0:N], func=AF.Ln, scale=1.0 / 64, bias=EPS),
                     reads=[("ps", p1)], writes=[("rsD", s2)])
                P.op("act", lambda e, s2=s2: e.activation(out=rs[:, s2, 0:N], in_=rs[:, s2, 0:N], func=AF.Exp, scale=-0.5),
                     reads=[("rsD", s2)], writes=[("rsD", s2)])
                gi = 0 if c < 4 else 1
                dst = gq[:, c, t0:t0 + N] if c < 4 else gk[:, t0:t0 + N]
                dres = ("gqD", c, b) if c < 4 else ("gkD", b)
                if b < 8:
                    P.op("dve", lambda e, c=c, s2=s2, gi=gi: e.scalar_tensor_tensor(out=qn[:, s2, 0:N], in0=raw[:, s, c, 0:N], scalar=qkg[:, l, gi:gi + 1], in1=rs[:, s2, 0:N],
                                                                             op0=ALU.mult, op1=ALU.mult),
                         reads=[("rawD", s, c), "qkg", ("rsD", s2)], writes=[("qnD", s2)])
                    p2 = pbank()
                    P.op("pe", lambda e, p2=p2, s2=s2: e.matmul(psum[p2][:, 0:N], PERM, qn[:, s2, 0:N], start=True, stop=True),
                         reads=["cst_b", ("qnD", s2)], writes=[("ps", p2)])
                    P.op("dve", lambda e, s2=s2: e.tensor_tensor(out=r1[:, s2, 0:N], in0=qn[:, s2, 0:N], in1=cs[:, 0, t0:t0 + N], op=ALU.mult),
                         reads=[("qnD", s2), "csD"], writes=[("r1D", s2)])
                    P.op("dve", lambda e, p2=p2, s2=s2: e.tensor_tensor(out=r2t[:, s2, 0:N], in0=psum[p2][:, 0:N], in1=cs[:, 1, t0:t0 + N], op=ALU.mult),
                         reads=[("ps", p2), "csD"], writes=[("r2D", s2)])
                    P.op("pool", lambda e, s2=s2, dst=dst: e.tensor_tensor(out=dst, in0=r1[:, s2, 0:N], in1=r2t[:, s2, 0:N], op=ALU.add),
                         reads=[("r1D", s2), ("r2D", s2)], writes=[dres])
                else:
                    P.op("dve", lambda e, c=c, s2=s2, gi=gi, dst=dst: e.scalar_tensor_tensor(out=dst, in0=raw[:, s, c, 0:N], scalar=qkg[:, l, gi:gi + 1], in1=rs[:, s2, 0:N],
                                                                                      op0=ALU.mult, op1=ALU.mult),
                         reads=[("rawD", s, c), "qkg", ("rsD", s2)], writes=[dres])
        oi = 0
        for qb in range(nblk_q):
            t0, N = blk_info(qb)
            chunks = list(range(34)) if qb < 8 else [32, 33]
            for jt in range(4):
                def mk(g, jt=jt, chunks=chunks):
                    p0 = g * 64

                    def qk_ops(i, ps):
                        c = chunks[i]
                        kb = c // 4 if c < 32 else 8
                        P.op("pe", lambda e: e.matmul(psum[ps][:, 0:N], gk[p0:p0 + 64, c * 128:(c + 1) * 128], gq[p0:p0 + 64, jt, t0:t0 + N], start=True, stop=True),
                             reads=[("gkD", kb), ("gqD", jt, qb)], writes=[("ps", ps)])
                        return va[:, c, g, :], [("vaD", c // 17)], None
                    return qk_ops

                soA = oi % 4
                soB = (oi + 1) % 4
                oi += 2
                attend2(N, mk(0), mk(1), len(chunks), pt, "ptD", rc, "rcD",
                        [(ost[:, soA, 0:N], ("ostD", soA)), (ost[:, soB, 0:N], ("ostD", soB))], 0.125)
                ld("sp", gqo_v[:, jt, t0:t0 + N], ost[:, soA, 0:N], [("gqo", qb, jt)], reads=[("ostD", soA)])
                ld("sp", gqo_v[:, jt + 4, t0:t0 + N], ost[:, soB, 0:N], [("gqo", qb, jt + 4)], reads=[("ostD", soB)])


def _host_prep(x, c, ctx, c_ctx, w_mod, b_mod, norm1_g, norm2_g, w_in, conv_w, conv_b, conv_ln_g, conv_ln_b, w_conv_out,
               na_rpb, w_na_out, q_norm_g, k_norm_g, w_gqa_out, w_out, w_ffn_in, w_ffn_out, final_g):
    f = np.float32
    x = np.asarray(x, f); ctx = np.asarray(ctx, f); c = np.asarray(c, f); c_ctx = np.asarray(c_ctx, f)
    B = x.shape[0]

    def pk(v):
        v = np.asarray(v, f)
        return np.ascontiguousarray(np.swapaxes(v.reshape(v.shape[:-1] + (-1, 128)), -1, -2))

    perm = np.concatenate([np.arange(0, 16), np.arange(32, 48), np.arange(16, 32), np.arange(48, 64)])
    w_in = np.asarray(w_in, f)
    cols = np.arange(INW)
    for j in range(4):
        for half in range(2):
            hh = j + 4 * half
            cols[2560 + j * 128 + half * 64:2560 + j * 128 + half * 64 + 64] = 2560 + hh * 64 + perm
    for g in range(2):
        cols[3072 + g * 64:3072 + g * 64 + 64] = 3072 + g * 64 + perm
    w_in_p = np.ascontiguousarray(w_in[:, :, cols])
    qg = np.asarray(q_norm_g, f)[:, perm]
    kg = np.asarray(k_norm_g, f)[:, perm]
    qkgT = np.stack([np.tile(qg, (1, 2)), np.tile(kg, (1, 2))], axis=-1)
    gnT = np.stack([pk(norm1_g[0]), pk(norm2_g[0]), pk(norm1_g[1]), pk(norm2_g[1]), pk(final_g)], axis=1)
    bmodT = pk(b_mod)
    convwT = np.ascontiguousarray(np.transpose(np.asarray(conv_w, f).reshape(L, CONV_K, 4, 128), (0, 3, 2, 1)))
    convvT = np.stack([pk(conv_b), pk(conv_ln_g), pk(conv_ln_b)], axis=2)
    rpb = np.asarray(na_rpb, f)
    kcg = np.arange(64)[:, None]
    cg = np.arange(64)[None, :]
    win0 = np.clip(cg - 8, 0, 48)
    colvalid = (kcg >= win0) & (kcg < win0 + 16)
    cidx = np.clip(kcg - cg + 15, 0, 30)

    def table(nm, etop, lo, hi):
        tab = np.full((L, 128, 8, nm, 64), NEG, f)
        for m in range(nm):
            e = etop - m
            for krr in range(2):
                dr = e + krr
                if lo <= dr <= hi:
                    vals = rpb[:, :, dr, :][:, :, cidx]
                    vals = np.where(colvalid[None, None], vals, f(NEG))
                    tab[:, krr * 64:(krr + 1) * 64, :, m, :] = np.transpose(vals, (0, 2, 1, 3))
        return np.ascontiguousarray(tab.reshape(L, 128, 8, nm * 64))

    tint = table(22, 17, 3, 10)
    tfull = table(14, 13, 0, 14)
    t = np.arange(S)
    prow = (t // GW).astype(np.float64)
    pcol = (t % GW).astype(np.float64)
    freqs = np.power(10000.0, -np.arange(0, 32, 2, dtype=np.float64) / 32).astype(np.float32).astype(np.float64)
    rope = np.zeros((2, 128, S), f)
    for p in range(128):
        dd = p % 64
        jf = dd % 16
        pos = prow if (dd // 16) % 2 == 0 else pcol
        ang = (pos.astype(np.float32) * freqs[jf].astype(np.float32)).astype(np.float32)
        rope[0, p] = np.cos(ang)
        rope[1, p] = np.sin(ang) * (-1.0 if dd < 32 else 1.0)
    consts = np.zeros((4, 128, 128), f)
    consts[0] = np.eye(128)
    consts[1] = 1.0
    consts[2, 0:64, 0:64] = 1.0
    consts[2, 64:128, 64:128] = 1.0
    for m in range(128):
        consts[3, m + 32 if (m % 64) < 32 else m - 32, m] = 1.0
    shared = dict(w_mod=np.asarray(w_mod, f), bmodT=bmodT, gnT=gnT, w_in=w_in_p, convwT=convwT, convvT=convvT,
                  w_conv_out=np.asarray(w_conv_out, f), tint=tint, tfull=tfull, w_na_out=np.asarray(w_na_out, f), qkgT=qkgT,
                  w_gqa_out=np.asarray(w_gqa_out, f), w_out=np.asarray(w_out, f), w_ffn_in=np.asarray(w_ffn_in, f),
                  w_ffn_out=np.asarray(w_ffn_out, f), rope=rope, consts=consts)
    cc = pk(c_ctx)
    in_maps = []
    for b in range(B):
        m = dict(shared)
        m["x"] = np.ascontiguousarray(x[b])
        m["ctx"] = np.ascontiguousarray(ctx[b])
        m["cT"] = np.ascontiguousarray(np.stack([pk(c[b]), cc], axis=-1))
        in_maps.append(m)
    return in_maps


_NC_CACHE = {}


def kernel(**inputs):
    in_maps = _host_prep(**inputs)
    if "nc" not in _NC_CACHE:
        _NC_CACHE["nc"] = build()
    res = run_bass_kernel_spmd(_NC_CACHE["nc"], in_maps, core_ids=list(range(8)))
    return np.stack([np.asarray(r["out"], np.float32) for r in res.results], axis=0)
```

```python
from contextlib import ExitStack
import os
import numpy as np
import concourse.bass as bass
import concourse.mybir as mybir
from concourse.bass_utils import run_bass_kernel_spmd

F32 = mybir.dt.float32
BF16 = mybir.dt.bfloat16
AF = mybir.ActivationFunctionType
ALU = mybir.AluOpType

D = 1024
S = 4096
T = 256
ST = S + T
L = 2
GW = 64
CONV_K = 31
HALO = 15
FFN = 2816
INW = 6400
EPS = 1e-6
NEG = -1e30
NBLK = 9
ENGS = ("pe", "act", "dve", "pool", "sp")
NDMA = 8


def blk_info(b):
    if b < 8:
        return b * 512, 512
    return S, T


class _Rec:
    def __init__(self):
        self.call = None

    def __getattr__(self, name):
        def f(*a, **k):
            self.call = (name, a, k)
            return self
        return f


def _capture(fn):
    r = _Rec()
    fn(r)
    assert r.call is not None
    return r.call


class Prog:
    def __init__(self, nc):
        self.nc = nc
        self.streams = {e: [] for e in ENGS}
        self.count = {e: 0 for e in ENGS}
        self.known = {e: {} for e in ENGS}
        self.res_w = {}
        self.res_r = {}
        self.dma_use = {}
        self.dma_rr = {e: 0 for e in ENGS}
        self.sems = {}

    def _deps(self, eng, reads, writes):
        need = {}

        def add(tok):
            if tok is None:
                return
            k, v = tok
            if k == "pe" and eng == "pe":
                return
            if need.get(k, 0) < v:
                need[k] = v

        for r in reads:
            add(self.res_w.get(r))
        for w in writes:
            add(self.res_w.get(w))
            for k, v in self.res_r.get(w, {}).items():
                add((k, v))
        return need

    def _commit(self, tok, reads, writes):
        k, v = tok
        for r in reads:
            d = self.res_r.setdefault(r, {})
            if d.get(k, 0) < v:
                d[k] = v
        for w in writes:
            self.res_w[w] = tok
            self.res_r[w] = {}

    def _emit_waits(self, eng, need):
        kn = self.known[eng]
        waits = []
        for k, v in need.items():
            if kn.get(k, 0) < v:
                kn[k] = v
                waits.append((k, v))
        return waits

    def op(self, eng, fn, reads=(), writes=()):
        need = self._deps(eng, reads, writes)
        waits = self._emit_waits(eng, need)
        self.count[eng] += 1
        tok = (eng, self.count[eng])
        self.streams[eng].append((_capture(fn), waits, (eng, 1)))
        self._commit(tok, reads, writes)
        return tok

    def dma(self, q, fn, reads=(), writes=()):
        need = self._deps(q, reads, writes)
        i = self.dma_rr[q]
        self.dma_rr[q] = (i + 1) % NDMA
        key = ("dma", q, i)
        u = self.dma_use.get(key, 0)
        if u > 0 and need.get(key, 0) < 16 * u:
            need[key] = 16 * u
        waits = self._emit_waits(q, need)
        self.dma_use[key] = u + 1
        tok = (key, 16 * (u + 1))
        self.streams[q].append((_capture(fn), waits, (key, 16)))
        self._commit(tok, reads, writes)
        return tok

    def wait_all(self, eng, toks):
        need = {}
        for k, v in toks:
            if need.get(k, 0) < v:
                need[k] = v
        waits = self._emit_waits(eng, need)
        self.streams[eng].append((None, waits, None))

    def barrier(self):
        toks = self.final_tokens()
        for e in ENGS:
            self.wait_all(e, toks)

    def final_tokens(self):
        toks = [(e, self.count[e]) for e in ENGS if self.count[e]]
        toks += [(key, 16 * u) for key, u in self.dma_use.items()]
        return toks

    def emit(self, stack):
        nc = self.nc
        targets = {e: set() for e in ENGS}
        keys = set()
        for e in ENGS:
            for fn, waits, inc in self.streams[e]:
                for k, v in waits:
                    keys.add(k)
                    if isinstance(k, str):
                        targets[k].add(v)
                if inc is not None and not isinstance(inc[0], str):
                    keys.add(inc[0])
        rank = {}
        for e in ENGS:
            srt = sorted(targets[e])
            rank[e] = {v: i + 1 for i, v in enumerate(srt)}
            if srt:
                keys.add(e)
        for k in sorted(keys, key=str):
            name = k if isinstance(k, str) else "d_%s_%d" % (k[1], k[2])
            self.sems[k] = stack.enter_context(nc.semaphore("s_" + name))
        block = stack.enter_context(nc.Block())

        def runner(ename):
            def run(engine):
                idx = 0
                for fn, waits, inc in self.streams[ename]:
                    for k, v in waits:
                        engine.wait_ge(self.sems[k], rank[k][v] if isinstance(k, str) else v)
                    if fn is not None:
                        name, a, kw = fn
                        ins = getattr(engine, name)(*a, **kw)
                        if isinstance(inc[0], str):
                            idx += 1
                            if idx in rank[ename]:
                                ins.then_inc(self.sems[ename], 1)
                        else:
                            ins.then_inc(self.sems[inc[0]], inc[1])
            return run

        block.tensor(runner("pe"))
        block.scalar(runner("act"))
        block.vector(runner("dve"))
        block.gpsimd(runner("pool"))
        block.sync(runner("sp"))


def build(n_layers=L, dbg=()):
    nc = bass.Bass("TRN2", target_bir_lowering=False)
    P = Prog(nc)

    def din(name, shape, dt=F32):
        return nc.dram_tensor(name, shape, dt, kind="ExternalInput").ap()

    def dscr(name, shape, dt=BF16):
        kind = "ExternalOutput" if name in dbg else "Internal"
        return nc.dram_tensor(name, shape, dt, kind=kind).ap()

    x_d = din("x", [S, D])
    ctx_d = din("ctx", [T, D])
    cT_d = din("cT", [128, 8, 2])
    wmod_d = din("w_mod", [L, D, 6 * D])
    bmod_d = din("bmodT", [L, 128, 48])
    gn_d = din("gnT", [128, 2 * L + 1, 8])
    win_d = din("w_in", [L, D, INW])
    convw_d = din("convwT", [L, 128, 4, CONV_K])
    convv_d = din("convvT", [L, 128, 3, 4])
    wco_d = din("w_conv_out", [L, 512, D])
    tint_d = din("tint", [L, 128, 8, 22 * 64])
    tfull_d = din("tfull", [L, 128, 8, 14 * 64])
    wno_d = din("w_na_out", [L, 512, D])
    qkg_d = din("qkgT", [L, 128, 2])
    wgo_d = din("w_gqa_out", [L, 512, D])
    wout_d = din("w_out", [L, D, D])
    wfi_d = din("w_ffn_in", [L, D, 2 * FFN])
    wfo_d = din("w_ffn_out", [L, FFN, D])
    rope_d = din("rope", [2, 128, S])
    cst_d = din("consts", [4, 128, 128])
    out_d = nc.dram_tensor("out", [S, D], F32, kind="ExternalOutput").ap()

    xT_d = dscr("xT", [D, ST], F32)
    UW = S + 2 * HALO
    UCW = T + 2 * HALO
    uT_d = dscr("uT", [512, UW + UCW])
    naq_d = dscr("naqT", [512, ST])
    nak_d = dscr("nakT", [512, ST])
    gqr_d = dscr("gqrT", [512, ST])
    gkr_d = dscr("gkrT", [128, ST])
    gat_d = dscr("gatesT", [3072, ST])
    vaug_d = dscr("vaug", [ST, 10, 128])
    ca_d = dscr("caT", [512, ST])
    nao_d = dscr("naoT", [512, ST])
    gqo_d = dscr("gqoT", [512, ST])

    with ExitStack() as top:
        uid = [0]

        def sbuf(st, name, shape, dt=F32):
            uid[0] += 1
            return st.enter_context(nc.sbuf_tensor("sb%d_%s" % (uid[0], name), shape, dt))

        psum_all = top.enter_context(nc.psum_tensor("pbanks", [128, 8, 512], F32))
        psum = [psum_all[:, i, :] for i in range(8)]
        prr = [0]

        def pbank():
            i = prr[0]
            prr[0] = (i + 1) % 8
            return i

        ident_f = sbuf(top, "ident_f", [128, 128])
        cst_b = sbuf(top, "cst_b", [128, 4, 128], BF16)
        modv = sbuf(top, "modv", [128, L, 48, 2])
        bmod = sbuf(top, "bmod", [128, L, 48])
        gn = sbuf(top, "gn", [128, 2 * L + 1, 8])
        Amod = sbuf(top, "Amod", [128, L, 2, 8, 2])
        rstd = sbuf(top, "rstd", [128, ST])
        stats = rstd
        sil = sbuf(top, "sil", [128, 8, 2])
        cT = sbuf(top, "cT_s", [128, 8, 2])
        zpad = sbuf(top, "zpad", [128, 4, HALO], BF16)
        convw = sbuf(top, "convw", [128, L, 4, CONV_K])
        convv = sbuf(top, "convv", [128, L, 3, 4])
        qkg = sbuf(top, "qkg", [128, L, 2])

        IDB = cst_b[:, 0, :]
        ONESB = cst_b[:, 1, :]
        BD64 = cst_b[:, 2, :]
        PERM = cst_b[:, 3, :]

        def ld(q, out, in_, writes, reads=()):
            P.dma(q, lambda e, o=out, i=in_: e.dma_start(out=o, in_=i), reads=reads, writes=writes)

        ld("sp", ident_f[:], cst_d[0], ["ident_f"])
        ld("pool", cst_b[:], cst_d.rearrange("c p n -> p c n"), ["cst_b"])
        ld("sp", bmod[:], bmod_d.rearrange("l p c -> p l c"), ["bmod"])
        ld("sp", gn[:], gn_d, ["gn"])
        ld("sp", cT[:], cT_d, ["cT"])
        ld("sp", convw[:], convw_d.rearrange("l p c k -> p l c k"), ["convw"])
        ld("sp", convv[:], convv_d.rearrange("l p a c -> p l a c"), ["convv"])
        ld("sp", qkg[:], qkg_d.rearrange("l p a -> p l a"), ["qkg"])
        P.op("pool", lambda e: e.memset(zpad[:], 0.0), writes=["zpad"])
        uT_v = uT_d.rearrange("(c p) t -> p c t", p=128)
        for off in (0, HALO + S, UW, UW + HALO + T):
            ld("sp", uT_v[:, :, off:off + HALO], zpad[:], [("uTpad", off)], reads=["zpad"])

        P.op("act", lambda e: e.activation(out=sil[:], in_=cT[:], func=AF.Exp, scale=-1.0), reads=["cT"], writes=["sil"])
        P.op("dve", lambda e: e.tensor_scalar_add(out=sil[:], in0=sil[:], scalar1=1.0), reads=["sil"], writes=["sil"])
        P.op("dve", lambda e: e.reciprocal(out=sil[:], in_=sil[:]), reads=["sil"], writes=["sil"])
        P.op("dve", lambda e: e.tensor_mul(out=sil[:], in0=sil[:], in1=cT[:]), reads=["sil", "cT"], writes=["sil"])
        with ExitStack() as ph:
            P.barrier()
            wm = sbuf(ph, "wm", [128, 2, 8, 1024])
            it = 0
            for l in range(n_layers):
                for cg in range(6):
                    sl = (it % 2) if not os.environ.get('WM1') else 0
                    it += 1
                    for k in range(8):
                        ld("sp", wm[:, sl, k, :], wmod_d[l, k * 128:(k + 1) * 128, cg * 1024:(cg + 1) * 1024],
                           [("wm", sl, k)])
                    for jc in range(8):
                        oc = cg * 8 + jc
                        pb = pbank()
                        for k in range(8):
                            P.op("pe", lambda e, pb=pb, sl=sl, k=k, jc=jc: e.matmul(
                                psum[pb][:, 0:2], wm[:, sl, k, jc * 128:(jc + 1) * 128], sil[:, k, :],
                                start=(k == 0), stop=(k == 7)),
                                reads=[("wm", sl, k), "sil"], writes=[("ps", pb)])
                        P.op("dve", lambda e, pb=pb, l=l, oc=oc: e.tensor_scalar(
                            out=modv[:, l, oc, :], in0=psum[pb][:, 0:2], scalar1=bmod[:, l, oc:oc + 1], scalar2=None,
                            op0=ALU.add), reads=[("ps", pb), "bmod"], writes=["modv"])
            for l in range(n_layers):
                for n in range(2):
                    for j in range(2):
                        P.op("dve", lambda e, l=l, n=n, j=j: e.scalar_tensor_tensor(
                            out=Amod[:, l, n, :, j], in0=modv[:, l, (8 + 24 * n):(16 + 24 * n), j], scalar=1.0,
                            in1=gn[:, 2 * l + n, :], op0=ALU.add, op1=ALU.mult),
                            reads=["modv", "gn"], writes=["Amod"])

        def shift_ap(l, n, k, j):
            return modv[:, l, 24 * n + k, j:j + 1]

        def gate_ap(l, n, k, j):
            return modv[:, l, 16 + 24 * n + k, j:j + 1]

        def stats_to_rstd(lo, hi, tag):
            P.op("act", lambda e: e.activation(out=rstd[:, lo:hi], in_=stats[:, lo:hi], func=AF.Ln, scale=1.0 / D, bias=EPS),
                 reads=["rstd"], writes=["rstd"])
            P.op("act", lambda e: e.activation(out=rstd[:, lo:hi], in_=rstd[:, lo:hi], func=AF.Exp, scale=-0.5),
                 reads=["rstd"], writes=["rstd"])

        def sumsq_block(src_tile_fn, src_res, t0, N, sqt, sqres):
            pb = pbank()
            for k in range(8):
                P.op("act", lambda e, k=k: e.activation(out=sqt[:, k, 0:N], in_=src_tile_fn(k), func=AF.Square),
                     reads=[src_res(k)], writes=[(sqres, k)])
                P.op("pe", lambda e, k=k, pb=pb: e.matmul(psum[pb][:, 0:N], ONESB, sqt[:, k, 0:N], start=(k == 0), stop=(k == 7)),
                     reads=[(sqres, k), "cst_b"], writes=[("ps", pb)])
            P.op("dve", lambda e, pb=pb: e.tensor_copy(out=stats[:, t0:t0 + N], in_=psum[pb][:, 0:N]),
                 reads=[("ps", pb)], writes=["rstd"])

        xT_v = xT_d.rearrange("(k p) t -> p k t", p=128)
        with ExitStack() as ph:
            P.barrier()
            xin = sbuf(ph, "xin", [128, 2, D])
            xst = sbuf(ph, "xst", [128, 2, 8, 512])
            sqt = sbuf(ph, "sqt0", [128, 8, 512], BF16)
            ti = 0
            for b in range(NBLK):
                t0, N = blk_info(b)
                sl = b % 2
                for tt in range(N // 128):
                    s2 = ti % 2
                    ti += 1
                    src = x_d[t0 + tt * 128:t0 + (tt + 1) * 128, :] if b < 8 else ctx_d[tt * 128:(tt + 1) * 128, :]
                    ld("sp", xin[:, s2, :], src, [("xin", s2)])
                    for kk in range(2):
                        pb = pbank()
                        for k4 in range(4):
                            k = kk * 4 + k4
                            P.op("pe", lambda e, pb=pb, k4=k4, k=k, s2=s2: e.transpose(
                                psum[pb][:, k4 * 128:(k4 + 1) * 128], xin[:, s2, k * 128:(k + 1) * 128], ident_f[:]),
                                reads=[("xin", s2), "ident_f"], writes=[("ps", pb)])
                        P.op("dve", lambda e, pb=pb, kk=kk, sl=sl, tt=tt: e.tensor_copy(
                            out=xst[:, sl, kk * 4:(kk + 1) * 4, tt * 128:(tt + 1) * 128],
                            in_=psum[pb][:, :].rearrange("p (k t) -> p k t", k=4)),
                            reads=[("ps", pb)], writes=[("xst", sl, kk)])
                ld("sp", xT_v[:, :, t0:t0 + N], xst[:, sl, :, 0:N], [("xT", b)], reads=[("xst", sl, 0), ("xst", sl, 1)])
                sumsq_block(lambda k, sl=sl, N=N: xst[:, sl, k, 0:N], lambda k, sl=sl: ("xst", sl, k // 4), t0, N, sqt, "sqt0")
        stats_to_rstd(0, ST, "n1l0")

        if "dump" in dbg:
            d1 = nc.dram_tensor("d_rstd", [128, ST], F32, kind="ExternalOutput").ap()
            d2 = nc.dram_tensor("d_modv", [128, L, 48, 2], F32, kind="ExternalOutput").ap()
            d3 = nc.dram_tensor("d_amod", [128, L, 2, 8, 2], F32, kind="ExternalOutput").ap()
            d4 = nc.dram_tensor("d_sil", [128, 8, 2], F32, kind="ExternalOutput").ap()
            ld("sp", d1, rstd[:], ["d1"], reads=["rstd"])
            ld("sp", d2, modv[:], ["d2"], reads=["modv"])
            ld("sp", d3, Amod[:], ["d3"], reads=["Amod"])
            ld("sp", d4, sil[:], ["d4"], reads=["sil"])
            P.wait_all("sp", P.final_tokens())
            P.emit(top)
            return nc
        def load_w_bf(dst_fn, src_fn, nk, res):
            for k in range(nk):
                ld("pool", dst_fn(k), src_fn(k), [(res, k)])

        def modulate_block(l, n, b, xt, xres, ht, hres, tmp, tmpres):
            t0, N = blk_info(b)
            j = 0 if b < 8 else 1
            for k in range(8):
                ts_ = k % 2
                P.op("dve", lambda e, k=k, ts_=ts_: e.tensor_tensor(out=tmp[:, ts_, 0:N], in0=xt(k), in1=rstd[:, t0:t0 + N], op=ALU.mult),
                     reads=[xres(k), "rstd"], writes=[(tmpres, ts_)])
                P.op("dve", lambda e, k=k, ts_=ts_: e.tensor_scalar(
                    out=ht[:, k, 0:N], in0=tmp[:, ts_, 0:N], scalar1=Amod[:, l, n, k, j:j + 1], scalar2=shift_ap(l, n, k, j),
                    op0=ALU.mult, op1=ALU.add), reads=[(tmpres, ts_), "Amod", "modv"], writes=[(hres, k)])

        for l in range(n_layers):
            last = (l == n_layers - 1) and (n_layers == L)
            nblk_q = 8 if last else NBLK
            with ExitStack() as ph:
                P.barrier()
                win = sbuf(ph, "win", [128, 8, INW], BF16)
                xt = sbuf(ph, "xtA", [128, 2, 8, 512])
                ht = sbuf(ph, "htA", [128, 8, 512], BF16)
                tmp = sbuf(ph, "tmpA", [128, 2, 512])
                sg = sbuf(ph, "sgA", [128, 2, 512])
                stg = sbuf(ph, "stgA", [128, 2, 8, 512], BF16)
                vst = sbuf(ph, "vstA", [128, 2, 10, 128], BF16)
                WIN_GROUPS = [(512, 1024), (0, 512), (1024, 1536), (2560, 3072), (3328, 4352), (4352, 5376), (5376, 6400),
                              (1536, 2048), (3072, 3328), (2048, 2560)]

                def wgrp(col):
                    for gi, (c0, c1) in enumerate(WIN_GROUPS):
                        if c0 <= col < c1:
                            return gi
                    raise AssertionError(col)

                for gi, (c0, c1) in enumerate(WIN_GROUPS):
                    for k in range(8):
                        ld("pool", win[:, k, c0:c1], win_d[l, k * 128:(k + 1) * 128, c0:c1], [("win", k, gi)])
                for s2 in range(2):
                    P.op("dve", lambda e, s2=s2: e.memset(vst[:, s2, :, 64:128], 1.0), writes=[("vst", s2)])

                def loadA(b):
                    t0, N = blk_info(b)
                    ld("sp", xt[:, b % 2, :, 0:N], xT_v[:, :, t0:t0 + N], [("xtA", b % 2, k) for k in range(8)], reads=[("xT", b)])

                uT_lat = uT_v[:, :, HALO:HALO + S]
                uT_ctx = uT_v[:, :, UW + HALO:UW + HALO + T]
                naq_v = naq_d.rearrange("(c p) t -> p c t", p=128)
                nak_v = nak_d.rearrange("(c p) t -> p c t", p=128)
                gqr_v = gqr_d.rearrange("(c p) t -> p c t", p=128)
                gkr_v = gkr_d.rearrange("(c p) t -> p c t", p=128)
                gat_v = gat_d.rearrange("(c p) t -> p c t", p=128)
                vaug_v = vaug_d.rearrange("(n p) h f -> p n h f", p=128)
                stgi = [0]
                vsi = [0]
                loadA(0)
                for b in range(NBLK):
                    t0, N = blk_info(b)
                    if b + 1 < NBLK:
                        loadA(b + 1)
                    sl = b % 2
                    modulate_block(l, 0, b, lambda k, sl=sl, N=N: xt[:, sl, k, 0:N], lambda k, sl=sl: ("xtA", sl, k),
                                   ht, "htA", tmp, "tmpA")

                    def proj(col, pb, N=N):
                        for k in range(8):
                            P.op("pe", lambda e, k=k: e.matmul(psum[pb][:, 0:N], win[:, k, col:col + 128], ht[:, k, 0:N],
                                                               start=(k == 0), stop=(k == 7)),
                                 reads=[("win", k, wgrp(col)), ("htA", k)], writes=[("ps", pb)])

                    def group(col0, nch, dst, kind, dres):
                        ss = stgi[0] % 2
                        stgi[0] += 1
                        for c in range(nch):
                            pb = pbank()
                            proj(col0 + c * 128, pb)
                            if kind == "sig":
                                P.op("act", lambda e, pb=pb, c=c: e.activation(out=stg[:, ss, c, 0:N], in_=psum[pb][:, 0:N], func=AF.Sigmoid),
                                     reads=[("ps", pb)], writes=[("stgA", ss, c)])
                            elif kind == "copy":
                                P.op("act", lambda e, pb=pb, c=c: e.activation(out=stg[:, ss, c, 0:N], in_=psum[pb][:, 0:N], func=AF.Identity),
                                     reads=[("ps", pb)], writes=[("stgA", ss, c)])
                            elif kind == "scale":
                                P.op("act", lambda e, pb=pb, c=c: e.activation(out=stg[:, ss, c, 0:N], in_=psum[pb][:, 0:N], func=AF.Identity, scale=0.125),
                                     reads=[("ps", pb)], writes=[("stgA", ss, c)])
                        ld("sp", dst, stg[:, ss, 0:nch, 0:N], [dres], reads=[("stgA", ss, c) for c in range(nch)])

                    ss = stgi[0] % 2
                    stgi[0] += 1
                    for c in range(4):
                        pg = pbank()
                        proj(512 + c * 128, pg)
                        s2 = c % 2
                        P.op("act", lambda e, pg=pg, s2=s2: e.activation(out=sg[:, s2, 0:N], in_=psum[pg][:, 0:N], func=AF.Sigmoid),
                             reads=[("ps", pg)], writes=[("sgA", s2)])
                        pv = pbank()
                        proj(c * 128, pv)
                        P.op("dve", lambda e, pv=pv, s2=s2, c=c, ss=ss: e.tensor_tensor(out=stg[:, ss, c, 0:N], in0=psum[pv][:, 0:N], in1=sg[:, s2, 0:N], op=ALU.mult),
                             reads=[("ps", pv), ("sgA", s2)], writes=[("stgA", ss, c)])
                    udst = uT_lat[:, :, t0:t0 + N] if b < 8 else uT_ctx
                    ld("sp", udst, stg[:, ss, 0:4, 0:N], [("uT", b)], reads=[("stgA", ss, c) for c in range(4)])
                    if b < nblk_q:
                        group(1024, 4, naq_v[:, :, t0:t0 + N], "scale", ("naq", b))
                        group(2560, 4, gqr_v[:, :, t0:t0 + N], "copy", ("gqr", b))
                        for g3 in range(3):
                            group(3328 + g3 * 1024, 8, gat_v[:, g3 * 8:(g3 + 1) * 8, t0:t0 + N], "sig", ("gat", b, g3))
                    group(1536, 4, nak_v[:, :, t0:t0 + N], "copy", ("nak", b))
                    group(3072, 1, gkr_v[:, :, t0:t0 + N], "copy", ("gkr", b))
                    for tt in range(N // 128):
                        s2 = vsi[0] % 2
                        vsi[0] += 1
                        p1 = pbank()
                        p2 = pbank()
                        for k in range(8):
                            P.op("pe", lambda e, k=k, tt=tt, p1=p1: e.matmul(psum[p1][:, 0:512], ht[:, k, tt * 128:(tt + 1) * 128], win[:, k, 2048:2560],
                                                                        start=(k == 0), stop=(k == 7)),
                                 reads=[("win", k, wgrp(2048)), ("htA", k)], writes=[("ps", p1)])
                        for k in range(8):
                            P.op("pe", lambda e, k=k, tt=tt, p2=p2: e.matmul(psum[p2][:, 0:128], ht[:, k, tt * 128:(tt + 1) * 128], win[:, k, 3200:3328],
                                                                        start=(k == 0), stop=(k == 7)),
                                 reads=[("win", k, wgrp(3200)), ("htA", k)], writes=[("ps", p2)])
                        P.op("dve", lambda e, p1=p1, s2=s2: e.tensor_copy(out=vst[:, s2, 0:8, 0:64], in_=psum[p1][:, 0:512].rearrange("p (h d) -> p h d", h=8)),
                             reads=[("ps", p1)], writes=[("vst", s2)])
                        P.op("dve", lambda e, p2=p2, s2=s2: e.tensor_copy(out=vst[:, s2, 8:10, 0:64], in_=psum[p2][:, 0:128].rearrange("p (h d) -> p h d", h=2)),
                             reads=[("ps", p2)], writes=[("vst", s2)])
                        ti = t0 // 128 + tt
                        ld("sp", vaug_v[:, ti, :, :], vst[:, s2, :, :], [("vaug", ti)], reads=[("vst", s2)])

            if "A" in dbg and l == 0:
                break
            ca_v = ca_d.rearrange("(c p) t -> p c t", p=128)
            with ExitStack() as ph:
                P.barrier()
                dg = sbuf(ph, "dg", [128, 4, CONV_K, 128], BF16)
                ut = sbuf(ph, "utB", [128, 2, 4, 512 + 2 * HALO], BF16)
                yb = sbuf(ph, "ybB", [128, 4, 512])
                ybf = sbuf(ph, "ybfB", [128, 4, 512], BF16)
                ysq = sbuf(ph, "ysqB", [128, 4, 512], BF16)
                mu = sbuf(ph, "muB", [128, 512])
                var = sbuf(ph, "varB", [128, 512])
                t1 = sbuf(ph, "t1B", [128, 2, 512])
                t2 = sbuf(ph, "t2B", [128, 2, 512])
                cast = sbuf(ph, "castB", [128, 2, 4, 512], BF16)
                for c in range(4):
                    for kk in range(CONV_K):
                        P.op("pool" if kk % 2 else "dve", lambda e, c=c, kk=kk: e.tensor_scalar(out=dg[:, c, kk, :], in0=IDB, scalar1=convw[:, l, c, kk:kk + 1], scalar2=None, op0=ALU.mult),
                             reads=["cst_b", "convw"], writes=[("dg", c, kk)])

                def loadB(b):
                    t0, N = blk_info(b)
                    if b < 8:
                        src = uT_v[:, :, t0:t0 + N + 2 * HALO]
                        rd = [("uT", bb) for bb in (b - 1, b, b + 1) if 0 <= bb < 8] + [("uTpad", 0), ("uTpad", HALO + S)]
                    else:
                        src = uT_v[:, :, UW:UW + UCW]
                        rd = [("uT", 8), ("uTpad", UW), ("uTpad", UW + HALO + T)]
                    ld("sp", ut[:, b % 2, :, 0:N + 2 * HALO], src, [("utB", b % 2)], reads=rd)

                loadB(0)
                for b in range(nblk_q):
                    t0, N = blk_info(b)
                    if b + 1 < nblk_q:
                        loadB(b + 1)
                    sl = b % 2
                    ps1 = pbank()
                    ps2 = pbank()
                    for c in range(4):
                        pb = pbank()
                        for kk in range(CONV_K):
                            P.op("pe", lambda e, c=c, kk=kk, pb=pb: e.matmul(psum[pb][:, 0:N], dg[:, c, kk, :], ut[:, sl, c, kk:kk + N],
                                                                        start=(kk == 0), stop=(kk == CONV_K - 1)),
                                 reads=[("dg", c, kk), ("utB", sl)], writes=[("ps", pb)])
                        P.op("dve", lambda e, c=c, pb=pb: e.tensor_scalar(out=yb[:, c, 0:N], in0=psum[pb][:, 0:N], scalar1=convv[:, l, 0, c:c + 1], scalar2=None, op0=ALU.add),
                             reads=[("ps", pb), "convv"], writes=[("ybB", c)])
                        P.op("pool", lambda e, c=c: e.tensor_copy(out=ybf[:, c, 0:N], in_=yb[:, c, 0:N]), reads=[("ybB", c)], writes=[("ybfB", c)])
                        P.op("act", lambda e, c=c: e.activation(out=ysq[:, c, 0:N], in_=yb[:, c, 0:N], func=AF.Square), reads=[("ybB", c)], writes=[("ysqB", c)])
                    for c in range(4):
                        P.op("pe", lambda e, c=c: e.matmul(psum[ps1][:, 0:N], ONESB, ybf[:, c, 0:N], start=(c == 0), stop=(c == 3)),
                             reads=[("ybfB", c), "cst_b"], writes=[("ps", ps1)])
                    for c in range(4):
                        P.op("pe", lambda e, c=c: e.matmul(psum[ps2][:, 0:N], ONESB, ysq[:, c, 0:N], start=(c == 0), stop=(c == 3)),
                             reads=[("ysqB", c), "cst_b"], writes=[("ps", ps2)])
                    P.op("dve", lambda e: e.tensor_scalar(out=mu[:, 0:N], in0=psum[ps1][:, 0:N], scalar1=1.0 / 512, scalar2=None, op0=ALU.mult),
                         reads=[("ps", ps1)], writes=["muB"])
                    P.op("dve", lambda e: e.tensor_tensor(out=var[:, 0:N], in0=mu[:, 0:N], in1=mu[:, 0:N], op=ALU.mult), reads=["muB"], writes=["varB"])
                    P.op("dve", lambda e: e.scalar_tensor_tensor(out=var[:, 0:N], in0=psum[ps2][:, 0:N], scalar=1.0 / 512, in1=var[:, 0:N], op0=ALU.mult, op1=ALU.subtract),
                         reads=[("ps", ps2), "varB"], writes=["varB"])
                    P.op("act", lambda e: e.activation(out=var[:, 0:N], in_=var[:, 0:N], func=AF.Ln, bias=EPS), reads=["varB"], writes=["varB"])
                    P.op("act", lambda e: e.activation(out=var[:, 0:N], in_=var[:, 0:N], func=AF.Exp, scale=-0.5), reads=["varB"], writes=["varB"])
                    for c in range(4):
                        s2 = c % 2
                        P.op("pool", lambda e, c=c, s2=s2: e.tensor_tensor(out=t1[:, s2, 0:N], in0=yb[:, c, 0:N], in1=mu[:, 0:N], op=ALU.subtract),
                             reads=[("ybB", c), "muB"], writes=[("t1B", s2)])
                        P.op("pool", lambda e, s2=s2: e.tensor_tensor(out=t1[:, s2, 0:N], in0=t1[:, s2, 0:N], in1=var[:, 0:N], op=ALU.mult),
                             reads=[("t1B", s2), "varB"], writes=[("t1B", s2)])
                        P.op("dve", lambda e, c=c, s2=s2: e.tensor_scalar(out=t1[:, s2, 0:N], in0=t1[:, s2, 0:N], scalar1=convv[:, l, 1, c:c + 1], scalar2=convv[:, l, 2, c:c + 1],
                                                                      op0=ALU.mult, op1=ALU.add), reads=[("t1B", s2), "convv"], writes=[("t1B", s2)])
                        P.op("act", lambda e, s2=s2, c=c: e.activation(out=cast[:, sl, c, 0:N], in_=t1[:, s2, 0:N], func=AF.Silu), reads=[("t1B", s2)], writes=[("castB", sl, c)])
                    ld("sp", ca_v[:, :, t0:t0 + N], cast[:, sl, :, 0:N], [("ca", b)], reads=[("castB", sl, c) for c in range(4)])

            if "B" in dbg and l == 0:
                break
            ATTN(nc, P, l, nblk_q, psum_all, psum, pbank, sbuf, ld, cst_b, qkg, naq_d, nak_d, vaug_d, gqr_d, gkr_d, nao_d, gqo_d,
                 tint_d, tfull_d, rope_d, dbg)
            if "D" in dbg and l == 0:
                break
            nao_v = nao_d.rearrange("(c p) t -> p c t", p=128)
            gqo_v = gqo_d.rearrange("(c p) t -> p c t", p=128)
            with ExitStack() as ph:
                P.barrier()
                wco = sbuf(ph, "wco", [128, 4, D], BF16)
                wno = sbuf(ph, "wno", [128, 4, D], BF16)
                wgo = sbuf(ph, "wgo", [128, 4, D], BF16)
                wo = sbuf(ph, "wo", [128, 8, D], BF16)
                xt = sbuf(ph, "xtE", [128, 2, 8, 512])
                cat = sbuf(ph, "catE", [128, 2, 4, 512], BF16)
                nat = sbuf(ph, "natE", [128, 2, 4, 512], BF16)
                gqt = sbuf(ph, "gqtE", [128, 2, 4, 512], BF16)
                gt = sbuf(ph, "gtE", [128, 2, 24, 512], BF16)
                yt = sbuf(ph, "ytE", [128, 8, 512], BF16)
                m1 = sbuf(ph, "m1E", [128, 2, 512])
                m2 = sbuf(ph, "m2E", [128, 2, 512])
                sqt = sbuf(ph, "sqtE", [128, 8, 512], BF16)
                load_w_bf(lambda k: wco[:, k, :], lambda k: wco_d[l, k * 128:(k + 1) * 128, :], 4, "wco")
                load_w_bf(lambda k: wno[:, k, :], lambda k: wno_d[l, k * 128:(k + 1) * 128, :], 4, "wno")
                load_w_bf(lambda k: wgo[:, k, :], lambda k: wgo_d[l, k * 128:(k + 1) * 128, :], 4, "wgo")
                load_w_bf(lambda k: wo[:, k, :], lambda k: wout_d[l, k * 128:(k + 1) * 128, :], 8, "wo")

                def loadE(b):
                    t0, N = blk_info(b)
                    s = b % 2
                    ld("sp", xt[:, s, :, 0:N], xT_v[:, :, t0:t0 + N], [("xtE", s, k) for k in range(8)], reads=[("xT", b)])
                    ld("sp", cat[:, s, :, 0:N], ca_v[:, :, t0:t0 + N], [("catE", s)], reads=[("ca", b)])
                    ld("sp", nat[:, s, :, 0:N], nao_v[:, :, t0:t0 + N], [("natE", s)], reads=[("nao", b)])
                    ld("sp", gqt[:, s, :, 0:N], gqo_v[:, :, t0:t0 + N], [("gqtE", s)], reads=[("gqo", b, h) for h in range(8)])
                    for g3 in range(3):
                        ld("sp", gt[:, s, g3 * 8:(g3 + 1) * 8, 0:N], gat_v[:, g3 * 8:(g3 + 1) * 8, t0:t0 + N], [("gtE", s, g3)], reads=[("gat", b, g3)])

                loadE(0)
                for b in range(nblk_q):
                    t0, N = blk_info(b)
                    j = 0 if b < 8 else 1
                    if b + 1 < nblk_q:
                        loadE(b + 1)
                    s = b % 2
                    for fc in range(8):
                        fs = slice(fc * 128, (fc + 1) * 128)
                        s2 = fc % 2
                        pa = pbank()
                        for k in range(4):
                            P.op("pe", lambda e, k=k, pa=pa, fs=fs: e.matmul(psum[pa][:, 0:N], wco[:, k, fs], cat[:, s, k, 0:N], start=(k == 0), stop=(k == 3)),
                                 reads=[("wco", k), ("catE", s)], writes=[("ps", pa)])
                        P.op("dve", lambda e, pa=pa, fc=fc, s2=s2: e.tensor_tensor(out=m1[:, s2, 0:N], in0=psum[pa][:, 0:N], in1=gt[:, s, fc, 0:N], op=ALU.mult),
                             reads=[("ps", pa), ("gtE", s, 0)], writes=[("m1E", s2)])
                        pn = pbank()
                        for h in range(4):
                            P.op("pe", lambda e, h=h, pn=pn, fs=fs: e.matmul(psum[pn][:, 0:N], wno[:, h, fs], nat[:, s, h, 0:N], start=(h == 0), stop=(h == 3)),
                                 reads=[("wno", h), ("natE", s)], writes=[("ps", pn)])
                        P.op("dve", lambda e, pn=pn, fc=fc, s2=s2: e.tensor_tensor(out=m2[:, s2, 0:N], in0=psum[pn][:, 0:N], in1=gt[:, s, 8 + fc, 0:N], op=ALU.mult),
                             reads=[("ps", pn), ("gtE", s, 1)], writes=[("m2E", s2)])
                        P.op("pool", lambda e, s2=s2: e.tensor_tensor(out=m1[:, s2, 0:N], in0=m1[:, s2, 0:N], in1=m2[:, s2, 0:N], op=ALU.add),
                             reads=[("m1E", s2), ("m2E", s2)], writes=[("m1E", s2)])
                        pg = pbank()
                        for h in range(4):
                            P.op("pe", lambda e, h=h, pg=pg, fs=fs: e.matmul(psum[pg][:, 0:N], wgo[:, h, fs], gqt[:, s, h, 0:N], start=(h == 0), stop=(h == 3)),
                                 reads=[("wgo", h), ("gqtE", s)], writes=[("ps", pg)])
                        P.op("dve", lambda e, pg=pg, fc=fc, s2=s2: e.tensor_tensor(out=m2[:, s2, 0:N], in0=psum[pg][:, 0:N], in1=gt[:, s, 16 + fc, 0:N], op=ALU.mult),
                             reads=[("ps", pg), ("gtE", s, 2)], writes=[("m2E", s2)])
                        P.op("pool", lambda e, s2=s2, fc=fc: e.tensor_tensor(out=yt[:, fc, 0:N], in0=m1[:, s2, 0:N], in1=m2[:, s2, 0:N], op=ALU.add),
                             reads=[("m1E", s2), ("m2E", s2)], writes=[("ytE", fc)])
                    for fc in range(8):
                        fs = slice(fc * 128, (fc + 1) * 128)
                        po = pbank()
                        for k in range(8):
                            P.op("pe", lambda e, k=k, po=po, fs=fs: e.matmul(psum[po][:, 0:N], wo[:, k, fs], yt[:, k, 0:N], start=(k == 0), stop=(k == 7)),
                                 reads=[("wo", k), ("ytE", k)], writes=[("ps", po)])
                        P.op("dve", lambda e, po=po, fc=fc: e.scalar_tensor_tensor(out=xt[:, s, fc, 0:N], in0=psum[po][:, 0:N], scalar=gate_ap(l, 0, fc, j), in1=xt[:, s, fc, 0:N],
                                                                              op0=ALU.mult, op1=ALU.add),
                             reads=[("ps", po), "modv", ("xtE", s, fc)], writes=[("xtE", s, fc)])
                    ld("sp", xT_v[:, :, t0:t0 + N], xt[:, s, :, 0:N], [("xT", b)], reads=[("xtE", s, k) for k in range(8)])
                    sumsq_block(lambda k, s=s, N=N: xt[:, s, k, 0:N], lambda k, s=s: ("xtE", s, k), t0, N, sqt, "sqtE")
            stats_to_rstd(0, ST if nblk_q == NBLK else S, "n2")
            if "E" in dbg and l == 0:
                break
            with ExitStack() as ph:
                P.barrier()
                wfi = sbuf(ph, "wfi", [128, 8, 2 * FFN], BF16)
                wfo = sbuf(ph, "wfo", [128, 22, D], BF16)
                xt = sbuf(ph, "xtF", [128, 1, 8, 512])
                ht = sbuf(ph, "htF", [128, 8, 512], BF16)
                tmp = sbuf(ph, "tmpF", [128, 2, 512])
                hid = sbuf(ph, "hidF", [128, 22, 512], BF16)
                sgt = sbuf(ph, "sgF", [128, 2, 512], BF16)
                sqt = ht
                HG = [(0, 4), (4, 8), (8, 12), (12, 16), (16, 20), (20, 22)]

                def hgrp(hc):
                    return hc // 4

                for (h0, h1) in HG:
                    for base in (0, FFN):
                        for k in range(8):
                            ld("pool", wfi[:, k, base + h0 * 128:base + h1 * 128], wfi_d[l, k * 128:(k + 1) * 128, base + h0 * 128:base + h1 * 128],
                               [("wfi", k, base, h0 // 4)])
                load_w_bf(lambda k: wfo[:, k, :], lambda k: wfo_d[l, k * 128:(k + 1) * 128, :], 22, "wfo")

                def loadF(b):
                    t0, N = blk_info(b)
                    for k in range(8):
                        ld("sp", xt[:, 0, k, 0:N], xT_v[:, k, t0:t0 + N], [("xtF", 0, k)], reads=[("xT", b)])

                loadF(0)
                for b in range(nblk_q):
                    t0, N = blk_info(b)
                    j = 0 if b < 8 else 1
                    s = 0
                    modulate_block(l, 1, b, lambda k, s=s, N=N: xt[:, s, k, 0:N], lambda k, s=s: ("xtF", s, k), ht, "htF", tmp, "tmpF")
                    for hc in range(22):
                        s2 = hc % 2
                        pg = pbank()
                        for k in range(8):
                            P.op("pe", lambda e, k=k, pg=pg, hc=hc: e.matmul(psum[pg][:, 0:N], wfi[:, k, hc * 128:(hc + 1) * 128], ht[:, k, 0:N], start=(k == 0), stop=(k == 7)),
                                 reads=[("wfi", k, 0, hgrp(hc)), ("htF", k)], writes=[("ps", pg)])
                        P.op("act", lambda e, pg=pg, s2=s2: e.activation(out=sgt[:, s2, 0:N], in_=psum[pg][:, 0:N], func=AF.Silu),
                             reads=[("ps", pg)], writes=[("sgF", s2)])
                        pu = pbank()
                        for k in range(8):
                            P.op("pe", lambda e, k=k, pu=pu, hc=hc: e.matmul(psum[pu][:, 0:N], wfi[:, k, FFN + hc * 128:FFN + (hc + 1) * 128], ht[:, k, 0:N], start=(k == 0), stop=(k == 7)),
                                 reads=[("wfi", k, FFN, hgrp(hc)), ("htF", k)], writes=[("ps", pu)])
                        P.op("dve", lambda e, pu=pu, s2=s2, hc=hc: e.tensor_tensor(out=hid[:, hc, 0:N], in0=psum[pu][:, 0:N], in1=sgt[:, s2, 0:N], op=ALU.mult),
                             reads=[("ps", pu), ("sgF", s2)], writes=[("hidF", hc)])
                    pst = pbank()
                    for fc in range(8):
                        po = pbank()
                        if po == pst:
                            po = pbank()
                        for hc in range(22):
                            P.op("pe", lambda e, hc=hc, po=po, fc=fc: e.matmul(psum[po][:, 0:N], wfo[:, hc, fc * 128:(fc + 1) * 128], hid[:, hc, 0:N], start=(hc == 0), stop=(hc == 21)),
                                 reads=[("wfo", hc), ("hidF", hc)], writes=[("ps", po)])
                        P.op("dve", lambda e, po=po, fc=fc: e.scalar_tensor_tensor(out=xt[:, s, fc, 0:N], in0=psum[po][:, 0:N], scalar=gate_ap(l, 1, fc, j), in1=xt[:, s, fc, 0:N],
                                                                              op0=ALU.mult, op1=ALU.add),
                             reads=[("ps", po), "modv", ("xtF", s, fc)], writes=[("xtF", s, fc)])
                        ld("sp", xT_v[:, fc, t0:t0 + N], xt[:, s, fc, 0:N], [("xT", b)], reads=[("xtF", s, fc)])
                        P.op("act", lambda e, fc=fc: e.activation(out=sqt[:, fc, 0:N], in_=xt[:, s, fc, 0:N], func=AF.Square),
                             reads=[("xtF", s, fc)], writes=[("htF", fc)])
                        P.op("pe", lambda e, fc=fc, pst=pst: e.matmul(psum[pst][:, 0:N], ONESB, sqt[:, fc, 0:N], start=(fc == 0), stop=(fc == 7)),
                             reads=[("htF", fc), "cst_b"], writes=[("ps", pst)])
                    P.op("dve", lambda e, pst=pst: e.tensor_copy(out=stats[:, t0:t0 + N], in_=psum[pst][:, 0:N]),
                         reads=[("ps", pst)], writes=["rstd"])
                    if b + 1 < nblk_q:
                        loadF(b + 1)
            stats_to_rstd(0, ST if nblk_q == NBLK else S, "n1next")

        if not dbg:
            with ExitStack() as ph:
                P.barrier()
                xt = sbuf(ph, "xtZ", [128, 2, 8, 512])
                tmp = sbuf(ph, "tmpZ", [128, 2, 512])
                ot = sbuf(ph, "otZ", [128, 4, D])
                oi = 0
                ld("sp", xt[:, 0, :, :], xT_v[:, :, 0:512], [("xtZ", 0, k) for k in range(8)], reads=[("xT", 0)])
                for b in range(8):
                    t0 = b * 512
                    if b + 1 < 8:
                        ld("sp", xt[:, (b + 1) % 2, :, :], xT_v[:, :, t0 + 512:t0 + 1024], [("xtZ", (b + 1) % 2, k) for k in range(8)], reads=[("xT", b + 1)])
                    s = b % 2
                    for k in range(8):
                        s2 = k % 2
                        P.op("pool", lambda e, k=k, s2=s2: e.tensor_tensor(out=tmp[:, s2, :], in0=xt[:, s, k, :], in1=rstd[:, t0:t0 + 512], op=ALU.mult),
                             reads=[("xtZ", s, k), "rstd"], writes=[("tmpZ", s2)])
                        P.op("dve", lambda e, k=k, s2=s2: e.tensor_scalar(out=xt[:, s, k, :], in0=tmp[:, s2, :], scalar1=gn[:, 2 * L, k:k + 1], scalar2=None, op0=ALU.mult),
                             reads=[("tmpZ", s2), "gn"], writes=[("xtZ", s, k)])
                    for tt in range(4):
                        so = oi % 4
                        oi += 1
                        for kk in range(2):
                            pb = pbank()
                            for k4 in range(4):
                                k = kk * 4 + k4
                                P.op("pe", lambda e, pb=pb, k4=k4, k=k, tt=tt: e.transpose(psum[pb][:, k4 * 128:(k4 + 1) * 128], xt[:, s, k, tt * 128:(tt + 1) * 128], ident_f[:]),
                                     reads=[("xtZ", s, k), "ident_f"], writes=[("ps", pb)])
                            P.op("dve", lambda e, pb=pb, kk=kk, so=so: e.tensor_copy(out=ot[:, so, kk * 512:(kk + 1) * 512], in_=psum[pb][:, :]),
                                 reads=[("ps", pb)], writes=[("otZ", so)])
                        ld("sp", out_d[t0 + tt * 128:t0 + (tt + 1) * 128, :], ot[:, so, :], [("out", b, tt)], reads=[("otZ", so)])
        P.wait_all("sp", P.final_tokens())
        P.emit(top)
    return nc


def ATTN(nc, P, l, nblk_q, psum_all, psum, pbank, sbuf, ld, cst_b, qkg, naq_d, nak_d, vaug_d, gqr_d, gkr_d, nao_d, gqo_d,
         tint_d, tfull_d, rope_d, dbg):
    IDB = cst_b[:, 0, :]
    BD64 = cst_b[:, 2, :]
    PERM = cst_b[:, 3, :]
    naq_v = naq_d.rearrange("(c p) t -> p c t", p=128)
    nak_v = nak_d.rearrange("(c p) t -> p c t", p=128)
    gqr_v = gqr_d.rearrange("(c p) t -> p c t", p=128)
    gkr_v = gkr_d.rearrange("(c p) t -> p c t", p=128)
    vaug_v = vaug_d.rearrange("(n p) h f -> p n h f", p=128)
    nao_v = nao_d.rearrange("(h d) t -> d h t", d=64)
    gqo_v = gqo_d.rearrange("(h d) t -> d h t", d=64)
    ACC = [0, 1]
    SPAIRS = [2, 4, 6]
    LA = 1
    cnt = {"acc": 0, "s": 0, "pt": 0, "rc": 0}

    def nxt(key, n):
        v = cnt[key] % n
        cnt[key] += 1
        return v

    def attend2(N, qk_A, qk_B, nchunks, pt, ptres, rc, rcres, outs, scale):
        assert nchunks % 2 == 0
        pos = [ACC[0], ACC[1]]
        pend = []

        def emit_pv(info):
            j, items = info
            for hh in range(2):
                pi, vls = items[hh]
                for t in range(2):
                    i = 2 * j + t
                    vl, vres, _ = vls[t]
                    P.op("pe", lambda e, vl=vl, pi=pi, i=i, t=t, hh=hh: e.matmul(psum[pos[hh]][:, 0:N], vl, pt[:, pi, t, 0:N], start=(i == 0), stop=(i == nchunks - 1)),
                         reads=list(vres) + [(ptres, pi)], writes=[("ps", pos[hh])])

        for j in range(nchunks // 2):
            sbs = [SPAIRS[nxt("s", 3)], SPAIRS[nxt("s", 3)]]
            vls = [[None, None], [None, None]]
            for t in range(2):
                order = (0, 1) if t == 0 else (1, 0)
                for hh in order:
                    vls[hh][t] = (qk_A if hh == 0 else qk_B)(2 * j + t, sbs[hh] + t)
                for hh in order:
                    if vls[hh][t][2] is not None:
                        vls[hh][t][2]()
            items = []
            for hh in range(2):
                sb = sbs[hh]
                pi = nxt("pt", 4)
                P.op("act", lambda e, sb=sb, pi=pi: e.activation(out=pt[:, pi, :, 0:N], in_=psum_all[:, sb:sb + 2, 0:N], func=AF.Exp, scale=scale),
                     reads=[("ps", sb), ("ps", sb + 1)], writes=[(ptres, pi)])
                items.append((pi, vls[hh]))
            pend.append((j, items))
            if len(pend) > LA:
                emit_pv(pend.pop(0))
        while pend:
            emit_pv(pend.pop(0))
        for hh in range(2):
            po = pos[hh]
            out_ap, out_res = outs[hh]
            r2 = nxt("rc", 2)
            P.op("dve", lambda e, po=po, r2=r2: e.tensor_copy(out=rc[:, r2, 0, 0:N], in_=psum[po][:, 0:N]), reads=[("ps", po)], writes=[(rcres, r2, 0)])
            P.op("dve", lambda e, r2=r2: e.reciprocal(out=rc[0:64, r2, 1, 0:N], in_=rc[64:128, r2, 0, 0:N]), reads=[(rcres, r2, 0)], writes=[(rcres, r2, 1)])
            P.op("pool", lambda e, r2=r2, out_ap=out_ap: e.tensor_tensor(out=out_ap, in0=rc[0:64, r2, 0, 0:N], in1=rc[0:64, r2, 1, 0:N], op=ALU.mult),
                 reads=[(rcres, r2, 0), (rcres, r2, 1)], writes=[out_res])

    with ExitStack() as ph:
        P.barrier()
        tint = sbuf(ph, "tint", [128, 8, 22 * 64], BF16)
        tfull = sbuf(ph, "tfull", [128, 8, 14 * 64], BF16)
        negt = sbuf(ph, "negt", [128, 256], BF16)
        kc = sbuf(ph, "kcC", [128, 4, T], BF16)
        vc = sbuf(ph, "vcC", [128, 2, 8, 128], BF16)
        qt = sbuf(ph, "qtC", [128, 2, 4, 512], BF16)
        kt = sbuf(ph, "ktC", [128, 2, 4, 1024], BF16)
        vt = sbuf(ph, "vtC", [128, 2, 8, 8, 128], BF16)
        pt = sbuf(ph, "ptC", [128, 4, 2, 512], BF16)
        rc = sbuf(ph, "rcC", [128, 2, 2, 512])
        ost = sbuf(ph, "ostC", [64, 2, 8, 512], BF16)
        for h in range(8):
            ld("pool", tint[:, h, :], tint_d[l, :, h, :], [("tint", h)])
            ld("pool", tfull[:, h, :], tfull_d[l, :, h, :], [("tfull", h)])
        P.op("pool", lambda e: e.memset(negt[:], NEG), writes=["negt"])
        ld("sp", kc[:], nak_v[:, :, S:ST], ["kcC"], reads=[("nak", 8)])
        ld("sp", vc[:], vaug_v[:, 32:34, 0:8, :], ["vcC"], reads=[("vaug", 32), ("vaug", 33)])

        def win(qb):
            a = 8 * qb
            lo = min(max(a - 4, 0), 56)
            hi = min(max(a + 3, 0), 56) + 7
            return a, lo // 2, hi // 2

        def loadC(qb):
            s = qb % 2
            t0, N = blk_info(qb)
            ld("sp", qt[:, s, :, 0:N], naq_v[:, :, t0:t0 + N], [("qtC", s)], reads=[("naq", qb)])
            if qb < 8:
                a, j0, j1 = win(qb)
                nj = j1 - j0 + 1
                blks = sorted(set([(j * 128) // 512 for j in range(j0, j1 + 1)]))
                ld("sp", kt[:, s, :, 0:nj * 128], nak_v[:, :, j0 * 128:(j1 + 1) * 128], [("ktC", s)], reads=[("nak", bb) for bb in blks])
                ld("sp", vt[:, s, 0:nj, :, :], vaug_v[:, j0:j1 + 1, 0:8, :], [("vtC", s)], reads=[("vaug", j) for j in range(j0, j1 + 1)])

        loadC(0)
        for qb in range(nblk_q):
            t0, N = blk_info(qb)
            if qb + 1 < nblk_q:
                loadC(qb + 1)
            s = qb % 2
            if qb < 8:
                a, j0, j1 = win(qb)
                nj = j1 - j0 + 1
            else:
                nj = 0
            for ci in range(4):
                def mk(h, ci=ci, nj=nj):
                    p0 = (h % 2) * 64

                    def qk_ops(i, ps):
                        if i < nj:
                            j = j0 + i
                            P.op("pe", lambda e: e.matmul(psum[ps][:, 0:N], kt[p0:p0 + 64, s, ci, i * 128:(i + 1) * 128], qt[p0:p0 + 64, s, ci, 0:N],
                                                          start=True, stop=False), reads=[("ktC", s), ("qtC", s)], writes=[("ps", ps)])
                            E0 = 2 * j - a + 7
                            segs = []
                            if a == 0:
                                if j <= 3:
                                    mf = 13 - E0
                                    segs.append((0, 256, tfull[:, h, mf * 64:(mf + 4) * 64], ("tfull", h)))
                                else:
                                    segs.append((0, 256, negt[:, 0:256], "negt"))
                                m0 = 17 - E0 + 4
                                segs.append((256, 512, tint[:, h, m0 * 64:(m0 + 4) * 64], ("tint", h)))
                            elif a == 56:
                                m0 = 17 - E0
                                segs.append((0, 320, tint[:, h, m0 * 64:(m0 + 5) * 64], ("tint", h)))
                                if j >= 28:
                                    mf = 13 - E0 + 5
                                    segs.append((320, 512, tfull[:, h, mf * 64:(mf + 3) * 64], ("tfull", h)))
                                else:
                                    segs.append((320, 512, negt[:, 0:192], "negt"))
                            else:
                                m0 = 17 - E0
                                segs.append((0, 512, tint[:, h, m0 * 64:(m0 + 8) * 64], ("tint", h)))

                            def post():
                                for (c0, c1, tab, tres) in segs:
                                    P.op("pe", lambda e, c0=c0, c1=c1, tab=tab: e.matmul(psum[ps][:, c0:c1], IDB, tab, start=False, stop=True),
                                         reads=["cst_b", tres], writes=[("ps", ps)])
                            return vt[:, s, i, h, :], [("vtC", s)], post
                        cj = i - nj
                        P.op("pe", lambda e: e.matmul(psum[ps][:, 0:N], kc[p0:p0 + 64, ci, cj * 128:(cj + 1) * 128], qt[p0:p0 + 64, s, ci, 0:N],
                                                      start=True, stop=True), reads=["kcC", ("qtC", s)], writes=[("ps", ps)])
                        return vc[:, cj, h, :], ["vcC"], None
                    return qk_ops

                hA, hB = 2 * ci, 2 * ci + 1
                attend2(N, mk(hA), mk(hB), nj + 2, pt, "ptC", rc, "rcC",
                        [(ost[:, s, hA, 0:N], ("ostC", s, hA)), (ost[:, s, hB, 0:N], ("ostC", s, hB))], 1.0)
            ld("sp", nao_v[:, :, t0:t0 + N], ost[:, s, :, 0:N], [("nao", qb)], reads=[("ostC", s, h) for h in range(8)])

    if "C" in dbg and l == 0:
        return
    with ExitStack() as ph:
        P.barrier()
        cs = sbuf(ph, "csD", [128, 2, S])
        gq = sbuf(ph, "gqD", [128, 4, ST], BF16)
        gk = sbuf(ph, "gkD", [128, ST], BF16)
        va = sbuf(ph, "vaD", [128, 34, 2, 128], BF16)
        raw = sbuf(ph, "rawD", [128, 2, 5, 512], BF16)
        sq = sbuf(ph, "sqD", [128, 4, 512], BF16)
        rs = sbuf(ph, "rsD", [128, 4, 512])
        qn = sbuf(ph, "qnD", [128, 4, 512], BF16)
        r1 = sbuf(ph, "r1D", [128, 4, 512])
        r2t = sbuf(ph, "r2D", [128, 4, 512])
        pt = sbuf(ph, "ptD", [128, 4, 2, 512], BF16)
        rc = sbuf(ph, "rcD", [128, 2, 2, 512])
        ost = sbuf(ph, "ostD", [64, 4, 512], BF16)
        ld("sp", cs[:], rope_d.rearrange("a p t -> p a t"), ["csD"])
        for half in range(2):
            ld("sp", va[:, half * 17:(half + 1) * 17, :, :], vaug_v[:, half * 17:(half + 1) * 17, 8:10, :], [("vaD", half)],
               reads=[("vaug", j) for j in range(half * 17, (half + 1) * 17)])

        def loadD(b):
            t0, N = blk_info(b)
            s = b % 2
            if b < nblk_q:
                ld("sp", raw[:, s, 0:4, 0:N], gqr_v[:, :, t0:t0 + N], [("rawD", s, c) for c in range(4)], reads=[("gqr", b)])
            ld("sp", raw[:, s, 4:5, 0:N], gkr_v[:, :, t0:t0 + N], [("rawD", s, 4)], reads=[("gkr", b)])

        loadD(0)
        ci = 0
        for b in range(NBLK):
            t0, N = blk_info(b)
            if b + 1 < NBLK:
                loadD(b + 1)
            s = b % 2
            for c in ([0, 1, 2, 3, 4] if b < nblk_q else [4]):
                s2 = ci % 4
                ci += 1
                P.op("act", lambda e, c=c, s2=s2: e.activation(out=sq[:, s2, 0:N], in_=raw[:, s, c, 0:N], func=AF.Square),
                     reads=[("rawD", s, c)], writes=[("sqD", s2)])
                p1 = pbank()
                P.op("pe", lambda e, p1=p1, s2=s2: e.matmul(psum[p1][:, 0:N], BD64, sq[:, s2, 0:N], start=True, stop=True),
                     reads=["cst_b", ("sqD", s2)], writes=[("ps", p1)])
                P.op("act", lambda e, p1=p1, s2=s2: e.activation(out=rs[:, s2, 0:N], in_=psum[p1][:, 0:N], func=AF.Ln, scale=1.0 / 64, bias=EPS),
                     reads=[("ps", p1)], writes=[("rsD", s2)])
                P.op("act", lambda e, s2=s2: e.activation(out=rs[:, s2, 0:N], in_=rs[:, s2, 0:N], func=AF.Exp, scale=-0.5),
                     reads=[("rsD", s2)], writes=[("rsD", s2)])
                gi = 0 if c < 4 else 1
                dst = gq[:, c, t0:t0 + N] if c < 4 else gk[:, t0:t0 + N]
                dres = ("gqD", c, b) if c < 4 else ("gkD", b)
                if b < 8:
                    P.op("dve", lambda e, c=c, s2=s2, gi=gi: e.scalar_tensor_tensor(out=qn[:, s2, 0:N], in0=raw[:, s, c, 0:N], scalar=qkg[:, l, gi:gi + 1], in1=rs[:, s2, 0:N],
                                                                             op0=ALU.mult, op1=ALU.mult),
                         reads=[("rawD", s, c), "qkg", ("rsD", s2)], writes=[("qnD", s2)])
                    p2 = pbank()
                    P.op("pe", lambda e, p2=p2, s2=s2: e.matmul(psum[p2][:, 0:N], PERM, qn[:, s2, 0:N], start=True, stop=True),
                         reads=["cst_b", ("qnD", s2)], writes=[("ps", p2)])
                    P.op("dve", lambda e, s2=s2: e.tensor_tensor(out=r1[:, s2, 0:N], in0=qn[:, s2, 0:N], in1=cs[:, 0, t0:t0 + N], op=ALU.mult),
                         reads=[("qnD", s2), "csD"], writes=[("r1D", s2)])
                    P.op("dve", lambda e, p2=p2, s2=s2: e.tensor_tensor(out=r2t[:, s2, 0:N], in0=psum[p2][:, 0:N], in1=cs[:, 1, t0:t0 + N], op=ALU.mult),
                         reads=[("ps", p2), "csD"], writes=[("r2D", s2)])
                    P.op("pool", lambda e, s2=s2, dst=dst: e.tensor_tensor(out=dst, in0=r1[:, s2, 0:N], in1=r2t[:, s2, 0:N], op=ALU.add),
                         reads=[("r1D", s2), ("r2D", s2)], writes=[dres])
                else:
                    P.op("dve", lambda e, c=c, s2=s2, gi=gi, dst=dst: e.scalar_tensor_tensor(out=dst, in0=raw[:, s, c, 0:N], scalar=qkg[:, l, gi:gi + 1], in1=rs[:, s2, 0:N],
                                                                                      op0=ALU.mult, op1=ALU.mult),
                         reads=[("rawD", s, c), "qkg", ("rsD", s2)], writes=[dres])
        oi = 0
        for qb in range(nblk_q):
            t0, N = blk_info(qb)
            chunks = list(range(34)) if qb < 8 else [32, 33]
            for jt in range(4):
                def mk(g, jt=jt, chunks=chunks):
                    p0 = g * 64

                    def qk_ops(i, ps):
                        c = chunks[i]
                        kb = c // 4 if c < 32 else 8
                        P.op("pe", lambda e: e.matmul(psum[ps][:, 0:N], gk[p0:p0 + 64, c * 128:(c + 1) * 128], gq[p0:p0 + 64, jt, t0:t0 + N], start=True, stop=True),
                             reads=[("gkD", kb), ("gqD", jt, qb)], writes=[("ps", ps)])
                        return va[:, c, g, :], [("vaD", c // 17)], None
                    return qk_ops

                soA = oi % 4
                soB = (oi + 1) % 4
                oi += 2
                attend2(N, mk(0), mk(1), len(chunks), pt, "ptD", rc, "rcD",
                        [(ost[:, soA, 0:N], ("ostD", soA)), (ost[:, soB, 0:N], ("ostD", soB))], 0.125)
                ld("sp", gqo_v[:, jt, t0:t0 + N], ost[:, soA, 0:N], [("gqo", qb, jt)], reads=[("ostD", soA)])
                ld("sp", gqo_v[:, jt + 4, t0:t0 + N], ost[:, soB, 0:N], [("gqo", qb, jt + 4)], reads=[("ostD", soB)])


def _host_prep(x, c, ctx, c_ctx, w_mod, b_mod, norm1_g, norm2_g, w_in, conv_w, conv_b, conv_ln_g, conv_ln_b, w_conv_out,
               na_rpb, w_na_out, q_norm_g, k_norm_g, w_gqa_out, w_out, w_ffn_in, w_ffn_out, final_g):
    f = np.float32
    x = np.asarray(x, f); ctx = np.asarray(ctx, f); c = np.asarray(c, f); c_ctx = np.asarray(c_ctx, f)
    B = x.shape[0]

    def pk(v):
        v = np.asarray(v, f)
        return np.ascontiguousarray(np.swapaxes(v.reshape(v.shape[:-1] + (-1, 128)), -1, -2))

    perm = np.concatenate([np.arange(0, 16), np.arange(32, 48), np.arange(16, 32), np.arange(48, 64)])
    w_in = np.asarray(w_in, f)
    cols = np.arange(INW)
    for j in range(4):
        for half in range(2):
            hh = j + 4 * half
            cols[2560 + j * 128 + half * 64:2560 + j * 128 + half * 64 + 64] = 2560 + hh * 64 + perm
    for g in range(2):
        cols[3072 + g * 64:3072 + g * 64 + 64] = 3072 + g * 64 + perm
    w_in_p = np.ascontiguousarray(w_in[:, :, cols])
    qg = np.asarray(q_norm_g, f)[:, perm]
    kg = np.asarray(k_norm_g, f)[:, perm]
    qkgT = np.stack([np.tile(qg, (1, 2)), np.tile(kg, (1, 2))], axis=-1)
    gnT = np.stack([pk(norm1_g[0]), pk(norm2_g[0]), pk(norm1_g[1]), pk(norm2_g[1]), pk(final_g)], axis=1)
    bmodT = pk(b_mod)
    convwT = np.ascontiguousarray(np.transpose(np.asarray(conv_w, f).reshape(L, CONV_K, 4, 128), (0, 3, 2, 1)))
    convvT = np.stack([pk(conv_b), pk(conv_ln_g), pk(conv_ln_b)], axis=2)
    rpb = np.asarray(na_rpb, f)
    kcg = np.arange(64)[:, None]
    cg = np.arange(64)[None, :]
    win0 = np.clip(cg - 8, 0, 48)
    colvalid = (kcg >= win0) & (kcg < win0 + 16)
    cidx = np.clip(kcg - cg + 15, 0, 30)

    def table(nm, etop, lo, hi):
        tab = np.full((L, 128, 8, nm, 64), NEG, f)
        for m in range(nm):
            e = etop - m
            for krr in range(2):
                dr = e + krr
                if lo <= dr <= hi:
                    vals = rpb[:, :, dr, :][:, :, cidx]
                    vals = np.where(colvalid[None, None], vals, f(NEG))
                    tab[:, krr * 64:(krr + 1) * 64, :, m, :] = np.transpose(vals, (0, 2, 1, 3))
        return np.ascontiguousarray(tab.reshape(L, 128, 8, nm * 64))

    tint = table(22, 17, 3, 10)
    tfull = table(14, 13, 0, 14)
    t = np.arange(S)
    prow = (t // GW).astype(np.float64)
    pcol = (t % GW).astype(np.float64)
    freqs = np.power(10000.0, -np.arange(0, 32, 2, dtype=np.float64) / 32).astype(np.float32).astype(np.float64)
    rope = np.zeros((2, 128, S), f)
    for p in range(128):
        dd = p % 64
        jf = dd % 16
        pos = prow if (dd // 16) % 2 == 0 else pcol
        ang = (pos.astype(np.float32) * freqs[jf].astype(np.float32)).astype(np.float32)
        rope[0, p] = np.cos(ang)
        rope[1, p] = np.sin(ang) * (-1.0 if dd < 32 else 1.0)
    consts = np.zeros((4, 128, 128), f)
    consts[0] = np.eye(128)
    consts[1] = 1.0
    consts[2, 0:64, 0:64] = 1.0
    consts[2, 64:128, 64:128] = 1.0
    for m in range(128):
        consts[3, m + 32 if (m % 64) < 32 else m - 32, m] = 1.0
    shared = dict(w_mod=np.asarray(w_mod, f), bmodT=bmodT, gnT=gnT, w_in=w_in_p, convwT=convwT, convvT=convvT,
                  w_conv_out=np.asarray(w_conv_out, f), tint=tint, tfull=tfull, w_na_out=np.asarray(w_na_out, f), qkgT=qkgT,
                  w_gqa_out=np.asarray(w_gqa_out, f), w_out=np.asarray(w_out, f), w_ffn_in=np.asarray(w_ffn_in, f),
                  w_ffn_out=np.asarray(w_ffn_out, f), rope=rope, consts=consts)
    cc = pk(c_ctx)
    in_maps = []
    for b in range(B):
        m = dict(shared)
        m["x"] = np.ascontiguousarray(x[b])
        m["ctx"] = np.ascontiguousarray(ctx[b])
        m["cT"] = np.ascontiguousarray(np.stack([pk(c[b]), cc], axis=-1))
        in_maps.append(m)
    return in_maps


_NC_CACHE = {}


def kernel(**inputs):
    in_maps = _host_prep(**inputs)
    if "nc" not in _NC_CACHE:
        _NC_CACHE["nc"] = build()
    res = run_bass_kernel_spmd(_NC_CACHE["nc"], in_maps, core_ids=list(range(8)))
    return np.stack([np.asarray(r["out"], np.float32) for r in res.results], axis=0)
```

```python
from contextlib import ExitStack
import os
import numpy as np
import concourse.bass as bass
import concourse.mybir as mybir
from concourse.bass_utils import run_bass_kernel_spmd

F32 = mybir.dt.float32
BF16 = mybir.dt.bfloat16
AF = mybir.ActivationFunctionType
ALU = mybir.AluOpType

D = 1024
S = 4096
T = 256
ST = S + T
L = 2
GW = 64
CONV_K = 31
HALO = 15
FFN = 2816
INW = 6400
EPS = 1e-6
NEG = -1e30
NBLK = 9
ENGS = ("pe", "act", "dve", "pool", "sp")
NDMA = 8


def blk_info(b):
    if b < 8:
        return b * 512, 512
    return S, T


class _Rec:
    def __init__(self):
        self.call = None

    def __getattr__(self, name):
        def f(*a, **k):
            self.call = (name, a, k)
            return self
        return f


def _capture(fn):
    r = _Rec()
    fn(r)
    assert r.call is not None
    return r.call


class Prog:
    def __init__(self, nc):
        self.nc = nc
        self.streams = {e: [] for e in ENGS}
        self.count = {e: 0 for e in ENGS}
        self.known = {e: {} for e in ENGS}
        self.res_w = {}
        self.res_r = {}
        self.dma_use = {}
        self.dma_rr = {e: 0 for e in ENGS}
        self.sems = {}

    def _deps(self, eng, reads, writes):
        need = {}

        def add(tok):
            if tok is None:
                return
            k, v = tok
            if k == "pe" and eng == "pe":
                return
            if need.get(k, 0) < v:
                need[k] = v

        for r in reads:
            add(self.res_w.get(r))
        for w in writes:
            add(self.res_w.get(w))
            for k, v in self.res_r.get(w, {}).items():
                add((k, v))
        return need

    def _commit(self, tok, reads, writes):
        k, v = tok
        for r in reads:
            d = self.res_r.setdefault(r, {})
            if d.get(k, 0) < v:
                d[k] = v
        for w in writes:
            self.res_w[w] = tok
            self.res_r[w] = {}

    def _emit_waits(self, eng, need):
        kn = self.known[eng]
        waits = []
        for k, v in need.items():
            if kn.get(k, 0) < v:
                kn[k] = v
                waits.append((k, v))
        return waits

    def op(self, eng, fn, reads=(), writes=()):
        need = self._deps(eng, reads, writes)
        waits = self._emit_waits(eng, need)
        self.count[eng] += 1
        tok = (eng, self.count[eng])
        self.streams[eng].append((_capture(fn), waits, (eng, 1)))
        self._commit(tok, reads, writes)
        return tok

    def dma(self, q, fn, reads=(), writes=()):
        need = self._deps(q, reads, writes)
        i = self.dma_rr[q]
        self.dma_rr[q] = (i + 1) % NDMA
        key = ("dma", q, i)
        u = self.dma_use.get(key, 0)
        if u > 0 and need.get(key, 0) < 16 * u:
            need[key] = 16 * u
        waits = self._emit_waits(q, need)
        self.dma_use[key] = u + 1
        tok = (key, 16 * (u + 1))
        self.streams[q].append((_capture(fn), waits, (key, 16)))
        self._commit(tok, reads, writes)
        return tok

    def wait_all(self, eng, toks):
        need = {}
        for k, v in toks:
            if need.get(k, 0) < v:
                need[k] = v
        waits = self._emit_waits(eng, need)
        self.streams[eng].append((None, waits, None))

    def barrier(self):
        toks = self.final_tokens()
        for e in ENGS:
            self.wait_all(e, toks)

    def final_tokens(self):
        toks = [(e, self.count[e]) for e in ENGS if self.count[e]]
        toks += [(key, 16 * u) for key, u in self.dma_use.items()]
        return toks

    def emit(self, stack):
        nc = self.nc
        targets = {e: set() for e in ENGS}
        keys = set()
        for e in ENGS:
            for fn, waits, inc in self.streams[e]:
                for k, v in waits:
                    keys.add(k)
                    if isinstance(k, str):
                        targets[k].add(v)
                if inc is not None and not isinstance(inc[0], str):
                    keys.add(inc[0])
        rank = {}
        for e in ENGS:
            srt = sorted(targets[e])
            rank[e] = {v: i + 1 for i, v in enumerate(srt)}
            if srt:
                keys.add(e)
        for k in sorted(keys, key=str):
            name = k if isinstance(k, str) else "d_%s_%d" % (k[1], k[2])
            self.sems[k] = stack.enter_context(nc.semaphore("s_" + name))
        block = stack.enter_context(nc.Block())

        def runner(ename):
            def run(engine):
                idx = 0
                for fn, waits, inc in self.streams[ename]:
                    for k, v in waits:
                        engine.wait_ge(self.sems[k], rank[k][v] if isinstance(k, str) else v)
                    if fn is not None:
                        name, a, kw = fn
                        ins = getattr(engine, name)(*a, **kw)
                        if isinstance(inc[0], str):
                            idx += 1
                            if idx in rank[ename]:
                                ins.then_inc(self.sems[ename], 1)
                        else:
                            ins.then_inc(self.sems[inc[0]], inc[1])
            return run

        block.tensor(runner("pe"))
        block.scalar(runner("act"))
        block.vector(runner("dve"))
        block.gpsimd(runner("pool"))
        block.sync(runner("sp"))


def build(n_layers=L, dbg=()):
    nc = bass.Bass("TRN2", target_bir_lowering=False)
    P = Prog(nc)

    def din(name, shape, dt=F32):
        return nc.dram_tensor(name, shape, dt, kind="ExternalInput").ap()

    def dscr(name, shape, dt=BF16):
        kind = "ExternalOutput" if name in dbg else "Internal"
        return nc.dram_tensor(name, shape, dt, kind=kind).ap()

    x_d = din("x", [S, D])
    ctx_d = din("ctx", [T, D])
    cT_d = din("cT", [128, 8, 2])
    wmod_d = din("w_mod", [L, D, 6 * D])
    bmod_d = din("bmodT", [L, 128, 48])
    gn_d = din("gnT", [128, 2 * L + 1, 8])
    win_d = din("w_in", [L, D, INW])
    convw_d = din("convwT", [L, 128, 4, CONV_K])
    convv_d = din("convvT", [L, 128, 3, 4])
    wco_d = din("w_conv_out", [L, 512, D])
    tint_d = din("tint", [L, 128, 8, 22 * 64])
    tfull_d = din("tfull", [L, 128, 8, 14 * 64])
    wno_d = din("w_na_out", [L, 512, D])
    qkg_d = din("qkgT", [L, 128, 2])
    wgo_d = din("w_gqa_out", [L, 512, D])
    wout_d = din("w_out", [L, D, D])
    wfi_d = din("w_ffn_in", [L, D, 2 * FFN])
    wfo_d = din("w_ffn_out", [L, FFN, D])
    rope_d = din("rope", [2, 128, S])
    cst_d = din("consts", [4, 128, 128])
    out_d = nc.dram_tensor("out", [S, D], F32, kind="ExternalOutput").ap()

    xT_d = dscr("xT", [D, ST], F32)
    UW = S + 2 * HALO
    UCW = T + 2 * HALO
    uT_d = dscr("uT", [512, UW + UCW])
    naq_d = dscr("naqT", [512, ST])
    nak_d = dscr("nakT", [512, ST])
    gqr_d = dscr("gqrT", [512, ST])
    gkr_d = dscr("gkrT", [128, ST])
    gat_d = dscr("gatesT", [3072, ST])
    vaug_d = dscr("vaug", [ST, 10, 128])
    ca_d = dscr("caT", [512, ST])
    nao_d = dscr("naoT", [512, ST])
    gqo_d = dscr("gqoT", [512, ST])

    with ExitStack() as top:
        uid = [0]

        def sbuf(st, name, shape, dt=F32):
            uid[0] += 1
            return st.enter_context(nc.sbuf_tensor("sb%d_%s" % (uid[0], name), shape, dt))

        psum_all = top.enter_context(nc.psum_tensor("pbanks", [128, 8, 512], F32))
        psum = [psum_all[:, i, :] for i in range(8)]
        prr = [0]

        def pbank():
            i = prr[0]
            prr[0] = (i + 1) % 8
            return i

        ident_f = sbuf(top, "ident_f", [128, 128])
        cst_b = sbuf(top, "cst_b", [128, 4, 128], BF16)
        modv = sbuf(top, "modv", [128, L, 48, 2])
        bmod = sbuf(top, "bmod", [128, L, 48])
        gn = sbuf(top, "gn", [128, 2 * L + 1, 8])
        Amod = sbuf(top, "Amod", [128, L, 2, 8, 2])
        rstd = sbuf(top, "rstd", [128, ST])
        stats = rstd
        sil = sbuf(top, "sil", [128, 8, 2])
        cT = sbuf(top, "cT_s", [128, 8, 2])
        zpad = sbuf(top, "zpad", [128, 4, HALO], BF16)
        convw = sbuf(top, "convw", [128, L, 4, CONV_K])
        convv = sbuf(top, "convv", [128, L, 3, 4])
        qkg = sbuf(top, "qkg", [128, L, 2])

        IDB = cst_b[:, 0, :]
        ONESB = cst_b[:, 1, :]
        BD64 = cst_b[:, 2, :]
        PERM = cst_b[:, 3, :]

        def ld(q, out, in_, writes, reads=()):
            P.dma(q, lambda e, o=out, i=in_: e.dma_start(out=o, in_=i), reads=reads, writes=writes)

        ld("sp", ident_f[:], cst_d[0], ["ident_f"])
        ld("pool", cst_b[:], cst_d.rearrange("c p n -> p c n"), ["cst_b"])
        ld("sp", bmod[:], bmod_d.rearrange("l p c -> p l c"), ["bmod"])
        ld("sp", gn[:], gn_d, ["gn"])
        ld("sp", cT[:], cT_d, ["cT"])
        ld("sp", convw[:], convw_d.rearrange("l p c k -> p l c k"), ["convw"])
        ld("sp", convv[:], convv_d.rearrange("l p a c -> p l a c"), ["convv"])
        ld("sp", qkg[:], qkg_d.rearrange("l p a -> p l a"), ["qkg"])
        P.op("pool", lambda e: e.memset(zpad[:], 0.0), writes=["zpad"])
        uT_v = uT_d.rearrange("(c p) t -> p c t", p=128)
        for off in (0, HALO + S, UW, UW + HALO + T):
            ld("sp", uT_v[:, :, off:off + HALO], zpad[:], [("uTpad", off)], reads=["zpad"])

        P.op("act", lambda e: e.activation(out=sil[:], in_=cT[:], func=AF.Exp, scale=-1.0), reads=["cT"], writes=["sil"])
        P.op("dve", lambda e: e.tensor_scalar_add(out=sil[:], in0=sil[:], scalar1=1.0), reads=["sil"], writes=["sil"])
        P.op("dve", lambda e: e.reciprocal(out=sil[:], in_=sil[:]), reads=["sil"], writes=["sil"])
        P.op("dve", lambda e: e.tensor_mul(out=sil[:], in0=sil[:], in1=cT[:]), reads=["sil", "cT"], writes=["sil"])
        with ExitStack() as ph:
            P.barrier()
            wm = sbuf(ph, "wm", [128, 2, 8, 1024])
            it = 0
            for l in range(n_layers):
                for cg in range(6):
                    sl = (it % 2) if not os.environ.get('WM1') else 0
                    it += 1
                    for k in range(8):
                        ld("sp", wm[:, sl, k, :], wmod_d[l, k * 128:(k + 1) * 128, cg * 1024:(cg + 1) * 1024],
                           [("wm", sl, k)])
                    for jc in range(8):
                        oc = cg * 8 + jc
                        pb = pbank()
                        for k in range(8):
                            P.op("pe", lambda e, pb=pb, sl=sl, k=k, jc=jc: e.matmul(
                                psum[pb][:, 0:2], wm[:, sl, k, jc * 128:(jc + 1) * 128], sil[:, k, :],
                                start=(k == 0), stop=(k == 7)),
                                reads=[("wm", sl, k), "sil"], writes=[("ps", pb)])
                        P.op("dve", lambda e, pb=pb, l=l, oc=oc: e.tensor_scalar(
                            out=modv[:, l, oc, :], in0=psum[pb][:, 0:2], scalar1=bmod[:, l, oc:oc + 1], scalar2=None,
                            op0=ALU.add), reads=[("ps", pb), "bmod"], writes=["modv"])
            for l in range(n_layers):
                for n in range(2):
                    for j in range(2):
                        P.op("dve", lambda e, l=l, n=n, j=j: e.scalar_tensor_tensor(
                            out=Amod[:, l, n, :, j], in0=modv[:, l, (8 + 24 * n):(16 + 24 * n), j], scalar=1.0,
                            in1=gn[:, 2 * l + n, :], op0=ALU.add, op1=ALU.mult),
                            reads=["modv", "gn"], writes=["Amod"])

        def shift_ap(l, n, k, j):
            return modv[:, l, 24 * n + k, j:j + 1]

        def gate_ap(l, n, k, j):
            return modv[:, l, 16 + 24 * n + k, j:j + 1]

        def stats_to_rstd(lo, hi, tag):
            P.op("act", lambda e: e.activation(out=rstd[:, lo:hi], in_=stats[:, lo:hi], func=AF.Ln, scale=1.0 / D, bias=EPS),
                 reads=["rstd"], writes=["rstd"])
            P.op("act", lambda e: e.activation(out=rstd[:, lo:hi], in_=rstd[:, lo:hi], func=AF.Exp, scale=-0.5),
                 reads=["rstd"], writes=["rstd"])

        def sumsq_block(src_tile_fn, src_res, t0, N, sqt, sqres):
            pb = pbank()
            for k in range(8):
                P.op("act", lambda e, k=k: e.activation(out=sqt[:, k, 0:N], in_=src_tile_fn(k), func=AF.Square),
                     reads=[src_res(k)], writes=[(sqres, k)])
                P.op("pe", lambda e, k=k, pb=pb: e.matmul(psum[pb][:, 0:N], ONESB, sqt[:, k, 0:N], start=(k == 0), stop=(k == 7)),
                     reads=[(sqres, k), "cst_b"], writes=[("ps", pb)])
            P.op("dve", lambda e, pb=pb: e.tensor_copy(out=stats[:, t0:t0 + N], in_=psum[pb][:, 0:N]),
                 reads=[("ps", pb)], writes=["rstd"])

        xT_v = xT_d.rearrange("(k p) t -> p k t", p=128)
        with ExitStack() as ph:
            P.barrier()
            xin = sbuf(ph, "xin", [128, 2, D])
            xst = sbuf(ph, "xst", [128, 2, 8, 512])
            sqt = sbuf(ph, "sqt0", [128, 8, 512], BF16)
            ti = 0
            for b in range(NBLK):
                t0, N = blk_info(b)
                sl = b % 2
                for tt in range(N // 128):
                    s2 = ti % 2
                    ti += 1
                    src = x_d[t0 + tt * 128:t0 + (tt + 1) * 128, :] if b < 8 else ctx_d[tt * 128:(tt + 1) * 128, :]
                    ld("sp", xin[:, s2, :], src, [("xin", s2)])
                    for kk in range(2):
                        pb = pbank()
                        for k4 in range(4):
                            k = kk * 4 + k4
                            P.op("pe", lambda e, pb=pb, k4=k4, k=k, s2=s2: e.transpose(
                                psum[pb][:, k4 * 128:(k4 + 1) * 128], xin[:, s2, k * 128:(k + 1) * 128], ident_f[:]),
                                reads=[("xin", s2), "ident_f"], writes=[("ps", pb)])
                        P.op("dve", lambda e, pb=pb, kk=kk, sl=sl, tt=tt: e.tensor_copy(
                            out=xst[:, sl, kk * 4:(kk + 1) * 4, tt * 128:(tt + 1) * 128],
                            in_=psum[pb][:, :].rearrange("p (k t) -> p k t", k=4)),
                            reads=[("ps", pb)], writes=[("xst", sl, kk)])
                ld("sp", xT_v[:, :, t0:t0 + N], xst[:, sl, :, 0:N], [("xT", b)], reads=[("xst", sl, 0), ("xst", sl, 1)])
                sumsq_block(lambda k, sl=sl, N=N: xst[:, sl, k, 0:N], lambda k, sl=sl: ("xst", sl, k // 4), t0, N, sqt, "sqt0")
        stats_to_rstd(0, ST, "n1l0")

        if "dump" in dbg:
            d1 = nc.dram_tensor("d_rstd", [128, ST], F32, kind="ExternalOutput").ap()
            d2 = nc.dram_tensor("d_modv", [128, L, 48, 2], F32, kind="ExternalOutput").ap()
            d3 = nc.dram_tensor("d_amod", [128, L, 2, 8, 2], F32, kind="ExternalOutput").ap()
            d4 = nc.dram_tensor("d_sil", [128, 8, 2], F32, kind="ExternalOutput").ap()
            ld("sp", d1, rstd[:], ["d1"], reads=["rstd"])
            ld("sp", d2, modv[:], ["d2"], reads=["modv"])
            ld("sp", d3, Amod[:], ["d3"], reads=["Amod"])
            ld("sp", d4, sil[:], ["d4"], reads=["sil"])
            P.wait_all("sp", P.final_tokens())
            P.emit(top)
            return nc
        def load_w_bf(dst_fn, src_fn, nk, res):
            for k in range(nk):
                ld("pool", dst_fn(k), src_fn(k), [(res, k)])

        def modulate_block(l, n, b, xt, xres, ht, hres, tmp, tmpres):
            t0, N = blk_info(b)
            j = 0 if b < 8 else 1
            for k in range(8):
                ts_ = k % 2
                P.op("dve", lambda e, k=k, ts_=ts_: e.tensor_tensor(out=tmp[:, ts_, 0:N], in0=xt(k), in1=rstd[:, t0:t0 + N], op=ALU.mult),
                     reads=[xres(k), "rstd"], writes=[(tmpres, ts_)])
                P.op("dve", lambda e, k=k, ts_=ts_: e.tensor_scalar(
                    out=ht[:, k, 0:N], in0=tmp[:, ts_, 0:N], scalar1=Amod[:, l, n, k, j:j + 1], scalar2=shift_ap(l, n, k, j),
                    op0=ALU.mult, op1=ALU.add), reads=[(tmpres, ts_), "Amod", "modv"], writes=[(hres, k)])

        for l in range(n_layers):
            last = (l == n_layers - 1) and (n_layers == L)
            nblk_q = 8 if last else NBLK
            with ExitStack() as ph:
                P.barrier()
                win = sbuf(ph, "win", [128, 8, INW], BF16)
                xt = sbuf(ph, "xtA", [128, 2, 8, 512])
                ht2 = sbuf(ph, "htA", [128, 2, 8, 512], BF16)
                tmp = sbuf(ph, "tmpA", [128, 2, 512])
                sg = sbuf(ph, "sgA", [128, 2, 512])
                stg = sbuf(ph, "stgA", [128, 2, 8, 512], BF16)
                vst = sbuf(ph, "vstA", [128, 2, 10, 128], BF16)
                WIN_GROUPS = [(512, 1024), (0, 512), (1024, 1536), (2560, 3072), (3328, 4352), (4352, 5376), (5376, 6400),
                              (1536, 2048), (3072, 3328), (2048, 2560)]

                def wgrp(col):
                    for gi, (c0, c1) in enumerate(WIN_GROUPS):
                        if c0 <= col < c1:
                            return gi
                    raise AssertionError(col)

                for gi, (c0, c1) in enumerate(WIN_GROUPS):
                    for k in range(8):
                        ld("pool", win[:, k, c0:c1], win_d[l, k * 128:(k + 1) * 128, c0:c1], [("win", k, gi)])
                for s2 in range(2):
                    P.op("dve", lambda e, s2=s2: e.memset(vst[:, s2, :, 64:128], 1.0), writes=[("vst", s2)])

                def loadA(b):
                    t0, N = blk_info(b)
                    ld("sp", xt[:, b % 2, :, 0:N], xT_v[:, :, t0:t0 + N], [("xtA", b % 2, k) for k in range(8)], reads=[("xT", b)])

                uT_lat = uT_v[:, :, HALO:HALO + S]
                uT_ctx = uT_v[:, :, UW + HALO:UW + HALO + T]
                naq_v = naq_d.rearrange("(c p) t -> p c t", p=128)
                nak_v = nak_d.rearrange("(c p) t -> p c t", p=128)
                gqr_v = gqr_d.rearrange("(c p) t -> p c t", p=128)
                gkr_v = gkr_d.rearrange("(c p) t -> p c t", p=128)
                gat_v = gat_d.rearrange("(c p) t -> p c t", p=128)
                vaug_v = vaug_d.rearrange("(n p) h f -> p n h f", p=128)
                stgi = [0]
                vsi = [0]
                loadA(0)
                for b in range(NBLK):
                    t0, N = blk_info(b)
                    if b + 1 < NBLK:
                        loadA(b + 1)
                    sl = b % 2
                    ht = ht2[:, sl]
                    hres = "htA%d" % sl
                    modulate_block(l, 0, b, lambda k, sl=sl, N=N: xt[:, sl, k, 0:N], lambda k, sl=sl: ("xtA", sl, k),
                                   ht, hres, tmp, "tmpA")

                    def proj(col, pb, N=N):
                        for k in range(8):
                            P.op("pe", lambda e, k=k: e.matmul(psum[pb][:, 0:N], win[:, k, col:col + 128], ht[:, k, 0:N],
                                                               start=(k == 0), stop=(k == 7)),
                                 reads=[("win", k, wgrp(col)), (hres, k)], writes=[("ps", pb)])

                    def group(col0, nch, dst, kind, dres):
                        ss = stgi[0] % 2
                        stgi[0] += 1
                        for c in range(nch):
                            pb = pbank()
                            proj(col0 + c * 128, pb)
                            if kind == "sig":
                                P.op("act", lambda e, pb=pb, c=c: e.activation(out=stg[:, ss, c, 0:N], in_=psum[pb][:, 0:N], func=AF.Sigmoid),
                                     reads=[("ps", pb)], writes=[("stgA", ss, c)])
                            elif kind == "copy":
                                P.op("act", lambda e, pb=pb, c=c: e.activation(out=stg[:, ss, c, 0:N], in_=psum[pb][:, 0:N], func=AF.Identity),
                                     reads=[("ps", pb)], writes=[("stgA", ss, c)])
                            elif kind == "scale":
                                P.op("act", lambda e, pb=pb, c=c: e.activation(out=stg[:, ss, c, 0:N], in_=psum[pb][:, 0:N], func=AF.Identity, scale=0.125),
                                     reads=[("ps", pb)], writes=[("stgA", ss, c)])
                        ld("sp", dst, stg[:, ss, 0:nch, 0:N], [dres], reads=[("stgA", ss, c) for c in range(nch)])

                    ss = stgi[0] % 2
                    stgi[0] += 1
                    for c in range(4):
                        pg = pbank()
                        proj(512 + c * 128, pg)
                        s2 = c % 2
                        P.op("act", lambda e, pg=pg, s2=s2: e.activation(out=sg[:, s2, 0:N], in_=psum[pg][:, 0:N], func=AF.Sigmoid),
                             reads=[("ps", pg)], writes=[("sgA", s2)])
                        pv = pbank()
                        proj(c * 128, pv)
                        P.op("dve", lambda e, pv=pv, s2=s2, c=c, ss=ss: e.tensor_tensor(out=stg[:, ss, c, 0:N], in0=psum[pv][:, 0:N], in1=sg[:, s2, 0:N], op=ALU.mult),
                             reads=[("ps", pv), ("sgA", s2)], writes=[("stgA", ss, c)])
                    udst = uT_lat[:, :, t0:t0 + N] if b < 8 else uT_ctx
                    ld("sp", udst, stg[:, ss, 0:4, 0:N], [("uT", b)], reads=[("stgA", ss, c) for c in range(4)])
                    if b < nblk_q:
                        group(1024, 4, naq_v[:, :, t0:t0 + N], "scale", ("naq", b))
                        group(2560, 4, gqr_v[:, :, t0:t0 + N], "copy", ("gqr", b))
                        for g3 in range(3):
                            group(3328 + g3 * 1024, 8, gat_v[:, g3 * 8:(g3 + 1) * 8, t0:t0 + N], "sig", ("gat", b, g3))
                    group(1536, 4, nak_v[:, :, t0:t0 + N], "copy", ("nak", b))
                    group(3072, 1, gkr_v[:, :, t0:t0 + N], "copy", ("gkr", b))
                    for tt in range(N // 128):
                        s2 = vsi[0] % 2
                        vsi[0] += 1
                        p1 = pbank()
                        p2 = pbank()
                        for k in range(8):
                            P.op("pe", lambda e, k=k, tt=tt, p1=p1: e.matmul(psum[p1][:, 0:512], ht[:, k, tt * 128:(tt + 1) * 128], win[:, k, 2048:2560],
                                                                        start=(k == 0), stop=(k == 7)),
                                 reads=[("win", k, wgrp(2048)), (hres, k)], writes=[("ps", p1)])
                        for k in range(8):
                            P.op("pe", lambda e, k=k, tt=tt, p2=p2: e.matmul(psum[p2][:, 0:128], ht[:, k, tt * 128:(tt + 1) * 128], win[:, k, 3200:3328],
                                                                        start=(k == 0), stop=(k == 7)),
                                 reads=[("win", k, wgrp(3200)), (hres, k)], writes=[("ps", p2)])
                        P.op("dve", lambda e, p1=p1, s2=s2: e.tensor_copy(out=vst[:, s2, 0:8, 0:64], in_=psum[p1][:, 0:512].rearrange("p (h d) -> p h d", h=8)),
                             reads=[("ps", p1)], writes=[("vst", s2)])
                        P.op("dve", lambda e, p2=p2, s2=s2: e.tensor_copy(out=vst[:, s2, 8:10, 0:64], in_=psum[p2][:, 0:128].rearrange("p (h d) -> p h d", h=2)),
                             reads=[("ps", p2)], writes=[("vst", s2)])
                        ti = t0 // 128 + tt
                        ld("sp", vaug_v[:, ti, :, :], vst[:, s2, :, :], [("vaug", ti)], reads=[("vst", s2)])

            if "A" in dbg and l == 0:
                break
            ca_v = ca_d.rearrange("(c p) t -> p c t", p=128)
            with ExitStack() as ph:
                P.barrier()
                dg = sbuf(ph, "dg", [128, 4, CONV_K, 128], BF16)
                ut = sbuf(ph, "utB", [128, 2, 4, 512 + 2 * HALO], BF16)
                yb = sbuf(ph, "ybB", [128, 4, 512])
                ybf = sbuf(ph, "ybfB", [128, 4, 512], BF16)
                ysq = sbuf(ph, "ysqB", [128, 4, 512], BF16)
                mu = sbuf(ph, "muB", [128, 512])
                var = sbuf(ph, "varB", [128, 512])
                t1 = sbuf(ph, "t1B", [128, 2, 512])
                t2 = sbuf(ph, "t2B", [128, 2, 512])
                cast = sbuf(ph, "castB", [128, 2, 4, 512], BF16)
                for c in range(4):
                    for kk in range(CONV_K):
                        P.op("pool" if kk % 2 else "dve", lambda e, c=c, kk=kk: e.tensor_scalar(out=dg[:, c, kk, :], in0=IDB, scalar1=convw[:, l, c, kk:kk + 1], scalar2=None, op0=ALU.mult),
                             reads=["cst_b", "convw"], writes=[("dg", c, kk)])

                def loadB(b):
                    t0, N = blk_info(b)
                    if b < 8:
                        src = uT_v[:, :, t0:t0 + N + 2 * HALO]
                        rd = [("uT", bb) for bb in (b - 1, b, b + 1) if 0 <= bb < 8] + [("uTpad", 0), ("uTpad", HALO + S)]
                    else:
                        src = uT_v[:, :, UW:UW + UCW]
                        rd = [("uT", 8), ("uTpad", UW), ("uTpad", UW + HALO + T)]
                    ld("sp", ut[:, b % 2, :, 0:N + 2 * HALO], src, [("utB", b % 2)], reads=rd)

                loadB(0)
                for b in range(nblk_q):
                    t0, N = blk_info(b)
                    if b + 1 < nblk_q:
                        loadB(b + 1)
                    sl = b % 2
                    ps1 = pbank()
                    ps2 = pbank()
                    for c in range(4):
                        pb = pbank()
                        for kk in range(CONV_K):
                            P.op("pe", lambda e, c=c, kk=kk, pb=pb: e.matmul(psum[pb][:, 0:N], dg[:, c, kk, :], ut[:, sl, c, kk:kk + N],
                                                                        start=(kk == 0), stop=(kk == CONV_K - 1)),
                                 reads=[("dg", c, kk), ("utB", sl)], writes=[("ps", pb)])
                        P.op("dve", lambda e, c=c, pb=pb: e.tensor_scalar(out=yb[:, c, 0:N], in0=psum[pb][:, 0:N], scalar1=convv[:, l, 0, c:c + 1], scalar2=None, op0=ALU.add),
                             reads=[("ps", pb), "convv"], writes=[("ybB", c)])
                        P.op("pool", lambda e, c=c: e.tensor_copy(out=ybf[:, c, 0:N], in_=yb[:, c, 0:N]), reads=[("ybB", c)], writes=[("ybfB", c)])
                        P.op("act", lambda e, c=c: e.activation(out=ysq[:, c, 0:N], in_=yb[:, c, 0:N], func=AF.Square), reads=[("ybB", c)], writes=[("ysqB", c)])
                    for c in range(4):
                        P.op("pe", lambda e, c=c: e.matmul(psum[ps1][:, 0:N], ONESB, ybf[:, c, 0:N], start=(c == 0), stop=(c == 3)),
                             reads=[("ybfB", c), "cst_b"], writes=[("ps", ps1)])
                    for c in range(4):
                        P.op("pe", lambda e, c=c: e.matmul(psum[ps2][:, 0:N], ONESB, ysq[:, c, 0:N], start=(c == 0), stop=(c == 3)),
                             reads=[("ysqB", c), "cst_b"], writes=[("ps", ps2)])
                    P.op("dve", lambda e: e.tensor_scalar(out=mu[:, 0:N], in0=psum[ps1][:, 0:N], scalar1=1.0 / 512, scalar2=None, op0=ALU.mult),
                         reads=[("ps", ps1)], writes=["muB"])
                    P.op("dve", lambda e: e.tensor_tensor(out=var[:, 0:N], in0=mu[:, 0:N], in1=mu[:, 0:N], op=ALU.mult), reads=["muB"], writes=["varB"])
                    P.op("dve", lambda e: e.scalar_tensor_tensor(out=var[:, 0:N], in0=psum[ps2][:, 0:N], scalar=1.0 / 512, in1=var[:, 0:N], op0=ALU.mult, op1=ALU.subtract),
                         reads=[("ps", ps2), "varB"], writes=["varB"])
                    P.op("act", lambda e: e.activation(out=var[:, 0:N], in_=var[:, 0:N], func=AF.Ln, bias=EPS), reads=["varB"], writes=["varB"])
                    P.op("act", lambda e: e.activation(out=var[:, 0:N], in_=var[:, 0:N], func=AF.Exp, scale=-0.5), reads=["varB"], writes=["varB"])
                    for c in range(4):
                        s2 = c % 2
                        P.op("pool", lambda e, c=c, s2=s2: e.tensor_tensor(out=t1[:, s2, 0:N], in0=yb[:, c, 0:N], in1=mu[:, 0:N], op=ALU.subtract),
                             reads=[("ybB", c), "muB"], writes=[("t1B", s2)])
                        P.op("pool", lambda e, s2=s2: e.tensor_tensor(out=t1[:, s2, 0:N], in0=t1[:, s2, 0:N], in1=var[:, 0:N], op=ALU.mult),
                             reads=[("t1B", s2), "varB"], writes=[("t1B", s2)])
                        P.op("dve", lambda e, c=c, s2=s2: e.tensor_scalar(out=t1[:, s2, 0:N], in0=t1[:, s2, 0:N], scalar1=convv[:, l, 1, c:c + 1], scalar2=convv[:, l, 2, c:c + 1],
                                                                      op0=ALU.mult, op1=ALU.add), reads=[("t1B", s2), "convv"], writes=[("t1B", s2)])
                        P.op("act", lambda e, s2=s2, c=c: e.activation(out=cast[:, sl, c, 0:N], in_=t1[:, s2, 0:N], func=AF.Silu), reads=[("t1B", s2)], writes=[("castB", sl, c)])
                    ld("sp", ca_v[:, :, t0:t0 + N], cast[:, sl, :, 0:N], [("ca", b)], reads=[("castB", sl, c) for c in range(4)])

            if "B" in dbg and l == 0:
                break
            ATTN(nc, P, l, nblk_q, psum_all, psum, pbank, sbuf, ld, cst_b, qkg, naq_d, nak_d, vaug_d, gqr_d, gkr_d, nao_d, gqo_d,
                 tint_d, tfull_d, rope_d, dbg)
            if "D" in dbg and l == 0:
                break
            nao_v = nao_d.rearrange("(c p) t -> p c t", p=128)
            gqo_v = gqo_d.rearrange("(c p) t -> p c t", p=128)
            with ExitStack() as ph:
                P.barrier()
                wco = sbuf(ph, "wco", [128, 4, D], BF16)
                wno = sbuf(ph, "wno", [128, 4, D], BF16)
                wgo = sbuf(ph, "wgo", [128, 4, D], BF16)
                wo = sbuf(ph, "wo", [128, 8, D], BF16)
                xt = sbuf(ph, "xtE", [128, 2, 8, 512])
                cat = sbuf(ph, "catE", [128, 2, 4, 512], BF16)
                nat = sbuf(ph, "natE", [128, 2, 4, 512], BF16)
                gqt = sbuf(ph, "gqtE", [128, 2, 4, 512], BF16)
                gt = sbuf(ph, "gtE", [128, 2, 24, 512], BF16)
                yt = sbuf(ph, "ytE", [128, 8, 512], BF16)
                m1 = sbuf(ph, "m1E", [128, 2, 512])
                m2 = sbuf(ph, "m2E", [128, 2, 512])
                sqt = sbuf(ph, "sqtE", [128, 8, 512], BF16)
                load_w_bf(lambda k: wco[:, k, :], lambda k: wco_d[l, k * 128:(k + 1) * 128, :], 4, "wco")
                load_w_bf(lambda k: wno[:, k, :], lambda k: wno_d[l, k * 128:(k + 1) * 128, :], 4, "wno")
                load_w_bf(lambda k: wgo[:, k, :], lambda k: wgo_d[l, k * 128:(k + 1) * 128, :], 4, "wgo")
                load_w_bf(lambda k: wo[:, k, :], lambda k: wout_d[l, k * 128:(k + 1) * 128, :], 8, "wo")

                def loadE(b):
                    t0, N = blk_info(b)
                    s = b % 2
                    ld("sp", xt[:, s, :, 0:N], xT_v[:, :, t0:t0 + N], [("xtE", s, k) for k in range(8)], reads=[("xT", b)])
                    ld("sp", cat[:, s, :, 0:N], ca_v[:, :, t0:t0 + N], [("catE", s)], reads=[("ca", b)])
                    ld("sp", nat[:, s, :, 0:N], nao_v[:, :, t0:t0 + N], [("natE", s)], reads=[("nao", b)])
                    ld("sp", gqt[:, s, :, 0:N], gqo_v[:, :, t0:t0 + N], [("gqtE", s)], reads=[("gqo", b, h) for h in range(8)])
                    for g3 in range(3):
                        ld("sp", gt[:, s, g3 * 8:(g3 + 1) * 8, 0:N], gat_v[:, g3 * 8:(g3 + 1) * 8, t0:t0 + N], [("gtE", s, g3)], reads=[("gat", b, g3)])

                loadE(0)
                for b in range(nblk_q):
                    t0, N = blk_info(b)
                    j = 0 if b < 8 else 1
                    if b + 1 < nblk_q:
                        loadE(b + 1)
                    s = b % 2
                    for fc in range(8):
                        fs = slice(fc * 128, (fc + 1) * 128)
                        s2 = fc % 2
                        pa = pbank()
                        for k in range(4):
                            P.op("pe", lambda e, k=k, pa=pa, fs=fs: e.matmul(psum[pa][:, 0:N], wco[:, k, fs], cat[:, s, k, 0:N], start=(k == 0), stop=(k == 3)),
                                 reads=[("wco", k), ("catE", s)], writes=[("ps", pa)])
                        P.op("dve", lambda e, pa=pa, fc=fc, s2=s2: e.tensor_tensor(out=m1[:, s2, 0:N], in0=psum[pa][:, 0:N], in1=gt[:, s, fc, 0:N], op=ALU.mult),
                             reads=[("ps", pa), ("gtE", s, 0)], writes=[("m1E", s2)])
                        pn = pbank()
                        for h in range(4):
                            P.op("pe", lambda e, h=h, pn=pn, fs=fs: e.matmul(psum[pn][:, 0:N], wno[:, h, fs], nat[:, s, h, 0:N], start=(h == 0), stop=(h == 3)),
                                 reads=[("wno", h), ("natE", s)], writes=[("ps", pn)])
                        P.op("dve", lambda e, pn=pn, fc=fc, s2=s2: e.tensor_tensor(out=m2[:, s2, 0:N], in0=psum[pn][:, 0:N], in1=gt[:, s, 8 + fc, 0:N], op=ALU.mult),
                             reads=[("ps", pn), ("gtE", s, 1)], writes=[("m2E", s2)])
                        P.op("pool", lambda e, s2=s2: e.tensor_tensor(out=m1[:, s2, 0:N], in0=m1[:, s2, 0:N], in1=m2[:, s2, 0:N], op=ALU.add),
                             reads=[("m1E", s2), ("m2E", s2)], writes=[("m1E", s2)])
                        pg = pbank()
                        for h in range(4):
                            P.op("pe", lambda e, h=h, pg=pg, fs=fs: e.matmul(psum[pg][:, 0:N], wgo[:, h, fs], gqt[:, s, h, 0:N], start=(h == 0), stop=(h == 3)),
                                 reads=[("wgo", h), ("gqtE", s)], writes=[("ps", pg)])
                        P.op("dve", lambda e, pg=pg, fc=fc, s2=s2: e.tensor_tensor(out=m2[:, s2, 0:N], in0=psum[pg][:, 0:N], in1=gt[:, s, 16 + fc, 0:N], op=ALU.mult),
                             reads=[("ps", pg), ("gtE", s, 2)], writes=[("m2E", s2)])
                        P.op("pool", lambda e, s2=s2, fc=fc: e.tensor_tensor(out=yt[:, fc, 0:N], in0=m1[:, s2, 0:N], in1=m2[:, s2, 0:N], op=ALU.add),
                             reads=[("m1E", s2), ("m2E", s2)], writes=[("ytE", fc)])
                    for fc in range(8):
                        fs = slice(fc * 128, (fc + 1) * 128)
                        po = pbank()
                        for k in range(8):
                            P.op("pe", lambda e, k=k, po=po, fs=fs: e.matmul(psum[po][:, 0:N], wo[:, k, fs], yt[:, k, 0:N], start=(k == 0), stop=(k == 7)),
                                 reads=[("wo", k), ("ytE", k)], writes=[("ps", po)])
                        P.op("dve", lambda e, po=po, fc=fc: e.scalar_tensor_tensor(out=xt[:, s, fc, 0:N], in0=psum[po][:, 0:N], scalar=gate_ap(l, 0, fc, j), in1=xt[:, s, fc, 0:N],
                                                                              op0=ALU.mult, op1=ALU.add),
                             reads=[("ps", po), "modv", ("xtE", s, fc)], writes=[("xtE", s, fc)])
                    ld("sp", xT_v[:, :, t0:t0 + N], xt[:, s, :, 0:N], [("xT", b)], reads=[("xtE", s, k) for k in range(8)])
                    sumsq_block(lambda k, s=s, N=N: xt[:, s, k, 0:N], lambda k, s=s: ("xtE", s, k), t0, N, sqt, "sqtE")
            stats_to_rstd(0, ST if nblk_q == NBLK else S, "n2")
            if "E" in dbg and l == 0:
                break
            with ExitStack() as ph:
                P.barrier()
                wfi = sbuf(ph, "wfi", [128, 8, 2 * FFN], BF16)
                wfo = sbuf(ph, "wfo", [128, 22, D], BF16)
                xt = sbuf(ph, "xtF", [128, 1, 8, 512])
                ht = sbuf(ph, "htF", [128, 8, 512], BF16)
                tmp = sbuf(ph, "tmpF", [128, 2, 512])
                hid = sbuf(ph, "hidF", [128, 22, 512], BF16)
                sgt = sbuf(ph, "sgF", [128, 2, 512], BF16)
                sqt = ht
                HG = [(0, 4), (4, 8), (8, 12), (12, 16), (16, 20), (20, 22)]

                def hgrp(hc):
                    return hc // 4

                for (h0, h1) in HG:
                    for base in (0, FFN):
                        for k in range(8):
                            ld("pool", wfi[:, k, base + h0 * 128:base + h1 * 128], wfi_d[l, k * 128:(k + 1) * 128, base + h0 * 128:base + h1 * 128],
                               [("wfi", k, base, h0 // 4)])
                load_w_bf(lambda k: wfo[:, k, :], lambda k: wfo_d[l, k * 128:(k + 1) * 128, :], 22, "wfo")

                def loadF(b):
                    t0, N = blk_info(b)
                    for k in range(8):
                        ld("sp", xt[:, 0, k, 0:N], xT_v[:, k, t0:t0 + N], [("xtF", 0, k)], reads=[("xT", b)])

                loadF(0)
                for b in range(nblk_q):
                    t0, N = blk_info(b)
                    j = 0 if b < 8 else 1
                    s = 0
                    modulate_block(l, 1, b, lambda k, s=s, N=N: xt[:, s, k, 0:N], lambda k, s=s: ("xtF", s, k), ht, "htF", tmp, "tmpF")
                    for hc in range(22):
                        s2 = hc % 2
                        pg = pbank()
                        for k in range(8):
                            P.op("pe", lambda e, k=k, pg=pg, hc=hc: e.matmul(psum[pg][:, 0:N], wfi[:, k, hc * 128:(hc + 1) * 128], ht[:, k, 0:N], start=(k == 0), stop=(k == 7)),
                                 reads=[("wfi", k, 0, hgrp(hc)), ("htF", k)], writes=[("ps", pg)])
                        P.op("act", lambda e, pg=pg, s2=s2: e.activation(out=sgt[:, s2, 0:N], in_=psum[pg][:, 0:N], func=AF.Silu),
                             reads=[("ps", pg)], writes=[("sgF", s2)])
                        pu = pbank()
                        for k in range(8):
                            P.op("pe", lambda e, k=k, pu=pu, hc=hc: e.matmul(psum[pu][:, 0:N], wfi[:, k, FFN + hc * 128:FFN + (hc + 1) * 128], ht[:, k, 0:N], start=(k == 0), stop=(k == 7)),
                                 reads=[("wfi", k, FFN, hgrp(hc)), ("htF", k)], writes=[("ps", pu)])
                        P.op("dve", lambda e, pu=pu, s2=s2, hc=hc: e.tensor_tensor(out=hid[:, hc, 0:N], in0=psum[pu][:, 0:N], in1=sgt[:, s2, 0:N], op=ALU.mult),
                             reads=[("ps", pu), ("sgF", s2)], writes=[("hidF", hc)])
                    pst = pbank()
                    for fc in range(8):
                        po = pbank()
                        if po == pst:
                            po = pbank()
                        for hc in range(22):
                            P.op("pe", lambda e, hc=hc, po=po, fc=fc: e.matmul(psum[po][:, 0:N], wfo[:, hc, fc * 128:(fc + 1) * 128], hid[:, hc, 0:N], start=(hc == 0), stop=(hc == 21)),
                                 reads=[("wfo", hc), ("hidF", hc)], writes=[("ps", po)])
                        P.op("dve", lambda e, po=po, fc=fc: e.scalar_tensor_tensor(out=xt[:, s, fc, 0:N], in0=psum[po][:, 0:N], scalar=gate_ap(l, 1, fc, j), in1=xt[:, s, fc, 0:N],
                                                                              op0=ALU.mult, op1=ALU.add),
                             reads=[("ps", po), "modv", ("xtF", s, fc)], writes=[("xtF", s, fc)])
                        ld("sp", xT_v[:, fc, t0:t0 + N], xt[:, s, fc, 0:N], [("xT", b)], reads=[("xtF", s, fc)])
                        P.op("act", lambda e, fc=fc: e.activation(out=sqt[:, fc, 0:N], in_=xt[:, s, fc, 0:N], func=AF.Square),
                             reads=[("xtF", s, fc)], writes=[("htF", fc)])
                        if fc > 0:
                            P.op("pe", lambda e, fc=fc, pst=pst: e.matmul(psum[pst][:, 0:N], ONESB, sqt[:, fc - 1, 0:N], start=(fc == 1), stop=False),
                                 reads=[("htF", fc - 1), "cst_b"], writes=[("ps", pst)])
                    P.op("pe", lambda e, pst=pst: e.matmul(psum[pst][:, 0:N], ONESB, sqt[:, 7, 0:N], start=False, stop=True),
                         reads=[("htF", 7), "cst_b"], writes=[("ps", pst)])
                    P.op("dve", lambda e, pst=pst: e.tensor_copy(out=stats[:, t0:t0 + N], in_=psum[pst][:, 0:N]),
                         reads=[("ps", pst)], writes=["rstd"])
                    if b + 1 < nblk_q:
                        loadF(b + 1)
            stats_to_rstd(0, ST if nblk_q == NBLK else S, "n1next")

        if not dbg:
            with ExitStack() as ph:
                P.barrier()
                xt = sbuf(ph, "xtZ", [128, 2, 8, 512])
                tmp = sbuf(ph, "tmpZ", [128, 2, 512])
                ot = sbuf(ph, "otZ", [128, 4, D])
                oi = 0
                ld("sp", xt[:, 0, :, :], xT_v[:, :, 0:512], [("xtZ", 0, k) for k in range(8)], reads=[("xT", 0)])
                for b in range(8):
                    t0 = b * 512
                    if b + 1 < 8:
                        ld("sp", xt[:, (b + 1) % 2, :, :], xT_v[:, :, t0 + 512:t0 + 1024], [("xtZ", (b + 1) % 2, k) for k in range(8)], reads=[("xT", b + 1)])
                    s = b % 2
                    for k in range(8):
                        s2 = k % 2
                        P.op("pool", lambda e, k=k, s2=s2: e.tensor_tensor(out=tmp[:, s2, :], in0=xt[:, s, k, :], in1=rstd[:, t0:t0 + 512], op=ALU.mult),
                             reads=[("xtZ", s, k), "rstd"], writes=[("tmpZ", s2)])
                        P.op("dve", lambda e, k=k, s2=s2: e.tensor_scalar(out=xt[:, s, k, :], in0=tmp[:, s2, :], scalar1=gn[:, 2 * L, k:k + 1], scalar2=None, op0=ALU.mult),
                             reads=[("tmpZ", s2), "gn"], writes=[("xtZ", s, k)])
                    for tt in range(4):
                        so = oi % 4
                        oi += 1
                        for kk in range(2):
                            pb = pbank()
                            for k4 in range(4):
                                k = kk * 4 + k4
                                P.op("pe", lambda e, pb=pb, k4=k4, k=k, tt=tt: e.transpose(psum[pb][:, k4 * 128:(k4 + 1) * 128], xt[:, s, k, tt * 128:(tt + 1) * 128], ident_f[:]),
                                     reads=[("xtZ", s, k), "ident_f"], writes=[("ps", pb)])
                            P.op("dve", lambda e, pb=pb, kk=kk, so=so: e.tensor_copy(out=ot[:, so, kk * 512:(kk + 1) * 512], in_=psum[pb][:, :]),
                                 reads=[("ps", pb)], writes=[("otZ", so)])
                        ld("sp", out_d[t0 + tt * 128:t0 + (tt + 1) * 128, :], ot[:, so, :], [("out", b, tt)], reads=[("otZ", so)])
        P.wait_all("sp", P.final_tokens())
        P.emit(top)
    return nc


def ATTN(nc, P, l, nblk_q, psum_all, psum, pbank, sbuf, ld, cst_b, qkg, naq_d, nak_d, vaug_d, gqr_d, gkr_d, nao_d, gqo_d,
         tint_d, tfull_d, rope_d, dbg):
    IDB = cst_b[:, 0, :]
    BD64 = cst_b[:, 2, :]
    PERM = cst_b[:, 3, :]
    naq_v = naq_d.rearrange("(c p) t -> p c t", p=128)
    nak_v = nak_d.rearrange("(c p) t -> p c t", p=128)
    gqr_v = gqr_d.rearrange("(c p) t -> p c t", p=128)
    gkr_v = gkr_d.rearrange("(c p) t -> p c t", p=128)
    vaug_v = vaug_d.rearrange("(n p) h f -> p n h f", p=128)
    nao_v = nao_d.rearrange("(h d) t -> d h t", d=64)
    gqo_v = gqo_d.rearrange("(h d) t -> d h t", d=64)
    ACC = [0, 1]
    SPAIRS = [2, 4, 6]
    LA = 1
    cnt = {"acc": 0, "s": 0, "pt": 0, "rc": 0}

    def nxt(key, n):
        v = cnt[key] % n
        cnt[key] += 1
        return v

    def attend2(N, qk_A, qk_B, nchunks, pt, ptres, rc, rcres, outs, scale):
        assert nchunks % 2 == 0
        pos = [ACC[0], ACC[1]]
        pend = []

        def emit_pv(info):
            j, items = info
            for hh in range(2):
                pi, vls = items[hh]
                for t in range(2):
                    i = 2 * j + t
                    vl, vres, _ = vls[t]
                    P.op("pe", lambda e, vl=vl, pi=pi, i=i, t=t, hh=hh: e.matmul(psum[pos[hh]][:, 0:N], vl, pt[:, pi, t, 0:N], start=(i == 0), stop=(i == nchunks - 1)),
                         reads=list(vres) + [(ptres, pi)], writes=[("ps", pos[hh])])

        for j in range(nchunks // 2):
            sbs = [SPAIRS[nxt("s", 3)], SPAIRS[nxt("s", 3)]]
            vls = [[None, None], [None, None]]
            for t in range(2):
                order = (0, 1) if t == 0 else (1, 0)
                for hh in order:
                    vls[hh][t] = (qk_A if hh == 0 else qk_B)(2 * j + t, sbs[hh] + t)
                for hh in order:
                    if vls[hh][t][2] is not None:
                        vls[hh][t][2]()
            items = []
            for hh in range(2):
                sb = sbs[hh]
                pi = nxt("pt", 4)
                P.op("act", lambda e, sb=sb, pi=pi: e.activation(out=pt[:, pi, :, 0:N], in_=psum_all[:, sb:sb + 2, 0:N], func=AF.Exp, scale=scale),
                     reads=[("ps", sb), ("ps", sb + 1)], writes=[(ptres, pi)])
                items.append((pi, vls[hh]))
            pend.append((j, items))
            if len(pend) > LA:
                emit_pv(pend.pop(0))
        while pend:
            emit_pv(pend.pop(0))
        for hh in range(2):
            po = pos[hh]
            out_ap, out_res = outs[hh]
            r2 = nxt("rc", 2)
            P.op("dve", lambda e, po=po, r2=r2: e.tensor_copy(out=rc[:, r2, 0, 0:N], in_=psum[po][:, 0:N]), reads=[("ps", po)], writes=[(rcres, r2, 0)])
            P.op("dve", lambda e, r2=r2: e.reciprocal(out=rc[0:64, r2, 1, 0:N], in_=rc[64:128, r2, 0, 0:N]), reads=[(rcres, r2, 0)], writes=[(rcres, r2, 1)])
            P.op("pool", lambda e, r2=r2, out_ap=out_ap: e.tensor_tensor(out=out_ap, in0=rc[0:64, r2, 0, 0:N], in1=rc[0:64, r2, 1, 0:N], op=ALU.mult),
                 reads=[(rcres, r2, 0), (rcres, r2, 1)], writes=[out_res])

    with ExitStack() as ph:
        P.barrier()
        tint = sbuf(ph, "tint", [128, 8, 22 * 64], BF16)
        tfull = sbuf(ph, "tfull", [128, 8, 14 * 64], BF16)
        negt = sbuf(ph, "negt", [128, 256], BF16)
        kc = sbuf(ph, "kcC", [128, 4, T], BF16)
        vc = sbuf(ph, "vcC", [128, 2, 8, 128], BF16)
        qt = sbuf(ph, "qtC", [128, 2, 4, 512], BF16)
        kt = sbuf(ph, "ktC", [128, 2, 4, 1024], BF16)
        vt = sbuf(ph, "vtC", [128, 2, 8, 8, 128], BF16)
        pt = sbuf(ph, "ptC", [128, 4, 2, 512], BF16)
        rc = sbuf(ph, "rcC", [128, 2, 2, 512])
        ost = sbuf(ph, "ostC", [64, 2, 8, 512], BF16)
        for h in range(8):
            ld("pool", tint[:, h, :], tint_d[l, :, h, :], [("tint", h)])
            ld("pool", tfull[:, h, :], tfull_d[l, :, h, :], [("tfull", h)])
        P.op("pool", lambda e: e.memset(negt[:], NEG), writes=["negt"])
        ld("sp", kc[:], nak_v[:, :, S:ST], ["kcC"], reads=[("nak", 8)])
        ld("sp", vc[:], vaug_v[:, 32:34, 0:8, :], ["vcC"], reads=[("vaug", 32), ("vaug", 33)])

        def win(qb):
            a = 8 * qb
            lo = min(max(a - 4, 0), 56)
            hi = min(max(a + 3, 0), 56) + 7
            return a, lo // 2, hi // 2

        def loadC(qb):
            s = qb % 2
            t0, N = blk_info(qb)
            ld("sp", qt[:, s, :, 0:N], naq_v[:, :, t0:t0 + N], [("qtC", s)], reads=[("naq", qb)])
            if qb < 8:
                a, j0, j1 = win(qb)
                nj = j1 - j0 + 1
                blks = sorted(set([(j * 128) // 512 for j in range(j0, j1 + 1)]))
                ld("sp", kt[:, s, :, 0:nj * 128], nak_v[:, :, j0 * 128:(j1 + 1) * 128], [("ktC", s)], reads=[("nak", bb) for bb in blks])
                ld("sp", vt[:, s, 0:nj, :, :], vaug_v[:, j0:j1 + 1, 0:8, :], [("vtC", s)], reads=[("vaug", j) for j in range(j0, j1 + 1)])

        loadC(0)
        for qb in range(nblk_q):
            t0, N = blk_info(qb)
            if qb + 1 < nblk_q:
                loadC(qb + 1)
            s = qb % 2
            if qb < 8:
                a, j0, j1 = win(qb)
                nj = j1 - j0 + 1
            else:
                nj = 0
            for ci in range(4):
                def mk(h, ci=ci, nj=nj):
                    p0 = (h % 2) * 64

                    def qk_ops(i, ps):
                        if i < nj:
                            j = j0 + i
                            P.op("pe", lambda e: e.matmul(psum[ps][:, 0:N], kt[p0:p0 + 64, s, ci, i * 128:(i + 1) * 128], qt[p0:p0 + 64, s, ci, 0:N],
                                                          start=True, stop=False), reads=[("ktC", s), ("qtC", s)], writes=[("ps", ps)])
                            E0 = 2 * j - a + 7
                            segs = []
                            if a == 0:
                                if j <= 3:
                                    mf = 13 - E0
                                    segs.append((0, 256, tfull[:, h, mf * 64:(mf + 4) * 64], ("tfull", h)))
                                else:
                                    segs.append((0, 256, negt[:, 0:256], "negt"))
                                m0 = 17 - E0 + 4
                                segs.append((256, 512, tint[:, h, m0 * 64:(m0 + 4) * 64], ("tint", h)))
                            elif a == 56:
                                m0 = 17 - E0
                                segs.append((0, 320, tint[:, h, m0 * 64:(m0 + 5) * 64], ("tint", h)))
                                if j >= 28:
                                    mf = 13 - E0 + 5
                                    segs.append((320, 512, tfull[:, h, mf * 64:(mf + 3) * 64], ("tfull", h)))
                                else:
                                    segs.append((320, 512, negt[:, 0:192], "negt"))
                            else:
                                m0 = 17 - E0
                                segs.append((0, 512, tint[:, h, m0 * 64:(m0 + 8) * 64], ("tint", h)))

                            def post():
                                for (c0, c1, tab, tres) in segs:
                                    P.op("pe", lambda e, c0=c0, c1=c1, tab=tab: e.matmul(psum[ps][:, c0:c1], IDB, tab, start=False, stop=True),
                                         reads=["cst_b", tres], writes=[("ps", ps)])
                            return vt[:, s, i, h, :], [("vtC", s)], post
                        cj = i - nj
                        P.op("pe", lambda e: e.matmul(psum[ps][:, 0:N], kc[p0:p0 + 64, ci, cj * 128:(cj + 1) * 128], qt[p0:p0 + 64, s, ci, 0:N],
                                                      start=True, stop=True), reads=["kcC", ("qtC", s)], writes=[("ps", ps)])
                        return vc[:, cj, h, :], ["vcC"], None
                    return qk_ops

                hA, hB = 2 * ci, 2 * ci + 1
                attend2(N, mk(hA), mk(hB), nj + 2, pt, "ptC", rc, "rcC",
                        [(ost[:, s, hA, 0:N], ("ostC", s, hA)), (ost[:, s, hB, 0:N], ("ostC", s, hB))], 1.0)
            ld("sp", nao_v[:, :, t0:t0 + N], ost[:, s, :, 0:N], [("nao", qb)], reads=[("ostC", s, h) for h in range(8)])

    if "C" in dbg and l == 0:
        return
    with ExitStack() as ph:
        P.barrier()
        cs = sbuf(ph, "csD", [128, 2, S])
        gq = sbuf(ph, "gqD", [128, 4, ST], BF16)
        gk = sbuf(ph, "gkD", [128, ST], BF16)
        va = sbuf(ph, "vaD", [128, 34, 2, 128], BF16)
        raw = sbuf(ph, "rawD", [128, 2, 5, 512], BF16)
        sq = sbuf(ph, "sqD", [128, 4, 512], BF16)
        rs = sbuf(ph, "rsD", [128, 4, 512])
        qn = sbuf(ph, "qnD", [128, 4, 512], BF16)
        r1 = sbuf(ph, "r1D", [128, 4, 512])
        r2t = sbuf(ph, "r2D", [128, 4, 512])
        pt = sbuf(ph, "ptD", [128, 4, 2, 512], BF16)
        rc = sbuf(ph, "rcD", [128, 2, 2, 512])
        ost = sbuf(ph, "ostD", [64, 4, 512], BF16)
        ld("sp", cs[:], rope_d.rearrange("a p t -> p a t"), ["csD"])
        for half in range(2):
            ld("sp", va[:, half * 17:(half + 1) * 17, :, :], vaug_v[:, half * 17:(half + 1) * 17, 8:10, :], [("vaD", half)],
               reads=[("vaug", j) for j in range(half * 17, (half + 1) * 17)])

        def loadD(b):
            t0, N = blk_info(b)
            s = b % 2
            if b < nblk_q:
                ld("sp", raw[:, s, 0:4, 0:N], gqr_v[:, :, t0:t0 + N], [("rawD", s, c) for c in range(4)], reads=[("gqr", b)])
            ld("sp", raw[:, s, 4:5, 0:N], gkr_v[:, :, t0:t0 + N], [("rawD", s, 4)], reads=[("gkr", b)])

        loadD(0)
        ci = 0
        for b in range(NBLK):
            t0, N = blk_info(b)
            if b + 1 < NBLK:
                loadD(b + 1)
            s = b % 2
            for c in ([0, 1, 2, 3, 4] if b < nblk_q else [4]):
                s2 = ci % 4
                ci += 1
                P.op("act", lambda e, c=c, s2=s2: e.activation(out=sq[:, s2, 0:N], in_=raw[:, s, c, 0:N], func=AF.Square),
                     reads=[("rawD", s, c)], writes=[("sqD", s2)])
                p1 = pbank()
                P.op("pe", lambda e, p1=p1, s2=s2: e.matmul(psum[p1][:, 0:N], BD64, sq[:, s2, 0:N], start=True, stop=True),
                     reads=["cst_b", ("sqD", s2)], writes=[("ps", p1)])
                P.op("act", lambda e, p1=p1, s2=s2: e.activation(out=rs[:, s2, 0:N], in_=psum[p1][:, 0:N], func=AF.Ln, scale=1.0 / 64, bias=EPS),
                     reads=[("ps", p1)], writes=[("rsD", s2)])
                P.op("act", lambda e, s2=s2: e.activation(out=rs[:, s2, 0:N], in_=rs[:, s2, 0:N], func=AF.Exp, scale=-0.5),
                     reads=[("rsD", s2)], writes=[("rsD", s2)])
                gi = 0 if c < 4 else 1
                dst = gq[:, c, t0:t0 + N] if c < 4 else gk[:, t0:t0 + N]
                dres = ("gqD", c, b) if c < 4 else ("gkD", b)
                if b < 8:
                    P.op("dve", lambda e, c=c, s2=s2, gi=gi: e.scalar_tensor_tensor(out=qn[:, s2, 0:N], in0=raw[:, s, c, 0:N], scalar=qkg[:, l, gi:gi + 1], in1=rs[:, s2, 0:N],
                                                                             op0=ALU.mult, op1=ALU.mult),
                         reads=[("rawD", s, c), "qkg", ("rsD", s2)], writes=[("qnD", s2)])
                    p2 = pbank()
                    P.op("pe", lambda e, p2=p2, s2=s2: e.matmul(psum[p2][:, 0:N], PERM, qn[:, s2, 0:N], start=True, stop=True),
                         reads=["cst_b", ("qnD", s2)], writes=[("ps", p2)])
                    P.op("dve", lambda e, s2=s2: e.tensor_tensor(out=r1[:, s2, 0:N], in0=qn[:, s2, 0:N], in1=cs[:, 0, t0:t0 + N], op=ALU.mult),
                         reads=[("qnD", s2), "csD"], writes=[("r1D", s2)])
                    P.op("dve", lambda e, p2=p2, s2=s2: e.tensor_tensor(out=r2t[:, s2, 0:N], in0=psum[p2][:, 0:N], in1=cs[:, 1, t0:t0 + N], op=ALU.mult),
                         reads=[("ps", p2), "csD"], writes=[("r2D", s2)])
                    P.op("pool", lambda e, s2=s2, dst=dst: e.tensor_tensor(out=dst, in0=r1[:, s2, 0:N], in1=r2t[:, s2, 0:N], op=ALU.add),
                         reads=[("r1D", s2), ("r2D", s2)], writes=[dres])
                else:
                    P.op("dve", lambda e, c=c, s2=s2, gi=gi, dst=dst: e.scalar_tensor_tensor(out=dst, in0=raw[:, s, c, 0:N], scalar=qkg[:, l, gi:gi + 1], in1=rs[:, s2, 0:N],
                                                                                      op0=ALU.mult, op1=ALU.mult),
                         reads=[("rawD", s, c), "qkg", ("rsD", s2)], writes=[dres])
        oi = 0
        for qb in range(nblk_q):
            t0, N = blk_info(qb)
            chunks = list(range(34)) if qb < 8 else [32, 33]
            for jt in range(4):
                def mk(g, jt=jt, chunks=chunks):
                    p0 = g * 64

                    def qk_ops(i, ps):
                        c = chunks[i]
                        kb = c // 4 if c < 32 else 8
                        P.op("pe", lambda e: e.matmul(psum[ps][:, 0:N], gk[p0:p0 + 64, c * 128:(c + 1) * 128], gq[p0:p0 + 64, jt, t0:t0 + N], start=True, stop=True),
                             reads=[("gkD", kb), ("gqD", jt, qb)], writes=[("ps", ps)])
                        return va[:, c, g, :], [("vaD", c // 17)], None
                    return qk_ops

                soA = oi % 4
                soB = (oi + 1) % 4
                oi += 2
                attend2(N, mk(0), mk(1), len(chunks), pt, "ptD", rc, "rcD",
                        [(ost[:, soA, 0:N], ("ostD", soA)), (ost[:, soB, 0:N], ("ostD", soB))], 0.125)
                ld("sp", gqo_v[:, jt, t0:t0 + N], ost[:, soA, 0:N], [("gqo", qb, jt)], reads=[("ostD", soA)])
                ld("sp", gqo_v[:, jt + 4, t0:t0 + N], ost[:, soB, 0:N], [("gqo", qb, jt + 4)], reads=[("ostD", soB)])


def _host_prep(x, c, ctx, c_ctx, w_mod, b_mod, norm1_g, norm2_g, w_in, conv_w, conv_b, conv_ln_g, conv_ln_b, w_conv_out,
               na_rpb, w_na_out, q_norm_g, k_norm_g, w_gqa_out, w_out, w_ffn_in, w_ffn_out, final_g):
    f = np.float32
    x = np.asarray(x, f); ctx = np.asarray(ctx, f); c = np.asarray(c, f); c_ctx = np.asarray(c_ctx, f)
    B = x.shape[0]

    def pk(v):
        v = np.asarray(v, f)
        return np.ascontiguousarray(np.swapaxes(v.reshape(v.shape[:-1] + (-1, 128)), -1, -2))

    perm = np.concatenate([np.arange(0, 16), np.arange(32, 48), np.arange(16, 32), np.arange(48, 64)])
    w_in = np.asarray(w_in, f)
    cols = np.arange(INW)
    for j in range(4):
        for half in range(2):
            hh = j + 4 * half
            cols[2560 + j * 128 + half * 64:2560 + j * 128 + half * 64 + 64] = 2560 + hh * 64 + perm
    for g in range(2):
        cols[3072 + g * 64:3072 + g * 64 + 64] = 3072 + g * 64 + perm
    w_in_p = np.ascontiguousarray(w_in[:, :, cols])
    qg = np.asarray(q_norm_g, f)[:, perm]
    kg = np.asarray(k_norm_g, f)[:, perm]
    qkgT = np.stack([np.tile(qg, (1, 2)), np.tile(kg, (1, 2))], axis=-1)
    gnT = np.stack([pk(norm1_g[0]), pk(norm2_g[0]), pk(norm1_g[1]), pk(norm2_g[1]), pk(final_g)], axis=1)
    bmodT = pk(b_mod)
    convwT = np.ascontiguousarray(np.transpose(np.asarray(conv_w, f).reshape(L, CONV_K, 4, 128), (0, 3, 2, 1)))
    convvT = np.stack([pk(conv_b), pk(conv_ln_g), pk(conv_ln_b)], axis=2)
    rpb = np.asarray(na_rpb, f)
    kcg = np.arange(64)[:, None]
    cg = np.arange(64)[None, :]
    win0 = np.clip(cg - 8, 0, 48)
    colvalid = (kcg >= win0) & (kcg < win0 + 16)
    cidx = np.clip(kcg - cg + 15, 0, 30)

    def table(nm, etop, lo, hi):
        tab = np.full((L, 128, 8, nm, 64), NEG, f)
        for m in range(nm):
            e = etop - m
            for krr in range(2):
                dr = e + krr
                if lo <= dr <= hi:
                    vals = rpb[:, :, dr, :][:, :, cidx]
                    vals = np.where(colvalid[None, None], vals, f(NEG))
                    tab[:, krr * 64:(krr + 1) * 64, :, m, :] = np.transpose(vals, (0, 2, 1, 3))
        return np.ascontiguousarray(tab.reshape(L, 128, 8, nm * 64))

    tint = table(22, 17, 3, 10)
    tfull = table(14, 13, 0, 14)
    t = np.arange(S)
    prow = (t // GW).astype(np.float64)
    pcol = (t % GW).astype(np.float64)
    freqs = np.power(10000.0, -np.arange(0, 32, 2, dtype=np.float64) / 32).astype(np.float32).astype(np.float64)
    rope = np.zeros((2, 128, S), f)
    for p in range(128):
        dd = p % 64
        jf = dd % 16
        pos = prow if (dd // 16) % 2 == 0 else pcol
        ang = (pos.astype(np.float32) * freqs[jf].astype(np.float32)).astype(np.float32)
        rope[0, p] = np.cos(ang)
        rope[1, p] = np.sin(ang) * (-1.0 if dd < 32 else 1.0)
    consts = np.zeros((4, 128, 128), f)
    consts[0] = np.eye(128)
    consts[1] = 1.0
    consts[2, 0:64, 0:64] = 1.0
    consts[2, 64:128, 64:128] = 1.0
    for m in range(128):
        consts[3, m + 32 if (m % 64) < 32 else m - 32, m] = 1.0
    shared = dict(w_mod=np.asarray(w_mod, f), bmodT=bmodT, gnT=gnT, w_in=w_in_p, convwT=convwT, convvT=convvT,
                  w_conv_out=np.asarray(w_conv_out, f), tint=tint, tfull=tfull, w_na_out=np.asarray(w_na_out, f), qkgT=qkgT,
                  w_gqa_out=np.asarray(w_gqa_out, f), w_out=np.asarray(w_out, f), w_ffn_in=np.asarray(w_ffn_in, f),
                  w_ffn_out=np.asarray(w_ffn_out, f), rope=rope, consts=consts)
    cc = pk(c_ctx)
    in_maps = []
    for b in range(B):
        m = dict(shared)
        m["x"] = np.ascontiguousarray(x[b])
        m["ctx"] = np.ascontiguousarray(ctx[b])
        m["cT"] = np.ascontiguousarray(np.stack([pk(c[b]), cc], axis=-1))
        in_maps.append(m)
    return in_maps


_NC_CACHE = {}


def kernel(**inputs):
    in_maps = _host_prep(**inputs)
    if "nc" not in _NC_CACHE:
        _NC_CACHE["nc"] = build()
    res = run_bass_kernel_spmd(_NC_CACHE["nc"], in_maps, core_ids=list(range(8)))
    return np.stack([np.asarray(r["out"], np.float32) for r in res.results], axis=0)
```

```python
from contextlib import ExitStack
import os
import numpy as np
import concourse.bass as bass
import concourse.mybir as mybir
from concourse.bass_utils import run_bass_kernel_spmd

F32 = mybir.dt.float32
BF16 = mybir.dt.bfloat16
AF = mybir.ActivationFunctionType
ALU = mybir.AluOpType

D = 1024
S = 4096
T = 256
ST = S + T
L = 2
GW = 64
CONV_K = 31
HALO = 15
FFN = 2816
INW = 6400
EPS = 1e-6
NEG = -1e30
NBLK = 9
ENGS = ("pe", "act", "dve", "pool", "sp")
NDMA = 8


def blk_info(b):
    if b < 8:
        return b * 512, 512
    return S, T


class _Rec:
    def __init__(self):
        self.call = None

    def __getattr__(self, name):
        def f(*a, **k):
            self.call = (name, a, k)
            return self
        return f


def _capture(fn):
    r = _Rec()
    fn(r)
    assert r.call is not None
    return r.call


class Prog:
    def __init__(self, nc):
        self.nc = nc
        self.streams = {e: [] for e in ENGS}
        self.count = {e: 0 for e in ENGS}
        self.known = {e: {} for e in ENGS}
        self.res_w = {}
        self.res_r = {}
        self.dma_use = {}
        self.dma_rr = {e: 0 for e in ENGS}
        self.sems = {}

    def _deps(self, eng, reads, writes):
        need = {}

        def add(tok):
            if tok is None:
                return
            k, v = tok
            if k == "pe" and eng == "pe":
                return
            if need.get(k, 0) < v:
                need[k] = v

        for r in reads:
            add(self.res_w.get(r))
        for w in writes:
            add(self.res_w.get(w))
            for k, v in self.res_r.get(w, {}).items():
                add((k, v))
        return need

    def _commit(self, tok, reads, writes):
        k, v = tok
        for r in reads:
            d = self.res_r.setdefault(r, {})
            if d.get(k, 0) < v:
                d[k] = v
        for w in writes:
            self.res_w[w] = tok
            self.res_r[w] = {}

    def _emit_waits(self, eng, need):
        kn = self.known[eng]
        waits = []
        for k, v in need.items():
            if kn.get(k, 0) < v:
                kn[k] = v
                waits.append((k, v))
        return waits

    def op(self, eng, fn, reads=(), writes=()):
        need = self._deps(eng, reads, writes)
        waits = self._emit_waits(eng, need)
        self.count[eng] += 1
        tok = (eng, self.count[eng])
        self.streams[eng].append((_capture(fn), waits, (eng, 1)))
        self._commit(tok, reads, writes)
        return tok

    def dma(self, q, fn, reads=(), writes=()):
        need = self._deps(q, reads, writes)
        i = self.dma_rr[q]
        self.dma_rr[q] = (i + 1) % NDMA
        key = ("dma", q, i)
        u = self.dma_use.get(key, 0)
        if u > 0 and need.get(key, 0) < 16 * u:
            need[key] = 16 * u
        waits = self._emit_waits(q, need)
        self.dma_use[key] = u + 1
        tok = (key, 16 * (u + 1))
        self.streams[q].append((_capture(fn), waits, (key, 16)))
        self._commit(tok, reads, writes)
        return tok

    def wait_all(self, eng, toks):
        need = {}
        for k, v in toks:
            if need.get(k, 0) < v:
                need[k] = v
        waits = self._emit_waits(eng, need)
        self.streams[eng].append((None, waits, None))

    def barrier(self):
        toks = self.final_tokens()
        for e in ENGS:
            self.wait_all(e, toks)

    def final_tokens(self):
        toks = [(e, self.count[e]) for e in ENGS if self.count[e]]
        toks += [(key, 16 * u) for key, u in self.dma_use.items()]
        return toks

    def emit(self, stack):
        nc = self.nc
        targets = {e: set() for e in ENGS}
        keys = set()
        for e in ENGS:
            for fn, waits, inc in self.streams[e]:
                for k, v in waits:
                    keys.add(k)
                    if isinstance(k, str):
                        targets[k].add(v)
                if inc is not None and not isinstance(inc[0], str):
                    keys.add(inc[0])
        rank = {}
        for e in ENGS:
            srt = sorted(targets[e])
            rank[e] = {v: i + 1 for i, v in enumerate(srt)}
            if srt:
                keys.add(e)
        for k in sorted(keys, key=str):
            name = k if isinstance(k, str) else "d_%s_%d" % (k[1], k[2])
            self.sems[k] = stack.enter_context(nc.semaphore("s_" + name))
        block = stack.enter_context(nc.Block())

        def runner(ename):
            def run(engine):
                idx = 0
                for fn, waits, inc in self.streams[ename]:
                    for k, v in waits:
                        engine.wait_ge(self.sems[k], rank[k][v] if isinstance(k, str) else v)
                    if fn is not None:
                        name, a, kw = fn
                        ins = getattr(engine, name)(*a, **kw)
                        if isinstance(inc[0], str):
                            idx += 1
                            if idx in rank[ename]:
                                ins.then_inc(self.sems[ename], 1)
                        else:
                            ins.then_inc(self.sems[inc[0]], inc[1])
            return run

        block.tensor(runner("pe"))
        block.scalar(runner("act"))
        block.vector(runner("dve"))
        block.gpsimd(runner("pool"))
        block.sync(runner("sp"))


def build(n_layers=L, dbg=()):
    nc = bass.Bass("TRN2", target_bir_lowering=False)
    P = Prog(nc)

    def din(name, shape, dt=F32):
        return nc.dram_tensor(name, shape, dt, kind="ExternalInput").ap()

    def dscr(name, shape, dt=BF16):
        kind = "ExternalOutput" if name in dbg else "Internal"
        return nc.dram_tensor(name, shape, dt, kind=kind).ap()

    x_d = din("x", [S, D])
    ctx_d = din("ctx", [T, D])
    cT_d = din("cT", [128, 8, 2])
    wmod_d = din("w_mod", [L, D, 6 * D])
    bmod_d = din("bmodT", [L, 128, 48])
    gn_d = din("gnT", [128, 2 * L + 1, 8])
    win_d = din("w_in", [L, D, INW])
    convw_d = din("convwT", [L, 128, 4, CONV_K])
    convv_d = din("convvT", [L, 128, 3, 4])
    wco_d = din("w_conv_out", [L, 512, D])
    tint_d = din("tint", [L, 128, 8, 22 * 64])
    tfull_d = din("tfull", [L, 128, 8, 14 * 64])
    wno_d = din("w_na_out", [L, 512, D])
    qkg_d = din("qkgT", [L, 128, 2])
    wgo_d = din("w_gqa_out", [L, 512, D])
    wout_d = din("w_out", [L, D, D])
    wfi_d = din("w_ffn_in", [L, D, 2 * FFN])
    wfo_d = din("w_ffn_out", [L, FFN, D])
    rope_d = din("rope", [2, 128, S])
    cst_d = din("consts", [4, 128, 128])
    out_d = nc.dram_tensor("out", [S, D], F32, kind="ExternalOutput").ap()

    xT_d = dscr("xT", [D, ST], F32)
    UW = S + 2 * HALO
    UCW = T + 2 * HALO
    uT_d = dscr("uT", [512, UW + UCW])
    naq_d = dscr("naqT", [512, ST])
    nak_d = dscr("nakT", [512, ST])
    gqr_d = dscr("gqrT", [512, ST])
    gkr_d = dscr("gkrT", [128, ST])
    gat_d = dscr("gatesT", [3072, ST])
    vaug_d = dscr("vaug", [ST, 10, 128])
    ca_d = dscr("caT", [512, ST])
    nao_d = dscr("naoT", [512, ST])
    gqo_d = dscr("gqoT", [512, ST])

    with ExitStack() as top:
        uid = [0]

        def sbuf(st, name, shape, dt=F32):
            uid[0] += 1
            return st.enter_context(nc.sbuf_tensor("sb%d_%s" % (uid[0], name), shape, dt))

        psum_all = top.enter_context(nc.psum_tensor("pbanks", [128, 8, 512], F32))
        psum = [psum_all[:, i, :] for i in range(8)]
        prr = [0]

        def pbank():
            i = prr[0]
            prr[0] = (i + 1) % 8
            return i

        ident_f = sbuf(top, "ident_f", [128, 128])
        cst_b = sbuf(top, "cst_b", [128, 4, 128], BF16)
        modv = sbuf(top, "modv", [128, L, 48, 2])
        bmod = sbuf(top, "bmod", [128, L, 48])
        gn = sbuf(top, "gn", [128, 2 * L + 1, 8])
        Amod = sbuf(top, "Amod", [128, L, 2, 8, 2])
        rstd = sbuf(top, "rstd", [128, ST])
        stats = rstd
        sil = sbuf(top, "sil", [128, 8, 2])
        cT = sbuf(top, "cT_s", [128, 8, 2])
        zpad = sbuf(top, "zpad", [128, 4, HALO], BF16)
        convw = sbuf(top, "convw", [128, L, 4, CONV_K])
        convv = sbuf(top, "convv", [128, L, 3, 4])
        qkg = sbuf(top, "qkg", [128, L, 2])

        IDB = cst_b[:, 0, :]
        ONESB = cst_b[:, 1, :]
        BD64 = cst_b[:, 2, :]
        PERM = cst_b[:, 3, :]

        def ld(q, out, in_, writes, reads=()):
            P.dma(q, lambda e, o=out, i=in_: e.dma_start(out=o, in_=i), reads=reads, writes=writes)

        ld("sp", ident_f[:], cst_d[0], ["ident_f"])
        ld("pool", cst_b[:], cst_d.rearrange("c p n -> p c n"), ["cst_b"])
        ld("sp", bmod[:], bmod_d.rearrange("l p c -> p l c"), ["bmod"])
        ld("sp", gn[:], gn_d, ["gn"])
        ld("sp", cT[:], cT_d, ["cT"])
        ld("sp", convw[:], convw_d.rearrange("l p c k -> p l c k"), ["convw"])
        ld("sp", convv[:], convv_d.rearrange("l p a c -> p l a c"), ["convv"])
        ld("sp", qkg[:], qkg_d.rearrange("l p a -> p l a"), ["qkg"])
        P.op("pool", lambda e: e.memset(zpad[:], 0.0), writes=["zpad"])
        uT_v = uT_d.rearrange("(c p) t -> p c t", p=128)
        for off in (0, HALO + S, UW, UW + HALO + T):
            ld("sp", uT_v[:, :, off:off + HALO], zpad[:], [("uTpad", off)], reads=["zpad"])

        P.op("act", lambda e: e.activation(out=sil[:], in_=cT[:], func=AF.Exp, scale=-1.0), reads=["cT"], writes=["sil"])
        P.op("dve", lambda e: e.tensor_scalar_add(out=sil[:], in0=sil[:], scalar1=1.0), reads=["sil"], writes=["sil"])
        P.op("dve", lambda e: e.reciprocal(out=sil[:], in_=sil[:]), reads=["sil"], writes=["sil"])
        P.op("dve", lambda e: e.tensor_mul(out=sil[:], in0=sil[:], in1=cT[:]), reads=["sil", "cT"], writes=["sil"])
        with ExitStack() as ph:
            P.barrier()
            wm = sbuf(ph, "wm", [128, 2, 8, 1024])
            it = 0
            for l in range(n_layers):
                for cg in range(6):
                    sl = (it % 2) if not os.environ.get('WM1') else 0
                    it += 1
                    for k in range(8):
                        ld("sp", wm[:, sl, k, :], wmod_d[l, k * 128:(k + 1) * 128, cg * 1024:(cg + 1) * 1024],
                           [("wm", sl, k)])
                    for jc in range(8):
                        oc = cg * 8 + jc
                        pb = pbank()
                        for k in range(8):
                            P.op("pe", lambda e, pb=pb, sl=sl, k=k, jc=jc: e.matmul(
                                psum[pb][:, 0:2], wm[:, sl, k, jc * 128:(jc + 1) * 128], sil[:, k, :],
                                start=(k == 0), stop=(k == 7)),
                                reads=[("wm", sl, k), "sil"], writes=[("ps", pb)])
                        P.op("dve", lambda e, pb=pb, l=l, oc=oc: e.tensor_scalar(
                            out=modv[:, l, oc, :], in0=psum[pb][:, 0:2], scalar1=bmod[:, l, oc:oc + 1], scalar2=None,
                            op0=ALU.add), reads=[("ps", pb), "bmod"], writes=["modv"])
            for l in range(n_layers):
                for n in range(2):
                    for j in range(2):
                        P.op("dve", lambda e, l=l, n=n, j=j: e.scalar_tensor_tensor(
                            out=Amod[:, l, n, :, j], in0=modv[:, l, (8 + 24 * n):(16 + 24 * n), j], scalar=1.0,
                            in1=gn[:, 2 * l + n, :], op0=ALU.add, op1=ALU.mult),
                            reads=["modv", "gn"], writes=["Amod"])

        def shift_ap(l, n, k, j):
            return modv[:, l, 24 * n + k, j:j + 1]

        def gate_ap(l, n, k, j):
            return modv[:, l, 16 + 24 * n + k, j:j + 1]

        def stats_to_rstd(lo, hi, tag):
            P.op("act", lambda e: e.activation(out=rstd[:, lo:hi], in_=stats[:, lo:hi], func=AF.Ln, scale=1.0 / D, bias=EPS),
                 reads=["rstd"], writes=["rstd"])
            P.op("act", lambda e: e.activation(out=rstd[:, lo:hi], in_=rstd[:, lo:hi], func=AF.Exp, scale=-0.5),
                 reads=["rstd"], writes=["rstd"])

        def sumsq_block(src_tile_fn, src_res, t0, N, sqt, sqres):
            pb = pbank()
            for k in range(8):
                P.op("act", lambda e, k=k: e.activation(out=sqt[:, k, 0:N], in_=src_tile_fn(k), func=AF.Square),
                     reads=[src_res(k)], writes=[(sqres, k)])
                P.op("pe", lambda e, k=k, pb=pb: e.matmul(psum[pb][:, 0:N], ONESB, sqt[:, k, 0:N], start=(k == 0), stop=(k == 7)),
                     reads=[(sqres, k), "cst_b"], writes=[("ps", pb)])
            P.op("dve", lambda e, pb=pb: e.tensor_copy(out=stats[:, t0:t0 + N], in_=psum[pb][:, 0:N]),
                 reads=[("ps", pb)], writes=["rstd"])

        xT_v = xT_d.rearrange("(k p) t -> p k t", p=128)
        with ExitStack() as ph:
            P.barrier()
            xin = sbuf(ph, "xin", [128, 2, D])
            xst = sbuf(ph, "xst", [128, 2, 8, 512])
            sqt = sbuf(ph, "sqt0", [128, 8, 512], BF16)
            ti = 0
            for b in range(NBLK):
                t0, N = blk_info(b)
                sl = b % 2
                for tt in range(N // 128):
                    s2 = ti % 2
                    ti += 1
                    src = x_d[t0 + tt * 128:t0 + (tt + 1) * 128, :] if b < 8 else ctx_d[tt * 128:(tt + 1) * 128, :]
                    ld("sp", xin[:, s2, :], src, [("xin", s2)])
                    for kk in range(2):
                        pb = pbank()
                        for k4 in range(4):
                            k = kk * 4 + k4
                            P.op("pe", lambda e, pb=pb, k4=k4, k=k, s2=s2: e.transpose(
                                psum[pb][:, k4 * 128:(k4 + 1) * 128], xin[:, s2, k * 128:(k + 1) * 128], ident_f[:]),
                                reads=[("xin", s2), "ident_f"], writes=[("ps", pb)])
                        P.op("dve", lambda e, pb=pb, kk=kk, sl=sl, tt=tt: e.tensor_copy(
                            out=xst[:, sl, kk * 4:(kk + 1) * 4, tt * 128:(tt + 1) * 128],
                            in_=psum[pb][:, :].rearrange("p (k t) -> p k t", k=4)),
                            reads=[("ps", pb)], writes=[("xst", sl, kk)])
                ld("sp", xT_v[:, :, t0:t0 + N], xst[:, sl, :, 0:N], [("xT", b)], reads=[("xst", sl, 0), ("xst", sl, 1)])
                sumsq_block(lambda k, sl=sl, N=N: xst[:, sl, k, 0:N], lambda k, sl=sl: ("xst", sl, k // 4), t0, N, sqt, "sqt0")
        stats_to_rstd(0, ST, "n1l0")

        if "dump" in dbg:
            d1 = nc.dram_tensor("d_rstd", [128, ST], F32, kind="ExternalOutput").ap()
            d2 = nc.dram_tensor("d_modv", [128, L, 48, 2], F32, kind="ExternalOutput").ap()
            d3 = nc.dram_tensor("d_amod", [128, L, 2, 8, 2], F32, kind="ExternalOutput").ap()
            d4 = nc.dram_tensor("d_sil", [128, 8, 2], F32, kind="ExternalOutput").ap()
            ld("sp", d1, rstd[:], ["d1"], reads=["rstd"])
            ld("sp", d2, modv[:], ["d2"], reads=["modv"])
            ld("sp", d3, Amod[:], ["d3"], reads=["Amod"])
            ld("sp", d4, sil[:], ["d4"], reads=["sil"])
            P.wait_all("sp", P.final_tokens())
            P.emit(top)
            return nc
        def load_w_bf(dst_fn, src_fn, nk, res):
            for k in range(nk):
                ld("pool", dst_fn(k), src_fn(k), [(res, k)])

        def modulate_block(l, n, b, xt, xres, ht, hres, tmp, tmpres):
            t0, N = blk_info(b)
            j = 0 if b < 8 else 1
            for k in range(8):
                ts_ = k % 2
                P.op("dve", lambda e, k=k, ts_=ts_: e.tensor_tensor(out=tmp[:, ts_, 0:N], in0=xt(k), in1=rstd[:, t0:t0 + N], op=ALU.mult),
                     reads=[xres(k), "rstd"], writes=[(tmpres, ts_)])
                P.op("dve", lambda e, k=k, ts_=ts_: e.tensor_scalar(
                    out=ht[:, k, 0:N], in0=tmp[:, ts_, 0:N], scalar1=Amod[:, l, n, k, j:j + 1], scalar2=shift_ap(l, n, k, j),
                    op0=ALU.mult, op1=ALU.add), reads=[(tmpres, ts_), "Amod", "modv"], writes=[(hres, k)])

        for l in range(n_layers):
            last = (l == n_layers - 1) and (n_layers == L)
            nblk_q = 8 if last else NBLK
            with ExitStack() as ph:
                P.barrier()
                win = sbuf(ph, "win", [128, 8, INW], BF16)
                xt = sbuf(ph, "xtA", [128, 2, 8, 512])
                ht2 = sbuf(ph, "htA", [128, 2, 8, 512], BF16)
                tmp = sbuf(ph, "tmpA", [128, 2, 512])
                sg = sbuf(ph, "sgA", [128, 2, 512])
                stg = sbuf(ph, "stgA", [128, 2, 8, 512], BF16)
                vst = sbuf(ph, "vstA", [128, 2, 10, 128], BF16)
                WIN_GROUPS = [(512, 1024), (0, 512), (1024, 1536), (2560, 3072), (3328, 4352), (4352, 5376), (5376, 6400),
                              (1536, 2048), (3072, 3328), (2048, 2560)]

                def wgrp(col):
                    for gi, (c0, c1) in enumerate(WIN_GROUPS):
                        if c0 <= col < c1:
                            return gi
                    raise AssertionError(col)

                for gi, (c0, c1) in enumerate(WIN_GROUPS):
                    for k in range(8):
                        ld("pool", win[:, k, c0:c1], win_d[l, k * 128:(k + 1) * 128, c0:c1], [("win", k, gi)])
                for s2 in range(2):
                    P.op("dve", lambda e, s2=s2: e.memset(vst[:, s2, :, 64:128], 1.0), writes=[("vst", s2)])

                def loadA(b):
                    t0, N = blk_info(b)
                    ld("sp", xt[:, b % 2, :, 0:N], xT_v[:, :, t0:t0 + N], [("xtA", b % 2, k) for k in range(8)], reads=[("xT", b)])

                uT_lat = uT_v[:, :, HALO:HALO + S]
                uT_ctx = uT_v[:, :, UW + HALO:UW + HALO + T]
                naq_v = naq_d.rearrange("(c p) t -> p c t", p=128)
                nak_v = nak_d.rearrange("(c p) t -> p c t", p=128)
                gqr_v = gqr_d.rearrange("(c p) t -> p c t", p=128)
                gkr_v = gkr_d.rearrange("(c p) t -> p c t", p=128)
                gat_v = gat_d.rearrange("(c p) t -> p c t", p=128)
                vaug_v = vaug_d.rearrange("(n p) h f -> p n h f", p=128)
                stgi = [0]
                vsi = [0]
                loadA(0)
                for b in range(NBLK):
                    t0, N = blk_info(b)
                    if b + 1 < NBLK:
                        loadA(b + 1)
                    sl = b % 2
                    ht = ht2[:, sl]
                    hres = "htA%d" % sl
                    modulate_block(l, 0, b, lambda k, sl=sl, N=N: xt[:, sl, k, 0:N], lambda k, sl=sl: ("xtA", sl, k),
                                   ht, hres, tmp, "tmpA")

                    def proj(col, pb, N=N):
                        for k in range(8):
                            P.op("pe", lambda e, k=k: e.matmul(psum[pb][:, 0:N], win[:, k, col:col + 128], ht[:, k, 0:N],
                                                               start=(k == 0), stop=(k == 7)),
                                 reads=[("win", k, wgrp(col)), (hres, k)], writes=[("ps", pb)])

                    def group(col0, nch, dst, kind, dres):
                        ss = stgi[0] % 2
                        stgi[0] += 1
                        for c in range(nch):
                            pb = pbank()
                            proj(col0 + c * 128, pb)
                            if kind == "sig":
                                P.op("act", lambda e, pb=pb, c=c: e.activation(out=stg[:, ss, c, 0:N], in_=psum[pb][:, 0:N], func=AF.Sigmoid),
                                     reads=[("ps", pb)], writes=[("stgA", ss, c)])
                            elif kind == "copy":
                                P.op("act", lambda e, pb=pb, c=c: e.activation(out=stg[:, ss, c, 0:N], in_=psum[pb][:, 0:N], func=AF.Identity),
                                     reads=[("ps", pb)], writes=[("stgA", ss, c)])
                            elif kind == "scale":
                                P.op("act", lambda e, pb=pb, c=c: e.activation(out=stg[:, ss, c, 0:N], in_=psum[pb][:, 0:N], func=AF.Identity, scale=0.125),
                                     reads=[("ps", pb)], writes=[("stgA", ss, c)])
                        ld("sp", dst, stg[:, ss, 0:nch, 0:N], [dres], reads=[("stgA", ss, c) for c in range(nch)])

                    ss = stgi[0] % 2
                    stgi[0] += 1
                    for c in range(4):
                        pg = pbank()
                        proj(512 + c * 128, pg)
                        s2 = c % 2
                        P.op("act", lambda e, pg=pg, s2=s2: e.activation(out=sg[:, s2, 0:N], in_=psum[pg][:, 0:N], func=AF.Sigmoid),
                             reads=[("ps", pg)], writes=[("sgA", s2)])
                        pv = pbank()
                        proj(c * 128, pv)
                        P.op("dve", lambda e, pv=pv, s2=s2, c=c, ss=ss: e.tensor_tensor(out=stg[:, ss, c, 0:N], in0=psum[pv][:, 0:N], in1=sg[:, s2, 0:N], op=ALU.mult),
                             reads=[("ps", pv), ("sgA", s2)], writes=[("stgA", ss, c)])
                    udst = uT_lat[:, :, t0:t0 + N] if b < 8 else uT_ctx
                    ld("sp", udst, stg[:, ss, 0:4, 0:N], [("uT", b)], reads=[("stgA", ss, c) for c in range(4)])
                    if b < nblk_q:
                        group(1024, 4, naq_v[:, :, t0:t0 + N], "scale", ("naq", b))
                        group(2560, 4, gqr_v[:, :, t0:t0 + N], "copy", ("gqr", b))
                        for g3 in range(3):
                            group(3328 + g3 * 1024, 8, gat_v[:, g3 * 8:(g3 + 1) * 8, t0:t0 + N], "sig", ("gat", b, g3))
                    group(1536, 4, nak_v[:, :, t0:t0 + N], "copy", ("nak", b))
                    group(3072, 1, gkr_v[:, :, t0:t0 + N], "copy", ("gkr", b))
                    for tt in range(N // 128):
                        s2 = vsi[0] % 2
                        vsi[0] += 1
                        p1 = pbank()
                        p2 = pbank()
                        for k in range(8):
                            P.op("pe", lambda e, k=k, tt=tt, p1=p1: e.matmul(psum[p1][:, 0:512], ht[:, k, tt * 128:(tt + 1) * 128], win[:, k, 2048:2560],
                                                                        start=(k == 0), stop=(k == 7)),
                                 reads=[("win", k, wgrp(2048)), (hres, k)], writes=[("ps", p1)])
                        for k in range(8):
                            P.op("pe", lambda e, k=k, tt=tt, p2=p2: e.matmul(psum[p2][:, 0:128], ht[:, k, tt * 128:(tt + 1) * 128], win[:, k, 3200:3328],
                                                                        start=(k == 0), stop=(k == 7)),
                                 reads=[("win", k, wgrp(3200)), (hres, k)], writes=[("ps", p2)])
                        P.op("dve", lambda e, p1=p1, s2=s2: e.tensor_copy(out=vst[:, s2, 0:8, 0:64], in_=psum[p1][:, 0:512].rearrange("p (h d) -> p h d", h=8)),
                             reads=[("ps", p1)], writes=[("vst", s2)])
                        P.op("dve", lambda e, p2=p2, s2=s2: e.tensor_copy(out=vst[:, s2, 8:10, 0:64], in_=psum[p2][:, 0:128].rearrange("p (h d) -> p h d", h=2)),
                             reads=[("ps", p2)], writes=[("vst", s2)])
                        ti = t0 // 128 + tt
                        ld("sp", vaug_v[:, ti, :, :], vst[:, s2, :, :], [("vaug", ti)], reads=[("vst", s2)])

            if "A" in dbg and l == 0:
                break
            ca_v = ca_d.rearrange("(c p) t -> p c t", p=128)
            with ExitStack() as ph:
                P.barrier()
                dg = sbuf(ph, "dg", [128, 4, CONV_K, 128], BF16)
                ut = sbuf(ph, "utB", [128, 2, 4, 512 + 2 * HALO], BF16)
                yb = sbuf(ph, "ybB", [128, 4, 512])
                ybf = sbuf(ph, "ybfB", [128, 4, 512], BF16)
                ysq = sbuf(ph, "ysqB", [128, 4, 512], BF16)
                mu = sbuf(ph, "muB", [128, 512])
                var = sbuf(ph, "varB", [128, 512])
                t1 = sbuf(ph, "t1B", [128, 2, 512])
                t2 = sbuf(ph, "t2B", [128, 2, 512])
                cast = sbuf(ph, "castB", [128, 2, 4, 512], BF16)
                for c in range(4):
                    for kk in range(CONV_K):
                        P.op("pool" if kk % 2 else "dve", lambda e, c=c, kk=kk: e.tensor_scalar(out=dg[:, c, kk, :], in0=IDB, scalar1=convw[:, l, c, kk:kk + 1], scalar2=None, op0=ALU.mult),
                             reads=["cst_b", "convw"], writes=[("dg", c, kk)])

                def loadB(b):
                    t0, N = blk_info(b)
                    if b < 8:
                        src = uT_v[:, :, t0:t0 + N + 2 * HALO]
                        rd = [("uT", bb) for bb in (b - 1, b, b + 1) if 0 <= bb < 8] + [("uTpad", 0), ("uTpad", HALO + S)]
                    else:
                        src = uT_v[:, :, UW:UW + UCW]
                        rd = [("uT", 8), ("uTpad", UW), ("uTpad", UW + HALO + T)]
                    ld("sp", ut[:, b % 2, :, 0:N + 2 * HALO], src, [("utB", b % 2)], reads=rd)

                loadB(0)
                for b in range(nblk_q):
                    t0, N = blk_info(b)
                    if b + 1 < nblk_q:
                        loadB(b + 1)
                    sl = b % 2
                    ps1 = pbank()
                    ps2 = pbank()
                    for c in range(4):
                        pb = pbank()
                        for kk in range(CONV_K):
                            P.op("pe", lambda e, c=c, kk=kk, pb=pb: e.matmul(psum[pb][:, 0:N], dg[:, c, kk, :], ut[:, sl, c, kk:kk + N],
                                                                        start=(kk == 0), stop=(kk == CONV_K - 1)),
                                 reads=[("dg", c, kk), ("utB", sl)], writes=[("ps", pb)])
                        P.op("dve", lambda e, c=c, pb=pb: e.tensor_scalar(out=yb[:, c, 0:N], in0=psum[pb][:, 0:N], scalar1=convv[:, l, 0, c:c + 1], scalar2=None, op0=ALU.add),
                             reads=[("ps", pb), "convv"], writes=[("ybB", c)])
                        P.op("pool", lambda e, c=c: e.tensor_copy(out=ybf[:, c, 0:N], in_=yb[:, c, 0:N]), reads=[("ybB", c)], writes=[("ybfB", c)])
                        P.op("act", lambda e, c=c: e.activation(out=ysq[:, c, 0:N], in_=yb[:, c, 0:N], func=AF.Square), reads=[("ybB", c)], writes=[("ysqB", c)])
                    for c in range(4):
                        P.op("pe", lambda e, c=c: e.matmul(psum[ps1][:, 0:N], ONESB, ybf[:, c, 0:N], start=(c == 0), stop=(c == 3)),
                             reads=[("ybfB", c), "cst_b"], writes=[("ps", ps1)])
                    for c in range(4):
                        P.op("pe", lambda e, c=c: e.matmul(psum[ps2][:, 0:N], ONESB, ysq[:, c, 0:N], start=(c == 0), stop=(c == 3)),
                             reads=[("ysqB", c), "cst_b"], writes=[("ps", ps2)])
                    P.op("dve", lambda e: e.tensor_scalar(out=mu[:, 0:N], in0=psum[ps1][:, 0:N], scalar1=1.0 / 512, scalar2=None, op0=ALU.mult),
                         reads=[("ps", ps1)], writes=["muB"])
                    P.op("dve", lambda e: e.tensor_tensor(out=var[:, 0:N], in0=mu[:, 0:N], in1=mu[:, 0:N], op=ALU.mult), reads=["muB"], writes=["varB"])
                    P.op("dve", lambda e: e.scalar_tensor_tensor(out=var[:, 0:N], in0=psum[ps2][:, 0:N], scalar=1.0 / 512, in1=var[:, 0:N], op0=ALU.mult, op1=ALU.subtract),
                         reads=[("ps", ps2), "varB"], writes=["varB"])
                    P.op("act", lambda e: e.activation(out=var[:, 0:N], in_=var[:, 0:N], func=AF.Ln, bias=EPS), reads=["varB"], writes=["varB"])
                    P.op("act", lambda e: e.activation(out=var[:, 0:N], in_=var[:, 0:N], func=AF.Exp, scale=-0.5), reads=["varB"], writes=["varB"])
                    for c in range(4):
                        s2 = c % 2
                        P.op("pool", lambda e, c=c, s2=s2: e.tensor_tensor(out=t1[:, s2, 0:N], in0=yb[:, c, 0:N], in1=mu[:, 0:N], op=ALU.subtract),
                             reads=[("ybB", c), "muB"], writes=[("t1B", s2)])
                        P.op("pool", lambda e, s2=s2: e.tensor_tensor(out=t1[:, s2, 0:N], in0=t1[:, s2, 0:N], in1=var[:, 0:N], op=ALU.mult),
                             reads=[("t1B", s2), "varB"], writes=[("t1B", s2)])
                        P.op("dve", lambda e, c=c, s2=s2: e.tensor_scalar(out=t1[:, s2, 0:N], in0=t1[:, s2, 0:N], scalar1=convv[:, l, 1, c:c + 1], scalar2=convv[:, l, 2, c:c + 1],
                                                                      op0=ALU.mult, op1=ALU.add), reads=[("t1B", s2), "convv"], writes=[("t1B", s2)])
                        P.op("act", lambda e, s2=s2, c=c: e.activation(out=cast[:, sl, c, 0:N], in_=t1[:, s2, 0:N], func=AF.Silu), reads=[("t1B", s2)], writes=[("castB", sl, c)])
                    ld("sp", ca_v[:, :, t0:t0 + N], cast[:, sl, :, 0:N], [("ca", b)], reads=[("castB", sl, c) for c in range(4)])

            if "B" in dbg and l == 0:
                break
            ATTN(nc, P, l, nblk_q, psum_all, psum, pbank, sbuf, ld, cst_b, qkg, naq_d, nak_d, vaug_d, gqr_d, gkr_d, nao_d, gqo_d,
                 tint_d, tfull_d, rope_d, dbg)
            if "D" in dbg and l == 0:
                break
            nao_v = nao_d.rearrange("(c p) t -> p c t", p=128)
            gqo_v = gqo_d.rearrange("(c p) t -> p c t", p=128)
            with ExitStack() as ph:
                P.barrier()
                wco = sbuf(ph, "wco", [128, 4, D], BF16)
                wno = sbuf(ph, "wno", [128, 4, D], BF16)
                wgo = sbuf(ph, "wgo", [128, 4, D], BF16)
                wo = sbuf(ph, "wo", [128, 8, D], BF16)
                xt = sbuf(ph, "xtE", [128, 2, 8, 512])
                cat = sbuf(ph, "catE", [128, 2, 4, 512], BF16)
                nat = sbuf(ph, "natE", [128, 2, 4, 512], BF16)
                gqt = sbuf(ph, "gqtE", [128, 2, 4, 512], BF16)
                gt = sbuf(ph, "gtE", [128, 2, 24, 512], BF16)
                yt = sbuf(ph, "ytE", [128, 8, 512], BF16)
                m1 = sbuf(ph, "m1E", [128, 2, 512])
                m2 = sbuf(ph, "m2E", [128, 2, 512])
                sqt = sbuf(ph, "sqtE", [128, 8, 512], BF16)
                load_w_bf(lambda k: wco[:, k, :], lambda k: wco_d[l, k * 128:(k + 1) * 128, :], 4, "wco")
                load_w_bf(lambda k: wno[:, k, :], lambda k: wno_d[l, k * 128:(k + 1) * 128, :], 4, "wno")
                load_w_bf(lambda k: wgo[:, k, :], lambda k: wgo_d[l, k * 128:(k + 1) * 128, :], 4, "wgo")
                load_w_bf(lambda k: wo[:, k, :], lambda k: wout_d[l, k * 128:(k + 1) * 128, :], 8, "wo")

                def loadE(b):
                    t0, N = blk_info(b)
                    s = b % 2
                    ld("sp", xt[:, s, :, 0:N], xT_v[:, :, t0:t0 + N], [("xtE", s, k) for k in range(8)], reads=[("xT", b)])
                    ld("sp", cat[:, s, :, 0:N], ca_v[:, :, t0:t0 + N], [("catE", s)], reads=[("ca", b)])
                    ld("sp", nat[:, s, :, 0:N], nao_v[:, :, t0:t0 + N], [("natE", s)], reads=[("nao", b)])
                    ld("sp", gqt[:, s, :, 0:N], gqo_v[:, :, t0:t0 + N], [("gqtE", s)], reads=[("gqo", b, h) for h in range(8)])
                    for g3 in range(3):
                        ld("sp", gt[:, s, g3 * 8:(g3 + 1) * 8, 0:N], gat_v[:, g3 * 8:(g3 + 1) * 8, t0:t0 + N], [("gtE", s, g3)], reads=[("gat", b, g3)])

                loadE(0)
                for b in range(nblk_q):
                    t0, N = blk_info(b)
                    j = 0 if b < 8 else 1
                    if b + 1 < nblk_q:
                        loadE(b + 1)
                    s = b % 2
                    for fc in range(8):
                        fs = slice(fc * 128, (fc + 1) * 128)
                        s2 = fc % 2
                        pa = pbank()
                        for k in range(4):
                            P.op("pe", lambda e, k=k, pa=pa, fs=fs: e.matmul(psum[pa][:, 0:N], wco[:, k, fs], cat[:, s, k, 0:N], start=(k == 0), stop=(k == 3)),
                                 reads=[("wco", k), ("catE", s)], writes=[("ps", pa)])
                        P.op("dve", lambda e, pa=pa, fc=fc, s2=s2: e.tensor_tensor(out=m1[:, s2, 0:N], in0=psum[pa][:, 0:N], in1=gt[:, s, fc, 0:N], op=ALU.mult),
                             reads=[("ps", pa), ("gtE", s, 0)], writes=[("m1E", s2)])
                        pn = pbank()
                        for h in range(4):
                            P.op("pe", lambda e, h=h, pn=pn, fs=fs: e.matmul(psum[pn][:, 0:N], wno[:, h, fs], nat[:, s, h, 0:N], start=(h == 0), stop=(h == 3)),
                                 reads=[("wno", h), ("natE", s)], writes=[("ps", pn)])
                        P.op("dve", lambda e, pn=pn, fc=fc, s2=s2: e.tensor_tensor(out=m2[:, s2, 0:N], in0=psum[pn][:, 0:N], in1=gt[:, s, 8 + fc, 0:N], op=ALU.mult),
                             reads=[("ps", pn), ("gtE", s, 1)], writes=[("m2E", s2)])
                        P.op("pool", lambda e, s2=s2: e.tensor_tensor(out=m1[:, s2, 0:N], in0=m1[:, s2, 0:N], in1=m2[:, s2, 0:N], op=ALU.add),
                             reads=[("m1E", s2), ("m2E", s2)], writes=[("m1E", s2)])
                        pg = pbank()
                        for h in range(4):
                            P.op("pe", lambda e, h=h, pg=pg, fs=fs: e.matmul(psum[pg][:, 0:N], wgo[:, h, fs], gqt[:, s, h, 0:N], start=(h == 0), stop=(h == 3)),
                                 reads=[("wgo", h), ("gqtE", s)], writes=[("ps", pg)])
                        P.op("dve", lambda e, pg=pg, fc=fc, s2=s2: e.tensor_tensor(out=m2[:, s2, 0:N], in0=psum[pg][:, 0:N], in1=gt[:, s, 16 + fc, 0:N], op=ALU.mult),
                             reads=[("ps", pg), ("gtE", s, 2)], writes=[("m2E", s2)])
                        P.op("pool", lambda e, s2=s2, fc=fc: e.tensor_tensor(out=yt[:, fc, 0:N], in0=m1[:, s2, 0:N], in1=m2[:, s2, 0:N], op=ALU.add),
                             reads=[("m1E", s2), ("m2E", s2)], writes=[("ytE", fc)])
                    for fc in range(8):
                        fs = slice(fc * 128, (fc + 1) * 128)
                        po = pbank()
                        for k in range(8):
                            P.op("pe", lambda e, k=k, po=po, fs=fs: e.matmul(psum[po][:, 0:N], wo[:, k, fs], yt[:, k, 0:N], start=(k == 0), stop=(k == 7)),
                                 reads=[("wo", k), ("ytE", k)], writes=[("ps", po)])
                        P.op("dve", lambda e, po=po, fc=fc: e.scalar_tensor_tensor(out=xt[:, s, fc, 0:N], in0=psum[po][:, 0:N], scalar=gate_ap(l, 0, fc, j), in1=xt[:, s, fc, 0:N],
                                                                              op0=ALU.mult, op1=ALU.add),
                             reads=[("ps", po), "modv", ("xtE", s, fc)], writes=[("xtE", s, fc)])
                    ld("sp", xT_v[:, :, t0:t0 + N], xt[:, s, :, 0:N], [("xT", b)], reads=[("xtE", s, k) for k in range(8)])
                    sumsq_block(lambda k, s=s, N=N: xt[:, s, k, 0:N], lambda k, s=s: ("xtE", s, k), t0, N, sqt, "sqtE")
            stats_to_rstd(0, ST if nblk_q == NBLK else S, "n2")
            if "E" in dbg and l == 0:
                break
            with ExitStack() as ph:
                P.barrier()
                wfi = sbuf(ph, "wfi", [128, 8, 2 * FFN], BF16)
                wfo = sbuf(ph, "wfo", [128, 22, D], BF16)
                xt = sbuf(ph, "xtF", [128, 1, 8, 512])
                ht = sbuf(ph, "htF", [128, 8, 512], BF16)
                tmp = sbuf(ph, "tmpF", [128, 2, 512])
                hid = sbuf(ph, "hidF", [128, 22, 512], BF16)
                sgt = sbuf(ph, "sgF", [128, 2, 512], BF16)
                sqt = ht
                HG = [(0, 4), (4, 8), (8, 12), (12, 16), (16, 20), (20, 22)]

                def hgrp(hc):
                    return hc // 4

                for (h0, h1) in HG:
                    for base in (0, FFN):
                        for k in range(8):
                            ld("pool", wfi[:, k, base + h0 * 128:base + h1 * 128], wfi_d[l, k * 128:(k + 1) * 128, base + h0 * 128:base + h1 * 128],
                               [("wfi", k, base, h0 // 4)])
                load_w_bf(lambda k: wfo[:, k, :], lambda k: wfo_d[l, k * 128:(k + 1) * 128, :], 22, "wfo")

                def loadF(b):
                    t0, N = blk_info(b)
                    for k in range(8):
                        ld("sp", xt[:, 0, k, 0:N], xT_v[:, k, t0:t0 + N], [("xtF", 0, k)], reads=[("xT", b)])

                loadF(0)
                for b in range(nblk_q):
                    t0, N = blk_info(b)
                    j = 0 if b < 8 else 1
                    s = 0
                    modulate_block(l, 1, b, lambda k, s=s, N=N: xt[:, s, k, 0:N], lambda k, s=s: ("xtF", s, k), ht, "htF", tmp, "tmpF")
                    for hc in range(22):
                        s2 = hc % 2
                        pg = pbank()
                        for k in range(8):
                            P.op("pe", lambda e, k=k, pg=pg, hc=hc: e.matmul(psum[pg][:, 0:N], wfi[:, k, hc * 128:(hc + 1) * 128], ht[:, k, 0:N], start=(k == 0), stop=(k == 7)),
                                 reads=[("wfi", k, 0, hgrp(hc)), ("htF", k)], writes=[("ps", pg)])
                        P.op("act", lambda e, pg=pg, s2=s2: e.activation(out=sgt[:, s2, 0:N], in_=psum[pg][:, 0:N], func=AF.Silu),
                             reads=[("ps", pg)], writes=[("sgF", s2)])
                        pu = pbank()
                        for k in range(8):
                            P.op("pe", lambda e, k=k, pu=pu, hc=hc: e.matmul(psum[pu][:, 0:N], wfi[:, k, FFN + hc * 128:FFN + (hc + 1) * 128], ht[:, k, 0:N], start=(k == 0), stop=(k == 7)),
                                 reads=[("wfi", k, FFN, hgrp(hc)), ("htF", k)], writes=[("ps", pu)])
                        P.op("dve", lambda e, pu=pu, s2=s2, hc=hc: e.tensor_tensor(out=hid[:, hc, 0:N], in0=psum[pu][:, 0:N], in1=sgt[:, s2, 0:N], op=ALU.mult),
                             reads=[("ps", pu), ("sgF", s2)], writes=[("hidF", hc)])
                    pst = pbank()
                    for fc in range(8):
                        po = pbank()
                        if po == pst:
                            po = pbank()
                        for hc in range(22):
                            P.op("pe", lambda e, hc=hc, po=po, fc=fc: e.matmul(psum[po][:, 0:N], wfo[:, hc, fc * 128:(fc + 1) * 128], hid[:, hc, 0:N], start=(hc == 0), stop=(hc == 21)),
                                 reads=[("wfo", hc), ("hidF", hc)], writes=[("ps", po)])
                        P.op("dve", lambda e, po=po, fc=fc: e.scalar_tensor_tensor(out=xt[:, s, fc, 0:N], in0=psum[po][:, 0:N], scalar=gate_ap(l, 1, fc, j), in1=xt[:, s, fc, 0:N],
                                                                              op0=ALU.mult, op1=ALU.add),
                             reads=[("ps", po), "modv", ("xtF", s, fc)], writes=[("xtF", s, fc)])
                        ld("sp", xT_v[:, fc, t0:t0 + N], xt[:, s, fc, 0:N], [("xT", b)], reads=[("xtF", s, fc)])
                        P.op("act", lambda e, fc=fc: e.activation(out=sqt[:, fc, 0:N], in_=xt[:, s, fc, 0:N], func=AF.Square),
                             reads=[("xtF", s, fc)], writes=[("htF", fc)])
                        if fc > 0:
                            P.op("pe", lambda e, fc=fc, pst=pst: e.matmul(psum[pst][:, 0:N], ONESB, sqt[:, fc - 1, 0:N], start=(fc == 1), stop=False),
                                 reads=[("htF", fc - 1), "cst_b"], writes=[("ps", pst)])
                    P.op("pe", lambda e, pst=pst: e.matmul(psum[pst][:, 0:N], ONESB, sqt[:, 7, 0:N], start=False, stop=True),
                         reads=[("htF", 7), "cst_b"], writes=[("ps", pst)])
                    P.op("dve", lambda e, pst=pst: e.tensor_copy(out=stats[:, t0:t0 + N], in_=psum[pst][:, 0:N]),
                         reads=[("ps", pst)], writes=["rstd"])
                    if b + 1 < nblk_q:
                        loadF(b + 1)
            stats_to_rstd(0, ST if nblk_q == NBLK else S, "n1next")

        if not dbg:
            with ExitStack() as ph:
                P.barrier()
                xt = sbuf(ph, "xtZ", [128, 2, 8, 512])
                tmp = sbuf(ph, "tmpZ", [128, 2, 512])
                ot = sbuf(ph, "otZ", [128, 4, D])
                oi = 0
                ld("sp", xt[:, 0, :, :], xT_v[:, :, 0:512], [("xtZ", 0, k) for k in range(8)], reads=[("xT", 0)])
                for b in range(8):
                    t0 = b * 512
                    if b + 1 < 8:
                        ld("sp", xt[:, (b + 1) % 2, :, :], xT_v[:, :, t0 + 512:t0 + 1024], [("xtZ", (b + 1) % 2, k) for k in range(8)], reads=[("xT", b + 1)])
                    s = b % 2
                    for k in range(8):
                        s2 = k % 2
                        P.op("pool", lambda e, k=k, s2=s2: e.tensor_tensor(out=tmp[:, s2, :], in0=xt[:, s, k, :], in1=rstd[:, t0:t0 + 512], op=ALU.mult),
                             reads=[("xtZ", s, k), "rstd"], writes=[("tmpZ", s2)])
                        P.op("dve", lambda e, k=k, s2=s2: e.tensor_scalar(out=xt[:, s, k, :], in0=tmp[:, s2, :], scalar1=gn[:, 2 * L, k:k + 1], scalar2=None, op0=ALU.mult),
                             reads=[("tmpZ", s2), "gn"], writes=[("xtZ", s, k)])
                    for tt in range(4):
                        so = oi % 4
                        oi += 1
                        for kk in range(2):
                            pb = pbank()
                            for k4 in range(4):
                                k = kk * 4 + k4
                                P.op("pe", lambda e, pb=pb, k4=k4, k=k, tt=tt: e.transpose(psum[pb][:, k4 * 128:(k4 + 1) * 128], xt[:, s, k, tt * 128:(tt + 1) * 128], ident_f[:]),
                                     reads=[("xtZ", s, k), "ident_f"], writes=[("ps", pb)])
                            P.op("dve", lambda e, pb=pb, kk=kk, so=so: e.tensor_copy(out=ot[:, so, kk * 512:(kk + 1) * 512], in_=psum[pb][:, :]),
                                 reads=[("ps", pb)], writes=[("otZ", so)])
                        ld("sp", out_d[t0 + tt * 128:t0 + (tt + 1) * 128, :], ot[:, so, :], [("out", b, tt)], reads=[("otZ", so)])
        P.wait_all("sp", P.final_tokens())
        P.emit(top)
    return nc


def ATTN(nc, P, l, nblk_q, psum_all, psum, pbank, sbuf, ld, cst_b, qkg, naq_d, nak_d, vaug_d, gqr_d, gkr_d, nao_d, gqo_d,
         tint_d, tfull_d, rope_d, dbg):
    IDB = cst_b[:, 0, :]
    BD64 = cst_b[:, 2, :]
    PERM = cst_b[:, 3, :]
    naq_v = naq_d.rearrange("(c p) t -> p c t", p=128)
    nak_v = nak_d.rearrange("(c p) t -> p c t", p=128)
    gqr_v = gqr_d.rearrange("(c p) t -> p c t", p=128)
    gkr_v = gkr_d.rearrange("(c p) t -> p c t", p=128)
    vaug_v = vaug_d.rearrange("(n p) h f -> p n h f", p=128)
    nao_v = nao_d.rearrange("(h d) t -> d h t", d=64)
    gqo_v = gqo_d.rearrange("(h d) t -> d h t", d=64)
    ACC = [0, 1]
    SPAIRS = [2, 4, 6]
    LA = 1
    cnt = {"acc": 0, "s": 0, "pt": 0, "rc": 0}

    def nxt(key, n):
        v = cnt[key] % n
        cnt[key] += 1
        return v

    def attend2(N, qk_A, qk_B, nchunks, pt, ptres, rc, rcres, outs, scale):
        assert nchunks % 2 == 0
        pos = [ACC[0], ACC[1]]
        pend = []

        def emit_pv(info):
            j, items = info
            for hh in range(2):
                pi, vls = items[hh]
                for t in range(2):
                    i = 2 * j + t
                    vl, vres, _ = vls[t]
                    P.op("pe", lambda e, vl=vl, pi=pi, i=i, t=t, hh=hh: e.matmul(psum[pos[hh]][:, 0:N], vl, pt[:, pi, t, 0:N], start=(i == 0), stop=(i == nchunks - 1)),
                         reads=list(vres) + [(ptres, pi)], writes=[("ps", pos[hh])])

        for j in range(nchunks // 2):
            sbs = [SPAIRS[nxt("s", 3)], SPAIRS[nxt("s", 3)]]
            vls = [[None, None], [None, None]]
            for t in range(2):
                order = (0, 1) if t == 0 else (1, 0)
                for hh in order:
                    vls[hh][t] = (qk_A if hh == 0 else qk_B)(2 * j + t, sbs[hh] + t)
                for hh in order:
                    if vls[hh][t][2] is not None:
                        vls[hh][t][2]()
            items = []
            for hh in range(2):
                sb = sbs[hh]
                pi = nxt("pt", 4)
                P.op("act", lambda e, sb=sb, pi=pi: e.activation(out=pt[:, pi, :, 0:N], in_=psum_all[:, sb:sb + 2, 0:N], func=AF.Exp, scale=scale),
                     reads=[("ps", sb), ("ps", sb + 1)], writes=[(ptres, pi)])
                items.append((pi, vls[hh]))
            pend.append((j, items))
            if len(pend) > LA:
                emit_pv(pend.pop(0))
        while pend:
            emit_pv(pend.pop(0))
        for hh in range(2):
            po = pos[hh]
            out_ap, out_res = outs[hh]
            r2 = nxt("rc", 2)
            P.op("dve", lambda e, po=po, r2=r2: e.tensor_copy(out=rc[:, r2, 0, 0:N], in_=psum[po][:, 0:N]), reads=[("ps", po)], writes=[(rcres, r2, 0)])
            P.op("dve", lambda e, r2=r2: e.reciprocal(out=rc[0:64, r2, 1, 0:N], in_=rc[64:128, r2, 0, 0:N]), reads=[(rcres, r2, 0)], writes=[(rcres, r2, 1)])
            P.op("pool", lambda e, r2=r2, out_ap=out_ap: e.tensor_tensor(out=out_ap, in0=rc[0:64, r2, 0, 0:N], in1=rc[0:64, r2, 1, 0:N], op=ALU.mult),
                 reads=[(rcres, r2, 0), (rcres, r2, 1)], writes=[out_res])

    with ExitStack() as ph:
        P.barrier()
        tint = sbuf(ph, "tint", [128, 8, 22 * 64], BF16)
        tfull = sbuf(ph, "tfull", [128, 8, 14 * 64], BF16)
        negt = sbuf(ph, "negt", [128, 256], BF16)
        kc = sbuf(ph, "kcC", [128, 4, T], BF16)
        vc = sbuf(ph, "vcC", [128, 2, 8, 128], BF16)
        qt = sbuf(ph, "qtC", [128, 2, 4, 512], BF16)
        kt = sbuf(ph, "ktC", [128, 2, 4, 1024], BF16)
        vt = sbuf(ph, "vtC", [128, 2, 8, 8, 128], BF16)
        pt = sbuf(ph, "ptC", [128, 4, 2, 512], BF16)
        rc = sbuf(ph, "rcC", [128, 2, 2, 512])
        ost = sbuf(ph, "ostC", [64, 2, 8, 512], BF16)
        for h in range(8):
            ld("pool", tint[:, h, :], tint_d[l, :, h, :], [("tint", h)])
            ld("pool", tfull[:, h, :], tfull_d[l, :, h, :], [("tfull", h)])
        P.op("pool", lambda e: e.memset(negt[:], NEG), writes=["negt"])
        ld("sp", kc[:], nak_v[:, :, S:ST], ["kcC"], reads=[("nak", 8)])
        ld("sp", vc[:], vaug_v[:, 32:34, 0:8, :], ["vcC"], reads=[("vaug", 32), ("vaug", 33)])

        def win(qb):
            a = 8 * qb
            lo = min(max(a - 4, 0), 56)
            hi = min(max(a + 3, 0), 56) + 7
            return a, lo // 2, hi // 2

        def loadC(qb):
            s = qb % 2
            t0, N = blk_info(qb)
            ld("sp", qt[:, s, :, 0:N], naq_v[:, :, t0:t0 + N], [("qtC", s)], reads=[("naq", qb)])
            if qb < 8:
                a, j0, j1 = win(qb)
                nj = j1 - j0 + 1
                blks = sorted(set([(j * 128) // 512 for j in range(j0, j1 + 1)]))
                ld("sp", kt[:, s, :, 0:nj * 128], nak_v[:, :, j0 * 128:(j1 + 1) * 128], [("ktC", s)], reads=[("nak", bb) for bb in blks])
                ld("sp", vt[:, s, 0:nj, :, :], vaug_v[:, j0:j1 + 1, 0:8, :], [("vtC", s)], reads=[("vaug", j) for j in range(j0, j1 + 1)])

        loadC(0)
        for qb in range(nblk_q):
            t0, N = blk_info(qb)
            if qb + 1 < nblk_q:
                loadC(qb + 1)
            s = qb % 2
            if qb < 8:
                a, j0, j1 = win(qb)
                nj = j1 - j0 + 1
            else:
                nj = 0
            for ci in range(4):
                def mk(h, ci=ci, nj=nj):
                    p0 = (h % 2) * 64

                    def qk_ops(i, ps):
                        if i < nj:
                            j = j0 + i
                            P.op("pe", lambda e: e.matmul(psum[ps][:, 0:N], kt[p0:p0 + 64, s, ci, i * 128:(i + 1) * 128], qt[p0:p0 + 64, s, ci, 0:N],
                                                          start=True, stop=False), reads=[("ktC", s), ("qtC", s)], writes=[("ps", ps)])
                            E0 = 2 * j - a + 7
                            segs = []
                            if a == 0:
                                if j <= 3:
                                    mf = 13 - E0
                                    segs.append((0, 256, tfull[:, h, mf * 64:(mf + 4) * 64], ("tfull", h)))
                                else:
                                    segs.append((0, 256, negt[:, 0:256], "negt"))
                                m0 = 17 - E0 + 4
                                segs.append((256, 512, tint[:, h, m0 * 64:(m0 + 4) * 64], ("tint", h)))
                            elif a == 56:
                                m0 = 17 - E0
                                segs.append((0, 320, tint[:, h, m0 * 64:(m0 + 5) * 64], ("tint", h)))
                                if j >= 28:
                                    mf = 13 - E0 + 5
                                    segs.append((320, 512, tfull[:, h, mf * 64:(mf + 3) * 64], ("tfull", h)))
                                else:
                                    segs.append((320, 512, negt[:, 0:192], "negt"))
                            else:
                                m0 = 17 - E0
                                segs.append((0, 512, tint[:, h, m0 * 64:(m0 + 8) * 64], ("tint", h)))

                            def post():
                                for (c0, c1, tab, tres) in segs:
                                    P.op("pe", lambda e, c0=c0, c1=c1, tab=tab: e.matmul(psum[ps][:, c0:c1], IDB, tab, start=False, stop=True),
                                         reads=["cst_b", tres], writes=[("ps", ps)])
                            return vt[:, s, i, h, :], [("vtC", s)], post
                        cj = i - nj
                        P.op("pe", lambda e: e.matmul(psum[ps][:, 0:N], kc[p0:p0 + 64, ci, cj * 128:(cj + 1) * 128], qt[p0:p0 + 64, s, ci, 0:N],
                                                      start=True, stop=True), reads=["kcC", ("qtC", s)], writes=[("ps", ps)])
                        return vc[:, cj, h, :], ["vcC"], None
                    return qk_ops

                hA, hB = 2 * ci, 2 * ci + 1
                attend2(N, mk(hA), mk(hB), nj + 2, pt, "ptC", rc, "rcC",
                        [(ost[:, s, hA, 0:N], ("ostC", s, hA)), (ost[:, s, hB, 0:N], ("ostC", s, hB))], 1.0)
            ld("sp", nao_v[:, :, t0:t0 + N], ost[:, s, :, 0:N], [("nao", qb)], reads=[("ostC", s, h) for h in range(8)])

    if "C" in dbg and l == 0:
        return
    with ExitStack() as ph:
        P.barrier()
        cs = sbuf(ph, "csD", [128, 2, S])
        gq = sbuf(ph, "gqD", [128, 4, ST], BF16)
        gk = sbuf(ph, "gkD", [128, ST], BF16)
        va = sbuf(ph, "vaD", [128, 34, 2, 128], BF16)
        raw = sbuf(ph, "rawD", [128, 2, 5, 512], BF16)
        sq = sbuf(ph, "sqD", [128, 4, 512], BF16)
        rs = sbuf(ph, "rsD", [128, 4, 512])
        qn = sbuf(ph, "qnD", [128, 4, 512], BF16)
        r1 = sbuf(ph, "r1D", [128, 4, 512])
        r2t = sbuf(ph, "r2D", [128, 4, 512])
        pt = sbuf(ph, "ptD", [128, 4, 2, 512], BF16)
        rc = sbuf(ph, "rcD", [128, 2, 2, 512])
        ost = sbuf(ph, "ostD", [64, 4, 512], BF16)
        ld("sp", cs[:], rope_d.rearrange("a p t -> p a t"), ["csD"])
        for half in range(2):
            ld("sp", va[:, half * 17:(half + 1) * 17, :, :], vaug_v[:, half * 17:(half + 1) * 17, 8:10, :], [("vaD", half)],
               reads=[("vaug", j) for j in range(half * 17, (half + 1) * 17)])

        def loadD(b):
            t0, N = blk_info(b)
            s = b % 2
            if b < nblk_q:
                ld("sp", raw[:, s, 0:4, 0:N], gqr_v[:, :, t0:t0 + N], [("rawD", s, c) for c in range(4)], reads=[("gqr", b)])
            ld("sp", raw[:, s, 4:5, 0:N], gkr_v[:, :, t0:t0 + N], [("rawD", s, 4)], reads=[("gkr", b)])

        items = [(bb, c) for bb in range(NBLK) for c in ([0, 1, 2, 3, 4] if bb < nblk_q else [4])]
        st = {}

        def stage1(ix):
            bb, c = items[ix]
            t0, N = blk_info(bb)
            s = bb % 2
            s2 = ix % 4
            P.op("act", lambda e: e.activation(out=sq[:, s2, 0:N], in_=raw[:, s, c, 0:N], func=AF.Square),
                 reads=[("rawD", s, c)], writes=[("sqD", s2)])
            p1 = pbank()
            P.op("pe", lambda e: e.matmul(psum[p1][:, 0:N], BD64, sq[:, s2, 0:N], start=True, stop=True),
                 reads=["cst_b", ("sqD", s2)], writes=[("ps", p1)])
            st[ix] = {"p1": p1}

        def stage2(ix):
            bb, c = items[ix]
            t0, N = blk_info(bb)
            s = bb % 2
            s2 = ix % 4
            p1 = st[ix]["p1"]
            P.op("act", lambda e: e.activation(out=rs[:, s2, 0:N], in_=psum[p1][:, 0:N], func=AF.Ln, scale=1.0 / 64, bias=EPS),
                 reads=[("ps", p1)], writes=[("rsD", s2)])
            P.op("act", lambda e: e.activation(out=rs[:, s2, 0:N], in_=rs[:, s2, 0:N], func=AF.Exp, scale=-0.5),
                 reads=[("rsD", s2)], writes=[("rsD", s2)])
            gi = 0 if c < 4 else 1
            dst = gq[:, c, t0:t0 + N] if c < 4 else gk[:, t0:t0 + N]
            dres = ("gqD", c, bb) if c < 4 else ("gkD", bb)
            if bb < 8:
                P.op("dve", lambda e: e.scalar_tensor_tensor(out=qn[:, s2, 0:N], in0=raw[:, s, c, 0:N], scalar=qkg[:, l, gi:gi + 1], in1=rs[:, s2, 0:N],
                                                             op0=ALU.mult, op1=ALU.mult),
                     reads=[("rawD", s, c), "qkg", ("rsD", s2)], writes=[("qnD", s2)])
                p2 = pbank()
                P.op("pe", lambda e: e.matmul(psum[p2][:, 0:N], PERM, qn[:, s2, 0:N], start=True, stop=True),
                     reads=["cst_b", ("qnD", s2)], writes=[("ps", p2)])
                P.op("dve", lambda e: e.tensor_tensor(out=r1[:, s2, 0:N], in0=qn[:, s2, 0:N], in1=cs[:, 0, t0:t0 + N], op=ALU.mult),
                     reads=[("qnD", s2), "csD"], writes=[("r1D", s2)])
                st[ix]["p2"] = p2
            else:
                P.op("dve", lambda e: e.scalar_tensor_tensor(out=dst, in0=raw[:, s, c, 0:N], scalar=qkg[:, l, gi:gi + 1], in1=rs[:, s2, 0:N],
                                                             op0=ALU.mult, op1=ALU.mult),
                     reads=[("rawD", s, c), "qkg", ("rsD", s2)], writes=[dres])

        def stage3(ix):
            bb, c = items[ix]
            if bb >= 8:
                return
            t0, N = blk_info(bb)
            s2 = ix % 4
            p2 = st[ix]["p2"]
            dst = gq[:, c, t0:t0 + N] if c < 4 else gk[:, t0:t0 + N]
            dres = ("gqD", c, bb) if c < 4 else ("gkD", bb)
            P.op("dve", lambda e: e.tensor_tensor(out=r2t[:, s2, 0:N], in0=psum[p2][:, 0:N], in1=cs[:, 1, t0:t0 + N], op=ALU.mult),
                 reads=[("ps", p2), "csD"], writes=[("r2D", s2)])
            P.op("pool", lambda e: e.tensor_tensor(out=dst, in0=r1[:, s2, 0:N], in1=r2t[:, s2, 0:N], op=ALU.add),
                 reads=[("r1D", s2), ("r2D", s2)], writes=[dres])

        loadD(0)
        if NBLK > 1:
            loadD(1)
        nit = len(items)
        for it in range(nit + 2):
            if it < nit:
                stage1(it)
            if 0 <= it - 1 < nit:
                stage2(it - 1)
            if 0 <= it - 2 < nit:
                stage3(it - 2)
            if it < nit and it >= 1 and items[it][0] != items[it - 1][0]:
                nb = items[it][0] + 1
                if nb < NBLK:
                    loadD(nb)
        oi = 0
        for qb in range(nblk_q):
            t0, N = blk_info(qb)
            chunks = list(range(34)) if qb < 8 else [32, 33]
            for jt in range(4):
                def mk(g, jt=jt, chunks=chunks):
                    p0 = g * 64

                    def qk_ops(i, ps):
                        c = chunks[i]
                        kb = c // 4 if c < 32 else 8
                        P.op("pe", lambda e: e.matmul(psum[ps][:, 0:N], gk[p0:p0 + 64, c * 128:(c + 1) * 128], gq[p0:p0 + 64, jt, t0:t0 + N], start=True, stop=True),
                             reads=[("gkD", kb), ("gqD", jt, qb)], writes=[("ps", ps)])
                        return va[:, c, g, :], [("vaD", c // 17)], None
                    return qk_ops

                soA = oi % 4
                soB = (oi + 1) % 4
                oi += 2
                attend2(N, mk(0), mk(1), len(chunks), pt, "ptD", rc, "rcD",
                        [(ost[:, soA, 0:N], ("ostD", soA)), (ost[:, soB, 0:N], ("ostD", soB))], 0.125)
                ld("sp", gqo_v[:, jt, t0:t0 + N], ost[:, soA, 0:N], [("gqo", qb, jt)], reads=[("ostD", soA)])
                ld("sp", gqo_v[:, jt + 4, t0:t0 + N], ost[:, soB, 0:N], [("gqo", qb, jt + 4)], reads=[("ostD", soB)])


def _host_prep(x, c, ctx, c_ctx, w_mod, b_mod, norm1_g, norm2_g, w_in, conv_w, conv_b, conv_ln_g, conv_ln_b, w_conv_out,
               na_rpb, w_na_out, q_norm_g, k_norm_g, w_gqa_out, w_out, w_ffn_in, w_ffn_out, final_g):
    f = np.float32
    x = np.asarray(x, f); ctx = np.asarray(ctx, f); c = np.asarray(c, f); c_ctx = np.asarray(c_ctx, f)
    B = x.shape[0]

    def pk(v):
        v = np.asarray(v, f)
        return np.ascontiguousarray(np.swapaxes(v.reshape(v.shape[:-1] + (-1, 128)), -1, -2))

    perm = np.concatenate([np.arange(0, 16), np.arange(32, 48), np.arange(16, 32), np.arange(48, 64)])
    w_in = np.asarray(w_in, f)
    cols = np.arange(INW)
    for j in range(4):
        for half in range(2):
            hh = j + 4 * half
            cols[2560 + j * 128 + half * 64:2560 + j * 128 + half * 64 + 64] = 2560 + hh * 64 + perm
    for g in range(2):
        cols[3072 + g * 64:3072 + g * 64 + 64] = 3072 + g * 64 + perm
    w_in_p = np.ascontiguousarray(w_in[:, :, cols])
    qg = np.asarray(q_norm_g, f)[:, perm]
    kg = np.asarray(k_norm_g, f)[:, perm]
    qkgT = np.stack([np.tile(qg, (1, 2)), np.tile(kg, (1, 2))], axis=-1)
    gnT = np.stack([pk(norm1_g[0]), pk(norm2_g[0]), pk(norm1_g[1]), pk(norm2_g[1]), pk(final_g)], axis=1)
    bmodT = pk(b_mod)
    convwT = np.ascontiguousarray(np.transpose(np.asarray(conv_w, f).reshape(L, CONV_K, 4, 128), (0, 3, 2, 1)))
    convvT = np.stack([pk(conv_b), pk(conv_ln_g), pk(conv_ln_b)], axis=2)
    rpb = np.asarray(na_rpb, f)
    kcg = np.arange(64)[:, None]
    cg = np.arange(64)[None, :]
    win0 = np.clip(cg - 8, 0, 48)
    colvalid = (kcg >= win0) & (kcg < win0 + 16)
    cidx = np.clip(kcg - cg + 15, 0, 30)

    def table(nm, etop, lo, hi):
        tab = np.full((L, 128, 8, nm, 64), NEG, f)
        for m in range(nm):
            e = etop - m
            for krr in range(2):
                dr = e + krr
                if lo <= dr <= hi:
                    vals = rpb[:, :, dr, :][:, :, cidx]
                    vals = np.where(colvalid[None, None], vals, f(NEG))
                    tab[:, krr * 64:(krr + 1) * 64, :, m, :] = np.transpose(vals, (0, 2, 1, 3))
        return np.ascontiguousarray(tab.reshape(L, 128, 8, nm * 64))

    tint = table(22, 17, 3, 10)
    tfull = table(14, 13, 0, 14)
    t = np.arange(S)
    prow = (t // GW).astype(np.float64)
    pcol = (t % GW).astype(np.float64)
    freqs = np.power(10000.0, -np.arange(0, 32, 2, dtype=np.float64) / 32).astype(np.float32).astype(np.float64)
    rope = np.zeros((2, 128, S), f)
    for p in range(128):
        dd = p % 64
        jf = dd % 16
        pos = prow if (dd // 16) % 2 == 0 else pcol
        ang = (pos.astype(np.float32) * freqs[jf].astype(np.float32)).astype(np.float32)
        rope[0, p] = np.cos(ang)
        rope[1, p] = np.sin(ang) * (-1.0 if dd < 32 else 1.0)
    consts = np.zeros((4, 128, 128), f)
    consts[0] = np.eye(128)
    consts[1] = 1.0
    consts[2, 0:64, 0:64] = 1.0
    consts[2, 64:128, 64:128] = 1.0
    for m in range(128):
        consts[3, m + 32 if (m % 64) < 32 else m - 32, m] = 1.0
    shared = dict(w_mod=np.asarray(w_mod, f), bmodT=bmodT, gnT=gnT, w_in=w_in_p, convwT=convwT, convvT=convvT,
                  w_conv_out=np.asarray(w_conv_out, f), tint=tint, tfull=tfull, w_na_out=np.asarray(w_na_out, f), qkgT=qkgT,
                  w_gqa_out=np.asarray(w_gqa_out, f), w_out=np.asarray(w_out, f), w_ffn_in=np.asarray(w_ffn_in, f),
                  w_ffn_out=np.asarray(w_ffn_out, f), rope=rope, consts=consts)
    cc = pk(c_ctx)
    in_maps = []
    for b in range(B):
        m = dict(shared)
        m["x"] = np.ascontiguousarray(x[b])
        m["ctx"] = np.ascontiguousarray(ctx[b])
        m["cT"] = np.ascontiguousarray(np.stack([pk(c[b]), cc], axis=-1))
        in_maps.append(m)
    return in_maps


_NC_CACHE = {}


def kernel(**inputs):
    in_maps = _host_prep(**inputs)
    if "nc" not in _NC_CACHE:
        _NC_CACHE["nc"] = build()
    res = run_bass_kernel_spmd(_NC_CACHE["nc"], in_maps, core_ids=list(range(8)))
    return np.stack([np.asarray(r["out"], np.float32) for r in res.results], axis=0)
```

```python
from contextlib import ExitStack
import os
import numpy as np
import concourse.bass as bass
import concourse.mybir as mybir
from concourse.bass_utils import run_bass_kernel_spmd

F32 = mybir.dt.float32
BF16 = mybir.dt.bfloat16
AF = mybir.ActivationFunctionType
ALU = mybir.AluOpType

D = 1024
S = 4096
T = 256
ST = S + T
L = 2
GW = 64
CONV_K = 31
HALO = 15
FFN = 2816
INW = 6400
EPS = 1e-6
NEG = -1e30
NBLK = 9
ENGS = ("pe", "act", "dve", "pool", "sp")
NDMA = 8


def blk_info(b):
    if b < 8:
        return b * 512, 512
    return S, T


class _Rec:
    def __init__(self):
        self.call = None

    def __getattr__(self, name):
        def f(*a, **k):
            self.call = (name, a, k)
            return self
        return f


def _capture(fn):
    r = _Rec()
    fn(r)
    assert r.call is not None
    return r.call


class Prog:
    def __init__(self, nc):
        self.nc = nc
        self.streams = {e: [] for e in ENGS}
        self.count = {e: 0 for e in ENGS}
        self.known = {e: {} for e in ENGS}
        self.res_w = {}
        self.res_r = {}
        self.dma_use = {}
        self.dma_rr = {e: 0 for e in ENGS}
        self.sems = {}

    def _deps(self, eng, reads, writes):
        need = {}

        def add(tok):
            if tok is None:
                return
            k, v = tok
            if k == "pe" and eng == "pe":
                return
            if need.get(k, 0) < v:
                need[k] = v

        for r in reads:
            add(self.res_w.get(r))
        for w in writes:
            add(self.res_w.get(w))
            for k, v in self.res_r.get(w, {}).items():
                add((k, v))
        return need

    def _commit(self, tok, reads, writes):
        k, v = tok
        for r in reads:
            d = self.res_r.setdefault(r, {})
            if d.get(k, 0) < v:
                d[k] = v
        for w in writes:
            self.res_w[w] = tok
            self.res_r[w] = {}

    def _emit_waits(self, eng, need):
        kn = self.known[eng]
        waits = []
        for k, v in need.items():
            if kn.get(k, 0) < v:
                kn[k] = v
                waits.append((k, v))
        return waits

    def op(self, eng, fn, reads=(), writes=()):
        need = self._deps(eng, reads, writes)
        waits = self._emit_waits(eng, need)
        self.count[eng] += 1
        tok = (eng, self.count[eng])
        self.streams[eng].append((_capture(fn), waits, (eng, 1)))
        self._commit(tok, reads, writes)
        return tok

    def dma(self, q, fn, reads=(), writes=()):
        need = self._deps(q, reads, writes)
        i = self.dma_rr[q]
        self.dma_rr[q] = (i + 1) % NDMA
        key = ("dma", q, i)
        u = self.dma_use.get(key, 0)
        if u > 0 and need.get(key, 0) < 16 * u:
            need[key] = 16 * u
        waits = self._emit_waits(q, need)
        self.dma_use[key] = u + 1
        tok = (key, 16 * (u + 1))
        self.streams[q].append((_capture(fn), waits, (key, 16)))
        self._commit(tok, reads, writes)
        return tok

    def wait_all(self, eng, toks):
        need = {}
        for k, v in toks:
            if need.get(k, 0) < v:
                need[k] = v
        waits = self._emit_waits(eng, need)
        self.streams[eng].append((None, waits, None))

    def barrier(self):
        toks = self.final_tokens()
        for e in ENGS:
            self.wait_all(e, toks)

    def final_tokens(self):
        toks = [(e, self.count[e]) for e in ENGS if self.count[e]]
        toks += [(key, 16 * u) for key, u in self.dma_use.items()]
        return toks

    def emit(self, stack):
        nc = self.nc
        targets = {e: set() for e in ENGS}
        keys = set()
        for e in ENGS:
            for fn, waits, inc in self.streams[e]:
                for k, v in waits:
                    keys.add(k)
                    if isinstance(k, str):
                        targets[k].add(v)
                if inc is not None and not isinstance(inc[0], str):
                    keys.add(inc[0])
        rank = {}
        for e in ENGS:
            srt = sorted(targets[e])
            rank[e] = {v: i + 1 for i, v in enumerate(srt)}
            if srt:
                keys.add(e)
        for k in sorted(keys, key=str):
            name = k if isinstance(k, str) else "d_%s_%d" % (k[1], k[2])
            self.sems[k] = stack.enter_context(nc.semaphore("s_" + name))
        block = stack.enter_context(nc.Block())

        def runner(ename):
            def run(engine):
                idx = 0
                for fn, waits, inc in self.streams[ename]:
                    for k, v in waits:
                        engine.wait_ge(self.sems[k], rank[k][v] if isinstance(k, str) else v)
                    if fn is not None:
                        name, a, kw = fn
                        ins = getattr(engine, name)(*a, **kw)
                        if isinstance(inc[0], str):
                            idx += 1
                            if idx in rank[ename]:
                                ins.then_inc(self.sems[ename], 1)
                        else:
                            ins.then_inc(self.sems[inc[0]], inc[1])
            return run

        block.tensor(runner("pe"))
        block.scalar(runner("act"))
        block.vector(runner("dve"))
        block.gpsimd(runner("pool"))
        block.sync(runner("sp"))


def build(n_layers=L, dbg=()):
    nc = bass.Bass("TRN2", target_bir_lowering=False)
    P = Prog(nc)

    def din(name, shape, dt=F32):
        return nc.dram_tensor(name, shape, dt, kind="ExternalInput").ap()

    def dscr(name, shape, dt=BF16):
        kind = "ExternalOutput" if name in dbg else "Internal"
        return nc.dram_tensor(name, shape, dt, kind=kind).ap()

    x_d = din("x", [S, D])
    ctx_d = din("ctx", [T, D])
    cT_d = din("cT", [128, 8, 2])
    wmod_d = din("w_mod", [L, D, 6 * D])
    bmod_d = din("bmodT", [L, 128, 48])
    gn_d = din("gnT", [128, 2 * L + 1, 8])
    win_d = din("w_in", [L, D, INW])
    convw_d = din("convwT", [L, 128, 4, CONV_K])
    convv_d = din("convvT", [L, 128, 3, 4])
    wco_d = din("w_conv_out", [L, 512, D])
    tint_d = din("tint", [L, 128, 8, 22 * 64])
    tfull_d = din("tfull", [L, 128, 8, 14 * 64])
    wno_d = din("w_na_out", [L, 512, D])
    qkg_d = din("qkgT", [L, 128, 2])
    wgo_d = din("w_gqa_out", [L, 512, D])
    wout_d = din("w_out", [L, D, D])
    wfi_d = din("w_ffn_in", [L, D, 2 * FFN])
    wfo_d = din("w_ffn_out", [L, FFN, D])
    rope_d = din("rope", [2, 128, S])
    cst_d = din("consts", [4, 128, 128])
    out_d = nc.dram_tensor("out", [S, D], F32, kind="ExternalOutput").ap()

    xT_d = dscr("xT", [D, ST], F32)
    UW = S + 2 * HALO
    UCW = T + 2 * HALO
    uT_d = dscr("uT", [512, UW + UCW])
    naq_d = dscr("naqT", [512, ST])
    nak_d = dscr("nakT", [512, ST])
    gqr_d = dscr("gqrT", [512, ST])
    gkr_d = dscr("gkrT", [128, ST])
    gat_d = dscr("gatesT", [3072, ST])
    vaug_d = dscr("vaug", [ST, 10, 128])
    ca_d = dscr("caT", [512, ST])
    nao_d = dscr("naoT", [512, ST])
    gqo_d = dscr("gqoT", [512, ST])

    with ExitStack() as top:
        uid = [0]

        def sbuf(st, name, shape, dt=F32):
            uid[0] += 1
            return st.enter_context(nc.sbuf_tensor("sb%d_%s" % (uid[0], name), shape, dt))

        psum_all = top.enter_context(nc.psum_tensor("pbanks", [128, 8, 512], F32))
        psum = [psum_all[:, i, :] for i in range(8)]
        prr = [0]

        def pbank():
            i = prr[0]
            prr[0] = (i + 1) % 8
            return i

        ident_f = sbuf(top, "ident_f", [128, 128])
        cst_b = sbuf(top, "cst_b", [128, 4, 128], BF16)
        modv = sbuf(top, "modv", [128, L, 48, 2])
        bmod = sbuf(top, "bmod", [128, L, 48])
        gn = sbuf(top, "gn", [128, 2 * L + 1, 8])
        Amod = sbuf(top, "Amod", [128, L, 2, 8, 2])
        rstd = sbuf(top, "rstd", [128, ST])
        stats = rstd
        sil = sbuf(top, "sil", [128, 8, 2])
        cT = sbuf(top, "cT_s", [128, 8, 2])
        zpad = sbuf(top, "zpad", [128, 4, HALO], BF16)
        convw = sbuf(top, "convw", [128, L, 4, CONV_K])
        convv = sbuf(top, "convv", [128, L, 3, 4])
        qkg = sbuf(top, "qkg", [128, L, 2])

        IDB = cst_b[:, 0, :]
        ONESB = cst_b[:, 1, :]
        BD64 = cst_b[:, 2, :]
        PERM = cst_b[:, 3, :]

        def ld(q, out, in_, writes, reads=()):
            P.dma(q, lambda e, o=out, i=in_: e.dma_start(out=o, in_=i), reads=reads, writes=writes)

        ld("sp", ident_f[:], cst_d[0], ["ident_f"])
        ld("pool", cst_b[:], cst_d.rearrange("c p n -> p c n"), ["cst_b"])
        ld("sp", bmod[:], bmod_d.rearrange("l p c -> p l c"), ["bmod"])
        ld("sp", gn[:], gn_d, ["gn"])
        ld("sp", cT[:], cT_d, ["cT"])
        ld("sp", convw[:], convw_d.rearrange("l p c k -> p l c k"), ["convw"])
        ld("sp", convv[:], convv_d.rearrange("l p a c -> p l a c"), ["convv"])
        ld("sp", qkg[:], qkg_d.rearrange("l p a -> p l a"), ["qkg"])
        P.op("pool", lambda e: e.memset(zpad[:], 0.0), writes=["zpad"])
        uT_v = uT_d.rearrange("(c p) t -> p c t", p=128)
        for off in (0, HALO + S, UW, UW + HALO + T):
            ld("sp", uT_v[:, :, off:off + HALO], zpad[:], [("uTpad", off)], reads=["zpad"])

        P.op("act", lambda e: e.activation(out=sil[:], in_=cT[:], func=AF.Exp, scale=-1.0), reads=["cT"], writes=["sil"])
        P.op("dve", lambda e: e.tensor_scalar_add(out=sil[:], in0=sil[:], scalar1=1.0), reads=["sil"], writes=["sil"])
        P.op("dve", lambda e: e.reciprocal(out=sil[:], in_=sil[:]), reads=["sil"], writes=["sil"])
        P.op("dve", lambda e: e.tensor_mul(out=sil[:], in0=sil[:], in1=cT[:]), reads=["sil", "cT"], writes=["sil"])
        with ExitStack() as ph:
            P.barrier()
            wm = sbuf(ph, "wm", [128, 2, 8, 1024])
            it = 0
            for l in range(n_layers):
                for cg in range(6):
                    sl = (it % 2) if not os.environ.get('WM1') else 0
                    it += 1
                    for k in range(8):
                        ld("sp", wm[:, sl, k, :], wmod_d[l, k * 128:(k + 1) * 128, cg * 1024:(cg + 1) * 1024],
                           [("wm", sl, k)])
                    for jc in range(8):
                        oc = cg * 8 + jc
                        pb = pbank()
                        for k in range(8):
                            P.op("pe", lambda e, pb=pb, sl=sl, k=k, jc=jc: e.matmul(
                                psum[pb][:, 0:2], wm[:, sl, k, jc * 128:(jc + 1) * 128], sil[:, k, :],
                                start=(k == 0), stop=(k == 7)),
                                reads=[("wm", sl, k), "sil"], writes=[("ps", pb)])
                        P.op("dve", lambda e, pb=pb, l=l, oc=oc: e.tensor_scalar(
                            out=modv[:, l, oc, :], in0=psum[pb][:, 0:2], scalar1=bmod[:, l, oc:oc + 1], scalar2=None,
                            op0=ALU.add), reads=[("ps", pb), "bmod"], writes=["modv"])
            for l in range(n_layers):
                for n in range(2):
                    for j in range(2):
                        P.op("dve", lambda e, l=l, n=n, j=j: e.scalar_tensor_tensor(
                            out=Amod[:, l, n, :, j], in0=modv[:, l, (8 + 24 * n):(16 + 24 * n), j], scalar=1.0,
                            in1=gn[:, 2 * l + n, :], op0=ALU.add, op1=ALU.mult),
                            reads=["modv", "gn"], writes=["Amod"])

        def shift_ap(l, n, k, j):
            return modv[:, l, 24 * n + k, j:j + 1]

        def gate_ap(l, n, k, j):
            return modv[:, l, 16 + 24 * n + k, j:j + 1]

        def stats_to_rstd(lo, hi, tag):
            P.op("act", lambda e: e.activation(out=rstd[:, lo:hi], in_=stats[:, lo:hi], func=AF.Ln, scale=1.0 / D, bias=EPS),
                 reads=["rstd"], writes=["rstd"])
            P.op("act", lambda e: e.activation(out=rstd[:, lo:hi], in_=rstd[:, lo:hi], func=AF.Exp, scale=-0.5),
                 reads=["rstd"], writes=["rstd"])

        def sumsq_block(src_tile_fn, src_res, t0, N, sqt, sqres):
            pb = pbank()
            for k in range(8):
                P.op("act", lambda e, k=k: e.activation(out=sqt[:, k, 0:N], in_=src_tile_fn(k), func=AF.Square),
                     reads=[src_res(k)], writes=[(sqres, k)])
                P.op("pe", lambda e, k=k, pb=pb: e.matmul(psum[pb][:, 0:N], ONESB, sqt[:, k, 0:N], start=(k == 0), stop=(k == 7)),
                     reads=[(sqres, k), "cst_b"], writes=[("ps", pb)])
            P.op("dve", lambda e, pb=pb: e.tensor_copy(out=stats[:, t0:t0 + N], in_=psum[pb][:, 0:N]),
                 reads=[("ps", pb)], writes=["rstd"])

        xT_v = xT_d.rearrange("(k p) t -> p k t", p=128)
        with ExitStack() as ph:
            P.barrier()
            xin = sbuf(ph, "xin", [128, 2, D])
            xst = sbuf(ph, "xst", [128, 2, 8, 512])
            sqt = sbuf(ph, "sqt0", [128, 8, 512], BF16)
            ti = 0
            for b in range(NBLK):
                t0, N = blk_info(b)
                sl = b % 2
                for tt in range(N // 128):
                    s2 = ti % 2
                    ti += 1
                    src = x_d[t0 + tt * 128:t0 + (tt + 1) * 128, :] if b < 8 else ctx_d[tt * 128:(tt + 1) * 128, :]
                    ld("sp", xin[:, s2, :], src, [("xin", s2)])
                    for kk in range(2):
                        pb = pbank()
                        for k4 in range(4):
                            k = kk * 4 + k4
                            P.op("pe", lambda e, pb=pb, k4=k4, k=k, s2=s2: e.transpose(
                                psum[pb][:, k4 * 128:(k4 + 1) * 128], xin[:, s2, k * 128:(k + 1) * 128], ident_f[:]),
                                reads=[("xin", s2), "ident_f"], writes=[("ps", pb)])
                        P.op("dve", lambda e, pb=pb, kk=kk, sl=sl, tt=tt: e.tensor_copy(
                            out=xst[:, sl, kk * 4:(kk + 1) * 4, tt * 128:(tt + 1) * 128],
                            in_=psum[pb][:, :].rearrange("p (k t) -> p k t", k=4)),
                            reads=[("ps", pb)], writes=[("xst", sl, kk)])
                ld("sp", xT_v[:, :, t0:t0 + N], xst[:, sl, :, 0:N], [("xT", b)], reads=[("xst", sl, 0), ("xst", sl, 1)])
                sumsq_block(lambda k, sl=sl, N=N: xst[:, sl, k, 0:N], lambda k, sl=sl: ("xst", sl, k // 4), t0, N, sqt, "sqt0")
        stats_to_rstd(0, ST, "n1l0")

        if "dump" in dbg:
            d1 = nc.dram_tensor("d_rstd", [128, ST], F32, kind="ExternalOutput").ap()
            d2 = nc.dram_tensor("d_modv", [128, L, 48, 2], F32, kind="ExternalOutput").ap()
            d3 = nc.dram_tensor("d_amod", [128, L, 2, 8, 2], F32, kind="ExternalOutput").ap()
            d4 = nc.dram_tensor("d_sil", [128, 8, 2], F32, kind="ExternalOutput").ap()
            ld("sp", d1, rstd[:], ["d1"], reads=["rstd"])
            ld("sp", d2, modv[:], ["d2"], reads=["modv"])
            ld("sp", d3, Amod[:], ["d3"], reads=["Amod"])
            ld("sp", d4, sil[:], ["d4"], reads=["sil"])
            P.wait_all("sp", P.final_tokens())
            P.emit(top)
            return nc
        def load_w_bf(dst_fn, src_fn, nk, res):
            for k in range(nk):
                ld("pool", dst_fn(k), src_fn(k), [(res, k)])

        def modulate_block(l, n, b, xt, xres, ht, hres, tmp, tmpres):
            t0, N = blk_info(b)
            j = 0 if b < 8 else 1
            for k in range(8):
                ts_ = k % 2
                P.op("dve", lambda e, k=k, ts_=ts_: e.tensor_tensor(out=tmp[:, ts_, 0:N], in0=xt(k), in1=rstd[:, t0:t0 + N], op=ALU.mult),
                     reads=[xres(k), "rstd"], writes=[(tmpres, ts_)])
                P.op("dve", lambda e, k=k, ts_=ts_: e.tensor_scalar(
                    out=ht[:, k, 0:N], in0=tmp[:, ts_, 0:N], scalar1=Amod[:, l, n, k, j:j + 1], scalar2=shift_ap(l, n, k, j),
                    op0=ALU.mult, op1=ALU.add), reads=[(tmpres, ts_), "Amod", "modv"], writes=[(hres, k)])

        for l in range(n_layers):
            last = (l == n_layers - 1) and (n_layers == L)
            nblk_q = 8 if last else NBLK
            with ExitStack() as ph:
                P.barrier()
                win = sbuf(ph, "win", [128, 8, INW], BF16)
                xt = sbuf(ph, "xtA", [128, 2, 8, 512])
                ht2 = sbuf(ph, "htA", [128, 2, 8, 512], BF16)
                tmp = sbuf(ph, "tmpA", [128, 2, 512])
                sg = sbuf(ph, "sgA", [128, 2, 512])
                stg = sbuf(ph, "stgA", [128, 2, 8, 512], BF16)
                vst = sbuf(ph, "vstA", [128, 2, 10, 128], BF16)
                WIN_GROUPS = [(512, 1024), (0, 512), (1024, 1536), (2560, 3072), (3328, 4352), (4352, 5376), (5376, 6400),
                              (1536, 2048), (3072, 3328), (2048, 2560)]

                def wgrp(col):
                    for gi, (c0, c1) in enumerate(WIN_GROUPS):
                        if c0 <= col < c1:
                            return gi
                    raise AssertionError(col)

                for gi, (c0, c1) in enumerate(WIN_GROUPS):
                    for k in range(8):
                        ld("pool", win[:, k, c0:c1], win_d[l, k * 128:(k + 1) * 128, c0:c1], [("win", k, gi)])
                for s2 in range(2):
                    P.op("dve", lambda e, s2=s2: e.memset(vst[:, s2, :, 64:128], 1.0), writes=[("vst", s2)])

                def loadA(b):
                    t0, N = blk_info(b)
                    ld("sp", xt[:, b % 2, :, 0:N], xT_v[:, :, t0:t0 + N], [("xtA", b % 2, k) for k in range(8)], reads=[("xT", b)])

                uT_lat = uT_v[:, :, HALO:HALO + S]
                uT_ctx = uT_v[:, :, UW + HALO:UW + HALO + T]
                naq_v = naq_d.rearrange("(c p) t -> p c t", p=128)
                nak_v = nak_d.rearrange("(c p) t -> p c t", p=128)
                gqr_v = gqr_d.rearrange("(c p) t -> p c t", p=128)
                gkr_v = gkr_d.rearrange("(c p) t -> p c t", p=128)
                gat_v = gat_d.rearrange("(c p) t -> p c t", p=128)
                vaug_v = vaug_d.rearrange("(n p) h f -> p n h f", p=128)
                stgi = [0]
                vsi = [0]
                loadA(0)
                for b in range(NBLK):
                    t0, N = blk_info(b)
                    if b + 1 < NBLK:
                        loadA(b + 1)
                    sl = b % 2
                    ht = ht2[:, sl]
                    hres = "htA%d" % sl
                    modulate_block(l, 0, b, lambda k, sl=sl, N=N: xt[:, sl, k, 0:N], lambda k, sl=sl: ("xtA", sl, k),
                                   ht, hres, tmp, "tmpA")

                    def proj(col, pb, N=N):
                        for k in range(8):
                            P.op("pe", lambda e, k=k: e.matmul(psum[pb][:, 0:N], win[:, k, col:col + 128], ht[:, k, 0:N],
                                                               start=(k == 0), stop=(k == 7)),
                                 reads=[("win", k, wgrp(col)), (hres, k)], writes=[("ps", pb)])

                    def group(col0, nch, dst, kind, dres):
                        ss = stgi[0] % 2
                        stgi[0] += 1
                        for c in range(nch):
                            pb = pbank()
                            proj(col0 + c * 128, pb)
                            if kind == "sig":
                                P.op("act", lambda e, pb=pb, c=c: e.activation(out=stg[:, ss, c, 0:N], in_=psum[pb][:, 0:N], func=AF.Sigmoid),
                                     reads=[("ps", pb)], writes=[("stgA", ss, c)])
                            elif kind == "copy":
                                P.op("act", lambda e, pb=pb, c=c: e.activation(out=stg[:, ss, c, 0:N], in_=psum[pb][:, 0:N], func=AF.Identity),
                                     reads=[("ps", pb)], writes=[("stgA", ss, c)])
                            elif kind == "scale":
                                P.op("act", lambda e, pb=pb, c=c: e.activation(out=stg[:, ss, c, 0:N], in_=psum[pb][:, 0:N], func=AF.Identity, scale=0.125),
                                     reads=[("ps", pb)], writes=[("stgA", ss, c)])
                        ld("sp", dst, stg[:, ss, 0:nch, 0:N], [dres], reads=[("stgA", ss, c) for c in range(nch)])

                    ss = stgi[0] % 2
                    stgi[0] += 1
                    for c in range(4):
                        pg = pbank()
                        proj(512 + c * 128, pg)
                        s2 = c % 2
                        P.op("act", lambda e, pg=pg, s2=s2: e.activation(out=sg[:, s2, 0:N], in_=psum[pg][:, 0:N], func=AF.Sigmoid),
                             reads=[("ps", pg)], writes=[("sgA", s2)])
                        pv = pbank()
                        proj(c * 128, pv)
                        P.op("dve", lambda e, pv=pv, s2=s2, c=c, ss=ss: e.tensor_tensor(out=stg[:, ss, c, 0:N], in0=psum[pv][:, 0:N], in1=sg[:, s2, 0:N], op=ALU.mult),
                             reads=[("ps", pv), ("sgA", s2)], writes=[("stgA", ss, c)])
                    udst = uT_lat[:, :, t0:t0 + N] if b < 8 else uT_ctx
                    ld("sp", udst, stg[:, ss, 0:4, 0:N], [("uT", b)], reads=[("stgA", ss, c) for c in range(4)])
                    if b < nblk_q:
                        group(1024, 4, naq_v[:, :, t0:t0 + N], "scale", ("naq", b))
                        group(2560, 4, gqr_v[:, :, t0:t0 + N], "copy", ("gqr", b))
                        for g3 in range(3):
                            group(3328 + g3 * 1024, 8, gat_v[:, g3 * 8:(g3 + 1) * 8, t0:t0 + N], "sig", ("gat", b, g3))
                    group(1536, 4, nak_v[:, :, t0:t0 + N], "copy", ("nak", b))
                    group(3072, 1, gkr_v[:, :, t0:t0 + N], "copy", ("gkr", b))
                    for tt in range(N // 128):
                        s2 = vsi[0] % 2
                        vsi[0] += 1
                        p1 = pbank()
                        p2 = pbank()
                        for k in range(8):
                            P.op("pe", lambda e, k=k, tt=tt, p1=p1: e.matmul(psum[p1][:, 0:512], ht[:, k, tt * 128:(tt + 1) * 128], win[:, k, 2048:2560],
                                                                        start=(k == 0), stop=(k == 7)),
                                 reads=[("win", k, wgrp(2048)), (hres, k)], writes=[("ps", p1)])
                        for k in range(8):
                            P.op("pe", lambda e, k=k, tt=tt, p2=p2: e.matmul(psum[p2][:, 0:128], ht[:, k, tt * 128:(tt + 1) * 128], win[:, k, 3200:3328],
                                                                        start=(k == 0), stop=(k == 7)),
                                 reads=[("win", k, wgrp(3200)), (hres, k)], writes=[("ps", p2)])
                        P.op("dve", lambda e, p1=p1, s2=s2: e.tensor_copy(out=vst[:, s2, 0:8, 0:64], in_=psum[p1][:, 0:512].rearrange("p (h d) -> p h d", h=8)),
                             reads=[("ps", p1)], writes=[("vst", s2)])
                        P.op("dve", lambda e, p2=p2, s2=s2: e.tensor_copy(out=vst[:, s2, 8:10, 0:64], in_=psum[p2][:, 0:128].rearrange("p (h d) -> p h d", h=2)),
                             reads=[("ps", p2)], writes=[("vst", s2)])
                        ti = t0 // 128 + tt
                        ld("sp", vaug_v[:, ti, :, :], vst[:, s2, :, :], [("vaug", ti)], reads=[("vst", s2)])

            if "A" in dbg and l == 0:
                break
            ca_v = ca_d.rearrange("(c p) t -> p c t", p=128)
            with ExitStack() as ph:
                P.barrier()
                dg = sbuf(ph, "dg", [128, 4, CONV_K, 128], BF16)
                ut = sbuf(ph, "utB", [128, 2, 4, 512 + 2 * HALO], BF16)
                yb = sbuf(ph, "ybB", [128, 4, 512])
                ybf = sbuf(ph, "ybfB", [128, 4, 512], BF16)
                ysq = sbuf(ph, "ysqB", [128, 4, 512], BF16)
                mu = sbuf(ph, "muB", [128, 512])
                var = sbuf(ph, "varB", [128, 512])
                t1 = sbuf(ph, "t1B", [128, 2, 512])
                t2 = sbuf(ph, "t2B", [128, 2, 512])
                cast = sbuf(ph, "castB", [128, 2, 4, 512], BF16)
                for c in range(4):
                    P.op("pool" if c % 2 else "dve", lambda e, c=c: e.tensor_tensor(
                        out=dg[:, c, :, :], in0=cst_b[:, 0:1, :].broadcast_to([128, CONV_K, 128]),
                        in1=convw[:, l, c, :].rearrange("p (k o) -> p k o", o=1).broadcast_to([128, CONV_K, 128]), op=ALU.mult),
                        reads=["cst_b", "convw"], writes=[("dg", c, kk) for kk in range(CONV_K)])

                def loadB(b):
                    t0, N = blk_info(b)
                    if b < 8:
                        src = uT_v[:, :, t0:t0 + N + 2 * HALO]
                        rd = [("uT", bb) for bb in (b - 1, b, b + 1) if 0 <= bb < 8] + [("uTpad", 0), ("uTpad", HALO + S)]
                    else:
                        src = uT_v[:, :, UW:UW + UCW]
                        rd = [("uT", 8), ("uTpad", UW), ("uTpad", UW + HALO + T)]
                    ld("sp", ut[:, b % 2, :, 0:N + 2 * HALO], src, [("utB", b % 2)], reads=rd)

                loadB(0)
                for b in range(nblk_q):
                    t0, N = blk_info(b)
                    if b + 1 < nblk_q:
                        loadB(b + 1)
                    sl = b % 2
                    ps1 = pbank()
                    ps2 = pbank()
                    for c in range(4):
                        pb = pbank()
                        for kk in range(CONV_K):
                            P.op("pe", lambda e, c=c, kk=kk, pb=pb: e.matmul(psum[pb][:, 0:N], dg[:, c, kk, :], ut[:, sl, c, kk:kk + N],
                                                                        start=(kk == 0), stop=(kk == CONV_K - 1)),
                                 reads=[("dg", c, kk), ("utB", sl)], writes=[("ps", pb)])
                        P.op("dve", lambda e, c=c, pb=pb: e.tensor_scalar(out=yb[:, c, 0:N], in0=psum[pb][:, 0:N], scalar1=convv[:, l, 0, c:c + 1], scalar2=None, op0=ALU.add),
                             reads=[("ps", pb), "convv"], writes=[("ybB", c)])
                        P.op("pool", lambda e, c=c: e.tensor_copy(out=ybf[:, c, 0:N], in_=yb[:, c, 0:N]), reads=[("ybB", c)], writes=[("ybfB", c)])
                        P.op("act", lambda e, c=c: e.activation(out=ysq[:, c, 0:N], in_=yb[:, c, 0:N], func=AF.Square), reads=[("ybB", c)], writes=[("ysqB", c)])
                    for c in range(4):
                        P.op("pe", lambda e, c=c: e.matmul(psum[ps1][:, 0:N], ONESB, ybf[:, c, 0:N], start=(c == 0), stop=(c == 3)),
                             reads=[("ybfB", c), "cst_b"], writes=[("ps", ps1)])
                    for c in range(4):
                        P.op("pe", lambda e, c=c: e.matmul(psum[ps2][:, 0:N], ONESB, ysq[:, c, 0:N], start=(c == 0), stop=(c == 3)),
                             reads=[("ysqB", c), "cst_b"], writes=[("ps", ps2)])
                    P.op("dve", lambda e: e.tensor_scalar(out=mu[:, 0:N], in0=psum[ps1][:, 0:N], scalar1=1.0 / 512, scalar2=None, op0=ALU.mult),
                         reads=[("ps", ps1)], writes=["muB"])
                    P.op("dve", lambda e: e.tensor_tensor(out=var[:, 0:N], in0=mu[:, 0:N], in1=mu[:, 0:N], op=ALU.mult), reads=["muB"], writes=["varB"])
                    P.op("dve", lambda e: e.scalar_tensor_tensor(out=var[:, 0:N], in0=psum[ps2][:, 0:N], scalar=1.0 / 512, in1=var[:, 0:N], op0=ALU.mult, op1=ALU.subtract),
                         reads=[("ps", ps2), "varB"], writes=["varB"])
                    P.op("act", lambda e: e.activation(out=var[:, 0:N], in_=var[:, 0:N], func=AF.Ln, bias=EPS), reads=["varB"], writes=["varB"])
                    P.op("act", lambda e: e.activation(out=var[:, 0:N], in_=var[:, 0:N], func=AF.Exp, scale=-0.5), reads=["varB"], writes=["varB"])
                    for c in range(4):
                        s2 = c % 2
                        P.op("pool", lambda e, c=c, s2=s2: e.tensor_tensor(out=t1[:, s2, 0:N], in0=yb[:, c, 0:N], in1=mu[:, 0:N], op=ALU.subtract),
                             reads=[("ybB", c), "muB"], writes=[("t1B", s2)])
                        P.op("pool", lambda e, s2=s2: e.tensor_tensor(out=t1[:, s2, 0:N], in0=t1[:, s2, 0:N], in1=var[:, 0:N], op=ALU.mult),
                             reads=[("t1B", s2), "varB"], writes=[("t1B", s2)])
                        P.op("dve", lambda e, c=c, s2=s2: e.tensor_scalar(out=t1[:, s2, 0:N], in0=t1[:, s2, 0:N], scalar1=convv[:, l, 1, c:c + 1], scalar2=convv[:, l, 2, c:c + 1],
                                                                      op0=ALU.mult, op1=ALU.add), reads=[("t1B", s2), "convv"], writes=[("t1B", s2)])
                        P.op("act", lambda e, s2=s2, c=c: e.activation(out=cast[:, sl, c, 0:N], in_=t1[:, s2, 0:N], func=AF.Silu), reads=[("t1B", s2)], writes=[("castB", sl, c)])
                    ld("sp", ca_v[:, :, t0:t0 + N], cast[:, sl, :, 0:N], [("ca", b)], reads=[("castB", sl, c) for c in range(4)])

            if "B" in dbg and l == 0:
                break
            ATTN(nc, P, l, nblk_q, psum_all, psum, pbank, sbuf, ld, cst_b, qkg, naq_d, nak_d, vaug_d, gqr_d, gkr_d, nao_d, gqo_d,
                 tint_d, tfull_d, rope_d, dbg)
            if "D" in dbg and l == 0:
                break
            nao_v = nao_d.rearrange("(c p) t -> p c t", p=128)
            gqo_v = gqo_d.rearrange("(c p) t -> p c t", p=128)
            with ExitStack() as ph:
                P.barrier()
                wco = sbuf(ph, "wco", [128, 4, D], BF16)
                wno = sbuf(ph, "wno", [128, 4, D], BF16)
                wgo = sbuf(ph, "wgo", [128, 4, D], BF16)
                wo = sbuf(ph, "wo", [128, 8, D], BF16)
                xt = sbuf(ph, "xtE", [128, 2, 8, 512])
                cat = sbuf(ph, "catE", [128, 2, 4, 512], BF16)
                nat = sbuf(ph, "natE", [128, 2, 4, 512], BF16)
                gqt = sbuf(ph, "gqtE", [128, 2, 4, 512], BF16)
                gt = sbuf(ph, "gtE", [128, 2, 24, 512], BF16)
                yt = sbuf(ph, "ytE", [128, 8, 512], BF16)
                m1 = sbuf(ph, "m1E", [128, 2, 512])
                m2 = sbuf(ph, "m2E", [128, 2, 512])
                sqt = sbuf(ph, "sqtE", [128, 8, 512], BF16)
                load_w_bf(lambda k: wco[:, k, :], lambda k: wco_d[l, k * 128:(k + 1) * 128, :], 4, "wco")
                load_w_bf(lambda k: wno[:, k, :], lambda k: wno_d[l, k * 128:(k + 1) * 128, :], 4, "wno")
                load_w_bf(lambda k: wgo[:, k, :], lambda k: wgo_d[l, k * 128:(k + 1) * 128, :], 4, "wgo")
                load_w_bf(lambda k: wo[:, k, :], lambda k: wout_d[l, k * 128:(k + 1) * 128, :], 8, "wo")

                def loadE(b):
                    t0, N = blk_info(b)
                    s = b % 2
                    ld("sp", xt[:, s, :, 0:N], xT_v[:, :, t0:t0 + N], [("xtE", s, k) for k in range(8)], reads=[("xT", b)])
                    ld("sp", cat[:, s, :, 0:N], ca_v[:, :, t0:t0 + N], [("catE", s)], reads=[("ca", b)])
                    ld("sp", nat[:, s, :, 0:N], nao_v[:, :, t0:t0 + N], [("natE", s)], reads=[("nao", b)])
                    ld("sp", gqt[:, s, :, 0:N], gqo_v[:, :, t0:t0 + N], [("gqtE", s)], reads=[("gqo", b, h) for h in range(8)])
                    for g3 in range(3):
                        ld("sp", gt[:, s, g3 * 8:(g3 + 1) * 8, 0:N], gat_v[:, g3 * 8:(g3 + 1) * 8, t0:t0 + N], [("gtE", s, g3)], reads=[("gat", b, g3)])

                loadE(0)
                for b in range(nblk_q):
                    t0, N = blk_info(b)
                    j = 0 if b < 8 else 1
                    if b + 1 < nblk_q:
                        loadE(b + 1)
                    s = b % 2
                    for fc in range(8):
                        fs = slice(fc * 128, (fc + 1) * 128)
                        s2 = fc % 2
                        pa = pbank()
                        for k in range(4):
                            P.op("pe", lambda e, k=k, pa=pa, fs=fs: e.matmul(psum[pa][:, 0:N], wco[:, k, fs], cat[:, s, k, 0:N], start=(k == 0), stop=(k == 3)),
                                 reads=[("wco", k), ("catE", s)], writes=[("ps", pa)])
                        P.op("dve", lambda e, pa=pa, fc=fc, s2=s2: e.tensor_tensor(out=m1[:, s2, 0:N], in0=psum[pa][:, 0:N], in1=gt[:, s, fc, 0:N], op=ALU.mult),
                             reads=[("ps", pa), ("gtE", s, 0)], writes=[("m1E", s2)])
                        pn = pbank()
                        for h in range(4):
                            P.op("pe", lambda e, h=h, pn=pn, fs=fs: e.matmul(psum[pn][:, 0:N], wno[:, h, fs], nat[:, s, h, 0:N], start=(h == 0), stop=(h == 3)),
                                 reads=[("wno", h), ("natE", s)], writes=[("ps", pn)])
                        P.op("dve", lambda e, pn=pn, fc=fc, s2=s2: e.tensor_tensor(out=m2[:, s2, 0:N], in0=psum[pn][:, 0:N], in1=gt[:, s, 8 + fc, 0:N], op=ALU.mult),
                             reads=[("ps", pn), ("gtE", s, 1)], writes=[("m2E", s2)])
                        P.op("pool", lambda e, s2=s2: e.tensor_tensor(out=m1[:, s2, 0:N], in0=m1[:, s2, 0:N], in1=m2[:, s2, 0:N], op=ALU.add),
                             reads=[("m1E", s2), ("m2E", s2)], writes=[("m1E", s2)])
                        pg = pbank()
                        for h in range(4):
                            P.op("pe", lambda e, h=h, pg=pg, fs=fs: e.matmul(psum[pg][:, 0:N], wgo[:, h, fs], gqt[:, s, h, 0:N], start=(h == 0), stop=(h == 3)),
                                 reads=[("wgo", h), ("gqtE", s)], writes=[("ps", pg)])
                        P.op("dve", lambda e, pg=pg, fc=fc, s2=s2: e.tensor_tensor(out=m2[:, s2, 0:N], in0=psum[pg][:, 0:N], in1=gt[:, s, 16 + fc, 0:N], op=ALU.mult),
                             reads=[("ps", pg), ("gtE", s, 2)], writes=[("m2E", s2)])
                        P.op("pool", lambda e, s2=s2, fc=fc: e.tensor_tensor(out=yt[:, fc, 0:N], in0=m1[:, s2, 0:N], in1=m2[:, s2, 0:N], op=ALU.add),
                             reads=[("m1E", s2), ("m2E", s2)], writes=[("ytE", fc)])
                    for fc in range(8):
                        fs = slice(fc * 128, (fc + 1) * 128)
                        po = pbank()
                        for k in range(8):
                            P.op("pe", lambda e, k=k, po=po, fs=fs: e.matmul(psum[po][:, 0:N], wo[:, k, fs], yt[:, k, 0:N], start=(k == 0), stop=(k == 7)),
                                 reads=[("wo", k), ("ytE", k)], writes=[("ps", po)])
                        P.op("dve", lambda e, po=po, fc=fc: e.scalar_tensor_tensor(out=xt[:, s, fc, 0:N], in0=psum[po][:, 0:N], scalar=gate_ap(l, 0, fc, j), in1=xt[:, s, fc, 0:N],
                                                                              op0=ALU.mult, op1=ALU.add),
                             reads=[("ps", po), "modv", ("xtE", s, fc)], writes=[("xtE", s, fc)])
                    ld("sp", xT_v[:, :, t0:t0 + N], xt[:, s, :, 0:N], [("xT", b)], reads=[("xtE", s, k) for k in range(8)])
                    sumsq_block(lambda k, s=s, N=N: xt[:, s, k, 0:N], lambda k, s=s: ("xtE", s, k), t0, N, sqt, "sqtE")
            stats_to_rstd(0, ST if nblk_q == NBLK else S, "n2")
            if "E" in dbg and l == 0:
                break
            with ExitStack() as ph:
                P.barrier()
                wfi = sbuf(ph, "wfi", [128, 8, 2 * FFN], BF16)
                wfo = sbuf(ph, "wfo", [128, 22, D], BF16)
                xt = sbuf(ph, "xtF", [128, 1, 8, 512])
                ht = sbuf(ph, "htF", [128, 8, 512], BF16)
                tmp = sbuf(ph, "tmpF", [128, 2, 512])
                hid = sbuf(ph, "hidF", [128, 22, 512], BF16)
                sgt = sbuf(ph, "sgF", [128, 2, 512], BF16)
                sqt = ht
                HG = [(0, 4), (4, 8), (8, 12), (12, 16), (16, 20), (20, 22)]

                def hgrp(hc):
                    return hc // 4

                for (h0, h1) in HG:
                    for base in (0, FFN):
                        for k in range(8):
                            ld("pool", wfi[:, k, base + h0 * 128:base + h1 * 128], wfi_d[l, k * 128:(k + 1) * 128, base + h0 * 128:base + h1 * 128],
                               [("wfi", k, base, h0 // 4)])
                load_w_bf(lambda k: wfo[:, k, :], lambda k: wfo_d[l, k * 128:(k + 1) * 128, :], 22, "wfo")

                def loadF(b):
                    t0, N = blk_info(b)
                    for k in range(8):
                        ld("sp", xt[:, 0, k, 0:N], xT_v[:, k, t0:t0 + N], [("xtF", 0, k)], reads=[("xT", b)])

                loadF(0)
                for b in range(nblk_q):
                    t0, N = blk_info(b)
                    j = 0 if b < 8 else 1
                    s = 0
                    modulate_block(l, 1, b, lambda k, s=s, N=N: xt[:, s, k, 0:N], lambda k, s=s: ("xtF", s, k), ht, "htF", tmp, "tmpF")
                    for hc in range(22):
                        s2 = hc % 2
                        pg = pbank()
                        for k in range(8):
                            P.op("pe", lambda e, k=k, pg=pg, hc=hc: e.matmul(psum[pg][:, 0:N], wfi[:, k, hc * 128:(hc + 1) * 128], ht[:, k, 0:N], start=(k == 0), stop=(k == 7)),
                                 reads=[("wfi", k, 0, hgrp(hc)), ("htF", k)], writes=[("ps", pg)])
                        P.op("act", lambda e, pg=pg, s2=s2: e.activation(out=sgt[:, s2, 0:N], in_=psum[pg][:, 0:N], func=AF.Silu),
                             reads=[("ps", pg)], writes=[("sgF", s2)])
                        pu = pbank()
                        for k in range(8):
                            P.op("pe", lambda e, k=k, pu=pu, hc=hc: e.matmul(psum[pu][:, 0:N], wfi[:, k, FFN + hc * 128:FFN + (hc + 1) * 128], ht[:, k, 0:N], start=(k == 0), stop=(k == 7)),
                                 reads=[("wfi", k, FFN, hgrp(hc)), ("htF", k)], writes=[("ps", pu)])
                        P.op("dve", lambda e, pu=pu, s2=s2, hc=hc: e.tensor_tensor(out=hid[:, hc, 0:N], in0=psum[pu][:, 0:N], in1=sgt[:, s2, 0:N], op=ALU.mult),
                             reads=[("ps", pu), ("sgF", s2)], writes=[("hidF", hc)])
                    pst = pbank()
                    for fc in range(8):
                        po = pbank()
                        if po == pst:
                            po = pbank()
                        for hc in range(22):
                            P.op("pe", lambda e, hc=hc, po=po, fc=fc: e.matmul(psum[po][:, 0:N], wfo[:, hc, fc * 128:(fc + 1) * 128], hid[:, hc, 0:N], start=(hc == 0), stop=(hc == 21)),
                                 reads=[("wfo", hc), ("hidF", hc)], writes=[("ps", po)])
                        P.op("dve", lambda e, po=po, fc=fc: e.scalar_tensor_tensor(out=xt[:, s, fc, 0:N], in0=psum[po][:, 0:N], scalar=gate_ap(l, 1, fc, j), in1=xt[:, s, fc, 0:N],
                                                                              op0=ALU.mult, op1=ALU.add),
                             reads=[("ps", po), "modv", ("xtF", s, fc)], writes=[("xtF", s, fc)])
                        ld("sp", xT_v[:, fc, t0:t0 + N], xt[:, s, fc, 0:N], [("xT", b)], reads=[("xtF", s, fc)])
                        P.op("act", lambda e, fc=fc: e.activation(out=sqt[:, fc, 0:N], in_=xt[:, s, fc, 0:N], func=AF.Square),
                             reads=[("xtF", s, fc)], writes=[("htF", fc)])
                        if fc > 0:
                            P.op("pe", lambda e, fc=fc, pst=pst: e.matmul(psum[pst][:, 0:N], ONESB, sqt[:, fc - 1, 0:N], start=(fc == 1), stop=False),
                                 reads=[("htF", fc - 1), "cst_b"], writes=[("ps", pst)])
                    P.op("pe", lambda e, pst=pst: e.matmul(psum[pst][:, 0:N], ONESB, sqt[:, 7, 0:N], start=False, stop=True),
                         reads=[("htF", 7), "cst_b"], writes=[("ps", pst)])
                    P.op("dve", lambda e, pst=pst: e.tensor_copy(out=stats[:, t0:t0 + N], in_=psum[pst][:, 0:N]),
                         reads=[("ps", pst)], writes=["rstd"])
                    if b + 1 < nblk_q:
                        loadF(b + 1)
            stats_to_rstd(0, ST if nblk_q == NBLK else S, "n1next")

        if not dbg:
            with ExitStack() as ph:
                P.barrier()
                xt = sbuf(ph, "xtZ", [128, 2, 8, 512])
                tmp = sbuf(ph, "tmpZ", [128, 2, 512])
                ot = sbuf(ph, "otZ", [128, 4, D])
                oi = 0
                ld("sp", xt[:, 0, :, :], xT_v[:, :, 0:512], [("xtZ", 0, k) for k in range(8)], reads=[("xT", 0)])
                for b in range(8):
                    t0 = b * 512
                    if b + 1 < 8:
                        ld("sp", xt[:, (b + 1) % 2, :, :], xT_v[:, :, t0 + 512:t0 + 1024], [("xtZ", (b + 1) % 2, k) for k in range(8)], reads=[("xT", b + 1)])
                    s = b % 2
                    for k in range(8):
                        s2 = k % 2
                        P.op("pool", lambda e, k=k, s2=s2: e.tensor_tensor(out=tmp[:, s2, :], in0=xt[:, s, k, :], in1=rstd[:, t0:t0 + 512], op=ALU.mult),
                             reads=[("xtZ", s, k), "rstd"], writes=[("tmpZ", s2)])
                        P.op("dve", lambda e, k=k, s2=s2: e.tensor_scalar(out=xt[:, s, k, :], in0=tmp[:, s2, :], scalar1=gn[:, 2 * L, k:k + 1], scalar2=None, op0=ALU.mult),
                             reads=[("tmpZ", s2), "gn"], writes=[("xtZ", s, k)])
                    for tt in range(4):
                        so = oi % 4
                        oi += 1
                        for kk in range(2):
                            pb = pbank()
                            for k4 in range(4):
                                k = kk * 4 + k4
                                P.op("pe", lambda e, pb=pb, k4=k4, k=k, tt=tt: e.transpose(psum[pb][:, k4 * 128:(k4 + 1) * 128], xt[:, s, k, tt * 128:(tt + 1) * 128], ident_f[:]),
                                     reads=[("xtZ", s, k), "ident_f"], writes=[("ps", pb)])
                            P.op("dve", lambda e, pb=pb, kk=kk, so=so: e.tensor_copy(out=ot[:, so, kk * 512:(kk + 1) * 512], in_=psum[pb][:, :]),
                                 reads=[("ps", pb)], writes=[("otZ", so)])
                        ld("sp", out_d[t0 + tt * 128:t0 + (tt + 1) * 128, :], ot[:, so, :], [("out", b, tt)], reads=[("otZ", so)])
        P.wait_all("sp", P.final_tokens())
        P.emit(top)
    return nc


def ATTN(nc, P, l, nblk_q, psum_all, psum, pbank, sbuf, ld, cst_b, qkg, naq_d, nak_d, vaug_d, gqr_d, gkr_d, nao_d, gqo_d,
         tint_d, tfull_d, rope_d, dbg):
    IDB = cst_b[:, 0, :]
    BD64 = cst_b[:, 2, :]
    PERM = cst_b[:, 3, :]
    naq_v = naq_d.rearrange("(c p) t -> p c t", p=128)
    nak_v = nak_d.rearrange("(c p) t -> p c t", p=128)
    gqr_v = gqr_d.rearrange("(c p) t -> p c t", p=128)
    gkr_v = gkr_d.rearrange("(c p) t -> p c t", p=128)
    vaug_v = vaug_d.rearrange("(n p) h f -> p n h f", p=128)
    nao_v = nao_d.rearrange("(h d) t -> d h t", d=64)
    gqo_v = gqo_d.rearrange("(h d) t -> d h t", d=64)
    ACC = [0, 1]
    SPAIRS = [2, 4, 6]
    LA = 1
    cnt = {"acc": 0, "s": 0, "pt": 0, "rc": 0}

    def nxt(key, n):
        v = cnt[key] % n
        cnt[key] += 1
        return v

    def attend2(N, qk_A, qk_B, nchunks, pt, ptres, rc, rcres, outs, scale):
        assert nchunks % 2 == 0
        pos = [ACC[0], ACC[1]]
        pend = []

        def emit_pv(info):
            j, items = info
            for hh in range(2):
                pi, vls = items[hh]
                for t in range(2):
                    i = 2 * j + t
                    vl, vres, _ = vls[t]
                    P.op("pe", lambda e, vl=vl, pi=pi, i=i, t=t, hh=hh: e.matmul(psum[pos[hh]][:, 0:N], vl, pt[:, pi, t, 0:N], start=(i == 0), stop=(i == nchunks - 1)),
                         reads=list(vres) + [(ptres, pi)], writes=[("ps", pos[hh])])

        for j in range(nchunks // 2):
            sbs = [SPAIRS[nxt("s", 3)], SPAIRS[nxt("s", 3)]]
            vls = [[None, None], [None, None]]
            for t in range(2):
                order = (0, 1) if t == 0 else (1, 0)
                for hh in order:
                    vls[hh][t] = (qk_A if hh == 0 else qk_B)(2 * j + t, sbs[hh] + t)
                for hh in order:
                    if vls[hh][t][2] is not None:
                        vls[hh][t][2]()
            items = []
            for hh in range(2):
                sb = sbs[hh]
                pi = nxt("pt", 4)
                P.op("act", lambda e, sb=sb, pi=pi: e.activation(out=pt[:, pi, :, 0:N], in_=psum_all[:, sb:sb + 2, 0:N], func=AF.Exp, scale=scale),
                     reads=[("ps", sb), ("ps", sb + 1)], writes=[(ptres, pi)])
                items.append((pi, vls[hh]))
            pend.append((j, items))
            if len(pend) > LA:
                emit_pv(pend.pop(0))
        while pend:
            emit_pv(pend.pop(0))
        for hh in range(2):
            po = pos[hh]
            out_ap, out_res = outs[hh]
            r2 = nxt("rc", 2)
            P.op("dve", lambda e, po=po, r2=r2: e.tensor_copy(out=rc[:, r2, 0, 0:N], in_=psum[po][:, 0:N]), reads=[("ps", po)], writes=[(rcres, r2, 0)])
            P.op("dve", lambda e, r2=r2: e.reciprocal(out=rc[0:64, r2, 1, 0:N], in_=rc[64:128, r2, 0, 0:N]), reads=[(rcres, r2, 0)], writes=[(rcres, r2, 1)])
            P.op("pool", lambda e, r2=r2, out_ap=out_ap: e.tensor_tensor(out=out_ap, in0=rc[0:64, r2, 0, 0:N], in1=rc[0:64, r2, 1, 0:N], op=ALU.mult),
                 reads=[(rcres, r2, 0), (rcres, r2, 1)], writes=[out_res])

    with ExitStack() as ph:
        P.barrier()
        tint = sbuf(ph, "tint", [128, 8, 22 * 64], BF16)
        tfull = sbuf(ph, "tfull", [128, 8, 14 * 64], BF16)
        negt = sbuf(ph, "negt", [128, 256], BF16)
        kc = sbuf(ph, "kcC", [128, 4, T], BF16)
        vc = sbuf(ph, "vcC", [128, 2, 8, 128], BF16)
        qt = sbuf(ph, "qtC", [128, 2, 4, 512], BF16)
        kt = sbuf(ph, "ktC", [128, 2, 4, 1024], BF16)
        vt = sbuf(ph, "vtC", [128, 2, 8, 8, 128], BF16)
        pt = sbuf(ph, "ptC", [128, 4, 2, 512], BF16)
        rc = sbuf(ph, "rcC", [128, 2, 2, 512])
        ost = sbuf(ph, "ostC", [64, 2, 8, 512], BF16)
        for h in range(8):
            ld("pool", tint[:, h, :], tint_d[l, :, h, :], [("tint", h)])
            ld("pool", tfull[:, h, :], tfull_d[l, :, h, :], [("tfull", h)])
        P.op("pool", lambda e: e.memset(negt[:], NEG), writes=["negt"])
        ld("sp", kc[:], nak_v[:, :, S:ST], ["kcC"], reads=[("nak", 8)])
        ld("sp", vc[:], vaug_v[:, 32:34, 0:8, :], ["vcC"], reads=[("vaug", 32), ("vaug", 33)])

        def win(qb):
            a = 8 * qb
            lo = min(max(a - 4, 0), 56)
            hi = min(max(a + 3, 0), 56) + 7
            return a, lo // 2, hi // 2

        def loadC(qb):
            s = qb % 2
            t0, N = blk_info(qb)
            ld("sp", qt[:, s, :, 0:N], naq_v[:, :, t0:t0 + N], [("qtC", s)], reads=[("naq", qb)])
            if qb < 8:
                a, j0, j1 = win(qb)
                nj = j1 - j0 + 1
                blks = sorted(set([(j * 128) // 512 for j in range(j0, j1 + 1)]))
                ld("sp", kt[:, s, :, 0:nj * 128], nak_v[:, :, j0 * 128:(j1 + 1) * 128], [("ktC", s)], reads=[("nak", bb) for bb in blks])
                ld("sp", vt[:, s, 0:nj, :, :], vaug_v[:, j0:j1 + 1, 0:8, :], [("vtC", s)], reads=[("vaug", j) for j in range(j0, j1 + 1)])

        loadC(0)
        for qb in range(nblk_q):
            t0, N = blk_info(qb)
            if qb + 1 < nblk_q:
                loadC(qb + 1)
            s = qb % 2
            if qb < 8:
                a, j0, j1 = win(qb)
                nj = j1 - j0 + 1
            else:
                nj = 0
            for ci in range(4):
                def mk(h, ci=ci, nj=nj):
                    p0 = (h % 2) * 64

                    def qk_ops(i, ps):
                        if i < nj:
                            j = j0 + i
                            P.op("pe", lambda e: e.matmul(psum[ps][:, 0:N], kt[p0:p0 + 64, s, ci, i * 128:(i + 1) * 128], qt[p0:p0 + 64, s, ci, 0:N],
                                                          start=True, stop=False), reads=[("ktC", s), ("qtC", s)], writes=[("ps", ps)])
                            E0 = 2 * j - a + 7
                            segs = []
                            if a == 0:
                                if j <= 3:
                                    mf = 13 - E0
                                    segs.append((0, 256, tfull[:, h, mf * 64:(mf + 4) * 64], ("tfull", h)))
                                else:
                                    segs.append((0, 256, negt[:, 0:256], "negt"))
                                m0 = 17 - E0 + 4
                                segs.append((256, 512, tint[:, h, m0 * 64:(m0 + 4) * 64], ("tint", h)))
                            elif a == 56:
                                m0 = 17 - E0
                                segs.append((0, 320, tint[:, h, m0 * 64:(m0 + 5) * 64], ("tint", h)))
                                if j >= 28:
                                    mf = 13 - E0 + 5
                                    segs.append((320, 512, tfull[:, h, mf * 64:(mf + 3) * 64], ("tfull", h)))
                                else:
                                    segs.append((320, 512, negt[:, 0:192], "negt"))
                            else:
                                m0 = 17 - E0
                                segs.append((0, 512, tint[:, h, m0 * 64:(m0 + 8) * 64], ("tint", h)))

                            def post():
                                for si, (c0, c1, tab, tres) in enumerate(segs):
                                    P.op("pe", lambda e, c0=c0, c1=c1, tab=tab, si=si: e.matmul(psum[ps][:, c0:c1], IDB, tab, start=False, stop=(si == len(segs) - 1)),
                                         reads=["cst_b", tres], writes=[("ps", ps)])
                            return vt[:, s, i, h, :], [("vtC", s)], post
                        cj = i - nj
                        P.op("pe", lambda e: e.matmul(psum[ps][:, 0:N], kc[p0:p0 + 64, ci, cj * 128:(cj + 1) * 128], qt[p0:p0 + 64, s, ci, 0:N],
                                                      start=True, stop=True), reads=["kcC", ("qtC", s)], writes=[("ps", ps)])
                        return vc[:, cj, h, :], ["vcC"], None
                    return qk_ops

                hA, hB = 2 * ci, 2 * ci + 1
                attend2(N, mk(hA), mk(hB), nj + 2, pt, "ptC", rc, "rcC",
                        [(ost[:, s, hA, 0:N], ("ostC", s, hA)), (ost[:, s, hB, 0:N], ("ostC", s, hB))], 1.0)
            ld("sp", nao_v[:, :, t0:t0 + N], ost[:, s, :, 0:N], [("nao", qb)], reads=[("ostC", s, h) for h in range(8)])

    if "C" in dbg and l == 0:
        return
    with ExitStack() as ph:
        P.barrier()
        cs = sbuf(ph, "csD", [128, 2, S])
        gq = sbuf(ph, "gqD", [128, 4, ST], BF16)
        gk = sbuf(ph, "gkD", [128, ST], BF16)
        va = sbuf(ph, "vaD", [128, 34, 2, 128], BF16)
        raw = sbuf(ph, "rawD", [128, 2, 5, 512], BF16)
        sq = sbuf(ph, "sqD", [128, 4, 512], BF16)
        rs = sbuf(ph, "rsD", [128, 4, 512])
        qn = sbuf(ph, "qnD", [128, 4, 512], BF16)
        r1 = sbuf(ph, "r1D", [128, 4, 512])
        r2t = sbuf(ph, "r2D", [128, 4, 512])
        pt = sbuf(ph, "ptD", [128, 4, 2, 512], BF16)
        rc = sbuf(ph, "rcD", [128, 2, 2, 512])
        ost = sbuf(ph, "ostD", [64, 4, 512], BF16)
        ld("sp", cs[:], rope_d.rearrange("a p t -> p a t"), ["csD"])
        for half in range(2):
            ld("sp", va[:, half * 17:(half + 1) * 17, :, :], vaug_v[:, half * 17:(half + 1) * 17, 8:10, :], [("vaD", half)],
               reads=[("vaug", j) for j in range(half * 17, (half + 1) * 17)])

        def loadD(b):
            t0, N = blk_info(b)
            s = b % 2
            if b < nblk_q:
                ld("sp", raw[:, s, 0:4, 0:N], gqr_v[:, :, t0:t0 + N], [("rawD", s, c) for c in range(4)], reads=[("gqr", b)])
            ld("sp", raw[:, s, 4:5, 0:N], gkr_v[:, :, t0:t0 + N], [("rawD", s, 4)], reads=[("gkr", b)])

        items = [(bb, c) for bb in range(NBLK) for c in ([0, 1, 2, 3, 4] if bb < nblk_q else [4])]
        st = {}

        def stage1(ix):
            bb, c = items[ix]
            t0, N = blk_info(bb)
            s = bb % 2
            s2 = ix % 4
            P.op("act", lambda e: e.activation(out=sq[:, s2, 0:N], in_=raw[:, s, c, 0:N], func=AF.Square),
                 reads=[("rawD", s, c)], writes=[("sqD", s2)])
            p1 = pbank()
            P.op("pe", lambda e: e.matmul(psum[p1][:, 0:N], BD64, sq[:, s2, 0:N], start=True, stop=True),
                 reads=["cst_b", ("sqD", s2)], writes=[("ps", p1)])
            st[ix] = {"p1": p1}

        def stage2(ix):
            bb, c = items[ix]
            t0, N = blk_info(bb)
            s = bb % 2
            s2 = ix % 4
            p1 = st[ix]["p1"]
            P.op("act", lambda e: e.activation(out=rs[:, s2, 0:N], in_=psum[p1][:, 0:N], func=AF.Ln, scale=1.0 / 64, bias=EPS),
                 reads=[("ps", p1)], writes=[("rsD", s2)])
            P.op("act", lambda e: e.activation(out=rs[:, s2, 0:N], in_=rs[:, s2, 0:N], func=AF.Exp, scale=-0.5),
                 reads=[("rsD", s2)], writes=[("rsD", s2)])
            gi = 0 if c < 4 else 1
            dst = gq[:, c, t0:t0 + N] if c < 4 else gk[:, t0:t0 + N]
            dres = ("gqD", c, bb) if c < 4 else ("gkD", bb)
            if bb < 8:
                P.op("dve", lambda e: e.scalar_tensor_tensor(out=qn[:, s2, 0:N], in0=raw[:, s, c, 0:N], scalar=qkg[:, l, gi:gi + 1], in1=rs[:, s2, 0:N],
                                                             op0=ALU.mult, op1=ALU.mult),
                     reads=[("rawD", s, c), "qkg", ("rsD", s2)], writes=[("qnD", s2)])
                p2 = pbank()
                P.op("pe", lambda e: e.matmul(psum[p2][:, 0:N], PERM, qn[:, s2, 0:N], start=True, stop=True),
                     reads=["cst_b", ("qnD", s2)], writes=[("ps", p2)])
                P.op("dve", lambda e: e.tensor_tensor(out=r1[:, s2, 0:N], in0=qn[:, s2, 0:N], in1=cs[:, 0, t0:t0 + N], op=ALU.mult),
                     reads=[("qnD", s2), "csD"], writes=[("r1D", s2)])
                st[ix]["p2"] = p2
            else:
                P.op("dve", lambda e: e.scalar_tensor_tensor(out=dst, in0=raw[:, s, c, 0:N], scalar=qkg[:, l, gi:gi + 1], in1=rs[:, s2, 0:N],
                                                             op0=ALU.mult, op1=ALU.mult),
                     reads=[("rawD", s, c), "qkg", ("rsD", s2)], writes=[dres])

        def stage3(ix):
            bb, c = items[ix]
            if bb >= 8:
                return
            t0, N = blk_info(bb)
            s2 = ix % 4
            p2 = st[ix]["p2"]
            dst = gq[:, c, t0:t0 + N] if c < 4 else gk[:, t0:t0 + N]
            dres = ("gqD", c, bb) if c < 4 else ("gkD", bb)
            P.op("dve", lambda e: e.tensor_tensor(out=r2t[:, s2, 0:N], in0=psum[p2][:, 0:N], in1=cs[:, 1, t0:t0 + N], op=ALU.mult),
                 reads=[("ps", p2), "csD"], writes=[("r2D", s2)])
            P.op("pool", lambda e: e.tensor_tensor(out=dst, in0=r1[:, s2, 0:N], in1=r2t[:, s2, 0:N], op=ALU.add),
                 reads=[("r1D", s2), ("r2D", s2)], writes=[dres])

        loadD(0)
        if NBLK > 1:
            loadD(1)
        nit = len(items)
        for it in range(nit + 2):
            if it < nit:
                stage1(it)
            if 0 <= it - 1 < nit:
                stage2(it - 1)
            if 0 <= it - 2 < nit:
                stage3(it - 2)
            if it < nit and it >= 1 and items[it][0] != items[it - 1][0]:
                nb = items[it][0] + 1
                if nb < NBLK:
                    loadD(nb)
        oi = 0
        for qb in range(nblk_q):
            t0, N = blk_info(qb)
            chunks = list(range(34)) if qb < 8 else [32, 33]
            for jt in range(4):
                def mk(g, jt=jt, chunks=chunks):
                    p0 = g * 64

                    def qk_ops(i, ps):
                        c = chunks[i]
                        kb = c // 4 if c < 32 else 8
                        P.op("pe", lambda e: e.matmul(psum[ps][:, 0:N], gk[p0:p0 + 64, c * 128:(c + 1) * 128], gq[p0:p0 + 64, jt, t0:t0 + N], start=True, stop=True),
                             reads=[("gkD", kb), ("gqD", jt, qb)], writes=[("ps", ps)])
                        return va[:, c, g, :], [("vaD", c // 17)], None
                    return qk_ops

                soA = oi % 4
                soB = (oi + 1) % 4
                oi += 2
                attend2(N, mk(0), mk(1), len(chunks), pt, "ptD", rc, "rcD",
                        [(ost[:, soA, 0:N], ("ostD", soA)), (ost[:, soB, 0:N], ("ostD", soB))], 0.125)
                ld("sp", gqo_v[:, jt, t0:t0 + N], ost[:, soA, 0:N], [("gqo", qb, jt)], reads=[("ostD", soA)])
                ld("sp", gqo_v[:, jt + 4, t0:t0 + N], ost[:, soB, 0:N], [("gqo", qb, jt + 4)], reads=[("ostD", soB)])


def _host_prep(x, c, ctx, c_ctx, w_mod, b_mod, norm1_g, norm2_g, w_in, conv_w, conv_b, conv_ln_g, conv_ln_b, w_conv_out,
               na_rpb, w_na_out, q_norm_g, k_norm_g, w_gqa_out, w_out, w_ffn_in, w_ffn_out, final_g):
    f = np.float32
    x = np.asarray(x, f); ctx = np.asarray(ctx, f); c = np.asarray(c, f); c_ctx = np.asarray(c_ctx, f)
    B = x.shape[0]

    def pk(v):
        v = np.asarray(v, f)
        return np.ascontiguousarray(np.swapaxes(v.reshape(v.shape[:-1] + (-1, 128)), -1, -2))

    perm = np.concatenate([np.arange(0, 16), np.arange(32, 48), np.arange(16, 32), np.arange(48, 64)])
    w_in = np.asarray(w_in, f)
    cols = np.arange(INW)
    for j in range(4):
        for half in range(2):
            hh = j + 4 * half
            cols[2560 + j * 128 + half * 64:2560 + j * 128 + half * 64 + 64] = 2560 + hh * 64 + perm
    for g in range(2):
        cols[3072 + g * 64:3072 + g * 64 + 64] = 3072 + g * 64 + perm
    w_in_p = np.ascontiguousarray(w_in[:, :, cols])
    qg = np.asarray(q_norm_g, f)[:, perm]
    kg = np.asarray(k_norm_g, f)[:, perm]
    qkgT = np.stack([np.tile(qg, (1, 2)), np.tile(kg, (1, 2))], axis=-1)
    gnT = np.stack([pk(norm1_g[0]), pk(norm2_g[0]), pk(norm1_g[1]), pk(norm2_g[1]), pk(final_g)], axis=1)
    bmodT = pk(b_mod)
    convwT = np.ascontiguousarray(np.transpose(np.asarray(conv_w, f).reshape(L, CONV_K, 4, 128), (0, 3, 2, 1)))
    convvT = np.stack([pk(conv_b), pk(conv_ln_g), pk(conv_ln_b)], axis=2)
    rpb = np.asarray(na_rpb, f)
    kcg = np.arange(64)[:, None]
    cg = np.arange(64)[None, :]
    win0 = np.clip(cg - 8, 0, 48)
    colvalid = (kcg >= win0) & (kcg < win0 + 16)
    cidx = np.clip(kcg - cg + 15, 0, 30)

    def table(nm, etop, lo, hi):
        tab = np.full((L, 128, 8, nm, 64), NEG, f)
        for m in range(nm):
            e = etop - m
            for krr in range(2):
                dr = e + krr
                if lo <= dr <= hi:
                    vals = rpb[:, :, dr, :][:, :, cidx]
                    vals = np.where(colvalid[None, None], vals, f(NEG))
                    tab[:, krr * 64:(krr + 1) * 64, :, m, :] = np.transpose(vals, (0, 2, 1, 3))
        return np.ascontiguousarray(tab.reshape(L, 128, 8, nm * 64))

    tint = table(22, 17, 3, 10)
    tfull = table(14, 13, 0, 14)
    t = np.arange(S)
    prow = (t // GW).astype(np.float64)
    pcol = (t % GW).astype(np.float64)
    freqs = np.power(10000.0, -np.arange(0, 32, 2, dtype=np.float64) / 32).astype(np.float32).astype(np.float64)
    rope = np.zeros((2, 128, S), f)
    for p in range(128):
        dd = p % 64
        jf = dd % 16
        pos = prow if (dd // 16) % 2 == 0 else pcol
        ang = (pos.astype(np.float32) * freqs[jf].astype(np.float32)).astype(np.float32)
        rope[0, p] = np.cos(ang)
        rope[1, p] = np.sin(ang) * (-1.0 if dd < 32 else 1.0)
    consts = np.zeros((4, 128, 128), f)
    consts[0] = np.eye(128)
    consts[1] = 1.0
    consts[2, 0:64, 0:64] = 1.0
    consts[2, 64:128, 64:128] = 1.0
    for m in range(128):
        consts[3, m + 32 if (m % 64) < 32 else m - 32, m] = 1.0
    shared = dict(w_mod=np.asarray(w_mod, f), bmodT=bmodT, gnT=gnT, w_in=w_in_p, convwT=convwT, convvT=convvT,
                  w_conv_out=np.asarray(w_conv_out, f), tint=tint, tfull=tfull, w_na_out=np.asarray(w_na_out, f), qkgT=qkgT,
                  w_gqa_out=np.asarray(w_gqa_out, f), w_out=np.asarray(w_out, f), w_ffn_in=np.asarray(w_ffn_in, f),
                  w_ffn_out=np.asarray(w_ffn_out, f), rope=rope, consts=consts)
    cc = pk(c_ctx)
    in_maps = []
    for b in range(B):
        m = dict(shared)
        m["x"] = np.ascontiguousarray(x[b])
        m["ctx"] = np.ascontiguousarray(ctx[b])
        m["cT"] = np.ascontiguousarray(np.stack([pk(c[b]), cc], axis=-1))
        in_maps.append(m)
    return in_maps


_NC_CACHE = {}


def kernel(**inputs):
    in_maps = _host_prep(**inputs)
    if "nc" not in _NC_CACHE:
        _NC_CACHE["nc"] = build()
    res = run_bass_kernel_spmd(_NC_CACHE["nc"], in_maps, core_ids=list(range(8)))
    return np.stack([np.asarray(r["out"], np.float32) for r in res.results], axis=0)
```

```python
from contextlib import ExitStack
import os
import numpy as np
import concourse.bass as bass
import concourse.mybir as mybir
from concourse.bass_utils import run_bass_kernel_spmd

F32 = mybir.dt.float32
BF16 = mybir.dt.bfloat16
AF = mybir.ActivationFunctionType
ALU = mybir.AluOpType

D = 1024
S = 4096
T = 256
ST = S + T
L = 2
GW = 64
CONV_K = 31
HALO = 15
FFN = 2816
INW = 6400
EPS = 1e-6
NEG = -1e30
NBLK = 9
ENGS = ("pe", "act", "dve", "pool", "sp")
NDMA = 8


def blk_info(b):
    if b < 8:
        return b * 512, 512
    return S, T


class _Rec:
    def __init__(self):
        self.call = None

    def __getattr__(self, name):
        def f(*a, **k):
            self.call = (name, a, k)
            return self
        return f


def _capture(fn):
    r = _Rec()
    fn(r)
    assert r.call is not None
    return r.call


class Prog:
    def __init__(self, nc):
        self.nc = nc
        self.streams = {e: [] for e in ENGS}
        self.count = {e: 0 for e in ENGS}
        self.known = {e: {} for e in ENGS}
        self.res_w = {}
        self.res_r = {}
        self.dma_use = {}
        self.dma_rr = {e: 0 for e in ENGS}
        self.sems = {}

    def _deps(self, eng, reads, writes):
        need = {}

        def add(tok):
            if tok is None:
                return
            k, v = tok
            if k == "pe" and eng == "pe":
                return
            if need.get(k, 0) < v:
                need[k] = v

        for r in reads:
            add(self.res_w.get(r))
        for w in writes:
            add(self.res_w.get(w))
            for k, v in self.res_r.get(w, {}).items():
                add((k, v))
        return need

    def _commit(self, tok, reads, writes):
        k, v = tok
        for r in reads:
            d = self.res_r.setdefault(r, {})
            if d.get(k, 0) < v:
                d[k] = v
        for w in writes:
            self.res_w[w] = tok
            self.res_r[w] = {}

    def _emit_waits(self, eng, need):
        kn = self.known[eng]
        waits = []
        for k, v in need.items():
            if kn.get(k, 0) < v:
                kn[k] = v
                waits.append((k, v))
        return waits

    def op(self, eng, fn, reads=(), writes=()):
        need = self._deps(eng, reads, writes)
        waits = self._emit_waits(eng, need)
        self.count[eng] += 1
        tok = (eng, self.count[eng])
        self.streams[eng].append((_capture(fn), waits, (eng, 1)))
        self._commit(tok, reads, writes)
        return tok

    def dma(self, q, fn, reads=(), writes=()):
        need = self._deps(q, reads, writes)
        i = self.dma_rr[q]
        self.dma_rr[q] = (i + 1) % NDMA
        key = ("dma", q, i)
        u = self.dma_use.get(key, 0)
        if u > 0 and need.get(key, 0) < 16 * u:
            need[key] = 16 * u
        waits = self._emit_waits(q, need)
        self.dma_use[key] = u + 1
        tok = (key, 16 * (u + 1))
        self.streams[q].append((_capture(fn), waits, (key, 16)))
        self._commit(tok, reads, writes)
        return tok

    def wait_all(self, eng, toks):
        need = {}
        for k, v in toks:
            if need.get(k, 0) < v:
                need[k] = v
        waits = self._emit_waits(eng, need)
        self.streams[eng].append((None, waits, None))

    def barrier(self):
        toks = self.final_tokens()
        for e in ENGS:
            self.wait_all(e, toks)

    def final_tokens(self):
        toks = [(e, self.count[e]) for e in ENGS if self.count[e]]
        toks += [(key, 16 * u) for key, u in self.dma_use.items()]
        return toks

    def emit(self, stack):
        nc = self.nc
        targets = {e: set() for e in ENGS}
        keys = set()
        for e in ENGS:
            for fn, waits, inc in self.streams[e]:
                for k, v in waits:
                    keys.add(k)
                    if isinstance(k, str):
                        targets[k].add(v)
                if inc is not None and not isinstance(inc[0], str):
                    keys.add(inc[0])
        rank = {}
        for e in ENGS:
            srt = sorted(targets[e])
            rank[e] = {v: i + 1 for i, v in enumerate(srt)}
            if srt:
                keys.add(e)
        for k in sorted(keys, key=str):
            name = k if isinstance(k, str) else "d_%s_%d" % (k[1], k[2])
            self.sems[k] = stack.enter_context(nc.semaphore("s_" + name))
        block = stack.enter_context(nc.Block())

        def runner(ename):
            def run(engine):
                idx = 0
                for fn, waits, inc in self.streams[ename]:
                    for k, v in waits:
                        engine.wait_ge(self.sems[k], rank[k][v] if isinstance(k, str) else v)
                    if fn is not None:
                        name, a, kw = fn
                        ins = getattr(engine, name)(*a, **kw)
                        if isinstance(inc[0], str):
                            idx += 1
                            if idx in rank[ename]:
                                ins.then_inc(self.sems[ename], 1)
                        else:
                            ins.then_inc(self.sems[inc[0]], inc[1])
            return run

        block.tensor(runner("pe"))
        block.scalar(runner("act"))
        block.vector(runner("dve"))
        block.gpsimd(runner("pool"))
        block.sync(runner("sp"))


def build(n_layers=L, dbg=()):
    nc = bass.Bass("TRN2", target_bir_lowering=False)
    P = Prog(nc)

    def din(name, shape, dt=F32):
        return nc.dram_tensor(name, shape, dt, kind="ExternalInput").ap()

    def dscr(name, shape, dt=BF16):
        kind = "ExternalOutput" if name in dbg else "Internal"
        return nc.dram_tensor(name, shape, dt, kind=kind).ap()

    x_d = din("x", [S, D])
    ctx_d = din("ctx", [T, D])
    cT_d = din("cT", [128, 8, 2])
    wmod_d = din("w_mod", [L, D, 6 * D])
    bmod_d = din("bmodT", [L, 128, 48])
    gn_d = din("gnT", [128, 2 * L + 1, 8])
    win_d = din("w_in", [L, D, INW])
    convw_d = din("convwT", [L, 128, 4, CONV_K])
    convv_d = din("convvT", [L, 128, 3, 4])
    wco_d = din("w_conv_out", [L, 512, D])
    tint_d = din("tint", [L, 128, 8, 22 * 64])
    tfull_d = din("tfull", [L, 128, 8, 14 * 64])
    wno_d = din("w_na_out", [L, 512, D])
    qkg_d = din("qkgT", [L, 128, 2])
    wgo_d = din("w_gqa_out", [L, 512, D])
    wout_d = din("w_out", [L, D, D])
    wfi_d = din("w_ffn_in", [L, D, 2 * FFN])
    wfo_d = din("w_ffn_out", [L, FFN, D])
    rope_d = din("rope", [2, 128, S])
    cst_d = din("consts", [4, 128, 128])
    out_d = nc.dram_tensor("out", [S, D], F32, kind="ExternalOutput").ap()

    xT_d = dscr("xT", [D, ST], F32)
    UW = S + 2 * HALO
    UCW = T + 2 * HALO
    uT_d = dscr("uT", [512, UW + UCW])
    naq_d = dscr("naqT", [512, ST])
    nak_d = dscr("nakT", [512, ST])
    gqr_d = dscr("gqrT", [512, ST])
    gkr_d = dscr("gkrT", [128, ST])
    gat_d = dscr("gatesT", [3072, ST])
    vaug_d = dscr("vaug", [ST, 10, 128])
    ca_d = dscr("caT", [512, ST])
    nao_d = dscr("naoT", [512, ST])
    gqo_d = dscr("gqoT", [512, ST])

    with ExitStack() as top:
        uid = [0]

        def sbuf(st, name, shape, dt=F32):
            uid[0] += 1
            return st.enter_context(nc.sbuf_tensor("sb%d_%s" % (uid[0], name), shape, dt))

        psum_all = top.enter_context(nc.psum_tensor("pbanks", [128, 8, 512], F32))
        psum = [psum_all[:, i, :] for i in range(8)]
        prr = [0]

        def pbank():
            i = prr[0]
            prr[0] = (i + 1) % 8
            return i

        ident_f = sbuf(top, "ident_f", [128, 128])
        cst_b = sbuf(top, "cst_b", [128, 4, 128], BF16)
        modv = sbuf(top, "modv", [128, L, 48, 2])
        bmod = sbuf(top, "bmod", [128, L, 48])
        gn = sbuf(top, "gn", [128, 2 * L + 1, 8])
        Amod = sbuf(top, "Amod", [128, L, 2, 8, 2])
        rstd = sbuf(top, "rstd", [128, ST])
        stats = rstd
        sil = sbuf(top, "sil", [128, 8, 2])
        cT = sbuf(top, "cT_s", [128, 8, 2])
        zpad = sbuf(top, "zpad", [128, 4, HALO], BF16)
        convw = sbuf(top, "convw", [128, L, 4, CONV_K])
        convv = sbuf(top, "convv", [128, L, 3, 4])
        qkg = sbuf(top, "qkg", [128, L, 2])

        IDB = cst_b[:, 0, :]
        ONESB = cst_b[:, 1, :]
        BD64 = cst_b[:, 2, :]
        PERM = cst_b[:, 3, :]

        def ld(q, out, in_, writes, reads=()):
            P.dma(q, lambda e, o=out, i=in_: e.dma_start(out=o, in_=i), reads=reads, writes=writes)

        ld("sp", ident_f[:], cst_d[0], ["ident_f"])
        ld("pool", cst_b[:], cst_d.rearrange("c p n -> p c n"), ["cst_b"])
        ld("sp", bmod[:], bmod_d.rearrange("l p c -> p l c"), ["bmod"])
        ld("sp", gn[:], gn_d, ["gn"])
        ld("sp", cT[:], cT_d, ["cT"])
        ld("sp", convw[:], convw_d.rearrange("l p c k -> p l c k"), ["convw"])
        ld("sp", convv[:], convv_d.rearrange("l p a c -> p l a c"), ["convv"])
        ld("sp", qkg[:], qkg_d.rearrange("l p a -> p l a"), ["qkg"])
        P.op("pool", lambda e: e.memset(zpad[:], 0.0), writes=["zpad"])
        uT_v = uT_d.rearrange("(c p) t -> p c t", p=128)
        for off in (0, HALO + S, UW, UW + HALO + T):
            ld("sp", uT_v[:, :, off:off + HALO], zpad[:], [("uTpad", off)], reads=["zpad"])

        P.op("act", lambda e: e.activation(out=sil[:], in_=cT[:], func=AF.Exp, scale=-1.0), reads=["cT"], writes=["sil"])
        P.op("dve", lambda e: e.tensor_scalar_add(out=sil[:], in0=sil[:], scalar1=1.0), reads=["sil"], writes=["sil"])
        P.op("dve", lambda e: e.reciprocal(out=sil[:], in_=sil[:]), reads=["sil"], writes=["sil"])
        P.op("dve", lambda e: e.tensor_mul(out=sil[:], in0=sil[:], in1=cT[:]), reads=["sil", "cT"], writes=["sil"])
        with ExitStack() as ph:
            P.barrier()
            wm = sbuf(ph, "wm", [128, 2, 8, 1024])
            it = 0
            for l in range(n_layers):
                for cg in range(6):
                    sl = (it % 2) if not os.environ.get('WM1') else 0
                    it += 1
                    for k in range(8):
                        ld("sp", wm[:, sl, k, :], wmod_d[l, k * 128:(k + 1) * 128, cg * 1024:(cg + 1) * 1024],
                           [("wm", sl, k)])
                    for jc in range(8):
                        oc = cg * 8 + jc
                        pb = pbank()
                        for k in range(8):
                            P.op("pe", lambda e, pb=pb, sl=sl, k=k, jc=jc: e.matmul(
                                psum[pb][:, 0:2], wm[:, sl, k, jc * 128:(jc + 1) * 128], sil[:, k, :],
                                start=(k == 0), stop=(k == 7)),
                                reads=[("wm", sl, k), "sil"], writes=[("ps", pb)])
                        P.op("dve", lambda e, pb=pb, l=l, oc=oc: e.tensor_scalar(
                            out=modv[:, l, oc, :], in0=psum[pb][:, 0:2], scalar1=bmod[:, l, oc:oc + 1], scalar2=None,
                            op0=ALU.add), reads=[("ps", pb), "bmod"], writes=["modv"])
            for l in range(n_layers):
                for n in range(2):
                    for j in range(2):
                        P.op("dve", lambda e, l=l, n=n, j=j: e.scalar_tensor_tensor(
                            out=Amod[:, l, n, :, j], in0=modv[:, l, (8 + 24 * n):(16 + 24 * n), j], scalar=1.0,
                            in1=gn[:, 2 * l + n, :], op0=ALU.add, op1=ALU.mult),
                            reads=["modv", "gn"], writes=["Amod"])

        def shift_ap(l, n, k, j):
            return modv[:, l, 24 * n + k, j:j + 1]

        def gate_ap(l, n, k, j):
            return modv[:, l, 16 + 24 * n + k, j:j + 1]

        def stats_to_rstd(lo, hi, tag):
            P.op("act", lambda e: e.activation(out=rstd[:, lo:hi], in_=stats[:, lo:hi], func=AF.Ln, scale=1.0 / D, bias=EPS),
                 reads=["rstd"], writes=["rstd"])
            P.op("act", lambda e: e.activation(out=rstd[:, lo:hi], in_=rstd[:, lo:hi], func=AF.Exp, scale=-0.5),
                 reads=["rstd"], writes=["rstd"])

        def sumsq_block(src_tile_fn, src_res, t0, N, sqt, sqres):
            pb = pbank()
            for k in range(8):
                P.op("act", lambda e, k=k: e.activation(out=sqt[:, k, 0:N], in_=src_tile_fn(k), func=AF.Square),
                     reads=[src_res(k)], writes=[(sqres, k)])
                P.op("pe", lambda e, k=k, pb=pb: e.matmul(psum[pb][:, 0:N], ONESB, sqt[:, k, 0:N], start=(k == 0), stop=(k == 7)),
                     reads=[(sqres, k), "cst_b"], writes=[("ps", pb)])
            P.op("dve", lambda e, pb=pb: e.tensor_copy(out=stats[:, t0:t0 + N], in_=psum[pb][:, 0:N]),
                 reads=[("ps", pb)], writes=["rstd"])

        xT_v = xT_d.rearrange("(k p) t -> p k t", p=128)
        with ExitStack() as ph:
            P.barrier()
            xin = sbuf(ph, "xin", [128, 2, D])
            xst = sbuf(ph, "xst", [128, 2, 8, 512])
            sqt = sbuf(ph, "sqt0", [128, 8, 512], BF16)
            ti = 0
            for b in range(NBLK):
                t0, N = blk_info(b)
                sl = b % 2
                for tt in range(N // 128):
                    s2 = ti % 2
                    ti += 1
                    src = x_d[t0 + tt * 128:t0 + (tt + 1) * 128, :] if b < 8 else ctx_d[tt * 128:(tt + 1) * 128, :]
                    ld("sp", xin[:, s2, :], src, [("xin", s2)])
                    for kk in range(2):
                        pb = pbank()
                        for k4 in range(4):
                            k = kk * 4 + k4
                            P.op("pe", lambda e, pb=pb, k4=k4, k=k, s2=s2: e.transpose(
                                psum[pb][:, k4 * 128:(k4 + 1) * 128], xin[:, s2, k * 128:(k + 1) * 128], ident_f[:]),
                                reads=[("xin", s2), "ident_f"], writes=[("ps", pb)])
                        P.op("dve", lambda e, pb=pb, kk=kk, sl=sl, tt=tt: e.tensor_copy(
                            out=xst[:, sl, kk * 4:(kk + 1) * 4, tt * 128:(tt + 1) * 128],
                            in_=psum[pb][:, :].rearrange("p (k t) -> p k t", k=4)),
                            reads=[("ps", pb)], writes=[("xst", sl, kk)])
                ld("sp", xT_v[:, :, t0:t0 + N], xst[:, sl, :, 0:N], [("xT", b)], reads=[("xst", sl, 0), ("xst", sl, 1)])
                sumsq_block(lambda k, sl=sl, N=N: xst[:, sl, k, 0:N], lambda k, sl=sl: ("xst", sl, k // 4), t0, N, sqt, "sqt0")
        stats_to_rstd(0, ST, "n1l0")

        if "dump" in dbg:
            d1 = nc.dram_tensor("d_rstd", [128, ST], F32, kind="ExternalOutput").ap()
            d2 = nc.dram_tensor("d_modv", [128, L, 48, 2], F32, kind="ExternalOutput").ap()
            d3 = nc.dram_tensor("d_amod", [128, L, 2, 8, 2], F32, kind="ExternalOutput").ap()
            d4 = nc.dram_tensor("d_sil", [128, 8, 2], F32, kind="ExternalOutput").ap()
            ld("sp", d1, rstd[:], ["d1"], reads=["rstd"])
            ld("sp", d2, modv[:], ["d2"], reads=["modv"])
            ld("sp", d3, Amod[:], ["d3"], reads=["Amod"])
            ld("sp", d4, sil[:], ["d4"], reads=["sil"])
            P.wait_all("sp", P.final_tokens())
            P.emit(top)
            return nc
        def load_w_bf(dst_fn, src_fn, nk, res):
            for k in range(nk):
                ld("pool", dst_fn(k), src_fn(k), [(res, k)])

        def modulate_block(l, n, b, xt, xres, ht, hres, tmp, tmpres):
            t0, N = blk_info(b)
            j = 0 if b < 8 else 1
            for k in range(8):
                ts_ = k % 2
                P.op("dve", lambda e, k=k, ts_=ts_: e.tensor_tensor(out=tmp[:, ts_, 0:N], in0=xt(k), in1=rstd[:, t0:t0 + N], op=ALU.mult),
                     reads=[xres(k), "rstd"], writes=[(tmpres, ts_)])
                P.op("dve", lambda e, k=k, ts_=ts_: e.tensor_scalar(
                    out=ht[:, k, 0:N], in0=tmp[:, ts_, 0:N], scalar1=Amod[:, l, n, k, j:j + 1], scalar2=shift_ap(l, n, k, j),
                    op0=ALU.mult, op1=ALU.add), reads=[(tmpres, ts_), "Amod", "modv"], writes=[(hres, k)])

        for l in range(n_layers):
            last = (l == n_layers - 1) and (n_layers == L)
            nblk_q = 8 if last else NBLK
            with ExitStack() as ph:
                P.barrier()
                win = sbuf(ph, "win", [128, 8, INW], BF16)
                xt = sbuf(ph, "xtA", [128, 2, 8, 512])
                ht2 = sbuf(ph, "htA", [128, 2, 8, 512], BF16)
                tmp = sbuf(ph, "tmpA", [128, 2, 512])
                sg = sbuf(ph, "sgA", [128, 2, 512])
                stg = sbuf(ph, "stgA", [128, 2, 8, 512], BF16)
                vst = sbuf(ph, "vstA", [128, 2, 10, 128], BF16)
                WIN_GROUPS = [(512, 1024), (0, 512), (1024, 1536), (2560, 3072), (3328, 4352), (4352, 5376), (5376, 6400),
                              (1536, 2048), (3072, 3328), (2048, 2560)]

                def wgrp(col):
                    for gi, (c0, c1) in enumerate(WIN_GROUPS):
                        if c0 <= col < c1:
                            return gi
                    raise AssertionError(col)

                for gi, (c0, c1) in enumerate(WIN_GROUPS):
                    for k in range(8):
                        ld("pool", win[:, k, c0:c1], win_d[l, k * 128:(k + 1) * 128, c0:c1], [("win", k, gi)])
                for s2 in range(2):
                    P.op("dve", lambda e, s2=s2: e.memset(vst[:, s2, :, 64:128], 1.0), writes=[("vst", s2)])

                def loadA(b):
                    t0, N = blk_info(b)
                    ld("sp", xt[:, b % 2, :, 0:N], xT_v[:, :, t0:t0 + N], [("xtA", b % 2, k) for k in range(8)], reads=[("xT", b)])

                uT_lat = uT_v[:, :, HALO:HALO + S]
                uT_ctx = uT_v[:, :, UW + HALO:UW + HALO + T]
                naq_v = naq_d.rearrange("(c p) t -> p c t", p=128)
                nak_v = nak_d.rearrange("(c p) t -> p c t", p=128)
                gqr_v = gqr_d.rearrange("(c p) t -> p c t", p=128)
                gkr_v = gkr_d.rearrange("(c p) t -> p c t", p=128)
                gat_v = gat_d.rearrange("(c p) t -> p c t", p=128)
                vaug_v = vaug_d.rearrange("(n p) h f -> p n h f", p=128)
                stgi = [0]
                vsi = [0]
                loadA(0)
                for b in range(NBLK):
                    t0, N = blk_info(b)
                    if b + 1 < NBLK:
                        loadA(b + 1)
                    sl = b % 2
                    ht = ht2[:, sl]
                    hres = "htA%d" % sl
                    modulate_block(l, 0, b, lambda k, sl=sl, N=N: xt[:, sl, k, 0:N], lambda k, sl=sl: ("xtA", sl, k),
                                   ht, hres, tmp, "tmpA")

                    def proj(col, pb, N=N):
                        for k in range(8):
                            P.op("pe", lambda e, k=k: e.matmul(psum[pb][:, 0:N], win[:, k, col:col + 128], ht[:, k, 0:N],
                                                               start=(k == 0), stop=(k == 7)),
                                 reads=[("win", k, wgrp(col)), (hres, k)], writes=[("ps", pb)])

                    def group(col0, nch, dst, kind, dres):
                        ss = stgi[0] % 2
                        stgi[0] += 1
                        for c in range(nch):
                            pb = pbank()
                            proj(col0 + c * 128, pb)
                            if kind == "sig":
                                P.op("act", lambda e, pb=pb, c=c: e.activation(out=stg[:, ss, c, 0:N], in_=psum[pb][:, 0:N], func=AF.Sigmoid),
                                     reads=[("ps", pb)], writes=[("stgA", ss, c)])
                            elif kind == "copy":
                                P.op("act", lambda e, pb=pb, c=c: e.activation(out=stg[:, ss, c, 0:N], in_=psum[pb][:, 0:N], func=AF.Identity),
                                     reads=[("ps", pb)], writes=[("stgA", ss, c)])
                            elif kind == "scale":
                                P.op("act", lambda e, pb=pb, c=c: e.activation(out=stg[:, ss, c, 0:N], in_=psum[pb][:, 0:N], func=AF.Identity, scale=0.125),
                                     reads=[("ps", pb)], writes=[("stgA", ss, c)])
                        ld("sp", dst, stg[:, ss, 0:nch, 0:N], [dres], reads=[("stgA", ss, c) for c in range(nch)])

                    ss = stgi[0] % 2
                    stgi[0] += 1
                    for c in range(4):
                        pg = pbank()
                        proj(512 + c * 128, pg)
                        s2 = c % 2
                        P.op("act", lambda e, pg=pg, s2=s2: e.activation(out=sg[:, s2, 0:N], in_=psum[pg][:, 0:N], func=AF.Sigmoid),
                             reads=[("ps", pg)], writes=[("sgA", s2)])
                        pv = pbank()
                        proj(c * 128, pv)
                        P.op("dve", lambda e, pv=pv, s2=s2, c=c, ss=ss: e.tensor_tensor(out=stg[:, ss, c, 0:N], in0=psum[pv][:, 0:N], in1=sg[:, s2, 0:N], op=ALU.mult),
                             reads=[("ps", pv), ("sgA", s2)], writes=[("stgA", ss, c)])
                    udst = uT_lat[:, :, t0:t0 + N] if b < 8 else uT_ctx
                    ld("sp", udst, stg[:, ss, 0:4, 0:N], [("uT", b)], reads=[("stgA", ss, c) for c in range(4)])
                    if b < nblk_q:
                        group(1024, 4, naq_v[:, :, t0:t0 + N], "scale", ("naq", b))
                        group(2560, 4, gqr_v[:, :, t0:t0 + N], "copy", ("gqr", b))
                        for g3 in range(3):
                            group(3328 + g3 * 1024, 8, gat_v[:, g3 * 8:(g3 + 1) * 8, t0:t0 + N], "sig", ("gat", b, g3))
                    group(1536, 4, nak_v[:, :, t0:t0 + N], "copy", ("nak", b))
                    group(3072, 1, gkr_v[:, :, t0:t0 + N], "copy", ("gkr", b))
                    for tt in range(N // 128):
                        s2 = vsi[0] % 2
                        vsi[0] += 1
                        p1 = pbank()
                        p2 = pbank()
                        for k in range(8):
                            P.op("pe", lambda e, k=k, tt=tt, p1=p1: e.matmul(psum[p1][:, 0:512], ht[:, k, tt * 128:(tt + 1) * 128], win[:, k, 2048:2560],
                                                                        start=(k == 0), stop=(k == 7)),
                                 reads=[("win", k, wgrp(2048)), (hres, k)], writes=[("ps", p1)])
                        for k in range(8):
                            P.op("pe", lambda e, k=k, tt=tt, p2=p2: e.matmul(psum[p2][:, 0:128], ht[:, k, tt * 128:(tt + 1) * 128], win[:, k, 3200:3328],
                                                                        start=(k == 0), stop=(k == 7)),
                                 reads=[("win", k, wgrp(3200)), (hres, k)], writes=[("ps", p2)])
                        P.op("dve", lambda e, p1=p1, s2=s2: e.tensor_copy(out=vst[:, s2, 0:8, 0:64], in_=psum[p1][:, 0:512].rearrange("p (h d) -> p h d", h=8)),
                             reads=[("ps", p1)], writes=[("vst", s2)])
                        P.op("dve", lambda e, p2=p2, s2=s2: e.tensor_copy(out=vst[:, s2, 8:10, 0:64], in_=psum[p2][:, 0:128].rearrange("p (h d) -> p h d", h=2)),
                             reads=[("ps", p2)], writes=[("vst", s2)])
                        ti = t0 // 128 + tt
                        ld("sp", vaug_v[:, ti, :, :], vst[:, s2, :, :], [("vaug", ti)], reads=[("vst", s2)])

            if "A" in dbg and l == 0:
                break
            ca_v = ca_d.rearrange("(c p) t -> p c t", p=128)
            with ExitStack() as ph:
                P.barrier()
                dg = sbuf(ph, "dg", [128, 4, CONV_K, 128], BF16)
                ut = sbuf(ph, "utB", [128, 2, 4, 512 + 2 * HALO], BF16)
                yb = sbuf(ph, "ybB", [128, 4, 512])
                ybf = sbuf(ph, "ybfB", [128, 4, 512], BF16)
                ysq = sbuf(ph, "ysqB", [128, 4, 512], BF16)
                mu = sbuf(ph, "muB", [128, 512])
                var = sbuf(ph, "varB", [128, 512])
                t1 = sbuf(ph, "t1B", [128, 2, 512])
                t2 = sbuf(ph, "t2B", [128, 2, 512])
                cast = sbuf(ph, "castB", [128, 2, 4, 512], BF16)
                for c in range(4):
                    P.op("pool" if c % 2 else "dve", lambda e, c=c: e.tensor_tensor(
                        out=dg[:, c, :, :], in0=cst_b[:, 0:1, :].broadcast_to([128, CONV_K, 128]),
                        in1=convw[:, l, c, :].rearrange("p (k o) -> p k o", o=1).broadcast_to([128, CONV_K, 128]), op=ALU.mult),
                        reads=["cst_b", "convw"], writes=[("dg", c, kk) for kk in range(CONV_K)])

                def loadB(b):
                    t0, N = blk_info(b)
                    if b < 8:
                        src = uT_v[:, :, t0:t0 + N + 2 * HALO]
                        rd = [("uT", bb) for bb in (b - 1, b, b + 1) if 0 <= bb < 8] + [("uTpad", 0), ("uTpad", HALO + S)]
                    else:
                        src = uT_v[:, :, UW:UW + UCW]
                        rd = [("uT", 8), ("uTpad", UW), ("uTpad", UW + HALO + T)]
                    ld("sp", ut[:, b % 2, :, 0:N + 2 * HALO], src, [("utB", b % 2)], reads=rd)

                loadB(0)
                for b in range(nblk_q):
                    t0, N = blk_info(b)
                    if b + 1 < nblk_q:
                        loadB(b + 1)
                    sl = b % 2
                    ps1 = pbank()
                    ps2 = pbank()
                    for c in range(4):
                        pb = pbank()
                        for kk in range(CONV_K):
                            P.op("pe", lambda e, c=c, kk=kk, pb=pb: e.matmul(psum[pb][:, 0:N], dg[:, c, kk, :], ut[:, sl, c, kk:kk + N],
                                                                        start=(kk == 0), stop=(kk == CONV_K - 1)),
                                 reads=[("dg", c, kk), ("utB", sl)], writes=[("ps", pb)])
                        P.op("dve", lambda e, c=c, pb=pb: e.tensor_scalar(out=yb[:, c, 0:N], in0=psum[pb][:, 0:N], scalar1=convv[:, l, 0, c:c + 1], scalar2=None, op0=ALU.add),
                             reads=[("ps", pb), "convv"], writes=[("ybB", c)])
                        P.op("pool", lambda e, c=c: e.tensor_copy(out=ybf[:, c, 0:N], in_=yb[:, c, 0:N]), reads=[("ybB", c)], writes=[("ybfB", c)])
                        P.op("act", lambda e, c=c: e.activation(out=ysq[:, c, 0:N], in_=yb[:, c, 0:N], func=AF.Square), reads=[("ybB", c)], writes=[("ysqB", c)])
                    for c in range(4):
                        P.op("pe", lambda e, c=c: e.matmul(psum[ps1][:, 0:N], ONESB, ybf[:, c, 0:N], start=(c == 0), stop=(c == 3)),
                             reads=[("ybfB", c), "cst_b"], writes=[("ps", ps1)])
                    for c in range(4):
                        P.op("pe", lambda e, c=c: e.matmul(psum[ps2][:, 0:N], ONESB, ysq[:, c, 0:N], start=(c == 0), stop=(c == 3)),
                             reads=[("ysqB", c), "cst_b"], writes=[("ps", ps2)])
                    P.op("dve", lambda e: e.tensor_scalar(out=mu[:, 0:N], in0=psum[ps1][:, 0:N], scalar1=1.0 / 512, scalar2=None, op0=ALU.mult),
                         reads=[("ps", ps1)], writes=["muB"])
                    P.op("dve", lambda e: e.tensor_tensor(out=var[:, 0:N], in0=mu[:, 0:N], in1=mu[:, 0:N], op=ALU.mult), reads=["muB"], writes=["varB"])
                    P.op("dve", lambda e: e.scalar_tensor_tensor(out=var[:, 0:N], in0=psum[ps2][:, 0:N], scalar=1.0 / 512, in1=var[:, 0:N], op0=ALU.mult, op1=ALU.subtract),
                         reads=[("ps", ps2), "varB"], writes=["varB"])
                    P.op("act", lambda e: e.activation(out=var[:, 0:N], in_=var[:, 0:N], func=AF.Ln, bias=EPS), reads=["varB"], writes=["varB"])
                    P.op("act", lambda e: e.activation(out=var[:, 0:N], in_=var[:, 0:N], func=AF.Exp, scale=-0.5), reads=["varB"], writes=["varB"])
                    for c in range(4):
                        s2 = c % 2
                        P.op("pool", lambda e, c=c, s2=s2: e.tensor_tensor(out=t1[:, s2, 0:N], in0=yb[:, c, 0:N], in1=mu[:, 0:N], op=ALU.subtract),
                             reads=[("ybB", c), "muB"], writes=[("t1B", s2)])
                        P.op("pool", lambda e, s2=s2: e.tensor_tensor(out=t1[:, s2, 0:N], in0=t1[:, s2, 0:N], in1=var[:, 0:N], op=ALU.mult),
                             reads=[("t1B", s2), "varB"], writes=[("t1B", s2)])
                        P.op("dve", lambda e, c=c, s2=s2: e.tensor_scalar(out=t1[:, s2, 0:N], in0=t1[:, s2, 0:N], scalar1=convv[:, l, 1, c:c + 1], scalar2=convv[:, l, 2, c:c + 1],
                                                                      op0=ALU.mult, op1=ALU.add), reads=[("t1B", s2), "convv"], writes=[("t1B", s2)])
                        P.op("act", lambda e, s2=s2, c=c: e.activation(out=cast[:, sl, c, 0:N], in_=t1[:, s2, 0:N], func=AF.Silu), reads=[("t1B", s2)], writes=[("castB", sl, c)])
                    ld("sp", ca_v[:, :, t0:t0 + N], cast[:, sl, :, 0:N], [("ca", b)], reads=[("castB", sl, c) for c in range(4)])

            if "B" in dbg and l == 0:
                break
            ATTN(nc, P, l, nblk_q, psum_all, psum, pbank, sbuf, ld, cst_b, qkg, naq_d, nak_d, vaug_d, gqr_d, gkr_d, nao_d, gqo_d,
                 tint_d, tfull_d, rope_d, dbg)
            if "D" in dbg and l == 0:
                break
            nao_v = nao_d.rearrange("(c p) t -> p c t", p=128)
            gqo_v = gqo_d.rearrange("(c p) t -> p c t", p=128)
            with ExitStack() as ph:
                P.barrier()
                wco = sbuf(ph, "wco", [128, 4, D], BF16)
                wno = sbuf(ph, "wno", [128, 4, D], BF16)
                wgo = sbuf(ph, "wgo", [128, 4, D], BF16)
                wo = sbuf(ph, "wo", [128, 8, D], BF16)
                xt = sbuf(ph, "xtE", [128, 2, 8, 512])
                cat = sbuf(ph, "catE", [128, 2, 4, 512], BF16)
                nat = sbuf(ph, "natE", [128, 2, 4, 512], BF16)
                gqt = sbuf(ph, "gqtE", [128, 2, 4, 512], BF16)
                gt = sbuf(ph, "gtE", [128, 2, 24, 512], BF16)
                yt = sbuf(ph, "ytE", [128, 8, 512], BF16)
                m1 = sbuf(ph, "m1E", [128, 2, 512])
                m2 = sbuf(ph, "m2E", [128, 2, 512])
                sqt = sbuf(ph, "sqtE", [128, 8, 512], BF16)
                load_w_bf(lambda k: wco[:, k, :], lambda k: wco_d[l, k * 128:(k + 1) * 128, :], 4, "wco")
                load_w_bf(lambda k: wno[:, k, :], lambda k: wno_d[l, k * 128:(k + 1) * 128, :], 4, "wno")
                load_w_bf(lambda k: wgo[:, k, :], lambda k: wgo_d[l, k * 128:(k + 1) * 128, :], 4, "wgo")
                load_w_bf(lambda k: wo[:, k, :], lambda k: wout_d[l, k * 128:(k + 1) * 128, :], 8, "wo")

                def loadE(b):
                    t0, N = blk_info(b)
                    s = b % 2
                    ld("sp", xt[:, s, :, 0:N], xT_v[:, :, t0:t0 + N], [("xtE", s, k) for k in range(8)], reads=[("xT", b)])
                    ld("sp", cat[:, s, :, 0:N], ca_v[:, :, t0:t0 + N], [("catE", s)], reads=[("ca", b)])
                    ld("sp", nat[:, s, :, 0:N], nao_v[:, :, t0:t0 + N], [("natE", s)], reads=[("nao", b)])
                    ld("sp", gqt[:, s, :, 0:N], gqo_v[:, :, t0:t0 + N], [("gqtE", s)], reads=[("gqo", b, h) for h in range(8)])
                    for g3 in range(3):
                        ld("sp", gt[:, s, g3 * 8:(g3 + 1) * 8, 0:N], gat_v[:, g3 * 8:(g3 + 1) * 8, t0:t0 + N], [("gtE", s, g3)], reads=[("gat", b, g3)])

                loadE(0)
                for b in range(nblk_q):
                    t0, N = blk_info(b)
                    j = 0 if b < 8 else 1
                    if b + 1 < nblk_q:
                        loadE(b + 1)
                    s = b % 2
                    for fc in range(8):
                        fs = slice(fc * 128, (fc + 1) * 128)
                        s2 = fc % 2
                        pa = pbank()
                        for k in range(4):
                            P.op("pe", lambda e, k=k, pa=pa, fs=fs: e.matmul(psum[pa][:, 0:N], wco[:, k, fs], cat[:, s, k, 0:N], start=(k == 0), stop=(k == 3)),
                                 reads=[("wco", k), ("catE", s)], writes=[("ps", pa)])
                        P.op("dve", lambda e, pa=pa, fc=fc, s2=s2: e.tensor_tensor(out=m1[:, s2, 0:N], in0=psum[pa][:, 0:N], in1=gt[:, s, fc, 0:N], op=ALU.mult),
                             reads=[("ps", pa), ("gtE", s, 0)], writes=[("m1E", s2)])
                        pn = pbank()
                        for h in range(4):
                            P.op("pe", lambda e, h=h, pn=pn, fs=fs: e.matmul(psum[pn][:, 0:N], wno[:, h, fs], nat[:, s, h, 0:N], start=(h == 0), stop=(h == 3)),
                                 reads=[("wno", h), ("natE", s)], writes=[("ps", pn)])
                        P.op("dve", lambda e, pn=pn, fc=fc, s2=s2: e.tensor_tensor(out=m2[:, s2, 0:N], in0=psum[pn][:, 0:N], in1=gt[:, s, 8 + fc, 0:N], op=ALU.mult),
                             reads=[("ps", pn), ("gtE", s, 1)], writes=[("m2E", s2)])
                        P.op("pool", lambda e, s2=s2: e.tensor_tensor(out=m1[:, s2, 0:N], in0=m1[:, s2, 0:N], in1=m2[:, s2, 0:N], op=ALU.add),
                             reads=[("m1E", s2), ("m2E", s2)], writes=[("m1E", s2)])
                        pg = pbank()
                        for h in range(4):
                            P.op("pe", lambda e, h=h, pg=pg, fs=fs: e.matmul(psum[pg][:, 0:N], wgo[:, h, fs], gqt[:, s, h, 0:N], start=(h == 0), stop=(h == 3)),
                                 reads=[("wgo", h), ("gqtE", s)], writes=[("ps", pg)])
                        P.op("dve", lambda e, pg=pg, fc=fc, s2=s2: e.tensor_tensor(out=m2[:, s2, 0:N], in0=psum[pg][:, 0:N], in1=gt[:, s, 16 + fc, 0:N], op=ALU.mult),
                             reads=[("ps", pg), ("gtE", s, 2)], writes=[("m2E", s2)])
                        P.op("pool", lambda e, s2=s2, fc=fc: e.tensor_tensor(out=yt[:, fc, 0:N], in0=m1[:, s2, 0:N], in1=m2[:, s2, 0:N], op=ALU.add),
                             reads=[("m1E", s2), ("m2E", s2)], writes=[("ytE", fc)])
                    for fc in range(8):
                        fs = slice(fc * 128, (fc + 1) * 128)
                        po = pbank()
                        for k in range(8):
                            P.op("pe", lambda e, k=k, po=po, fs=fs: e.matmul(psum[po][:, 0:N], wo[:, k, fs], yt[:, k, 0:N], start=(k == 0), stop=(k == 7)),
                                 reads=[("wo", k), ("ytE", k)], writes=[("ps", po)])
                        P.op("dve", lambda e, po=po, fc=fc: e.scalar_tensor_tensor(out=xt[:, s, fc, 0:N], in0=psum[po][:, 0:N], scalar=gate_ap(l, 0, fc, j), in1=xt[:, s, fc, 0:N],
                                                                              op0=ALU.mult, op1=ALU.add),
                             reads=[("ps", po), "modv", ("xtE", s, fc)], writes=[("xtE", s, fc)])
                    ld("sp", xT_v[:, :, t0:t0 + N], xt[:, s, :, 0:N], [("xT", b)], reads=[("xtE", s, k) for k in range(8)])
                    sumsq_block(lambda k, s=s, N=N: xt[:, s, k, 0:N], lambda k, s=s: ("xtE", s, k), t0, N, sqt, "sqtE")
            stats_to_rstd(0, ST if nblk_q == NBLK else S, "n2")
            if "E" in dbg and l == 0:
                break
            with ExitStack() as ph:
                P.barrier()
                wfi = sbuf(ph, "wfi", [128, 8, 2 * FFN], BF16)
                wfo = sbuf(ph, "wfo", [128, 22, D], BF16)
                xt = sbuf(ph, "xtF", [128, 1, 8, 512])
                ht = sbuf(ph, "htF", [128, 8, 512], BF16)
                tmp = sbuf(ph, "tmpF", [128, 2, 512])
                hid = sbuf(ph, "hidF", [128, 22, 512], BF16)
                sgt = sbuf(ph, "sgF", [128, 2, 512], BF16)
                sqt = ht
                HG = [(0, 4), (4, 8), (8, 12), (12, 16), (16, 20), (20, 22)]

                def hgrp(hc):
                    return hc // 4

                for (h0, h1) in HG:
                    for base in (0, FFN):
                        for k in range(8):
                            ld("pool", wfi[:, k, base + h0 * 128:base + h1 * 128], wfi_d[l, k * 128:(k + 1) * 128, base + h0 * 128:base + h1 * 128],
                               [("wfi", k, base, h0 // 4)])
                load_w_bf(lambda k: wfo[:, k, :], lambda k: wfo_d[l, k * 128:(k + 1) * 128, :], 22, "wfo")

                def loadF(b):
                    t0, N = blk_info(b)
                    for k in range(8):
                        ld("sp", xt[:, 0, k, 0:N], xT_v[:, k, t0:t0 + N], [("xtF", 0, k)], reads=[("xT", b)])

                loadF(0)
                for b in range(nblk_q):
                    t0, N = blk_info(b)
                    j = 0 if b < 8 else 1
                    s = 0
                    modulate_block(l, 1, b, lambda k, s=s, N=N: xt[:, s, k, 0:N], lambda k, s=s: ("xtF", s, k), ht, "htF", tmp, "tmpF")
                    for hc in range(22):
                        s2 = hc % 2
                        pg = pbank()
                        for k in range(8):
                            P.op("pe", lambda e, k=k, pg=pg, hc=hc: e.matmul(psum[pg][:, 0:N], wfi[:, k, hc * 128:(hc + 1) * 128], ht[:, k, 0:N], start=(k == 0), stop=(k == 7)),
                                 reads=[("wfi", k, 0, hgrp(hc)), ("htF", k)], writes=[("ps", pg)])
                        P.op("act", lambda e, pg=pg, s2=s2: e.activation(out=sgt[:, s2, 0:N], in_=psum[pg][:, 0:N], func=AF.Silu),
                             reads=[("ps", pg)], writes=[("sgF", s2)])
                        pu = pbank()
                        for k in range(8):
                            P.op("pe", lambda e, k=k, pu=pu, hc=hc: e.matmul(psum[pu][:, 0:N], wfi[:, k, FFN + hc * 128:FFN + (hc + 1) * 128], ht[:, k, 0:N], start=(k == 0), stop=(k == 7)),
                                 reads=[("wfi", k, FFN, hgrp(hc)), ("htF", k)], writes=[("ps", pu)])
                        P.op("dve", lambda e, pu=pu, s2=s2, hc=hc: e.tensor_tensor(out=hid[:, hc, 0:N], in0=psum[pu][:, 0:N], in1=sgt[:, s2, 0:N], op=ALU.mult),
                             reads=[("ps", pu), ("sgF", s2)], writes=[("hidF", hc)])
                    pst = pbank()
                    for fc in range(8):
                        po = pbank()
                        if po == pst:
                            po = pbank()
                        for hc in range(22):
                            P.op("pe", lambda e, hc=hc, po=po, fc=fc: e.matmul(psum[po][:, 0:N], wfo[:, hc, fc * 128:(fc + 1) * 128], hid[:, hc, 0:N], start=(hc == 0), stop=(hc == 21)),
                                 reads=[("wfo", hc), ("hidF", hc)], writes=[("ps", po)])
                        P.op("dve", lambda e, po=po, fc=fc: e.scalar_tensor_tensor(out=xt[:, s, fc, 0:N], in0=psum[po][:, 0:N], scalar=gate_ap(l, 1, fc, j), in1=xt[:, s, fc, 0:N],
                                                                              op0=ALU.mult, op1=ALU.add),
                             reads=[("ps", po), "modv", ("xtF", s, fc)], writes=[("xtF", s, fc)])
                        ld("sp", xT_v[:, fc, t0:t0 + N], xt[:, s, fc, 0:N], [("xT", b)], reads=[("xtF", s, fc)])
                        P.op("act", lambda e, fc=fc: e.activation(out=sqt[:, fc, 0:N], in_=xt[:, s, fc, 0:N], func=AF.Square),
                             reads=[("xtF", s, fc)], writes=[("htF", fc)])
                        if fc > 0:
                            P.op("pe", lambda e, fc=fc, pst=pst: e.matmul(psum[pst][:, 0:N], ONESB, sqt[:, fc - 1, 0:N], start=(fc == 1), stop=False),
                                 reads=[("htF", fc - 1), "cst_b"], writes=[("ps", pst)])
                    P.op("pe", lambda e, pst=pst: e.matmul(psum[pst][:, 0:N], ONESB, sqt[:, 7, 0:N], start=False, stop=True),
                         reads=[("htF", 7), "cst_b"], writes=[("ps", pst)])
                    P.op("dve", lambda e, pst=pst: e.tensor_copy(out=stats[:, t0:t0 + N], in_=psum[pst][:, 0:N]),
                         reads=[("ps", pst)], writes=["rstd"])
                    if b + 1 < nblk_q:
                        loadF(b + 1)
            stats_to_rstd(0, ST if nblk_q == NBLK else S, "n1next")

        if not dbg:
            with ExitStack() as ph:
                P.barrier()
                xt = sbuf(ph, "xtZ", [128, 2, 8, 512])
                tmp = sbuf(ph, "tmpZ", [128, 2, 512])
                ot = sbuf(ph, "otZ", [128, 4, D])
                oi = 0
                ld("sp", xt[:, 0, :, :], xT_v[:, :, 0:512], [("xtZ", 0, k) for k in range(8)], reads=[("xT", 0)])
                for b in range(8):
                    t0 = b * 512
                    if b + 1 < 8:
                        ld("sp", xt[:, (b + 1) % 2, :, :], xT_v[:, :, t0 + 512:t0 + 1024], [("xtZ", (b + 1) % 2, k) for k in range(8)], reads=[("xT", b + 1)])
                    s = b % 2
                    for k in range(8):
                        s2 = k % 2
                        P.op("pool" if k % 2 == 0 else "dve", lambda e, k=k, s2=s2: e.tensor_tensor(out=tmp[:, s2, :], in0=xt[:, s, k, :], in1=rstd[:, t0:t0 + 512], op=ALU.mult),
                             reads=[("xtZ", s, k), "rstd"], writes=[("tmpZ", s2)])
                        P.op("dve", lambda e, k=k, s2=s2: e.tensor_scalar(out=xt[:, s, k, :], in0=tmp[:, s2, :], scalar1=gn[:, 2 * L, k:k + 1], scalar2=None, op0=ALU.mult),
                             reads=[("tmpZ", s2), "gn"], writes=[("xtZ", s, k)])
                    for tt in range(4):
                        so = oi % 4
                        oi += 1
                        for kk in range(2):
                            pb = pbank()
                            for k4 in range(4):
                                k = kk * 4 + k4
                                P.op("pe", lambda e, pb=pb, k4=k4, k=k, tt=tt: e.transpose(psum[pb][:, k4 * 128:(k4 + 1) * 128], xt[:, s, k, tt * 128:(tt + 1) * 128], ident_f[:]),
                                     reads=[("xtZ", s, k), "ident_f"], writes=[("ps", pb)])
                            P.op("act", lambda e, pb=pb, kk=kk, so=so: e.activation(out=ot[:, so, kk * 512:(kk + 1) * 512], in_=psum[pb][:, :], func=AF.Copy),
                                 reads=[("ps", pb)], writes=[("otZ", so, kk)])
                        ld("sp", out_d[t0 + tt * 128:t0 + (tt + 1) * 128, :], ot[:, so, :], [("out", b, tt)], reads=[("otZ", so, 0), ("otZ", so, 1)])
        P.wait_all("sp", P.final_tokens())
        P.emit(top)
    return nc


def ATTN(nc, P, l, nblk_q, psum_all, psum, pbank, sbuf, ld, cst_b, qkg, naq_d, nak_d, vaug_d, gqr_d, gkr_d, nao_d, gqo_d,
         tint_d, tfull_d, rope_d, dbg):
    IDB = cst_b[:, 0, :]
    BD64 = cst_b[:, 2, :]
    PERM = cst_b[:, 3, :]
    naq_v = naq_d.rearrange("(c p) t -> p c t", p=128)
    nak_v = nak_d.rearrange("(c p) t -> p c t", p=128)
    gqr_v = gqr_d.rearrange("(c p) t -> p c t", p=128)
    gkr_v = gkr_d.rearrange("(c p) t -> p c t", p=128)
    vaug_v = vaug_d.rearrange("(n p) h f -> p n h f", p=128)
    nao_v = nao_d.rearrange("(h d) t -> d h t", d=64)
    gqo_v = gqo_d.rearrange("(h d) t -> d h t", d=64)
    ACC = [0, 1]
    SPAIRS = [2, 4, 6]
    LA = 1
    cnt = {"acc": 0, "s": 0, "pt": 0, "rc": 0}

    def nxt(key, n):
        v = cnt[key] % n
        cnt[key] += 1
        return v

    def attend2(N, qk_A, qk_B, nchunks, pt, ptres, rc, rcres, outs, scale):
        assert nchunks % 2 == 0
        pos = [ACC[0], ACC[1]]
        pend = []

        def emit_pv(info):
            j, items = info
            for hh in range(2):
                pi, vls = items[hh]
                for t in range(2):
                    i = 2 * j + t
                    vl, vres, _ = vls[t]
                    P.op("pe", lambda e, vl=vl, pi=pi, i=i, t=t, hh=hh: e.matmul(psum[pos[hh]][:, 0:N], vl, pt[:, pi, t, 0:N], start=(i == 0), stop=(i == nchunks - 1)),
                         reads=list(vres) + [(ptres, pi)], writes=[("ps", pos[hh])])

        for j in range(nchunks // 2):
            sbs = [SPAIRS[nxt("s", 3)], SPAIRS[nxt("s", 3)]]
            vls = [[None, None], [None, None]]
            for t in range(2):
                order = (0, 1) if t == 0 else (1, 0)
                for hh in order:
                    vls[hh][t] = (qk_A if hh == 0 else qk_B)(2 * j + t, sbs[hh] + t)
                for hh in order:
                    if vls[hh][t][2] is not None:
                        vls[hh][t][2]()
            items = []
            for hh in range(2):
                sb = sbs[hh]
                pi = nxt("pt", 4)
                P.op("act", lambda e, sb=sb, pi=pi: e.activation(out=pt[:, pi, :, 0:N], in_=psum_all[:, sb:sb + 2, 0:N], func=AF.Exp, scale=scale),
                     reads=[("ps", sb), ("ps", sb + 1)], writes=[(ptres, pi)])
                items.append((pi, vls[hh]))
            pend.append((j, items))
            if len(pend) > LA:
                emit_pv(pend.pop(0))
        while pend:
            emit_pv(pend.pop(0))
        for hh in range(2):
            po = pos[hh]
            out_ap, out_res = outs[hh]
            r2 = nxt("rc", 2)
            P.op("dve", lambda e, po=po, r2=r2: e.tensor_copy(out=rc[:, r2, 0, 0:N], in_=psum[po][:, 0:N]), reads=[("ps", po)], writes=[(rcres, r2, 0)])
            P.op("dve", lambda e, r2=r2: e.reciprocal(out=rc[0:64, r2, 1, 0:N], in_=rc[64:128, r2, 0, 0:N]), reads=[(rcres, r2, 0)], writes=[(rcres, r2, 1)])
            P.op("pool", lambda e, r2=r2, out_ap=out_ap: e.tensor_tensor(out=out_ap, in0=rc[0:64, r2, 0, 0:N], in1=rc[0:64, r2, 1, 0:N], op=ALU.mult),
                 reads=[(rcres, r2, 0), (rcres, r2, 1)], writes=[out_res])

    with ExitStack() as ph:
        P.barrier()
        tint = sbuf(ph, "tint", [128, 8, 22 * 64], BF16)
        tfull = sbuf(ph, "tfull", [128, 8, 14 * 64], BF16)
        negt = sbuf(ph, "negt", [128, 256], BF16)
        kc = sbuf(ph, "kcC", [128, 4, T], BF16)
        vc = sbuf(ph, "vcC", [128, 2, 8, 128], BF16)
        qt = sbuf(ph, "qtC", [128, 2, 4, 512], BF16)
        kt = sbuf(ph, "ktC", [128, 2, 4, 1024], BF16)
        vt = sbuf(ph, "vtC", [128, 2, 8, 8, 128], BF16)
        pt = sbuf(ph, "ptC", [128, 4, 2, 512], BF16)
        rc = sbuf(ph, "rcC", [128, 2, 2, 512])
        ost = sbuf(ph, "ostC", [64, 2, 8, 512], BF16)
        for h in range(8):
            ld("pool", tint[:, h, :], tint_d[l, :, h, :], [("tint", h)])
            ld("pool", tfull[:, h, :], tfull_d[l, :, h, :], [("tfull", h)])
        P.op("pool", lambda e: e.memset(negt[:], NEG), writes=["negt"])
        ld("sp", kc[:], nak_v[:, :, S:ST], ["kcC"], reads=[("nak", 8)])
        ld("sp", vc[:], vaug_v[:, 32:34, 0:8, :], ["vcC"], reads=[("vaug", 32), ("vaug", 33)])

        def win(qb):
            a = 8 * qb
            lo = min(max(a - 4, 0), 56)
            hi = min(max(a + 3, 0), 56) + 7
            return a, lo // 2, hi // 2

        def loadC(qb):
            s = qb % 2
            t0, N = blk_info(qb)
            ld("sp", qt[:, s, :, 0:N], naq_v[:, :, t0:t0 + N], [("qtC", s)], reads=[("naq", qb)])
            if qb < 8:
                a, j0, j1 = win(qb)
                nj = j1 - j0 + 1
                blks = sorted(set([(j * 128) // 512 for j in range(j0, j1 + 1)]))
                ld("sp", kt[:, s, :, 0:nj * 128], nak_v[:, :, j0 * 128:(j1 + 1) * 128], [("ktC", s)], reads=[("nak", bb) for bb in blks])
                ld("sp", vt[:, s, 0:nj, :, :], vaug_v[:, j0:j1 + 1, 0:8, :], [("vtC", s)], reads=[("vaug", j) for j in range(j0, j1 + 1)])

        loadC(0)
        for qb in range(nblk_q):
            t0, N = blk_info(qb)
            if qb + 1 < nblk_q:
                loadC(qb + 1)
            s = qb % 2
            if qb < 8:
                a, j0, j1 = win(qb)
                nj = j1 - j0 + 1
            else:
                nj = 0
            for ci in range(4):
                def mk(h, ci=ci, nj=nj):
                    p0 = (h % 2) * 64

                    def qk_ops(i, ps):
                        if i < nj:
                            j = j0 + i
                            P.op("pe", lambda e: e.matmul(psum[ps][:, 0:N], kt[p0:p0 + 64, s, ci, i * 128:(i + 1) * 128], qt[p0:p0 + 64, s, ci, 0:N],
                                                          start=True, stop=False), reads=[("ktC", s), ("qtC", s)], writes=[("ps", ps)])
                            E0 = 2 * j - a + 7
                            segs = []
                            if a == 0:
                                if j <= 3:
                                    mf = 13 - E0
                                    segs.append((0, 256, tfull[:, h, mf * 64:(mf + 4) * 64], ("tfull", h)))
                                else:
                                    segs.append((0, 256, negt[:, 0:256], "negt"))
                                m0 = 17 - E0 + 4
                                segs.append((256, 512, tint[:, h, m0 * 64:(m0 + 4) * 64], ("tint", h)))
                            elif a == 56:
                                m0 = 17 - E0
                                segs.append((0, 320, tint[:, h, m0 * 64:(m0 + 5) * 64], ("tint", h)))
                                if j >= 28:
                                    mf = 13 - E0 + 5
                                    segs.append((320, 512, tfull[:, h, mf * 64:(mf + 3) * 64], ("tfull", h)))
                                else:
                                    segs.append((320, 512, negt[:, 0:192], "negt"))
                            else:
                                m0 = 17 - E0
                                segs.append((0, 512, tint[:, h, m0 * 64:(m0 + 8) * 64], ("tint", h)))

                            def post():
                                for si, (c0, c1, tab, tres) in enumerate(segs):
                                    P.op("pe", lambda e, c0=c0, c1=c1, tab=tab, si=si: e.matmul(psum[ps][:, c0:c1], IDB, tab, start=False, stop=(si == len(segs) - 1)),
                                         reads=["cst_b", tres], writes=[("ps", ps)])
                            return vt[:, s, i, h, :], [("vtC", s)], post
                        cj = i - nj
                        P.op("pe", lambda e: e.matmul(psum[ps][:, 0:N], kc[p0:p0 + 64, ci, cj * 128:(cj + 1) * 128], qt[p0:p0 + 64, s, ci, 0:N],
                                                      start=True, stop=True), reads=["kcC", ("qtC", s)], writes=[("ps", ps)])
                        return vc[:, cj, h, :], ["vcC"], None
                    return qk_ops

                hA, hB = 2 * ci, 2 * ci + 1
                attend2(N, mk(hA), mk(hB), nj + 2, pt, "ptC", rc, "rcC",
                        [(ost[:, s, hA, 0:N], ("ostC", s, hA)), (ost[:, s, hB, 0:N], ("ostC", s, hB))], 1.0)
            ld("sp", nao_v[:, :, t0:t0 + N], ost[:, s, :, 0:N], [("nao", qb)], reads=[("ostC", s, h) for h in range(8)])

    if "C" in dbg and l == 0:
        return
    with ExitStack() as ph:
        P.barrier()
        cs = sbuf(ph, "csD", [128, 2, S])
        gq = sbuf(ph, "gqD", [128, 4, ST], BF16)
        gk = sbuf(ph, "gkD", [128, ST], BF16)
        va = sbuf(ph, "vaD", [128, 34, 2, 128], BF16)
        raw = sbuf(ph, "rawD", [128, 2, 5, 512], BF16)
        sq = sbuf(ph, "sqD", [128, 4, 512], BF16)
        rs = sbuf(ph, "rsD", [128, 4, 512])
        qn = sbuf(ph, "qnD", [128, 4, 512], BF16)
        r1 = sbuf(ph, "r1D", [128, 4, 512])
        r2t = sbuf(ph, "r2D", [128, 4, 512])
        pt = sbuf(ph, "ptD", [128, 4, 2, 512], BF16)
        rc = sbuf(ph, "rcD", [128, 2, 2, 512])
        ost = sbuf(ph, "ostD", [64, 4, 512], BF16)
        ld("sp", cs[:], rope_d.rearrange("a p t -> p a t"), ["csD"])
        for half in range(2):
            ld("sp", va[:, half * 17:(half + 1) * 17, :, :], vaug_v[:, half * 17:(half + 1) * 17, 8:10, :], [("vaD", half)],
               reads=[("vaug", j) for j in range(half * 17, (half + 1) * 17)])

        def loadD(b):
            t0, N = blk_info(b)
            s = b % 2
            if b < nblk_q:
                ld("sp", raw[:, s, 0:4, 0:N], gqr_v[:, :, t0:t0 + N], [("rawD", s, c) for c in range(4)], reads=[("gqr", b)])
            ld("sp", raw[:, s, 4:5, 0:N], gkr_v[:, :, t0:t0 + N], [("rawD", s, 4)], reads=[("gkr", b)])

        items = [(bb, c) for bb in range(NBLK) for c in ([0, 1, 2, 3, 4] if bb < nblk_q else [4])]
        st = {}

        def stage1(ix):
            bb, c = items[ix]
            t0, N = blk_info(bb)
            s = bb % 2
            s2 = ix % 4
            P.op("act", lambda e: e.activation(out=sq[:, s2, 0:N], in_=raw[:, s, c, 0:N], func=AF.Square),
                 reads=[("rawD", s, c)], writes=[("sqD", s2)])
            p1 = pbank()
            P.op("pe", lambda e: e.matmul(psum[p1][:, 0:N], BD64, sq[:, s2, 0:N], start=True, stop=True),
                 reads=["cst_b", ("sqD", s2)], writes=[("ps", p1)])
            st[ix] = {"p1": p1}

        def stage2(ix):
            bb, c = items[ix]
            t0, N = blk_info(bb)
            s = bb % 2
            s2 = ix % 4
            p1 = st[ix]["p1"]
            P.op("act", lambda e: e.activation(out=rs[:, s2, 0:N], in_=psum[p1][:, 0:N], func=AF.Ln, scale=1.0 / 64, bias=EPS),
                 reads=[("ps", p1)], writes=[("rsD", s2)])
            P.op("act", lambda e: e.activation(out=rs[:, s2, 0:N], in_=rs[:, s2, 0:N], func=AF.Exp, scale=-0.5),
                 reads=[("rsD", s2)], writes=[("rsD", s2)])
            gi = 0 if c < 4 else 1
            dst = gq[:, c, t0:t0 + N] if c < 4 else gk[:, t0:t0 + N]
            dres = ("gqD", c, bb) if c < 4 else ("gkD", bb)
            if bb < 8:
                P.op("dve", lambda e: e.scalar_tensor_tensor(out=qn[:, s2, 0:N], in0=raw[:, s, c, 0:N], scalar=qkg[:, l, gi:gi + 1], in1=rs[:, s2, 0:N],
                                                             op0=ALU.mult, op1=ALU.mult),
                     reads=[("rawD", s, c), "qkg", ("rsD", s2)], writes=[("qnD", s2)])
                p2 = pbank()
                P.op("pe", lambda e: e.matmul(psum[p2][:, 0:N], PERM, qn[:, s2, 0:N], start=True, stop=True),
                     reads=["cst_b", ("qnD", s2)], writes=[("ps", p2)])
                P.op("dve" if ix % 2 == 0 else "pool", lambda e: e.tensor_tensor(out=r1[:, s2, 0:N], in0=qn[:, s2, 0:N], in1=cs[:, 0, t0:t0 + N], op=ALU.mult),
                     reads=[("qnD", s2), "csD"], writes=[("r1D", s2)])
                st[ix]["p2"] = p2
            else:
                P.op("dve", lambda e: e.scalar_tensor_tensor(out=dst, in0=raw[:, s, c, 0:N], scalar=qkg[:, l, gi:gi + 1], in1=rs[:, s2, 0:N],
                                                             op0=ALU.mult, op1=ALU.mult),
                     reads=[("rawD", s, c), "qkg", ("rsD", s2)], writes=[dres])

        def stage3(ix):
            bb, c = items[ix]
            if bb >= 8:
                return
            t0, N = blk_info(bb)
            s2 = ix % 4
            p2 = st[ix]["p2"]
            dst = gq[:, c, t0:t0 + N] if c < 4 else gk[:, t0:t0 + N]
            dres = ("gqD", c, bb) if c < 4 else ("gkD", bb)
            P.op("dve", lambda e: e.tensor_tensor(out=r2t[:, s2, 0:N], in0=psum[p2][:, 0:N], in1=cs[:, 1, t0:t0 + N], op=ALU.mult),
                 reads=[("ps", p2), "csD"], writes=[("r2D", s2)])
            P.op("pool", lambda e: e.tensor_tensor(out=dst, in0=r1[:, s2, 0:N], in1=r2t[:, s2, 0:N], op=ALU.add),
                 reads=[("r1D", s2), ("r2D", s2)], writes=[dres])

        loadD(0)
        if NBLK > 1:
            loadD(1)
        nit = len(items)
        for it in range(nit + 2):
            if it < nit:
                stage1(it)
            if 0 <= it - 1 < nit:
                stage2(it - 1)
            if 0 <= it - 2 < nit:
                stage3(it - 2)
            if it < nit and it >= 1 and items[it][0] != items[it - 1][0]:
                nb = items[it][0] + 1
                if nb < NBLK:
                    loadD(nb)
        oi = 0
        for qb in range(nblk_q):
            t0, N = blk_info(qb)
            chunks = list(range(34)) if qb < 8 else [32, 33]
            for jt in range(4):
                def mk(g, jt=jt, chunks=chunks):
                    p0 = g * 64

                    def qk_ops(i, ps):
                        c = chunks[i]
                        kb = c // 4 if c < 32 else 8
                        P.op("pe", lambda e: e.matmul(psum[ps][:, 0:N], gk[p0:p0 + 64, c * 128:(c + 1) * 128], gq[p0:p0 + 64, jt, t0:t0 + N], start=True, stop=True),
                             reads=[("gkD", kb), ("gqD", jt, qb)], writes=[("ps", ps)])
                        return va[:, c, g, :], [("vaD", c // 17)], None
                    return qk_ops

                soA = oi % 4
                soB = (oi + 1) % 4
                oi += 2
                attend2(N, mk(0), mk(1), len(chunks), pt, "ptD", rc, "rcD",
                        [(ost[:, soA, 0:N], ("ostD", soA)), (ost[:, soB, 0:N], ("ostD", soB))], 0.125)
                ld("sp", gqo_v[:, jt, t0:t0 + N], ost[:, soA, 0:N], [("gqo", qb, jt)], reads=[("ostD", soA)])
                ld("sp", gqo_v[:, jt + 4, t0:t0 + N], ost[:, soB, 0:N], [("gqo", qb, jt + 4)], reads=[("ostD", soB)])


def _host_prep(x, c, ctx, c_ctx, w_mod, b_mod, norm1_g, norm2_g, w_in, conv_w, conv_b, conv_ln_g, conv_ln_b, w_conv_out,
               na_rpb, w_na_out, q_norm_g, k_norm_g, w_gqa_out, w_out, w_ffn_in, w_ffn_out, final_g):
    f = np.float32
    x = np.asarray(x, f); ctx = np.asarray(ctx, f); c = np.asarray(c, f); c_ctx = np.asarray(c_ctx, f)
    B = x.shape[0]

    def pk(v):
        v = np.asarray(v, f)
        return np.ascontiguousarray(np.swapaxes(v.reshape(v.shape[:-1] + (-1, 128)), -1, -2))

    perm = np.concatenate([np.arange(0, 16), np.arange(32, 48), np.arange(16, 32), np.arange(48, 64)])
    w_in = np.asarray(w_in, f)
    cols = np.arange(INW)
    for j in range(4):
        for half in range(2):
            hh = j + 4 * half
            cols[2560 + j * 128 + half * 64:2560 + j * 128 + half * 64 + 64] = 2560 + hh * 64 + perm
    for g in range(2):
        cols[3072 + g * 64:3072 + g * 64 + 64] = 3072 + g * 64 + perm
    w_in_p = np.ascontiguousarray(w_in[:, :, cols])
    qg = np.asarray(q_norm_g, f)[:, perm]
    kg = np.asarray(k_norm_g, f)[:, perm]
    qkgT = np.stack([np.tile(qg, (1, 2)), np.tile(kg, (1, 2))], axis=-1)
    gnT = np.stack([pk(norm1_g[0]), pk(norm2_g[0]), pk(norm1_g[1]), pk(norm2_g[1]), pk(final_g)], axis=1)
    bmodT = pk(b_mod)
    convwT = np.ascontiguousarray(np.transpose(np.asarray(conv_w, f).reshape(L, CONV_K, 4, 128), (0, 3, 2, 1)))
    convvT = np.stack([pk(conv_b), pk(conv_ln_g), pk(conv_ln_b)], axis=2)
    rpb = np.asarray(na_rpb, f)
    kcg = np.arange(64)[:, None]
    cg = np.arange(64)[None, :]
    win0 = np.clip(cg - 8, 0, 48)
    colvalid = (kcg >= win0) & (kcg < win0 + 16)
    cidx = np.clip(kcg - cg + 15, 0, 30)

    def table(nm, etop, lo, hi):
        tab = np.full((L, 128, 8, nm, 64), NEG, f)
        for m in range(nm):
            e = etop - m
            for krr in range(2):
                dr = e + krr
                if lo <= dr <= hi:
                    vals = rpb[:, :, dr, :][:, :, cidx]
                    vals = np.where(colvalid[None, None], vals, f(NEG))
                    tab[:, krr * 64:(krr + 1) * 64, :, m, :] = np.transpose(vals, (0, 2, 1, 3))
        return np.ascontiguousarray(tab.reshape(L, 128, 8, nm * 64))

    tint = table(22, 17, 3, 10)
    tfull = table(14, 13, 0, 14)
    t = np.arange(S)
    prow = (t // GW).astype(np.float64)
    pcol = (t % GW).astype(np.float64)
    freqs = np.power(10000.0, -np.arange(0, 32, 2, dtype=np.float64) / 32).astype(np.float32).astype(np.float64)
    rope = np.zeros((2, 128, S), f)
    for p in range(128):
        dd = p % 64
        jf = dd % 16
        pos = prow if (dd // 16) % 2 == 0 else pcol
        ang = (pos.astype(np.float32) * freqs[jf].astype(np.float32)).astype(np.float32)
        rope[0, p] = np.cos(ang)
        rope[1, p] = np.sin(ang) * (-1.0 if dd < 32 else 1.0)
    consts = np.zeros((4, 128, 128), f)
    consts[0] = np.eye(128)
    consts[1] = 1.0
    consts[2, 0:64, 0:64] = 1.0
    consts[2, 64:128, 64:128] = 1.0
    for m in range(128):
        consts[3, m + 32 if (m % 64) < 32 else m - 32, m] = 1.0
    shared = dict(w_mod=np.asarray(w_mod, f), bmodT=bmodT, gnT=gnT, w_in=w_in_p, convwT=convwT, convvT=convvT,
                  w_conv_out=np.asarray(w_conv_out, f), tint=tint, tfull=tfull, w_na_out=np.asarray(w_na_out, f), qkgT=qkgT,
                  w_gqa_out=np.asarray(w_gqa_out, f), w_out=np.asarray(w_out, f), w_ffn_in=np.asarray(w_ffn_in, f),
                  w_ffn_out=np.asarray(w_ffn_out, f), rope=rope, consts=consts)
    cc = pk(c_ctx)
    in_maps = []
    for b in range(B):
        m = dict(shared)
        m["x"] = np.ascontiguousarray(x[b])
        m["ctx"] = np.ascontiguousarray(ctx[b])
        m["cT"] = np.ascontiguousarray(np.stack([pk(c[b]), cc], axis=-1))
        in_maps.append(m)
    return in_maps


_NC_CACHE = {}


def kernel(**inputs):
    in_maps = _host_prep(**inputs)
    if "nc" not in _NC_CACHE:
        _NC_CACHE["nc"] = build()
    res = run_bass_kernel_spmd(_NC_CACHE["nc"], in_maps, core_ids=list(range(8)))
    return np.stack([np.asarray(r["out"], np.float32) for r in res.results], axis=0)
```
